# Optimizing a Trainium2 kernel written in Bass

```python
import jax, jax.numpy as jnp
from jax import lax
import numpy as np

D_MODEL = 1024
BATCH = 8
SEQ = 4096
DEPTH = 4

D_MIX = D_MODEL
HG_HEADS = 4
HG_DK = 128
HG_DV = 128
HG_WIDTH = HG_HEADS * HG_DV
HG_CHUNK = 64
SA_HEADS = 4
SA_HEAD_DIM = 64
SA_WIDTH = SA_HEADS * SA_HEAD_DIM
IDX_HEADS = 8
IDX_DIM = 32
IDX_TOPK_MAX = 256
IDX_TOPK_DIV = 4
Q_BLOCK = 128
CV_WIDTH = D_MIX - HG_WIDTH - SA_WIDTH
CV_WIDTH_FILTER = 31
D_FF = 4 * D_MODEL
N_MOD = 6
EPS = 1e-6
NEG_BIG = -1e30
TINY = 1e-30

IN_SIZES = (
    HG_HEADS * HG_DK,
    HG_HEADS * HG_DK,
    HG_WIDTH,
    HG_WIDTH,
    SA_WIDTH,
    SA_WIDTH,
    SA_WIDTH,
    IDX_HEADS * IDX_DIM,
    IDX_DIM,
    IDX_HEADS,
    2 * CV_WIDTH,
)
D_IN = sum(IN_SIZES)

kernel_name = "hymba_hgrn2_dsa_conformer_trunk"


def rmsnorm(x, w):
    xf = x.astype(jnp.float32)
    y = xf * lax.rsqrt(jnp.mean(xf * xf, axis=-1, keepdims=True) + EPS)
    return (y * w.astype(jnp.float32)).astype(x.dtype)


def hgrn2_lower_bounds(lb_logits):
    p = jax.nn.softmax(lb_logits.astype(jnp.float32), axis=0)
    return jnp.cumsum(p, axis=0) - p[0]


def hgrn2_mixer(q_raw, f_raw, i_in, g_raw, lb, onorm_w):
    f32 = jnp.float32
    dt = q_raw.dtype
    bsz, L, _ = q_raw.shape
    lb = lb.astype(f32)
    fr = f_raw.astype(f32)
    q = jax.nn.silu(q_raw.astype(f32)) * (HG_DK ** -0.5)
    f = lb + (1.0 - lb) * jax.nn.sigmoid(fr)
    log_f = jnp.log(jnp.maximum(f, TINY))
    k = (1.0 - lb) * jax.nn.sigmoid(-fr)
    v = i_in.astype(f32)
    n_chunks = L // HG_CHUNK

    def to_chunks(t, d):
        return t.reshape(bsz, n_chunks, HG_CHUNK, HG_HEADS, d).transpose(1, 0, 3, 2, 4)

    causal = jnp.tril(jnp.ones((HG_CHUNK, HG_CHUNK), dtype=bool))

    def step(S, xs):
        qc, kc, gc, vc = xs
        G = jnp.cumsum(gc, axis=-2)
        o_inter = jnp.einsum('bhik,bhkv->bhiv', qc * jnp.exp(G), S)
        diff = G[:, :, :, None, :] - G[:, :, None, :, :]
        decay = jnp.exp(jnp.where(causal[:, :, None], diff, NEG_BIG))
        A = jnp.einsum('bhik,bhijk,bhjk->bhij', qc, decay, kc)
        o_intra = jnp.einsum('bhij,bhjv->bhiv', A, vc)
        G_last = G[:, :, -1:, :]
        S_new = jnp.exp(G_last[:, :, 0, :])[..., None] * S + jnp.einsum(
            'bhjk,bhjv->bhkv', kc * jnp.exp(G_last - G), vc)
        return S_new, o_inter + o_intra

    S0 = jnp.zeros((bsz, HG_HEADS, HG_DK, HG_DV), f32)
    _, o = lax.scan(step, S0, (to_chunks(q, HG_DK), to_chunks(k, HG_DK),
                               to_chunks(log_f, HG_DK), to_chunks(v, HG_DV)))
    o = o.transpose(1, 0, 3, 2, 4).reshape(bsz, L, HG_HEADS, HG_DV)
    o = o * lax.rsqrt(jnp.mean(o * o, axis=-1, keepdims=True) + EPS)
    o = o * onorm_w.astype(f32).reshape(HG_HEADS, HG_DV)
    o = o.reshape(bsz, L, HG_WIDTH) * jax.nn.silu(g_raw.astype(f32))
    return o.astype(dt)


def dsa_mixer(q, k, v, q_idx, k_idx, w_idx):
    f32 = jnp.float32
    dt = q.dtype
    bsz, L, _ = q.shape
    topk = min(IDX_TOPK_MAX, L // IDX_TOPK_DIV)
    n_blk = L // Q_BLOCK
    q = q.reshape(bsz, L, SA_HEADS, SA_HEAD_DIM)
    k = k.reshape(bsz, L, SA_HEADS, SA_HEAD_DIM)
    v = v.reshape(bsz, L, SA_HEADS, SA_HEAD_DIM)
    q_idx = q_idx.reshape(bsz, L, IDX_HEADS, IDX_DIM)
    w_idx = w_idx.astype(f32) * (IDX_HEADS ** -0.5)
    k_idx_f = k_idx.astype(f32)
    key_pos = jnp.arange(L, dtype=jnp.int32)

    def blocks(t):
        return t.reshape(bsz, n_blk, Q_BLOCK, *t.shape[2:]).swapaxes(0, 1)

    def one_block(xs):
        qb, qib, wb, t0 = xs
        qpos = t0 + jnp.arange(Q_BLOCK, dtype=jnp.int32)
        s = jnp.einsum('bqhd,bsd->bqhs', qib.astype(f32), k_idx_f) * (IDX_DIM ** -0.5)
        score = jnp.einsum('bqh,bqhs->bqs', wb, jax.nn.relu(s))
        admissible = key_pos[None, :] <= qpos[:, None]
        score = jnp.where(admissible[None], score, NEG_BIG)
        _, sel = lax.top_k(score, topk)
        valid = sel <= qpos[None, :, None]
        kg = jax.vmap(lambda kk, ii: kk[ii])(k, sel)
        vg = jax.vmap(lambda vv, ii: vv[ii])(v, sel)
        logits = jnp.einsum('bqhd,bqkhd->bqhk', qb.astype(f32), kg.astype(f32)) * (SA_HEAD_DIM ** -0.5)
        logits = jnp.where(valid[:, :, None, :], logits, NEG_BIG)
        p = jax.nn.softmax(logits, axis=-1)
        o = jnp.einsum('bqhk,bqkhd->bqhd', p, vg.astype(f32))
        return o.reshape(bsz, Q_BLOCK, SA_WIDTH).astype(dt)

    t0s = jnp.arange(n_blk, dtype=jnp.int32) * Q_BLOCK
    out = lax.map(one_block, (blocks(q), blocks(q_idx), blocks(w_idx), t0s))
    return out.swapaxes(0, 1).reshape(bsz, L, SA_WIDTH)


def conv_module(u, cv_w, cv_b, ln_w, ln_b):
    f32 = jnp.float32
    dt = u.dtype
    a, b = jnp.split(u, 2, axis=-1)
    h = a * jax.nn.sigmoid(b)
    h = lax.conv_general_dilated(
        h, cv_w[:, None, :], window_strides=(1,),
        padding=[(CV_WIDTH_FILTER - 1, 0)],
        dimension_numbers=('NWC', 'WIO', 'NWC'),
        feature_group_count=CV_WIDTH) + cv_b
    hf = h.astype(f32)
    mu = jnp.mean(hf, axis=-1, keepdims=True)
    var = jnp.mean(jnp.square(hf - mu), axis=-1, keepdims=True)
    hf = (hf - mu) * lax.rsqrt(var + EPS) * ln_w.astype(f32) + ln_b.astype(f32)
    return jax.nn.silu(hf).astype(dt)


def setup_inputs(seed: int = 0) -> dict:
    key = jax.random.key(seed)
    ks = jax.random.split(key, 17)
    f32 = jnp.float32
    n = lambda k, shape, s: jax.random.normal(k, shape, f32) * s
    return {
        "x": n(ks[0], (BATCH, SEQ, D_MODEL), 1.0),
        "c": n(ks[1], (BATCH, D_MODEL), 1.0),
        "ada_w": n(ks[2], (DEPTH, D_MODEL, N_MOD * D_MODEL), 0.5 * D_MODEL ** -0.5),
        "ada_b": n(ks[3], (DEPTH, N_MOD * D_MODEL), 0.02),
        "norm_mix_w": 1.0 + n(ks[4], (DEPTH, D_MODEL), 0.05),
        "norm_mlp_w": 1.0 + n(ks[5], (DEPTH, D_MODEL), 0.05),
        "w_in": n(ks[6], (DEPTH, D_MODEL, D_IN), D_MODEL ** -0.5),
        "hg_lb_logits": n(ks[7], (DEPTH, HG_HEADS * HG_DK), 1.0),
        "hg_onorm_w": 1.0 + n(ks[8], (DEPTH, HG_WIDTH), 0.05),
        "cv_w": n(ks[9], (DEPTH, CV_WIDTH_FILTER, CV_WIDTH), CV_WIDTH_FILTER ** -0.5),
        "cv_b": n(ks[10], (DEPTH, CV_WIDTH), 0.02),
        "cv_ln_w": 1.0 + n(ks[11], (DEPTH, CV_WIDTH), 0.05),
        "cv_ln_b": n(ks[12], (DEPTH, CV_WIDTH), 0.02),
        "w_out": n(ks[13], (DEPTH, D_MIX, D_MODEL), D_MIX ** -0.5),
        "mlp_w1": n(ks[14], (DEPTH, D_MODEL, D_FF), D_MODEL ** -0.5),
        "mlp_w2": n(ks[15], (DEPTH, D_FF, D_MODEL), D_FF ** -0.5),
        "final_norm_w": 1.0 + n(ks[16], (D_MODEL,), 0.05),
    }


def reference(x, c, ada_w, ada_b, norm_mix_w, norm_mlp_w, w_in, hg_lb_logits, hg_onorm_w,
              cv_w, cv_b, cv_ln_w, cv_ln_b, w_out, mlp_w1, mlp_w2, final_norm_w):
    offsets = np.cumsum(IN_SIZES)[:-1].tolist()
    lower_bounds = hgrn2_lower_bounds(hg_lb_logits)
    c_act = jax.nn.silu(c)
    for l in range(DEPTH):
        mod = c_act @ ada_w[l] + ada_b[l]
        sh1, sc1, g1, sh2, sc2, g2 = [m[:, None, :] for m in jnp.split(mod, N_MOD, axis=-1)]

        h = rmsnorm(x, norm_mix_w[l]) * (1.0 + sc1) + sh1
        proj = h @ w_in[l]
        (hq, hf, hi, hg, sq, sk, sv, iq, ik, iw, cu) = jnp.split(proj, offsets, axis=-1)
        o_a = hgrn2_mixer(hq, hf, hi, hg, lower_bounds[l], hg_onorm_w[l])
        o_b = dsa_mixer(sq, sk, sv, iq, ik, iw)
        o_c = conv_module(cu, cv_w[l], cv_b[l], cv_ln_w[l], cv_ln_b[l])
        mix = jnp.concatenate([o_a, o_b, o_c], axis=-1) @ w_out[l]
        x = x + g1 * mix

        h = rmsnorm(x, norm_mlp_w[l]) * (1.0 + sc2) + sh2
        ff = jnp.square(jax.nn.relu(h @ mlp_w1[l])) @ mlp_w2[l]
        x = x + g2 * ff
    return rmsnorm(x, final_norm_w)
```

```python
import numpy as np
from contextlib import ExitStack
import concourse.bass as bass
import concourse.mybir as mybir
from concourse.bass_utils import run_bass_kernel_spmd

F32 = mybir.dt.float32
BF16 = mybir.dt.bfloat16
U8 = mybir.dt.uint8
AF = mybir.ActivationFunctionType
ALU = mybir.AluOpType

D = 1024
DIN = 3624
DFF = 4096
EPS = 1e-6
NIT = 16
ENGS = ("tensor", "vector", "scalar", "gpsimd", "sync")
NDMA_SEMS = 24
import os as _os
_STOP = int(_os.environ.get('KSTOP', '99'))
_CUT = int(_os.environ.get('KCUT', '99'))
_SUB = int(_os.environ.get('KSUB', '99'))


class _Op:
    __slots__ = ("eng", "fn", "deps", "signals", "sem", "val", "is_dma")

    def __init__(self, eng, fn, is_dma=False):
        self.eng = eng
        self.fn = fn
        self.deps = []
        self.signals = False
        self.sem = None
        self.val = 0
        self.is_dma = is_dma


class _Slot:
    __slots__ = ("writer", "readers")

    def __init__(self):
        self.writer = None
        self.readers = []


class Sched:
    def __init__(self, nc, es):
        self.nc = nc
        self.q = {e: [] for e in ENGS}
        self.slots = {}
        self.phase_dmas = []
        self.esem = {e: es.enter_context(nc.semaphore("es_" + e)) for e in ENGS}
        self.dsem = {e: [es.enter_context(nc.semaphore("ds_%s_%d" % (e, i))) for i in range(NDMA_SEMS)]
                     for e in ("sync", "gpsimd")}
        self.cnt = {e: 0 for e in ENGS}
        self.dcnt = {e: [0] * NDMA_SEMS for e in self.dsem}
        self.drr = {e: 0 for e in self.dsem}
        self.dprev = {e: [None] * NDMA_SEMS for e in self.dsem}
        self.waited = {e: {} for e in ENGS}
        self.nops = 0

    def _slot(self, k):
        s = self.slots.get(k)
        if s is None:
            s = self.slots[k] = _Slot()
        return s

    def _add(self, op, reads, writes):
        deps = set()
        for k in reads:
            s = self._slot(k)
            if s.writer is not None:
                deps.add(s.writer)
            if k.startswith("ps"):
                for r in s.readers:
                    if r.eng != op.eng:
                        deps.add(r)
        for k in writes:
            s = self._slot(k)
            if s.writer is not None:
                deps.add(s.writer)
            for r in s.readers:
                deps.add(r)
        deps.discard(op)
        for d in deps:
            if d.eng == "tensor" and op.eng == "tensor":
                continue
            op.deps.append(d)
            d.signals = True
        for k in writes:
            s = self._slot(k)
            s.writer = op
            s.readers = []
        for k in reads:
            if k not in writes:
                self._slot(k).readers.append(op)
        self.q[op.eng].append(op)
        self.nops += 1
        return op

    cap = None

    def begin(self):
        self.cap = [[]]

    def unit(self):
        if self.cap is not None and self.cap[-1]:
            self.cap.append([])

    def end(self):
        u = [x for x in self.cap if x]
        self.cap = None
        return u

    def run_merged(self, streams):
        pos = [0] * len(streams)
        while True:
            best, bf = -1, 2.0
            for i, s in enumerate(streams):
                if pos[i] < len(s):
                    f = (pos[i] + 0.5) / len(s)
                    if f < bf:
                        best, bf = i, f
            if best < 0:
                break
            for item in streams[best][pos[best]]:
                if item[0] == "op":
                    self.op(*item[1:])
                else:
                    self.dma(item[1], item[2], item[3], item[4], item[5], **item[6])
            pos[best] += 1

    def op(self, eng, fn, reads=(), writes=()):
        if self.cap is not None:
            self.cap[-1].append(("op", eng, fn, tuple(reads), tuple(writes)))
            return None
        return self._add(_Op(eng, fn), list(reads), list(writes))

    def dma(self, eng, out, in_, reads=(), writes=(), **kw):
        if self.cap is not None:
            self.cap[-1].append(("dma", eng, out, in_, tuple(reads), tuple(writes), kw))
            return None
        fn = lambda e: e.dma_start(out=out, in_=in_, **kw)
        op = _Op(eng, fn, is_dma=True)
        op.signals = True
        self._add(op, list(reads), list(writes))
        self.phase_dmas.append(op)
        return op

    def flush(self):
        nc = self.nc
        fin = _Op("sync", None)
        fin.deps = list(self.phase_dmas)
        self.phase_dmas = []
        self.q["sync"].append(fin)
        for e in ENGS:
            for op in self.q[e]:
                if op.fn is None:
                    continue
                if op.is_dma:
                    j = self.drr[e]
                    self.drr[e] = (j + 1) % NDMA_SEMS
                    self.dcnt[e][j] += 16
                    op.sem = self.dsem[e][j]
                    op.val = self.dcnt[e][j]
                    prev = self.dprev[e][j]
                    if prev is not None:
                        op.deps.append(prev)
                    self.dprev[e][j] = op
                elif op.signals:
                    self.cnt[e] += 1
                    op.sem = self.esem[e]
                    op.val = self.cnt[e]
        q = self.q
        waited_all = self.waited

        def run(e):
            def body(eng):
                waited = waited_all[e]
                for op in q[e]:
                    for d in op.deps:
                        if d.sem is None:
                            continue
                        key = id(d.sem)
                        if waited.get(key, 0) < d.val:
                            eng.wait_ge(d.sem, d.val)
                            waited[key] = d.val
                    if op.fn is None:
                        continue
                    ins = op.fn(eng)
                    if op.signals:
                        ins.then_inc(op.sem, 16 if op.is_dma else 1)
            return body

        with nc.Block() as block:
            if q["tensor"]:
                block.tensor(run("tensor"))
            if q["vector"]:
                block.vector(run("vector"))
            if q["scalar"]:
                block.scalar(run("scalar"))
            if q["gpsimd"]:
                block.gpsimd(run("gpsimd"))
            block.sync(run("sync"))
        self.q = {e: [] for e in ENGS}


def build_nc(S, L, TOPK, dbg=False):
    NT = S // 128
    nc = bass.Bass("TRN2", target_bir_lowering=False)

    def din(name, shape, dt=F32):
        return nc.dram_tensor(name, list(shape), dt, kind="ExternalInput").ap()

    x_d = din("x", [S, D])
    cT_d = din("cT", [128, 8])
    adaw_d = din("ada_w", [L, D, 6 * D])
    adabT_d = din("ada_bT", [128, L * 48])
    nmixT_d = din("nmixT", [128, L * 8])
    nmlpT_d = din("nmlpT", [128, L * 8])
    fnw_d = din("fnw", [1, D])
    win_d = din("w_in", [L, D, DIN])
    wout_d = din("w_out", [L, D, D])
    w1_d = din("w1", [L, D, DFF])
    w2_d = din("w2", [L, DFF, D])
    lbT_d = din("lbT", [128, L * 4])
    onwT_d = din("onwT", [128, L * 4])
    cvw_d = din("cvw", [128, L * 62])
    cvb_d = din("cvb", [128, L * 2])
    lnw_d = din("lnw", [128, L * 2])
    lnb_d = din("lnb", [128, L * 2])
    out_d = nc.dram_tensor("out", [S, D], F32, kind="ExternalOutput").ap()
    hT_d = nc.dram_tensor("hT_scr", [NT, 128, 1024], BF16, kind="Internal").ap()
    cat_d = nc.dram_tensor("cat_scr", [NT, 128, 768], BF16, kind="Internal").ap()

    with ExitStack() as es:
        S_ = Sched(nc, es)

        def V(fn, r=(), w=()):
            return S_.op("vector", fn, r, w)

        def A(fn, r=(), w=()):
            return S_.op("scalar", fn, r, w)

        def G(fn, r=(), w=()):
            return S_.op("gpsimd", fn, r, w)

        def T(fn, r=(), w=()):
            return S_.op("tensor", fn, r, w)

        _fillregs = {}

        def fillreg(e, val):
            if val not in _fillregs:
                _fillregs[val] = e.to_reg(val)
            return _fillregs[val]

        def mm(out, lhsT, rhs, start, stop, r, w):
            return T(lambda e: e.matmul(out, lhsT, rhs, start=start, stop=stop, skip_group_check=True), r, w)

        def tr(out, in_, ident, r, w):
            return T(lambda e: e.transpose(out, in_, ident), r, w)

        def act(out, in_, func, r, w, **kw):
            return A(lambda e: e.activation(out, in_, func, **kw), r, w)

        def ld(out, in_, w, r=(), **kw):
            return S_.dma("sync", out, in_, reads=r, writes=w, **kw)

        def ldc(out, in_, w, r=()):
            return S_.dma("gpsimd", out, in_, reads=r, writes=w, max_dma_last_dim=2048)

        _uid = [0]

        def sb(stack, name, shape, dt):
            _uid[0] += 1
            return stack.enter_context(nc.sbuf_tensor("s%d_%s" % (_uid[0], name), list(shape), dt))

        pst = [es.enter_context(nc.psum_tensor("pst%d" % i, [128, 1024], F32)) for i in range(4)]

        def bank(i):
            return pst[i // 2][:, (i % 2) * 512:(i % 2 + 1) * 512]

        def bankb(i):
            return bank(i).bitcast(BF16)

        PS = lambda i: "ps%d" % i

        identF = sb(es, "identF", [128, 128], F32)
        identB = sb(es, "identB", [128, 128], BF16)
        onesM = sb(es, "onesM", [128, 128], F32)
        cT = sb(es, "cTs", [128, 8], F32)
        cact = sb(es, "cact", [128, 8], F32)
        modT = sb(es, "modT", [128, L * 48], F32)
        adabT = sb(es, "adabT", [128, L * 48], F32)
        nmixT = sb(es, "nmixT", [128, L * 8], F32)
        nmlpT = sb(es, "nmlpT", [128, L * 8], F32)
        G1T = sb(es, "G1T", [128, L * 8], F32)
        G2T = sb(es, "G2T", [128, L * 8], F32)
        lbT = sb(es, "lbT", [128, L * 4], F32)
        omlT = sb(es, "omlT", [128, L * 4], F32)
        lbtmp = sb(es, "lbtmp", [128, 8], F32)
        onwT = sb(es, "onwT", [128, L * 4], F32)
        cvw = sb(es, "cvw", [128, L * 62], F32)
        cvb = sb(es, "cvb", [128, L * 2], F32)
        lnw = sb(es, "lnw", [128, L * 2], F32)
        lnb = sb(es, "lnb", [128, L * 2], F32)
        tabA0 = sb(es, "tabA0", [128, NIT], F32)
        tabB0 = sb(es, "tabB0", [128, NIT], F32)
        thrneg = sb(es, "thrneg", [128, 1], F32)
        negh = sb(es, "negh", [128, 128], F32)
        omlh = sb(es, "omlh", [128, L * 4], F32)
        lbp = sb(es, "lbp", [128, L * 4], F32)
        onwh = sb(es, "onwh", [128, L * 4], F32)
        lnwh = sb(es, "lnwh", [128, L * 2], F32)
        lnbh = sb(es, "lnbh", [128, L * 2], F32)

        def modcol(l, j, k):
            c = l * 48 + j * 8 + k
            return modT[:, c:c + 1]

        with ExitStack() as ps_:
            stage = [sb(ps_, "adast%d" % i, [128, 8, 512], F32) for i in range(2)]
            rowS = [sb(ps_, "rowS%d" % i, [1, 512], F32) for i in range(2)]
            one1f = sb(ps_, "one1f", [1, 1], F32)
            G(lambda e: e.memset(one1f[:, :], 1.0), w=["one1f"])
            G(lambda e: e.memset(identF[:, :], 1.0), w=["identF"])
            G(lambda e: e.affine_select(identF[:, :], identF[:, :], [[-1, 128]], ALU.is_equal, fillreg(e, 0.0),
                                        base=0, channel_multiplier=1), r=["identF"], w=["identF"])
            V(lambda e: e.tensor_copy(identB[:, :], identF[:, :]), r=["identF"], w=["identB"])
            G(lambda e: e.memset(onesM[:, :], 1.0 / 256.0), w=["onesM"])
            G(lambda e: e.memset(thrneg[:, :], -1e29), w=["thrneg"])
            for n in range(NIT):
                a_n = 2.0 ** -(n + 2) if n < NIT - 1 else 2.0 ** -(NIT)
                b_n = 2.0 ** -(n + 1)
                G(lambda e, n=n, a_n=a_n: e.memset(tabA0[:, n:n + 1], a_n), w=["tabA0"])
                G(lambda e, n=n, b_n=b_n: e.memset(tabB0[:, n:n + 1], b_n), w=["tabB0"])
            for (dst, src, nm) in ((cT, cT_d, "cT"), (adabT, adabT_d, "adabT"), (nmixT, nmixT_d, "nmixT"),
                                   (nmlpT, nmlpT_d, "nmlpT"), (lbT, lbT_d, "lbT"), (onwT, onwT_d, "onwT"),
                                   (cvw, cvw_d, "cvw"), (cvb, cvb_d, "cvb"), (lnw, lnw_d, "lnw"),
                                   (lnb, lnb_d, "lnb")):
                ld(dst[:, :], src[:, :], w=[nm])
            act(cact[:, :], cT[:, :], AF.Silu, r=["cT"], w=["cact"])
            lb3 = lbT[:, :].rearrange("p (l h) -> p l h", h=4)
            act(lbT[:, :], lbT[:, :], AF.Exp, r=["lbT"], w=["lbT"])
            V(lambda e: e.tensor_copy(lbtmp[:, 0:4], lb3[:, 0, :]), r=["lbT"], w=["lbtmp"])
            for l in range(1, L):
                V(lambda e, l=l: e.tensor_tensor(lbtmp[:, 0:4], lbtmp[:, 0:4], lb3[:, l, :], ALU.add),
                  r=["lbT", "lbtmp"], w=["lbtmp"])
            V(lambda e: e.reciprocal(lbtmp[:, 4:8], lbtmp[:, 0:4]), r=["lbtmp"], w=["lbtmp2"])
            for l in range(L):
                V(lambda e, l=l: e.tensor_tensor(lb3[:, l, :], lb3[:, l, :], lbtmp[:, 4:8], ALU.mult),
                  r=["lbT", "lbtmp2"], w=["lbT"])
            V(lambda e: e.memset(lb3[:, 0, :], 0.0), r=["lbT"], w=["lbT"])
            for l in range(2, L):
                V(lambda e, l=l: e.tensor_tensor(lb3[:, l, :], lb3[:, l, :], lb3[:, l - 1, :], ALU.add),
                  r=["lbT"], w=["lbT"])
            V(lambda e: e.tensor_scalar(omlT[:, :], lbT[:, :], -1.0, 1.0, ALU.mult, ALU.add),
              r=["lbT"], w=["omlT"])
            V(lambda e: e.tensor_scalar(omlh[:, :], omlT[:, :], 0.5, None, ALU.mult), r=["omlT"], w=["omlh"])
            V(lambda e: e.tensor_tensor(lbp[:, :], lbT[:, :], omlh[:, :], ALU.add), r=["lbT", "omlh"], w=["lbp"])
            V(lambda e: e.tensor_scalar(onwh[:, :], onwT[:, :], 0.5, None, ALU.mult), r=["onwT"], w=["onwh"])
            V(lambda e: e.tensor_scalar(lnwh[:, :], lnw[:, :], 0.5, None, ALU.mult), r=["lnw"], w=["lnwh"])
            V(lambda e: e.tensor_scalar(lnbh[:, :], lnb[:, :], 0.5, None, ALU.mult), r=["lnb"], w=["lnbh"])
            G(lambda e: e.memset(negh[:, :], -0.5), w=["negh"])
            piece = 0
            for l in range(L):
                for pc in range(12):
                    st = stage[piece % 2]
                    sn = "adast%d" % (piece % 2)
                    ld(st[:, :, :], adaw_d[l, :, pc * 512:(pc + 1) * 512].rearrange("(k p) n -> p k n", p=128),
                       w=[sn])
                    rb = 2 + (piece % 2)
                    for k in range(8):
                        mm(bank(rb)[0:1, :], cact[:, k:k + 1], st[:, k, :], k == 0, k == 7, r=[sn, "cact"], w=[PS(rb)])
                    rw = rowS[piece % 2]
                    rwn = "rowS%d" % (piece % 2)
                    act(rw[0:1, :], bank(rb)[0:1, :], AF.Copy, r=[PS(rb)], w=[rwn])
                    for f in range(4):
                        col = l * 48 + pc * 4 + f
                        mm(bank(0)[:, col:col + 1], rw[0:1, f * 128:(f + 1) * 128], one1f[0:1, 0:1],
                           piece == 0 and f == 0, True, r=[rwn, "one1f"], w=[PS(0)])
                    piece += 1
            V(lambda e: e.tensor_tensor(modT[:, :], bank(0)[:, 0:L * 48], adabT[:, :], ALU.add),
              r=[PS(0), "adabT"], w=["modT"])
            m4 = modT[:, :].rearrange("p (l j k) -> p l j k", j=6, k=8)
            V(lambda e: e.scalar_tensor_tensor(G1T[:, :].rearrange("p (l k) -> p l k", k=8), m4[:, :, 1, :], 1.0,
                                               nmixT[:, :].rearrange("p (l k) -> p l k", k=8), ALU.add, ALU.mult),
              r=["modT", "nmixT"], w=["G1T"])
            V(lambda e: e.scalar_tensor_tensor(G2T[:, :].rearrange("p (l k) -> p l k", k=8), m4[:, :, 4, :], 1.0,
                                               nmlpT[:, :].rearrange("p (l k) -> p l k", k=8), ALU.add, ALU.mult),
              r=["modT", "nmlpT"], w=["G2T"])
            S_.flush()
        if _STOP <= 0:
            return nc

        def build_bc(bc, bcname, l, j, dtmp, dname, pbanks):
            for k in range(8):
                V(lambda e, k=k: e.tensor_scalar(dtmp[:, :], identF[:, :], modcol(l, j, k), None, ALU.mult),
                  r=["identF", "modT", dname], w=[dname])
                bk = pbanks[k // 4]
                mm(bank(bk)[:, (k % 4) * 128:(k % 4 + 1) * 128], onesM[:, :], dtmp[:, :], True, True,
                   r=["onesM", dname], w=[PS(bk)])
            for hh in range(2):
                A(lambda e, hh=hh: e.activation(bc[:, hh * 512:(hh + 1) * 512], bank(pbanks[hh]), AF.Copy, scale=256.0),
                  r=[PS(pbanks[hh])], w=[bcname])

        def norm_tile(xt_ap, xtname, xn, ssq, hT_out, hTname, GT_, l, jsh, pb, ncols=128, coff=0):
            act(xn[:, :], xt_ap, AF.Square, r=[xtname], w=["xn", "ssq"], accum_out=ssq[:, 0:1])
            V(lambda e: e.tensor_scalar(ssq[:, 1:2], ssq[:, 0:1], 1.0 / D, EPS, ALU.mult, ALU.add),
              r=["ssq"], w=["ssq1"])
            act(ssq[:, 2:3], ssq[:, 1:2], AF.Ln, r=["ssq1"], w=["ssq2"])
            act(ssq[:, 3:4], ssq[:, 2:3], AF.Exp, r=["ssq2"], w=["ssq3"], scale=-0.5)
            V(lambda e: e.tensor_scalar(xn[:, :], xt_ap, ssq[:, 3:4], None, ALU.mult),
              r=[xtname, "ssq3", "xn"], w=["xn"])
            for half in range(2):
                bk = pb[half]
                for k in range(4 * half, 4 * half + 4):
                    tr(bank(bk)[:, (k % 4) * 128:(k % 4 + 1) * 128], xn[:, k * 128:(k + 1) * 128], identF[:, :],
                       r=["xn", "identF"], w=[PS(bk)])
                for k in range(4 * half, 4 * half + 4):
                    act(hT_out[:, k, coff:coff + ncols], bank(bk)[:, (k % 4) * 128:(k % 4 + 1) * 128], AF.Identity,
                        r=[PS(bk), "G1T", "G2T", "modT"], w=[hTname],
                        scale=GT_[:, l * 8 + k:l * 8 + k + 1], bias=modcol(l, jsh, k))

        for l in range(L):
            src_d = x_d if l == 0 else out_d
            pl = ExitStack()
            pl.__enter__()
            WB = 1128
            wB = sb(pl, "wB", [128, 8, WB], BF16)
            woS = sb(pl, "woS", [128, 8, D], BF16)
            with ExitStack() as p1:
                WA = 2560
                wA = sb(p1, "wA", [128, 8, WA], BF16)
                dg = sb(p1, "dg", [128, 2, 31, 128], BF16)
                xt = [sb(p1, "xt%d" % i, [128, D], F32) for i in range(2)]
                xn = sb(p1, "xn", [128, D], F32)
                ssq = sb(p1, "ssq", [128, 4], F32)
                hT = [sb(p1, "hT%d" % i, [128, 8, 128], BF16) for i in range(2)]
                tht = [sb(p1, "tht%d" % i, [128, 512], F32) for i in range(2)]
                qs = [sb(p1, "qs%d" % i, [128, 512], F32) for i in range(2)]
                sg = [sb(p1, "sg%d" % i, [128, 512], F32) for i in range(2)]
                gs = [sb(p1, "gs%d" % i, [128, 512], F32) for i in range(3)]
                vtok = [sb(p1, "vtok%d" % i, [128, 512], BF16) for i in range(3)]
                hcur = [sb(p1, "hcur%d" % i, [128, 2, 128], BF16) for i in range(2)]
                kT = sb(p1, "kT", [128, 512], F32)
                fT = sb(p1, "fT", [128, 512], F32)
                GT = sb(p1, "GT", [128, 512], F32)
                t1 = sb(p1, "t1", [128, 512], F32)
                E = sb(p1, "E", [128, 512], F32)
                E4 = [sb(p1, "E4%d" % i, [128, 512], F32) for i in range(2)]
                qtl = [sb(p1, "qtl%d" % i, [128, 512], BF16) for i in range(2)]
                ktl = [sb(p1, "ktl%d" % i, [128, 512], BF16) for i in range(2)]
                khT = sb(p1, "khT", [128, 512], BF16)
                qhA = [sb(p1, "qhA%d" % i, [128, 512], BF16) for i in range(2)]
                qhB = [sb(p1, "qhB%d" % i, [128, 512], BF16) for i in range(2)]
                khtok = [sb(p1, "khtok%d" % i, [128, 512], BF16) for i in range(2)]
                ATm = sb(p1, "ATm", [128, 512], BF16)
                Sst = sb(p1, "Sst", [128, 512], F32)
                Sbf0 = sb(p1, "Sbf0", [128, 512], BF16)
                Sbf1 = sb(p1, "Sbf1", [128, 512], BF16)
                oa = sb(p1, "oa", [128, 512], F32)
                oab = sb(p1, "oab", [128, 512], BF16)
                ss4 = sb(p1, "ss4", [128, 16], F32)
                scanm = sb(p1, "scanm", [128, 512], F32)
                cmask = sb(p1, "cmask", [128, 512], U8)
                cmf = sb(p1, "cmf", [128, 512], F32)
                hbuf = sb(p1, "hbuf", [128, 2, 160], BF16)
                cc = [sb(p1, "cc%d" % i, [128, 256], F32) for i in range(2)]
                csq = sb(p1, "csq", [128, 256], F32)
                stt_ = [sb(p1, "stt%d" % i, [128, 256], F32) for i in range(2)]
                yv = sb(p1, "yv", [128, 256], F32)
                thy = sb(p1, "thy", [128, 256], F32)
                rs_b = [sb(p1, "rs_b%d" % i, [128, 128], F32) for i in range(2)]
                cat6 = [sb(p1, "cat6_%d" % i, [128, 6, 128], BF16) for i in range(2)]

                for (gn, d0, s0) in (("v", 1024, 1024), ("g", 1536, 1536), ("q", 0, 0), ("f", 512, 512), ("cu", 2048, 3112)):
                    for k in range(8):
                        ldc(wA[:, k, d0:d0 + 512], win_d[l, k * 128:(k + 1) * 128, s0:s0 + 512], w=["wA_" + gn])
                wst1 = [sb(p1, "wst%d" % i, [128, D], F32) for i in range(2)]
                g1bc = sb(p1, "g1bc", [128, D], F32)
                dtmp1 = sb(p1, "dtmp", [128, 128], F32)
                def prefetch_a2():
                    for k in range(8):
                        rows = slice(k * 128, (k + 1) * 128)
                        ldc(wB[:, k, 0:1024], win_d[l, rows, 2048:3072], w=["wB"])
                        ldc(wB[:, k, 1024:1032], win_d[l, rows, 3104:3112], w=["wB"])
                        for r3 in range(3):
                            ldc(wB[:, k, 1032 + 32 * r3:1064 + 32 * r3], win_d[l, rows, 3072:3104], w=["wB"])
                    build_bc(g1bc, "g1bc", l, 2, dtmp1, "dtmp", (0, 1))
                    for k in range(8):
                        ld(wst1[k % 2][:, :], wout_d[l, k * 128:(k + 1) * 128, :], w=["wst%d" % (k % 2)])
                        V(lambda e, k=k: e.tensor_tensor(woS[:, k, :], wst1[k % 2][:, :], g1bc[:, :], ALU.mult),
                          r=["wst%d" % (k % 2), "g1bc"], w=["woS"])
                for ct in range(2):
                    for j in range(31):
                        c = l * 62 + ct * 31 + j
                        V(lambda e, ct=ct, j=j, c=c: e.tensor_scalar(dg[:, ct, j, :], identF[:, :], cvw[:, c:c + 1],
                                                                     None, ALU.mult),
                          r=["identF", "cvw"], w=["dg"])
                G(lambda e: e.memset(scanm[:, :], 1.0), w=["scanm"])
                sm3 = scanm[:, :].rearrange("p (c j) -> p c j", j=64)
                G(lambda e: e.memset(sm3[:, :, 0:1], 0.0), r=["scanm"], w=["scanm"])
                G(lambda e: e.memset(cmf[:, :], 1.0), w=["cmf"])
                for h in range(4):
                    G(lambda e, h=h: e.affine_select(cmf[:, h * 128:(h + 1) * 128], cmf[:, h * 128:(h + 1) * 128],
                                                     [[1, 128]], ALU.is_ge, fillreg(e, 0.0), base=0, channel_multiplier=-1),
                      r=["cmf"], w=["cmf"])
                    G(lambda e, h=h: e.memset(cmf[0:64, h * 128 + 64:(h + 1) * 128], 0.0), r=["cmf"], w=["cmf"])
                V(lambda e: e.tensor_copy(cmask[:, :], cmf[:, :]), r=["cmf"], w=["cmask"])
                for (tl, nm) in ((ATm, "ATm"), (qhA[0], "qhA0"), (qhA[1], "qhA1"), (qhB[0], "qhB0"), (qhB[1], "qhB1"),
                                 (Sst, "Sst"), (Sbf0, "Sbf0"), (Sbf1, "Sbf1")):
                    G(lambda e, tl=tl: e.memset(tl[:, :], 0.0), w=[nm])
                G(lambda e: e.memset(hbuf[:, :, :], 0.0), w=["hbuf"])

                bc4 = lambda tl_: tl_[:, l * 4:(l + 1) * 4].rearrange("p (h o) -> p h o", o=1).broadcast_to([128, 4, 128])
                lb_b, oml_b = bc4(lbT), bc4(omlT)
                v3 = lambda t_: t_[:, :].rearrange("p (h t) -> p h t", t=128)
                c3 = lambda t_: t_[:, :].rearrange("p (c j) -> p c j", j=64)
                c4 = lambda t_: t_[:, :].rearrange("p (h c j) -> p h c j", c=2, j=64)
                QSC = 128.0 ** -0.5
                st1 = {"th": 0}

                def nxt_th():
                    i = st1["th"] % 2
                    st1["th"] += 1
                    return tht[i], "tht%d" % i

                def sigm(dst, src_ap, r, w):
                    act(dst, src_ap, AF.Exp, r=r, w=w, scale=-1.0)
                    act(dst, dst, AF.Ln, r=w, w=w, bias=1.0)
                    act(dst, dst, AF.Exp, r=w, w=w, scale=-1.0)

                def stageX(t):
                    b = t % 2
                    b3 = t % 3
                    xtn, hTn = "xt%d" % b, "hT%d" % b
                    if t + 1 < NT:
                        ld(xt[1 - b][:, :], src_d[(t + 1) * 128:(t + 2) * 128, :], w=["xt%d" % (1 - b)])
                    norm_tile(xt[b][:, :], xtn, xn, ssq, hT[b], hTn, G1T, l, 0, (0, 0))
                    ld(hT_d[t, :, :], hT[b][:, :, :].rearrange("p k n -> p (k n)"), r=[hTn], w=["hTd%d" % t])
                    S_.unit()
                    for k in range(8):
                        mm(bank(1), hT[b][:, k, :], wA[:, k, 1024:1536], k == 0, k == 7, r=[hTn, "wA_v"], w=[PS(1)])
                    act(vtok[b3][:, :], bank(1), AF.Copy, r=[PS(1)], w=["vtok%d" % b3])
                    S_.unit()
                    for k in range(8):
                        mm(bank(2), hT[b][:, k, :], wA[:, k, 1536:2048], k == 0, k == 7, r=[hTn, "wA_g"], w=[PS(2)])
                    th_, thn_ = nxt_th()
                    sigm(th_[:, :], bank(2), [PS(2)], [thn_])
                    V(lambda e, th_=th_: e.tensor_tensor(gs[b3][:, :], th_[:, :], bank(2), ALU.mult),
                      r=[thn_, PS(2)], w=["gs%d" % b3])
                    S_.unit()
                    for f in range(4):
                        for k in range(8):
                            mm(bank(1)[:, f * 128:(f + 1) * 128], wA[:, k, f * 128:(f + 1) * 128], hT[b][:, k, :],
                               k == 0, k == 7, r=[hTn, "wA_q"], w=[PS(1)])
                    th_, thn_ = nxt_th()
                    sigm(th_[:, :], bank(1), [PS(1)], [thn_])
                    V(lambda e, th_=th_: e.tensor_tensor(qs[b][:, :], th_[:, :], bank(1), ALU.mult),
                      r=[thn_, PS(1)], w=["qs%d" % b])
                    S_.unit()
                    for f in range(4):
                        for k in range(8):
                            mm(bank(2)[:, f * 128:(f + 1) * 128], wA[:, k, 512 + f * 128:512 + (f + 1) * 128], hT[b][:, k, :],
                               k == 0, k == 7, r=[hTn, "wA_f"], w=[PS(2)])
                    sigm(sg[b][:, :], bank(2), [PS(2)], ["sg%d" % b])
                    S_.unit()
                    for f in range(4):
                        for k in range(8):
                            mm(bank(1)[:, f * 128:(f + 1) * 128], wA[:, k, 2048 + f * 128:2048 + (f + 1) * 128], hT[b][:, k, :],
                               k == 0, k == 7, r=[hTn, "wA_cu"], w=[PS(1)])
                    th_, thn_ = nxt_th()
                    sigm(th_[:, 0:256], bank(1)[:, 256:512], [PS(1)], [thn_])
                    V(lambda e, th_=th_: e.tensor_tensor(hcur[b][:, :, :].rearrange("p c t -> p (c t)"), th_[:, 0:256],
                                                         bank(1)[:, 0:256], ALU.mult),
                      r=[thn_, PS(1)], w=["hcur%d" % b])

                def stageY1(t):
                    b = t % 2
                    b3 = t % 3
                    qsn, sgn_, gsn, vtn, hcn, c6n = "qs%d" % b, "sg%d" % b, "gs%d" % b3, "vtok%d" % b3, "hcur%d" % b, "cat6_%d" % b
                    qs_, sg_, gs_, vt_, c6 = qs[b], sg[b], gs[b3], vtok[b3], cat6[b]
                    qtl_, ktl_, khtok_, qhA_, qhB_, E4_ = qtl[b], ktl[b], khtok[b], qhA[b], qhB[b], E4[b]
                    cc_, stt2, rsb_ = cc[b], stt_[b], rs_b[b]
                    qtln, ktln, khtokn, qhAn, qhBn, E4n = "qtl%d" % b, "ktl%d" % b, "khtok%d" % b, "qhA%d" % b, "qhB%d" % b, "E4%d" % b
                    ccn, sttn, rsbn = "cc%d" % b, "stt%d" % b, "rs_b%d" % b
                    G(lambda e: e.tensor_copy(hbuf[:, :, 32:160], hcur[b][:, :, :]), r=[hcn, "hbuf"], w=["hbuf"])
                    for ct in range(2):
                        for j in range(31):
                            mm(bank(3)[:, ct * 128:(ct + 1) * 128], dg[:, ct, j, :],
                               hbuf[:, ct, 2 + j:2 + j + 128], j == 0, j == 30, r=["dg", "hbuf"], w=[PS(3)])
                        S_.unit()
                    for ct in range(2):
                        cs = slice(ct * 128, (ct + 1) * 128)
                        act(cc_[:, cs], bank(3)[:, cs], AF.Identity,
                            r=[PS(3), "cvb"], w=[ccn], bias=cvb[:, l * 2 + ct:l * 2 + ct + 1])
                    act(csq[:, :], cc_[:, :], AF.Square, r=[ccn], w=["csq"])
                    G(lambda e: e.tensor_copy(hbuf[:, :, 0:32], hbuf[:, :, 128:160]), r=["hbuf", PS(3)], w=["hbuf"])
                    S_.unit()
                    for (si, src_, sn) in ((0, cc_, ccn), (1, csq, "csq")):
                        for ct in range(2):
                            mm(bank(4)[:, si * 128:(si + 1) * 128], onesM[:, :], src_[:, ct * 128:(ct + 1) * 128],
                               ct == 0, ct == 1, r=["onesM", sn], w=[PS(4)])
                    act(stt2[:, :], bank(4)[:, 0:256], AF.Copy, r=[PS(4)], w=[sttn])
                    V(lambda e: e.tensor_tensor(rsb_[:, :], stt2[:, 0:128], stt2[:, 0:128], ALU.mult), r=[sttn], w=[rsbn])
                    V(lambda e: e.tensor_tensor(rsb_[:, :], stt2[:, 128:256], rsb_[:, :], ALU.subtract),
                      r=[sttn, rsbn], w=[rsbn])
                    V(lambda e: e.tensor_scalar(rsb_[:, :], rsb_[:, :], EPS, None, ALU.add), r=[rsbn], w=[rsbn])
                    S_.unit()
                    V(lambda e: e.tensor_tensor(v3(t1), v3(sg_), oml_b, ALU.mult), r=[sgn_, "omlT"], w=["t1"])
                    V(lambda e: e.tensor_tensor(v3(fT), v3(t1), lb_b, ALU.add), r=["t1", "lbT"], w=["fT"])
                    V(lambda e: e.tensor_tensor(v3(kT), oml_b, v3(t1), ALU.subtract), r=["t1", "omlT"], w=["kT"])
                    V(lambda e: e.tensor_scalar(fT[:, :], fT[:, :], 1e-30, None, ALU.max), r=["fT"], w=["fT"])
                    act(fT[:, :], fT[:, :], AF.Ln, r=["fT"], w=["fT"])
                    act(rsb_[:, :], rsb_[:, :], AF.Ln, r=[rsbn], w=[rsbn])
                    act(rsb_[:, :], rsb_[:, :], AF.Exp, r=[rsbn], w=[rsbn], scale=-0.5)
                    S_.unit()
                    V(lambda e: e.tensor_tensor_scan(GT[:, :], scanm[:, :], fT[:, :], 0.0, ALU.mult, ALU.add),
                      r=["scanm", "fT"], w=["GT"])
                    V(lambda e: e.tensor_tensor(c3(t1), c3(GT), c3(GT)[:, :, 32:33].broadcast_to([128, 8, 64]),
                                                ALU.subtract), r=["GT"], w=["t1"])
                    act(E[:, :], t1[:, :], AF.Exp, r=["t1"], w=["E"])
                    V(lambda e: e.scalar_tensor_tensor(qtl_[:, :], qs_[:, :], QSC, E[:, :], ALU.mult, ALU.mult),
                      r=[qsn, "E"], w=[qtln])
                    S_.unit()
                    act(E[:, :], t1[:, :], AF.Exp, r=["t1", qtln], w=["E"], scale=-1.0)
                    V(lambda e: e.tensor_tensor(ktl_[:, :], kT[:, :], E[:, :], ALU.mult), r=["kT", "E"], w=[ktln])
                    V(lambda e: e.tensor_tensor(c3(t1), c3(GT)[:, :, 63:64].broadcast_to([128, 8, 64]), c3(GT),
                                                ALU.subtract), r=["GT", "E"], w=["t1"])
                    S_.unit()
                    act(E[:, :], t1[:, :], AF.Exp, r=["t1", ktln], w=["E"])
                    V(lambda e: e.tensor_tensor(khT[:, :], kT[:, :], E[:, :], ALU.mult), r=["kT", "E"], w=["khT"])
                    act(E4_[:, :], GT[:, :], AF.Exp, r=["GT"], w=[E4n])
                    S_.unit()
                    V(lambda e: e.scalar_tensor_tensor(c4(qhA_)[:, :, 0, :], c4(qs_)[:, :, 0, :], QSC, c4(E4_)[:, :, 0, :],
                                                       ALU.mult, ALU.mult), r=[qsn, E4n], w=[qhAn])
                    V(lambda e: e.scalar_tensor_tensor(c4(qhB_)[:, :, 1, :], c4(qs_)[:, :, 1, :], QSC, c4(E4_)[:, :, 1, :],
                                                       ALU.mult, ALU.mult), r=[qsn, E4n], w=[qhBn])
                    for h in range(4):
                        tr(bankb(4)[:, h * 128:(h + 1) * 128], khT[:, h * 128:(h + 1) * 128], identB[:, :],
                           r=["khT", "identB"], w=[PS(4)])
                    act(khtok_[:, :], bankb(4)[:, 0:512], AF.Copy, r=[PS(4)], w=[khtokn])
                    S_.unit()

                def stageY2(t):
                    b = t % 2
                    b3 = t % 3
                    qsn, sgn_, gsn, vtn, hcn, c6n = "qs%d" % b, "sg%d" % b, "gs%d" % b3, "vtok%d" % b3, "hcur%d" % b, "cat6_%d" % b
                    qs_, sg_, gs_, vt_, c6 = qs[b], sg[b], gs[b3], vtok[b3], cat6[b]
                    qtl_, ktl_, khtok_, qhA_, qhB_, E4_ = qtl[b], ktl[b], khtok[b], qhA[b], qhB[b], E4[b]
                    cc_, stt2, rsb_ = cc[b], stt_[b], rs_b[b]
                    qtln, ktln, khtokn, qhAn, qhBn, E4n = "qtl%d" % b, "ktl%d" % b, "khtok%d" % b, "qhA%d" % b, "qhB%d" % b, "E4%d" % b
                    ccn, sttn, rsbn = "cc%d" % b, "stt%d" % b, "rs_b%d" % b
                    for h in range(4):
                        mm(bank(5)[:, h * 128:(h + 1) * 128], ktl_[:, h * 128:(h + 1) * 128],
                           qtl_[:, h * 128:(h + 1) * 128], True, True, r=[ktln, qtln], w=[PS(5)])
                    V(lambda e: e.copy_predicated(ATm[:, :], cmask[:, :], bank(5)), r=[PS(5), "cmask"], w=["ATm"])
                    S_.unit()
                    for h in range(4):
                        hs = slice(h * 128, (h + 1) * 128)
                        mm(bank(6)[:, hs], ATm[:, hs], vt_[:, hs], h == 0, False, r=["ATm", vtn], w=[PS(6)])
                        mm(bank(6)[:, hs], qhA_[:, hs], Sbf0[:, hs], False, False, r=[qhAn, "Sbf0"], w=[PS(6)])
                    for h in range(4):
                        hs = slice(h * 128, (h + 1) * 128)
                        mm(bank(7)[:, hs], khtok_[0:64, hs], vt_[0:64, hs], True, True, r=[khtokn, vtn], w=[PS(7)])
                    dec = lambda c: c4(E4_)[:, :, c, 63:64].broadcast_to([128, 4, 128])
                    V(lambda e: e.tensor_tensor(v3(Sst), v3(Sst), dec(0), ALU.mult), r=["Sst", E4n], w=["Sst"])
                    V(lambda e: e.tensor_tensor(Sst[:, :], Sst[:, :], bank(7), ALU.add), r=["Sst", PS(7)], w=["Sst"])
                    act(Sbf1[:, :], Sst[:, :], AF.Copy, r=["Sst"], w=["Sbf1"])
                    S_.unit()
                    for h in range(4):
                        hs = slice(h * 128, (h + 1) * 128)
                        mm(bank(6)[:, hs], qhB_[:, hs], Sbf1[:, hs], False, True, r=[qhBn, "Sbf1"], w=[PS(6)])
                    for h in range(4):
                        hs = slice(h * 128, (h + 1) * 128)
                        mm(bank(7)[:, hs], khtok_[64:128, hs], vt_[64:128, hs], True, True, r=[khtokn, vtn], w=[PS(7)])
                    V(lambda e: e.tensor_tensor(v3(Sst), v3(Sst), dec(1), ALU.mult), r=["Sst", E4n], w=["Sst"])
                    V(lambda e: e.tensor_tensor(Sst[:, :], Sst[:, :], bank(7), ALU.add), r=["Sst", PS(7)], w=["Sst"])
                    act(Sbf0[:, :], Sst[:, :], AF.Copy, r=["Sst"], w=["Sbf0"])
                    S_.unit()
                    for h in range(4):
                        act(oa[:, h * 128:(h + 1) * 128], bank(6)[:, h * 128:(h + 1) * 128], AF.Square,
                            r=[PS(6)], w=["oa", "ss4"], accum_out=ss4[:, h:h + 1])
                    V(lambda e: e.tensor_scalar(ss4[:, 4:8], ss4[:, 0:4], 1.0 / 128, EPS, ALU.mult, ALU.add),
                      r=["ss4"], w=["ss4b"])
                    act(ss4[:, 8:12], ss4[:, 4:8], AF.Ln, r=["ss4b"], w=["ss4c"])
                    act(ss4[:, 12:16], ss4[:, 8:12], AF.Exp, r=["ss4c"], w=["ss4d"], scale=-0.5)
                    V(lambda e: e.tensor_tensor(v3(oa), v3(bank(6)),
                                                ss4[:, 12:16].rearrange("p (h o) -> p h o", o=1).broadcast_to([128, 4, 128]),
                                                ALU.mult), r=[PS(6), "ss4d", "oa"], w=["oa"])
                    V(lambda e: e.tensor_tensor(oab[:, :], oa[:, :], gs_[:, :], ALU.mult), r=["oa", gsn], w=["oab"])
                    S_.unit()
                    for h in range(4):
                        tr(bankb(5)[:, 512 + h * 128:512 + (h + 1) * 128], oab[:, h * 128:(h + 1) * 128], identB[:, :],
                           r=["oab", "identB"], w=[PS(5)])
                    for h in range(4):
                        act(c6[:, h, :], bankb(5)[:, 512 + h * 128:512 + (h + 1) * 128], AF.Copy,
                            r=[PS(5), "onwT"], w=[c6n], scale=onwT[:, l * 4 + h:l * 4 + h + 1])
                    S_.unit()
                    y3 = yv[:, :].rearrange("p (c t) -> p c t", t=128)
                    V(lambda e: e.tensor_tensor(y3, cc_[:, :].rearrange("p (c t) -> p c t", t=128),
                                                stt2[:, 0:128].rearrange("p (o t) -> p o t", o=1).broadcast_to([128, 2, 128]),
                                                ALU.subtract), r=[ccn, sttn], w=["yv"])
                    V(lambda e: e.tensor_tensor(y3, y3,
                                                rsb_[:, :].rearrange("p (o t) -> p o t", o=1).broadcast_to([128, 2, 128]),
                                                ALU.mult), r=["yv", rsbn], w=["yv"])
                    for ct in range(2):
                        cs = slice(ct * 128, (ct + 1) * 128)
                        V(lambda e, cs=cs, ct=ct: e.tensor_scalar(yv[:, cs], yv[:, cs], lnw[:, l * 2 + ct:l * 2 + ct + 1],
                                                                  lnb[:, l * 2 + ct:l * 2 + ct + 1], ALU.mult, ALU.add),
                          r=["yv", "lnw", "lnb"], w=["yv"])
                    sigm(thy[:, :], yv[:, :], ["yv"], ["thy"])
                    V(lambda e: e.tensor_tensor(c6[:, 4:6, :].rearrange("p c t -> p (c t)"), thy[:, :], yv[:, :], ALU.mult),
                      r=["thy", "yv"], w=[c6n])
                    ld(cat_d[t, :, :], c6[:, :, :].rearrange("p k n -> p (k n)"), r=[c6n], w=["catd%d" % t])

                def cap1(fn, t):
                    S_.begin()
                    fn(t)
                    return S_.end()

                ld(xt[0][:, :], src_d[0:128, :], w=["xt0"])
                for step in range(NT + 2):
                    streams = []
                    if step - 2 >= 0:
                        streams.append(cap1(stageY2, step - 2))
                    if 0 <= step - 1 < NT:
                        streams.append(cap1(stageY1, step - 1))
                    if step < NT:
                        streams.append(cap1(stageX, step))
                    S_.run_merged(streams)
                    if step == 1:
                        prefetch_a2()
                S_.flush()
            if _STOP <= 1:
                return nc

            with ExitStack() as p2:
                KTc = sb(p2, "KTc", [128, 2, S], BF16)
                Vaug = sb(p2, "Vaug", [128, NT, 4, 65], BF16)
                kidx = sb(p2, "kidx", [128, S], BF16)
                score = [sb(p2, "score%d" % i, [128, S], F32) for i in range(3)]
                junkb = sb(p2, "junkb", [128, S], BF16)
                junk8 = sb(p2, "junk8", [128, S], U8)
                cntd = sb(p2, "cntd", [128, NIT], F32)
                vc = sb(p2, "vc", [128, NIT], F32)
                thrc = sb(p2, "thrc", [128, 4], F32)
                ones1 = sb(p2, "ones1", [128, 1], BF16)
                xt = [sb(p2, "x2t%d" % i, [128, D], F32) for i in range(3)]
                hT = [sb(p2, "h2T%d" % i, [128, 8, 128], BF16) for i in range(3)]
                catT = [sb(p2, "catT%d" % i, [128, 8, 128], BF16) for i in range(3)]
                sqT = [sb(p2, "sqT%d" % i, [128, 2, 256], BF16) for i in range(3)]
                identB2 = sb(p2, "identB2", [128, 256], BF16)
                iqT = sb(p2, "iqT", [128, 3, 128], BF16)
                wab = sb(p2, "wab", [128, 8], F32)
                wsg = sb(p2, "wsg", [128, 8], F32)
                Rb = [sb(p2, "Rb%d" % i, [128, 512], F32) for i in range(3)]
                Mb = [sb(p2, "Mb%d" % i, [128, 512], BF16) for i in range(3)]
                PT = [sb(p2, "PT%d" % i, [128, 512], BF16) for i in range(3)]
                ob = sb(p2, "ob", [128, 256], BF16)
                bis = sb(p2, "bis", [128, 16], F32)
                tabA = sb(p2, "tabA", [128, NIT], F32)
                tabB = sb(p2, "tabB", [128, NIT], F32)
                cntc = sb(p2, "cntc", [128, NIT], F32)
                uc = sb(p2, "uc", [128, NIT], F32)
                mid = sb(p2, "mid", [128, NIT + 1], F32)
                top8 = sb(p2, "top8", [128, 8], F32)
                rs4 = sb(p2, "rs4", [128, 4], F32)

                G(lambda e: e.memset(Vaug[:, :, :, 64:65], 1.0), w=["Vaug"])
                G(lambda e: e.memset(ones1[:, :], 1.0), w=["ones1"])
                for i in range(3):
                    G(lambda e, i=i: e.memset(sqT[i][:, :, :], 0.0), w=["sqT%d" % i])
                for c in range(2):
                    V(lambda e, c=c: e.tensor_copy(identB2[:, c * 128:(c + 1) * 128], identB[:, :]), r=["identB"], w=["identB2"])

                IDXC = (32.0 ** -0.5) * (8.0 ** -0.5)
                state = {"rbi": 0, "mbi": 0, "mgi": 0}

                def stagePS(t):
                    b = t % 3
                    xtn, hTn, cTn, scn, sqn = "x2t%d" % b, "h2T%d" % b, "catT%d" % b, "score%d" % b, "sqT%d" % b
                    sc = score[b]
                    N = (t + 1) * 128
                    tok = slice(t * 128, (t + 1) * 128)
                    ld(xt[b][:, :], src_d[tok, :], w=[xtn])
                    ld(hT[b][:, :, :].rearrange("p k n -> p (k n)"), hT_d[t, :, :], r=["hTd%d" % t], w=[hTn])
                    ld(catT[b][:, 0:4, :].rearrange("p k n -> p (k n)"), cat_d[t, :, 0:512], r=["catd%d" % t], w=[cTn])
                    ld(catT[b][:, 6:8, :].rearrange("p k n -> p (k n)"), cat_d[t, :, 512:768], r=["catd%d" % t], w=[cTn])
                    S_.unit()
                    for k in range(8):
                        mm(bank(0)[:, 0:256], hT[b][:, k, :], wB[:, k, 512:768], k == 0, k == 7, r=[hTn, "wB"], w=[PS(0)])
                    for k in range(8):
                        mm(bank(0)[:, 256:264], hT[b][:, k, :], wB[:, k, 1024:1032], k == 0, k == 7,
                           r=[hTn, "wB"], w=[PS(0)])
                    act(Vaug[:, t, :, 0:64], bank(0)[:, 0:256].rearrange("p (h d) -> p h d", d=64), AF.Copy,
                        r=[PS(0)], w=["Vaug"])
                    act(wab[:, :], bank(0)[:, 256:264], AF.Abs, r=[PS(0)], w=["wab"], scale=IDXC)
                    act(wsg[:, :], bank(0)[:, 256:264], AF.Sign, r=[PS(0)], w=["wsg"])
                    S_.unit()
                    for f in range(4):
                        cb = (0, 128, 256, 384)[f]
                        for k in range(8):
                            mm(bank(1)[:, f * 128:(f + 1) * 128], wB[:, k, cb:cb + 128], hT[b][:, k, :],
                               k == 0, k == 7, r=[hTn, "wB"], w=[PS(1)])
                    for hl in range(2):
                        rows = slice(64 * hl, 64 * hl + 64)
                        act(sqT[b][rows, :, hl * 128:(hl + 1) * 128],
                            bank(1)[rows, 0:256].rearrange("p (c t) -> p c t", t=128), AF.Copy,
                            r=[PS(1)], w=[sqn], scale=0.125)
                    V(lambda e: e.tensor_copy(KTc[:, :, tok], bank(1)[:, 256:512].rearrange("p (c t) -> p c t", t=128)),
                      r=[PS(1)], w=["KTc"])
                    S_.unit()
                    for f, (cb, m) in enumerate(((768, 96), (864, 96), (960, 64), (1032, 96))):
                        for k in range(8):
                            mm(bank(2)[0:m, f * 128:(f + 1) * 128], wB[:, k, cb:cb + m], hT[b][:, k, :],
                               k == 0, k == 7, r=[hTn, "wB"], w=[PS(2)])
                    act(iqT[0:96, :, :], bank(2)[0:96, 0:384].rearrange("p (c t) -> p c t", t=128), AF.Copy,
                        r=[PS(2)], w=["iqT"])
                    V(lambda e: e.tensor_copy(kidx[0:96, tok], bank(2)[0:96, 384:512]), r=[PS(2)], w=["kidx"])
                    S_.unit()
                    NB = (N + 511) // 512
                    prev_acc = [None]
                    for kb in range(NB):
                        wN = min(512, N - kb * 512)
                        ks_ = slice(kb * 512, kb * 512 + wN)
                        for hh in range(8):
                            g_, r_ = hh // 3, hh % 3
                            pb = 3 + (hh % 2)
                            mm(bank(pb)[:, 0:wN], iqT[32 * r_:32 * r_ + 32, g_, :], kidx[32 * r_:32 * r_ + 32, ks_],
                               True, True, r=["iqT", "kidx"], w=[PS(pb)])
                            R_ = Rb[state["rbi"] % 3]
                            Rn = "Rb%d" % (state["rbi"] % 3)
                            state["rbi"] += 1
                            act(R_[:, 0:wN], bank(pb)[:, 0:wN], AF.Relu, r=[PS(pb), "wab"], w=[Rn],
                                scale=wab[:, hh:hh + 1])

                            def acc(R_=R_, Rn=Rn, ks_=ks_, wN=wN, hh=hh):
                                if hh == 0:
                                    V(lambda e: e.tensor_scalar(sc[:, ks_], R_[:, 0:wN], wsg[:, 0:1], None, ALU.mult),
                                      r=[Rn, "wsg"], w=[scn])
                                else:
                                    V(lambda e: e.scalar_tensor_tensor(sc[:, ks_], R_[:, 0:wN], wsg[:, hh:hh + 1], sc[:, ks_],
                                                                       ALU.mult, ALU.add),
                                      r=[Rn, "wsg", scn], w=[scn])
                            if prev_acc[0] is not None:
                                prev_acc[0]()
                            prev_acc[0] = acc
                            S_.unit()
                    prev_acc[0]()
                    G(lambda e: e.affine_select(sc[:, tok], sc[:, tok], [[-1, 128]], ALU.is_ge, fillreg(e, -1e30),
                                                base=0, channel_multiplier=1), r=[scn], w=[scn])

                def stageBI(t):
                    b = t % 3
                    scn = "score%d" % b
                    sc = score[b]
                    N = (t + 1) * 128
                    thn = "thr%d" % b
                    if t * 128 < TOPK:
                        V(lambda e: e.tensor_copy(thrc[:, b:b + 1], thrneg[:, 0:1]), r=["thrneg"], w=[thn])
                        return
                    Ka = max(128, min(N - 128, int(round(0.6 * (t + 1))) * 128))
                    V(lambda e: e.max(top8[:, :], sc[:, 0:N]), r=[scn], w=["top8"])
                    S_.unit()
                    V(lambda e: e.tensor_reduce(bis[:, 0:1], sc[:, 0:TOPK], mybir.AxisListType.X, ALU.min),
                      r=[scn], w=["bis0"])
                    V(lambda e: e.tensor_tensor(bis[:, 1:2], top8[:, 0:1], bis[:, 0:1], ALU.subtract),
                      r=["top8", "bis0"], w=["bis1"])
                    V(lambda e: e.tensor_scalar(tabA[:, :], tabA0[:, :], bis[:, 1:2], None, ALU.mult),
                      r=["bis1", "tabA0"], w=["tabA"])
                    V(lambda e: e.tensor_scalar(tabB[:, :], tabB0[:, :], bis[:, 1:2], None, ALU.mult),
                      r=["bis1", "tabB0"], w=["tabB"])
                    V(lambda e: e.scalar_tensor_tensor(mid[:, 0:1], bis[:, 1:2], 0.5, bis[:, 0:1], ALU.mult, ALU.add),
                      r=["bis0", "bis1"], w=["mid"])
                    S_.unit()
                    for n in range(NIT):
                        act(junkb[:, 0:Ka], sc[:, 0:Ka], AF.Sign, r=[scn, "mid"], w=["junkb", "cntA"],
                            scale=-1.0, bias=mid[:, n:n + 1], accum_out=cntc[:, n:n + 1])
                        V(lambda e, n=n: e.scalar_tensor_tensor(junk8[:, Ka:N], sc[:, Ka:N], mid[:, n:n + 1],
                                                                ones1[:, 0:1].broadcast_to([128, N - Ka]),
                                                                ALU.is_ge, ALU.mult, accum_out=cntd[:, n:n + 1]),
                          r=[scn, "mid", "ones1"], w=["junk8", "cntD"])
                        V(lambda e, n=n: e.scalar_tensor_tensor(vc[:, n:n + 1], cntd[:, n:n + 1], 2.0, cntc[:, n:n + 1],
                                                                ALU.mult, ALU.subtract),
                          r=["cntA", "cntD"], w=["vc"])
                        V(lambda e, n=n: e.scalar_tensor_tensor(uc[:, n:n + 1], vc[:, n:n + 1], float(2 * TOPK - Ka - 1),
                                                                tabB[:, n:n + 1], ALU.is_gt, ALU.mult),
                          r=["vc", "tabB"], w=["uc"])
                        V(lambda e, n=n: e.scalar_tensor_tensor(mid[:, n + 1:n + 2], mid[:, n:n + 1], tabA[:, n:n + 1],
                                                                uc[:, n:n + 1], ALU.subtract, ALU.add),
                          r=["mid", "tabA", "uc"], w=["mid"])
                        S_.unit()
                    V(lambda e: e.tensor_copy(thrc[:, b:b + 1], mid[:, NIT:NIT + 1]), r=["mid"], w=[thn])

                def stageAT(t):
                    b = t % 3
                    xtn, cTn, scn, sqn, thn = "x2t%d" % b, "catT%d" % b, "score%d" % b, "sqT%d" % b, "thr%d" % b
                    sc = score[b]
                    tok = slice(t * 128, (t + 1) * 128)
                    thr = thrc[:, b:b + 1]
                    prev_pv = [None]
                    NG = (t + 4) // 4
                    Nq = (t + 1) * 128
                    mg0 = state["mgi"]
                    state["mgi"] += NG

                    def genmask(g):
                        if g >= NG:
                            return
                        w_ = min(512, Nq - g * 512)
                        gi = mg0 + g
                        M_ = Mb[gi % 3]
                        V(lambda e: e.tensor_scalar(M_[:, 0:w_], sc[:, g * 512:g * 512 + w_], thr, -30000.0, ALU.is_lt, ALU.mult),
                          r=[scn, thn], w=["Mb%d" % (gi % 3)])
                    genmask(0)
                    genmask(1)
                    for kb in range(t + 1):
                        kcs = slice(kb * 128, (kb + 1) * 128)
                        mbi = state["mbi"]
                        state["mbi"] += 1
                        g = kb // 4
                        if kb % 4 == 0:
                            genmask(g + 2)
                        gi = mg0 + g
                        M_ = Mb[gi % 3][:, (kb % 4) * 128:(kb % 4 + 1) * 128]
                        Mn = "Mb%d" % (gi % 3)
                        P_ = PT[mbi % 3]
                        Pn = "PT%d" % (mbi % 3)
                        pb = 5 + (mbi % 2)
                        for c in range(2):
                            mm(bank(pb)[:, c * 256:(c + 1) * 256], KTc[:, c, kcs], sqT[b][:, c, :], True, False,
                               r=["KTc", sqn], w=[PS(pb)])
                            mm(bank(pb)[:, c * 256:(c + 1) * 256], M_, identB2[:, :], False, True,
                               r=[Mn, "identB2"], w=[PS(pb)])
                        act(P_[:, :], bank(pb), AF.Exp, r=[PS(pb)], w=[Pn])

                        def pv(kb=kb, P_=P_, Pn=Pn):
                            for h in range(4):
                                mm(bank(7)[:, h * 65:(h + 1) * 65], P_[:, h * 128:(h + 1) * 128], Vaug[:, kb, h, :],
                                   kb == 0 and h == 0, kb == t, r=[Pn, "Vaug"], w=[PS(7)])
                        if prev_pv[0] is not None:
                            prev_pv[0]()
                        prev_pv[0] = pv
                        S_.unit()
                    prev_pv[0]()
                    o3 = bank(7)[:, 0:260].rearrange("p (h d) -> p h d", d=65)
                    V(lambda e: e.reciprocal(rs4[:, :].rearrange("p (h o) -> p h o", o=1), o3[:, :, 64:65]), r=[PS(7)], w=["rs4"])
                    V(lambda e: e.tensor_tensor(ob[:, :].rearrange("p (h d) -> p h d", d=64), o3[:, :, 0:64],
                                                rs4[:, :].rearrange("p (h o) -> p h o", o=1).broadcast_to([128, 4, 64]),
                                                ALU.mult), r=[PS(7), "rs4"], w=["ob"])
                    for c in range(2):
                        tr(bankb(7)[:, c * 128:(c + 1) * 128], ob[:, c * 128:(c + 1) * 128], identB[:, :],
                           r=["ob", "identB"], w=[PS(7)])
                    act(catT[b][:, 4:6, :], bankb(7)[:, 0:256].rearrange("p (c t) -> p c t", t=128), AF.Copy,
                        r=[PS(7)], w=[cTn])
                    S_.unit()
                    for hf in range(2):
                        for c in range(8):
                            mm(bank(5 + hf), catT[b][:, c, :], woS[:, c, hf * 512:(hf + 1) * 512], c == 0, c == 7,
                               r=[cTn, "woS"], w=[PS(5 + hf)])
                        V(lambda e, hf=hf: e.tensor_tensor(xt[b][:, hf * 512:(hf + 1) * 512],
                                                           xt[b][:, hf * 512:(hf + 1) * 512], bank(5 + hf), ALU.add),
                          r=[xtn, PS(5 + hf)], w=[xtn])
                        S_.unit()
                    ld(out_d[tok, :], xt[b][:, :], r=[xtn], w=["outd%d" % t])

                def cap_(fn, t):
                    S_.begin()
                    fn(t)
                    return S_.end()

                for step in range(NT + 2):
                    streams = []
                    if step - 2 >= 0:
                        streams.append(cap_(stageAT, step - 2))
                    if 0 <= step - 1 < NT:
                        streams.append(cap_(stageBI, step - 1))
                    if step < NT:
                        streams.append(cap_(stagePS, step))
                    S_.run_merged(streams)
                S_.flush()
            if _STOP <= 2:
                return nc

            pl.__exit__(None, None, None)
            with ExitStack() as p3:
                w1S = sb(p3, "w1S", [128, 8, DFF], BF16)
                w2S = sb(p3, "w2S", [128, 32, D], BF16)
                g2bc = sb(p3, "g2bc", [128, D], F32)
                dtmp = sb(p3, "dtmp3", [128, 128], F32)
                xtN = [sb(p3, "x3n%d" % i, [128, D], F32) for i in range(2)]
                xr = [sb(p3, "x3r%d" % i, [128, D], F32) for i in range(2)]
                xn = sb(p3, "xn3", [128, D], F32)
                ssq = sb(p3, "ssq3", [128, 4], F32)
                ssf = sb(p3, "ssf3", [128, 4], F32)
                hTb = sb(p3, "hTb", [128, 8, 512], BF16)
                h1raw = sb(p3, "h1raw", [128, 8192], F32)
                h1T = h1raw[:, :].bitcast(BF16).rearrange("p (f t) -> p f t", t=512)
                wst = [h1raw[:, 0:1024], h1raw[:, 1024:2048]]
                rl = [sb(p3, "rl%d" % i, [128, 512], F32) for i in range(2)]
                last = (l == L - 1)
                NBLK = S // 512
                stB = {"ri": 0, "ni": 0, "xi": 0}

                def stageN(blk):
                    for i in range(4):
                        t = blk * 4 + i
                        j = stB["ni"] % 2
                        stB["ni"] += 1
                        ld(xtN[j][:, :], out_d[t * 128:(t + 1) * 128, :], r=["outd%d" % t], w=["x3n%d" % j])
                        norm_tile(xtN[j][:, :], "x3n%d" % j, xn, ssq, hTb, "hTb", G2T, l, 3, (0, 1),
                                  ncols=128, coff=i * 128)
                        S_.unit()

                for cb in range(4):
                    for k in range(8):
                        ldc(w1S[:, k, cb * 1024:(cb + 1) * 1024], w1_d[l, k * 128:(k + 1) * 128, cb * 1024:(cb + 1) * 1024],
                            w=["w1S_%d" % cb])
                stageN(0)
                build_bc(g2bc, "g2bc", l, 5, dtmp, "dtmp3", (0, 1))
                for k in range(32):
                    ld(wst[k % 2], w2_d[l, k * 128:(k + 1) * 128, :], w=["w2st%d" % (k % 2)])
                    eng = V if k % 2 == 0 else G
                    eng(lambda e, k=k: e.tensor_tensor(w2S[:, k, :], wst[k % 2], g2bc[:, :], ALU.mult),
                        r=["w2st%d" % (k % 2), "g2bc"], w=["w2S_%d" % k])
                if last:
                    ld(g2bc[:, :], fnw_d[0:1, :].partition_broadcast(128), r=["w2S_%d" % k for k in range(32)], w=["g2bc"])
                def stageM1(blk):
                    for f in range(32):
                        pb = 2 + (f % 2)
                        for k in range(8):
                            mm(bank(pb), w1S[:, k, f * 128:(f + 1) * 128], hTb[:, k, :], k == 0, k == 7,
                               r=["w1S_%d" % (f // 8), "hTb"], w=[PS(pb)])
                        r_ = rl[stB["ri"] % 2]
                        rn = "rl%d" % (stB["ri"] % 2)
                        stB["ri"] += 1
                        act(r_[:, :], bank(pb), AF.Relu, r=[PS(pb)], w=[rn])
                        G(lambda e, r_=r_, f=f: e.tensor_tensor(h1T[:, f, :], r_[:, :], r_[:, :], ALU.mult),
                          r=[rn], w=["h1T", "w2st0", "w2st1"])

                def stageM2(blk):
                    for i in range(4):
                        t = blk * 4 + i
                        j = stB["xi"] % 2
                        stB["xi"] += 1
                        xb, xbn = xr[j], "x3r%d" % j
                        ld(xb[:, :], out_d[t * 128:(t + 1) * 128, :], r=["outd%d" % t], w=[xbn])
                        for hf in range(2):
                            pb = 4 + 2 * (i % 2) + hf
                            for f in range(32):
                                mm(bank(pb), h1T[:, f, i * 128:(i + 1) * 128], w2S[:, f, hf * 512:(hf + 1) * 512],
                                   f == 0, f == 31, r=["h1T", "w2S_%d" % f], w=[PS(pb)])
                                if f % 8 == 7:
                                    S_.unit()
                            V(lambda e, xb=xb, hf=hf, pb=pb: e.tensor_tensor(xb[:, hf * 512:(hf + 1) * 512],
                                                                             xb[:, hf * 512:(hf + 1) * 512], bank(pb), ALU.add),
                              r=[xbn, PS(pb)], w=[xbn])
                        if last:
                            act(xn[:, :], xb[:, :], AF.Square, r=[xbn], w=["xn", "ssf"], accum_out=ssf[:, 0:1])
                            V(lambda e: e.tensor_scalar(ssf[:, 1:2], ssf[:, 0:1], 1.0 / D, EPS, ALU.mult, ALU.add),
                              r=["ssf"], w=["ssf1"])
                            act(ssf[:, 2:3], ssf[:, 1:2], AF.Ln, r=["ssf1"], w=["ssf2"])
                            act(ssf[:, 3:4], ssf[:, 2:3], AF.Exp, r=["ssf2"], w=["ssf3"], scale=-0.5)
                            V(lambda e, xb=xb: e.scalar_tensor_tensor(xb[:, :], xb[:, :], ssf[:, 3:4], g2bc[:, :],
                                                                      ALU.mult, ALU.mult),
                              r=[xbn, "ssf3", "g2bc"], w=[xbn])
                        ld(out_d[t * 128:(t + 1) * 128, :], xb[:, :], r=[xbn], w=["outd%d" % t])
                        S_.unit()

                def capB(fn, blk):
                    S_.begin()
                    fn(blk)
                    return S_.end()

                for blk in range(NBLK):
                    stageM1(blk)
                    streams = [capB(stageM2, blk)]
                    if blk + 1 < NBLK:
                        streams.append(capB(stageN, blk + 1))
                    S_.run_merged(streams)
                S_.flush()
    return nc


def _colsT(v, L, n):
    v = np.asarray(v, np.float32).reshape(L, n, 128)
    return np.ascontiguousarray(v.transpose(2, 0, 1).reshape(128, L * n))


def make_in_maps(inp, L, nb):
    f = lambda a: np.ascontiguousarray(np.asarray(a, np.float32))
    shared = {
        "ada_w": f(inp["ada_w"]),
        "ada_bT": _colsT(inp["ada_b"], L, 48),
        "nmixT": _colsT(inp["norm_mix_w"], L, 8),
        "nmlpT": _colsT(inp["norm_mlp_w"], L, 8),
        "fnw": f(inp["final_norm_w"]).reshape(1, D),
        "w_in": f(inp["w_in"]), "w_out": f(inp["w_out"]), "w1": f(inp["mlp_w1"]), "w2": f(inp["mlp_w2"]),
        "lbT": _colsT(inp["hg_lb_logits"], L, 4),
        "onwT": _colsT(inp["hg_onorm_w"], L, 4),
        "cvw": np.ascontiguousarray(np.asarray(inp["cv_w"], np.float32).reshape(L, 31, 2, 128)
                                    .transpose(3, 0, 2, 1).reshape(128, L * 62)),
        "cvb": _colsT(inp["cv_b"], L, 2),
        "lnw": _colsT(inp["cv_ln_w"], L, 2),
        "lnb": _colsT(inp["cv_ln_b"], L, 2),
    }
    maps = []
    x = np.asarray(inp["x"], np.float32)
    c = np.asarray(inp["c"], np.float32)
    for b in range(nb):
        m = dict(shared)
        m["x"] = np.ascontiguousarray(x[b])
        m["cT"] = np.ascontiguousarray(c[b].reshape(8, 128).T)
        maps.append(m)
    return maps


_NC_CACHE = {}


def kernel(x, c, ada_w, ada_b, norm_mix_w, norm_mlp_w, w_in, hg_lb_logits, hg_onorm_w,
           cv_w, cv_b, cv_ln_w, cv_ln_b, w_out, mlp_w1, mlp_w2, final_norm_w):
    inp = dict(x=x, c=c, ada_w=ada_w, ada_b=ada_b, norm_mix_w=norm_mix_w, norm_mlp_w=norm_mlp_w, w_in=w_in,
               hg_lb_logits=hg_lb_logits, hg_onorm_w=hg_onorm_w, cv_w=cv_w, cv_b=cv_b, cv_ln_w=cv_ln_w,
               cv_ln_b=cv_ln_b, w_out=w_out, mlp_w1=mlp_w1, mlp_w2=mlp_w2, final_norm_w=final_norm_w)
    B, S, _ = np.asarray(x).shape
    L = np.asarray(w_in).shape[0]
    topk = min(256, S // 4)
    key = (S, L, topk)
    if key not in _NC_CACHE:
        _NC_CACHE[key] = build_nc(S, L, topk)
    nc = _NC_CACHE[key]
    maps = make_in_maps(inp, L, B)
    res = run_bass_kernel_spmd(nc, maps, core_ids=list(range(B)))
    return np.stack([np.asarray(r["out"], np.float32) for r in res.results], axis=0)
```

```python
import numpy as np
from contextlib import ExitStack
import concourse.bass as bass
import concourse.mybir as mybir
from concourse.bass_utils import run_bass_kernel_spmd

F32 = mybir.dt.float32
BF16 = mybir.dt.bfloat16
U8 = mybir.dt.uint8
AF = mybir.ActivationFunctionType
ALU = mybir.AluOpType

D = 1024
DIN = 3624
DFF = 4096
EPS = 1e-6
NIT = 16
ENGS = ("tensor", "vector", "scalar", "gpsimd", "sync")
NDMA_SEMS = 24
import os as _os
_STOP = int(_os.environ.get('KSTOP', '99'))
_CUT = int(_os.environ.get('KCUT', '99'))
_SUB = int(_os.environ.get('KSUB', '99'))


class _Op:
    __slots__ = ("eng", "fn", "deps", "signals", "sem", "val", "is_dma")

    def __init__(self, eng, fn, is_dma=False):
        self.eng = eng
        self.fn = fn
        self.deps = []
        self.signals = False
        self.sem = None
        self.val = 0
        self.is_dma = is_dma


class _Slot:
    __slots__ = ("writer", "readers")

    def __init__(self):
        self.writer = None
        self.readers = []


class Sched:
    def __init__(self, nc, es):
        self.nc = nc
        self.q = {e: [] for e in ENGS}
        self.slots = {}
        self.phase_dmas = []
        self.esem = {e: es.enter_context(nc.semaphore("es_" + e)) for e in ENGS}
        self.dsem = {e: [es.enter_context(nc.semaphore("ds_%s_%d" % (e, i))) for i in range(NDMA_SEMS)]
                     for e in ("sync", "gpsimd")}
        self.cnt = {e: 0 for e in ENGS}
        self.dcnt = {e: [0] * NDMA_SEMS for e in self.dsem}
        self.drr = {e: 0 for e in self.dsem}
        self.dprev = {e: [None] * NDMA_SEMS for e in self.dsem}
        self.waited = {e: {} for e in ENGS}
        self.nops = 0

    def _slot(self, k):
        s = self.slots.get(k)
        if s is None:
            s = self.slots[k] = _Slot()
        return s

    def _add(self, op, reads, writes):
        deps = set()
        for k in reads:
            s = self._slot(k)
            if s.writer is not None:
                deps.add(s.writer)
            if k.startswith("ps"):
                for r in s.readers:
                    if r.eng != op.eng:
                        deps.add(r)
        for k in writes:
            s = self._slot(k)
            if s.writer is not None:
                deps.add(s.writer)
            for r in s.readers:
                deps.add(r)
        deps.discard(op)
        for d in deps:
            if d.eng == "tensor" and op.eng == "tensor":
                continue
            op.deps.append(d)
            d.signals = True
        for k in writes:
            s = self._slot(k)
            s.writer = op
            s.readers = []
        for k in reads:
            if k not in writes:
                self._slot(k).readers.append(op)
        self.q[op.eng].append(op)
        self.nops += 1
        return op

    cap = None

    def begin(self):
        self.cap = [[]]

    def unit(self):
        if self.cap is not None and self.cap[-1]:
            self.cap.append([])

    def end(self):
        u = [x for x in self.cap if x]
        self.cap = None
        return u

    def run_merged(self, streams):
        pos = [0] * len(streams)
        while True:
            best, bf = -1, 2.0
            for i, s in enumerate(streams):
                if pos[i] < len(s):
                    f = (pos[i] + 0.5) / len(s)
                    if f < bf:
                        best, bf = i, f
            if best < 0:
                break
            for item in streams[best][pos[best]]:
                if item[0] == "op":
                    self.op(*item[1:])
                else:
                    self.dma(item[1], item[2], item[3], item[4], item[5], **item[6])
            pos[best] += 1

    def op(self, eng, fn, reads=(), writes=()):
        if self.cap is not None:
            self.cap[-1].append(("op", eng, fn, tuple(reads), tuple(writes)))
            return None
        return self._add(_Op(eng, fn), list(reads), list(writes))

    def dma(self, eng, out, in_, reads=(), writes=(), **kw):
        if self.cap is not None:
            self.cap[-1].append(("dma", eng, out, in_, tuple(reads), tuple(writes), kw))
            return None
        fn = lambda e: e.dma_start(out=out, in_=in_, **kw)
        op = _Op(eng, fn, is_dma=True)
        op.signals = True
        self._add(op, list(reads), list(writes))
        self.phase_dmas.append(op)
        return op

    def flush(self):
        nc = self.nc
        fin = _Op("sync", None)
        fin.deps = list(self.phase_dmas)
        self.phase_dmas = []
        self.q["sync"].append(fin)
        for e in ENGS:
            for op in self.q[e]:
                if op.fn is None:
                    continue
                if op.is_dma:
                    j = self.drr[e]
                    self.drr[e] = (j + 1) % NDMA_SEMS
                    self.dcnt[e][j] += 16
                    op.sem = self.dsem[e][j]
                    op.val = self.dcnt[e][j]
                    prev = self.dprev[e][j]
                    if prev is not None:
                        op.deps.append(prev)
                    self.dprev[e][j] = op
                elif op.signals:
                    self.cnt[e] += 1
                    op.sem = self.esem[e]
                    op.val = self.cnt[e]
        q = self.q
        waited_all = self.waited

        def run(e):
            def body(eng):
                waited = waited_all[e]
                for op in q[e]:
                    for d in op.deps:
                        if d.sem is None:
                            continue
                        key = id(d.sem)
                        if waited.get(key, 0) < d.val:
                            eng.wait_ge(d.sem, d.val)
                            waited[key] = d.val
                    if op.fn is None:
                        continue
                    ins = op.fn(eng)
                    if op.signals:
                        ins.then_inc(op.sem, 16 if op.is_dma else 1)
            return body

        with nc.Block() as block:
            if q["tensor"]:
                block.tensor(run("tensor"))
            if q["vector"]:
                block.vector(run("vector"))
            if q["scalar"]:
                block.scalar(run("scalar"))
            if q["gpsimd"]:
                block.gpsimd(run("gpsimd"))
            block.sync(run("sync"))
        self.q = {e: [] for e in ENGS}


def build_nc(S, L, TOPK, dbg=False):
    NT = S // 128
    nc = bass.Bass("TRN2", target_bir_lowering=False)

    def din(name, shape, dt=F32):
        return nc.dram_tensor(name, list(shape), dt, kind="ExternalInput").ap()

    x_d = din("x", [S, D])
    cT_d = din("cT", [128, 8])
    adaw_d = din("ada_w", [L, D, 6 * D])
    adabT_d = din("ada_bT", [128, L * 48])
    nmixT_d = din("nmixT", [128, L * 8])
    nmlpT_d = din("nmlpT", [128, L * 8])
    fnw_d = din("fnw", [1, D])
    win_d = din("w_in", [L, D, DIN])
    wout_d = din("w_out", [L, D, D])
    w1_d = din("w1", [L, D, DFF])
    w2_d = din("w2", [L, DFF, D])
    lbT_d = din("lbT", [128, L * 4])
    onwT_d = din("onwT", [128, L * 4])
    cvw_d = din("cvw", [128, L * 62])
    cvb_d = din("cvb", [128, L * 2])
    lnw_d = din("lnw", [128, L * 2])
    lnb_d = din("lnb", [128, L * 2])
    out_d = nc.dram_tensor("out", [S, D], F32, kind="ExternalOutput").ap()
    hT_d = nc.dram_tensor("hT_scr", [NT, 128, 1024], BF16, kind="Internal").ap()
    cat_d = nc.dram_tensor("cat_scr", [NT, 128, 768], BF16, kind="Internal").ap()

    with ExitStack() as es:
        S_ = Sched(nc, es)

        def V(fn, r=(), w=()):
            return S_.op("vector", fn, r, w)

        def A(fn, r=(), w=()):
            return S_.op("scalar", fn, r, w)

        def G(fn, r=(), w=()):
            return S_.op("gpsimd", fn, r, w)

        def T(fn, r=(), w=()):
            return S_.op("tensor", fn, r, w)

        _fillregs = {}

        def fillreg(e, val):
            if val not in _fillregs:
                _fillregs[val] = e.to_reg(val)
            return _fillregs[val]

        def mm(out, lhsT, rhs, start, stop, r, w):
            return T(lambda e: e.matmul(out, lhsT, rhs, start=start, stop=stop, skip_group_check=True), r, w)

        def tr(out, in_, ident, r, w):
            return T(lambda e: e.transpose(out, in_, ident), r, w)

        def act(out, in_, func, r, w, **kw):
            return A(lambda e: e.activation(out, in_, func, **kw), r, w)

        def ld(out, in_, w, r=(), **kw):
            return S_.dma("sync", out, in_, reads=r, writes=w, **kw)

        def ldc(out, in_, w, r=()):
            return S_.dma("gpsimd", out, in_, reads=r, writes=w, max_dma_last_dim=2048)

        _uid = [0]

        def sb(stack, name, shape, dt):
            _uid[0] += 1
            return stack.enter_context(nc.sbuf_tensor("s%d_%s" % (_uid[0], name), list(shape), dt))

        pst = [es.enter_context(nc.psum_tensor("pst%d" % i, [128, 1024], F32)) for i in range(4)]

        def bank(i):
            return pst[i // 2][:, (i % 2) * 512:(i % 2 + 1) * 512]

        def bankb(i):
            return bank(i).bitcast(BF16)

        PS = lambda i: "ps%d" % i

        identF = sb(es, "identF", [128, 128], F32)
        identB = sb(es, "identB", [128, 128], BF16)
        onesM = sb(es, "onesM", [128, 128], F32)
        cT = sb(es, "cTs", [128, 8], F32)
        cact = sb(es, "cact", [128, 8], F32)
        modT = sb(es, "modT", [128, L * 48], F32)
        adabT = sb(es, "adabT", [128, L * 48], F32)
        nmixT = sb(es, "nmixT", [128, L * 8], F32)
        nmlpT = sb(es, "nmlpT", [128, L * 8], F32)
        G1T = sb(es, "G1T", [128, L * 8], F32)
        G2T = sb(es, "G2T", [128, L * 8], F32)
        lbT = sb(es, "lbT", [128, L * 4], F32)
        omlT = sb(es, "omlT", [128, L * 4], F32)
        lbtmp = sb(es, "lbtmp", [128, 8], F32)
        onwT = sb(es, "onwT", [128, L * 4], F32)
        cvw = sb(es, "cvw", [128, L * 62], F32)
        cvb = sb(es, "cvb", [128, L * 2], F32)
        lnw = sb(es, "lnw", [128, L * 2], F32)
        lnb = sb(es, "lnb", [128, L * 2], F32)
        tabA0 = sb(es, "tabA0", [128, NIT], F32)
        tabB0 = sb(es, "tabB0", [128, NIT], F32)
        thrneg = sb(es, "thrneg", [128, 1], F32)
        negh = sb(es, "negh", [128, 128], F32)
        omlh = sb(es, "omlh", [128, L * 4], F32)
        lbp = sb(es, "lbp", [128, L * 4], F32)
        onwh = sb(es, "onwh", [128, L * 4], F32)
        lnwh = sb(es, "lnwh", [128, L * 2], F32)
        lnbh = sb(es, "lnbh", [128, L * 2], F32)

        def modcol(l, j, k):
            c = l * 48 + j * 8 + k
            return modT[:, c:c + 1]

        with ExitStack() as ps_:
            stage = [sb(ps_, "adast%d" % i, [128, 8, 512], F32) for i in range(2)]
            rowS = [sb(ps_, "rowS%d" % i, [1, 512], F32) for i in range(2)]
            one1f = sb(ps_, "one1f", [1, 1], F32)
            G(lambda e: e.memset(one1f[:, :], 1.0), w=["one1f"])
            G(lambda e: e.memset(identF[:, :], 1.0), w=["identF"])
            G(lambda e: e.affine_select(identF[:, :], identF[:, :], [[-1, 128]], ALU.is_equal, fillreg(e, 0.0),
                                        base=0, channel_multiplier=1), r=["identF"], w=["identF"])
            V(lambda e: e.tensor_copy(identB[:, :], identF[:, :]), r=["identF"], w=["identB"])
            G(lambda e: e.memset(onesM[:, :], 1.0 / 256.0), w=["onesM"])
            G(lambda e: e.memset(thrneg[:, :], -1e29), w=["thrneg"])
            for n in range(NIT):
                a_n = 2.0 ** -(n + 2) if n < NIT - 1 else 2.0 ** -(NIT)
                b_n = 2.0 ** -(n + 1)
                G(lambda e, n=n, a_n=a_n: e.memset(tabA0[:, n:n + 1], a_n), w=["tabA0"])
                G(lambda e, n=n, b_n=b_n: e.memset(tabB0[:, n:n + 1], b_n), w=["tabB0"])
            for (dst, src, nm) in ((cT, cT_d, "cT"), (adabT, adabT_d, "adabT"), (nmixT, nmixT_d, "nmixT"),
                                   (nmlpT, nmlpT_d, "nmlpT"), (lbT, lbT_d, "lbT"), (onwT, onwT_d, "onwT"),
                                   (cvw, cvw_d, "cvw"), (cvb, cvb_d, "cvb"), (lnw, lnw_d, "lnw"),
                                   (lnb, lnb_d, "lnb")):
                ld(dst[:, :], src[:, :], w=[nm])
            act(cact[:, :], cT[:, :], AF.Silu, r=["cT"], w=["cact"])
            lb3 = lbT[:, :].rearrange("p (l h) -> p l h", h=4)
            act(lbT[:, :], lbT[:, :], AF.Exp, r=["lbT"], w=["lbT"])
            V(lambda e: e.tensor_copy(lbtmp[:, 0:4], lb3[:, 0, :]), r=["lbT"], w=["lbtmp"])
            for l in range(1, L):
                V(lambda e, l=l: e.tensor_tensor(lbtmp[:, 0:4], lbtmp[:, 0:4], lb3[:, l, :], ALU.add),
                  r=["lbT", "lbtmp"], w=["lbtmp"])
            V(lambda e: e.reciprocal(lbtmp[:, 4:8], lbtmp[:, 0:4]), r=["lbtmp"], w=["lbtmp2"])
            for l in range(L):
                V(lambda e, l=l: e.tensor_tensor(lb3[:, l, :], lb3[:, l, :], lbtmp[:, 4:8], ALU.mult),
                  r=["lbT", "lbtmp2"], w=["lbT"])
            V(lambda e: e.memset(lb3[:, 0, :], 0.0), r=["lbT"], w=["lbT"])
            for l in range(2, L):
                V(lambda e, l=l: e.tensor_tensor(lb3[:, l, :], lb3[:, l, :], lb3[:, l - 1, :], ALU.add),
                  r=["lbT"], w=["lbT"])
            V(lambda e: e.tensor_scalar(omlT[:, :], lbT[:, :], -1.0, 1.0, ALU.mult, ALU.add),
              r=["lbT"], w=["omlT"])
            V(lambda e: e.tensor_scalar(omlh[:, :], omlT[:, :], 0.5, None, ALU.mult), r=["omlT"], w=["omlh"])
            V(lambda e: e.tensor_tensor(lbp[:, :], lbT[:, :], omlh[:, :], ALU.add), r=["lbT", "omlh"], w=["lbp"])
            V(lambda e: e.tensor_scalar(onwh[:, :], onwT[:, :], 0.5, None, ALU.mult), r=["onwT"], w=["onwh"])
            V(lambda e: e.tensor_scalar(lnwh[:, :], lnw[:, :], 0.5, None, ALU.mult), r=["lnw"], w=["lnwh"])
            V(lambda e: e.tensor_scalar(lnbh[:, :], lnb[:, :], 0.5, None, ALU.mult), r=["lnb"], w=["lnbh"])
            G(lambda e: e.memset(negh[:, :], -0.5), w=["negh"])
            piece = 0
            for l in range(L):
                for pc in range(12):
                    st = stage[piece % 2]
                    sn = "adast%d" % (piece % 2)
                    ld(st[:, :, :], adaw_d[l, :, pc * 512:(pc + 1) * 512].rearrange("(k p) n -> p k n", p=128),
                       w=[sn])
                    rb = 2 + (piece % 2)
                    for k in range(8):
                        mm(bank(rb)[0:1, :], cact[:, k:k + 1], st[:, k, :], k == 0, k == 7, r=[sn, "cact"], w=[PS(rb)])
                    rw = rowS[piece % 2]
                    rwn = "rowS%d" % (piece % 2)
                    act(rw[0:1, :], bank(rb)[0:1, :], AF.Copy, r=[PS(rb)], w=[rwn])
                    for f in range(4):
                        col = l * 48 + pc * 4 + f
                        mm(bank(0)[:, col:col + 1], rw[0:1, f * 128:(f + 1) * 128], one1f[0:1, 0:1],
                           piece == 0 and f == 0, True, r=[rwn, "one1f"], w=[PS(0)])
                    piece += 1
            V(lambda e: e.tensor_tensor(modT[:, :], bank(0)[:, 0:L * 48], adabT[:, :], ALU.add),
              r=[PS(0), "adabT"], w=["modT"])
            m4 = modT[:, :].rearrange("p (l j k) -> p l j k", j=6, k=8)
            V(lambda e: e.scalar_tensor_tensor(G1T[:, :].rearrange("p (l k) -> p l k", k=8), m4[:, :, 1, :], 1.0,
                                               nmixT[:, :].rearrange("p (l k) -> p l k", k=8), ALU.add, ALU.mult),
              r=["modT", "nmixT"], w=["G1T"])
            V(lambda e: e.scalar_tensor_tensor(G2T[:, :].rearrange("p (l k) -> p l k", k=8), m4[:, :, 4, :], 1.0,
                                               nmlpT[:, :].rearrange("p (l k) -> p l k", k=8), ALU.add, ALU.mult),
              r=["modT", "nmlpT"], w=["G2T"])
            S_.flush()
        if _STOP <= 0:
            return nc

        def build_bc(bc, bcname, l, j, dtmp, dname, pbanks):
            for k in range(8):
                V(lambda e, k=k: e.tensor_scalar(dtmp[:, :], identF[:, :], modcol(l, j, k), None, ALU.mult),
                  r=["identF", "modT", dname], w=[dname])
                bk = pbanks[k // 4]
                mm(bank(bk)[:, (k % 4) * 128:(k % 4 + 1) * 128], onesM[:, :], dtmp[:, :], True, True,
                   r=["onesM", dname], w=[PS(bk)])
            for hh in range(2):
                A(lambda e, hh=hh: e.activation(bc[:, hh * 512:(hh + 1) * 512], bank(pbanks[hh]), AF.Copy, scale=256.0),
                  r=[PS(pbanks[hh])], w=[bcname])

        def norm_tile(xt_ap, xtname, xn, ssq, hT_out, hTname, GT_, l, jsh, pb, ncols=128, coff=0):
            act(xn[:, :], xt_ap, AF.Square, r=[xtname], w=["xn", "ssq"], accum_out=ssq[:, 0:1])
            V(lambda e: e.tensor_scalar(ssq[:, 1:2], ssq[:, 0:1], 1.0 / D, EPS, ALU.mult, ALU.add),
              r=["ssq"], w=["ssq1"])
            act(ssq[:, 2:3], ssq[:, 1:2], AF.Ln, r=["ssq1"], w=["ssq2"])
            act(ssq[:, 3:4], ssq[:, 2:3], AF.Exp, r=["ssq2"], w=["ssq3"], scale=-0.5)
            V(lambda e: e.tensor_scalar(xn[:, :], xt_ap, ssq[:, 3:4], None, ALU.mult),
              r=[xtname, "ssq3", "xn"], w=["xn"])
            for half in range(2):
                bk = pb[half]
                for k in range(4 * half, 4 * half + 4):
                    tr(bank(bk)[:, (k % 4) * 128:(k % 4 + 1) * 128], xn[:, k * 128:(k + 1) * 128], identF[:, :],
                       r=["xn", "identF"], w=[PS(bk)])
                for k in range(4 * half, 4 * half + 4):
                    act(hT_out[:, k, coff:coff + ncols], bank(bk)[:, (k % 4) * 128:(k % 4 + 1) * 128], AF.Identity,
                        r=[PS(bk), "G1T", "G2T", "modT"], w=[hTname],
                        scale=GT_[:, l * 8 + k:l * 8 + k + 1], bias=modcol(l, jsh, k))

        for l in range(L):
            src_d = x_d if l == 0 else out_d
            pl = ExitStack()
            pl.__enter__()
            WB = 1128
            wB = sb(pl, "wB", [128, 8, WB], BF16)
            woS = sb(pl, "woS", [128, 8, D], BF16)
            with ExitStack() as p1:
                WA = 2560
                wA = sb(p1, "wA", [128, 8, WA], BF16)
                dg = sb(p1, "dg", [128, 2, 31, 128], BF16)
                xt = [sb(p1, "xt%d" % i, [128, D], F32) for i in range(2)]
                xn = sb(p1, "xn", [128, D], F32)
                ssq = sb(p1, "ssq", [128, 4], F32)
                hT = [sb(p1, "hT%d" % i, [128, 8, 128], BF16) for i in range(2)]
                tht = [sb(p1, "tht%d" % i, [128, 512], F32) for i in range(2)]
                qs = [sb(p1, "qs%d" % i, [128, 512], F32) for i in range(2)]
                sg = [sb(p1, "sg%d" % i, [128, 512], F32) for i in range(2)]
                gs = [sb(p1, "gs%d" % i, [128, 512], F32) for i in range(3)]
                vtok = [sb(p1, "vtok%d" % i, [128, 512], BF16) for i in range(3)]
                hcur = [sb(p1, "hcur%d" % i, [128, 2, 128], BF16) for i in range(2)]
                kT = sb(p1, "kT", [128, 512], F32)
                fT = sb(p1, "fT", [128, 512], F32)
                GT = sb(p1, "GT", [128, 512], F32)
                t1 = sb(p1, "t1", [128, 512], F32)
                E = sb(p1, "E", [128, 512], F32)
                E4 = [sb(p1, "E4%d" % i, [128, 512], F32) for i in range(2)]
                qtl = [sb(p1, "qtl%d" % i, [128, 512], BF16) for i in range(2)]
                ktl = [sb(p1, "ktl%d" % i, [128, 512], BF16) for i in range(2)]
                khT = sb(p1, "khT", [128, 512], BF16)
                qhA = [sb(p1, "qhA%d" % i, [128, 512], BF16) for i in range(2)]
                qhB = [sb(p1, "qhB%d" % i, [128, 512], BF16) for i in range(2)]
                khtok = [sb(p1, "khtok%d" % i, [128, 512], BF16) for i in range(2)]
                ATm = sb(p1, "ATm", [128, 512], BF16)
                Sst = sb(p1, "Sst", [128, 512], F32)
                Sbf0 = sb(p1, "Sbf0", [128, 512], BF16)
                Sbf1 = sb(p1, "Sbf1", [128, 512], BF16)
                oa = sb(p1, "oa", [128, 512], F32)
                oab = sb(p1, "oab", [128, 512], BF16)
                ss4 = sb(p1, "ss4", [128, 16], F32)
                scanm = sb(p1, "scanm", [128, 512], F32)
                cmask = sb(p1, "cmask", [128, 512], U8)
                cmf = sb(p1, "cmf", [128, 512], F32)
                hbuf = sb(p1, "hbuf", [128, 2, 160], BF16)
                cc = [sb(p1, "cc%d" % i, [128, 256], F32) for i in range(2)]
                csq = sb(p1, "csq", [128, 256], F32)
                stt_ = [sb(p1, "stt%d" % i, [128, 256], F32) for i in range(2)]
                yv = sb(p1, "yv", [128, 256], F32)
                thy = sb(p1, "thy", [128, 256], F32)
                rs_b = [sb(p1, "rs_b%d" % i, [128, 128], F32) for i in range(2)]
                cat6 = [sb(p1, "cat6_%d" % i, [128, 6, 128], BF16) for i in range(2)]

                wst1 = [sb(p1, "wst%d" % i, [128, D], F32) for i in range(2)]
                wi = 0
                for (gn, d0, s0) in (("v", 1024, 1024), ("g", 1536, 1536), ("q", 0, 0), ("f", 512, 512), ("cu", 2048, 3112)):
                    for k2 in range(4):
                        stg, stn = wst1[wi % 2], "wst%d" % (wi % 2)
                        wi += 1
                        ld(stg[:, :].rearrange("p (k n) -> p k n", n=512),
                           win_d[l, k2 * 256:(k2 + 1) * 256, s0:s0 + 512].rearrange("(k p) n -> p k n", p=128), w=[stn])
                        act(wA[:, 2 * k2:2 * k2 + 2, d0:d0 + 512], stg[:, :].rearrange("p (k n) -> p k n", n=512), AF.Copy,
                            r=[stn], w=["wA_" + gn])
                g1bc = sb(p1, "g1bc", [128, D], F32)
                dtmp1 = sb(p1, "dtmp", [128, 128], F32)
                def prefetch_a2():
                    for k in range(8):
                        rows = slice(k * 128, (k + 1) * 128)
                        ldc(wB[:, k, 0:1024], win_d[l, rows, 2048:3072], w=["wB"])
                        ldc(wB[:, k, 1024:1032], win_d[l, rows, 3104:3112], w=["wB"])
                        for r3 in range(3):
                            ldc(wB[:, k, 1032 + 32 * r3:1064 + 32 * r3], win_d[l, rows, 3072:3104], w=["wB"])
                    build_bc(g1bc, "g1bc", l, 2, dtmp1, "dtmp", (0, 1))
                    for k in range(8):
                        ld(wst1[k % 2][:, :], wout_d[l, k * 128:(k + 1) * 128, :], w=["wst%d" % (k % 2)])
                        V(lambda e, k=k: e.tensor_tensor(woS[:, k, :], wst1[k % 2][:, :], g1bc[:, :], ALU.mult),
                          r=["wst%d" % (k % 2), "g1bc"], w=["woS"])
                for ct in range(2):
                    for j in range(31):
                        c = l * 62 + ct * 31 + j
                        V(lambda e, ct=ct, j=j, c=c: e.tensor_scalar(dg[:, ct, j, :], identF[:, :], cvw[:, c:c + 1],
                                                                     None, ALU.mult),
                          r=["identF", "cvw"], w=["dg"])
                G(lambda e: e.memset(scanm[:, :], 1.0), w=["scanm"])
                sm3 = scanm[:, :].rearrange("p (c j) -> p c j", j=64)
                G(lambda e: e.memset(sm3[:, :, 0:1], 0.0), r=["scanm"], w=["scanm"])
                G(lambda e: e.memset(cmf[:, :], 1.0), w=["cmf"])
                for h in range(4):
                    G(lambda e, h=h: e.affine_select(cmf[:, h * 128:(h + 1) * 128], cmf[:, h * 128:(h + 1) * 128],
                                                     [[1, 128]], ALU.is_ge, fillreg(e, 0.0), base=0, channel_multiplier=-1),
                      r=["cmf"], w=["cmf"])
                    G(lambda e, h=h: e.memset(cmf[0:64, h * 128 + 64:(h + 1) * 128], 0.0), r=["cmf"], w=["cmf"])
                V(lambda e: e.tensor_copy(cmask[:, :], cmf[:, :]), r=["cmf"], w=["cmask"])
                for (tl, nm) in ((ATm, "ATm"), (qhA[0], "qhA0"), (qhA[1], "qhA1"), (qhB[0], "qhB0"), (qhB[1], "qhB1"),
                                 (Sst, "Sst"), (Sbf0, "Sbf0"), (Sbf1, "Sbf1")):
                    G(lambda e, tl=tl: e.memset(tl[:, :], 0.0), w=[nm])
                G(lambda e: e.memset(hbuf[:, :, :], 0.0), w=["hbuf"])

                bc4 = lambda tl_: tl_[:, l * 4:(l + 1) * 4].rearrange("p (h o) -> p h o", o=1).broadcast_to([128, 4, 128])
                lb_b, oml_b = bc4(lbT), bc4(omlT)
                v3 = lambda t_: t_[:, :].rearrange("p (h t) -> p h t", t=128)
                c3 = lambda t_: t_[:, :].rearrange("p (c j) -> p c j", j=64)
                c4 = lambda t_: t_[:, :].rearrange("p (h c j) -> p h c j", c=2, j=64)
                QSC = 128.0 ** -0.5
                st1 = {"th": 0}

                def nxt_th():
                    i = st1["th"] % 2
                    st1["th"] += 1
                    return tht[i], "tht%d" % i

                def sigm(dst, src_ap, r, w):
                    act(dst, src_ap, AF.Exp, r=r, w=w, scale=-1.0)
                    act(dst, dst, AF.Ln, r=w, w=w, bias=1.0)
                    act(dst, dst, AF.Exp, r=w, w=w, scale=-1.0)

                def stageX(t):
                    b = t % 2
                    b3 = t % 3
                    xtn, hTn = "xt%d" % b, "hT%d" % b
                    if t + 1 < NT:
                        ld(xt[1 - b][:, :], src_d[(t + 1) * 128:(t + 2) * 128, :], w=["xt%d" % (1 - b)])
                    norm_tile(xt[b][:, :], xtn, xn, ssq, hT[b], hTn, G1T, l, 0, (0, 0))
                    ld(hT_d[t, :, :], hT[b][:, :, :].rearrange("p k n -> p (k n)"), r=[hTn], w=["hTd%d" % t])
                    S_.unit()
                    for k in range(8):
                        mm(bank(1), hT[b][:, k, :], wA[:, k, 1024:1536], k == 0, k == 7, r=[hTn, "wA_v"], w=[PS(1)])
                    act(vtok[b3][:, :], bank(1), AF.Copy, r=[PS(1)], w=["vtok%d" % b3])
                    S_.unit()
                    for k in range(8):
                        mm(bank(2), hT[b][:, k, :], wA[:, k, 1536:2048], k == 0, k == 7, r=[hTn, "wA_g"], w=[PS(2)])
                    th_, thn_ = nxt_th()
                    sigm(th_[:, :], bank(2), [PS(2)], [thn_])
                    V(lambda e, th_=th_: e.tensor_tensor(gs[b3][:, :], th_[:, :], bank(2), ALU.mult),
                      r=[thn_, PS(2)], w=["gs%d" % b3])
                    S_.unit()
                    for f in range(4):
                        for k in range(8):
                            mm(bank(1)[:, f * 128:(f + 1) * 128], wA[:, k, f * 128:(f + 1) * 128], hT[b][:, k, :],
                               k == 0, k == 7, r=[hTn, "wA_q"], w=[PS(1)])
                    th_, thn_ = nxt_th()
                    sigm(th_[:, :], bank(1), [PS(1)], [thn_])
                    V(lambda e, th_=th_: e.tensor_tensor(qs[b][:, :], th_[:, :], bank(1), ALU.mult),
                      r=[thn_, PS(1)], w=["qs%d" % b])
                    S_.unit()
                    for f in range(4):
                        for k in range(8):
                            mm(bank(2)[:, f * 128:(f + 1) * 128], wA[:, k, 512 + f * 128:512 + (f + 1) * 128], hT[b][:, k, :],
                               k == 0, k == 7, r=[hTn, "wA_f"], w=[PS(2)])
                    sigm(sg[b][:, :], bank(2), [PS(2)], ["sg%d" % b])
                    S_.unit()
                    for f in range(4):
                        for k in range(8):
                            mm(bank(1)[:, f * 128:(f + 1) * 128], wA[:, k, 2048 + f * 128:2048 + (f + 1) * 128], hT[b][:, k, :],
                               k == 0, k == 7, r=[hTn, "wA_cu"], w=[PS(1)])
                    th_, thn_ = nxt_th()
                    sigm(th_[:, 0:256], bank(1)[:, 256:512], [PS(1)], [thn_])
                    V(lambda e, th_=th_: e.tensor_tensor(hcur[b][:, :, :].rearrange("p c t -> p (c t)"), th_[:, 0:256],
                                                         bank(1)[:, 0:256], ALU.mult),
                      r=[thn_, PS(1)], w=["hcur%d" % b])

                def stageY1(t):
                    b = t % 2
                    b3 = t % 3
                    qsn, sgn_, gsn, vtn, hcn, c6n = "qs%d" % b, "sg%d" % b, "gs%d" % b3, "vtok%d" % b3, "hcur%d" % b, "cat6_%d" % b
                    qs_, sg_, gs_, vt_, c6 = qs[b], sg[b], gs[b3], vtok[b3], cat6[b]
                    qtl_, ktl_, khtok_, qhA_, qhB_, E4_ = qtl[b], ktl[b], khtok[b], qhA[b], qhB[b], E4[b]
                    cc_, stt2, rsb_ = cc[b], stt_[b], rs_b[b]
                    qtln, ktln, khtokn, qhAn, qhBn, E4n = "qtl%d" % b, "ktl%d" % b, "khtok%d" % b, "qhA%d" % b, "qhB%d" % b, "E4%d" % b
                    ccn, sttn, rsbn = "cc%d" % b, "stt%d" % b, "rs_b%d" % b
                    G(lambda e: e.tensor_copy(hbuf[:, :, 32:160], hcur[b][:, :, :]), r=[hcn, "hbuf"], w=["hbuf"])
                    for ct in range(2):
                        for j in range(31):
                            mm(bank(3)[:, ct * 128:(ct + 1) * 128], dg[:, ct, j, :],
                               hbuf[:, ct, 2 + j:2 + j + 128], j == 0, j == 30, r=["dg", "hbuf"], w=[PS(3)])
                        S_.unit()
                    for ct in range(2):
                        cs = slice(ct * 128, (ct + 1) * 128)
                        act(cc_[:, cs], bank(3)[:, cs], AF.Identity,
                            r=[PS(3), "cvb"], w=[ccn], bias=cvb[:, l * 2 + ct:l * 2 + ct + 1])
                    act(csq[:, :], cc_[:, :], AF.Square, r=[ccn], w=["csq"])
                    G(lambda e: e.tensor_copy(hbuf[:, :, 0:32], hbuf[:, :, 128:160]), r=["hbuf", PS(3)], w=["hbuf"])
                    S_.unit()
                    for (si, src_, sn) in ((0, cc_, ccn), (1, csq, "csq")):
                        for ct in range(2):
                            mm(bank(4)[:, si * 128:(si + 1) * 128], onesM[:, :], src_[:, ct * 128:(ct + 1) * 128],
                               ct == 0, ct == 1, r=["onesM", sn], w=[PS(4)])
                    act(stt2[:, :], bank(4)[:, 0:256], AF.Copy, r=[PS(4)], w=[sttn])
                    V(lambda e: e.tensor_tensor(rsb_[:, :], stt2[:, 0:128], stt2[:, 0:128], ALU.mult), r=[sttn], w=[rsbn])
                    V(lambda e: e.tensor_tensor(rsb_[:, :], stt2[:, 128:256], rsb_[:, :], ALU.subtract),
                      r=[sttn, rsbn], w=[rsbn])
                    V(lambda e: e.tensor_scalar(rsb_[:, :], rsb_[:, :], EPS, None, ALU.add), r=[rsbn], w=[rsbn])
                    S_.unit()
                    V(lambda e: e.tensor_tensor(v3(t1), v3(sg_), oml_b, ALU.mult), r=[sgn_, "omlT"], w=["t1"])
                    V(lambda e: e.tensor_tensor(v3(fT), v3(t1), lb_b, ALU.add), r=["t1", "lbT"], w=["fT"])
                    V(lambda e: e.tensor_tensor(v3(kT), oml_b, v3(t1), ALU.subtract), r=["t1", "omlT"], w=["kT"])
                    V(lambda e: e.tensor_scalar(fT[:, :], fT[:, :], 1e-30, None, ALU.max), r=["fT"], w=["fT"])
                    act(fT[:, :], fT[:, :], AF.Ln, r=["fT"], w=["fT"])
                    act(rsb_[:, :], rsb_[:, :], AF.Ln, r=[rsbn], w=[rsbn])
                    act(rsb_[:, :], rsb_[:, :], AF.Exp, r=[rsbn], w=[rsbn], scale=-0.5)
                    S_.unit()
                    V(lambda e: e.tensor_tensor_scan(GT[:, :], scanm[:, :], fT[:, :], 0.0, ALU.mult, ALU.add),
                      r=["scanm", "fT"], w=["GT"])
                    V(lambda e: e.tensor_tensor(c3(t1), c3(GT), c3(GT)[:, :, 32:33].broadcast_to([128, 8, 64]),
                                                ALU.subtract), r=["GT"], w=["t1"])
                    act(E[:, :], t1[:, :], AF.Exp, r=["t1"], w=["E"])
                    V(lambda e: e.scalar_tensor_tensor(qtl_[:, :], qs_[:, :], QSC, E[:, :], ALU.mult, ALU.mult),
                      r=[qsn, "E"], w=[qtln])
                    S_.unit()
                    act(E[:, :], t1[:, :], AF.Exp, r=["t1", qtln], w=["E"], scale=-1.0)
                    V(lambda e: e.tensor_tensor(ktl_[:, :], kT[:, :], E[:, :], ALU.mult), r=["kT", "E"], w=[ktln])
                    V(lambda e: e.tensor_tensor(c3(t1), c3(GT)[:, :, 63:64].broadcast_to([128, 8, 64]), c3(GT),
                                                ALU.subtract), r=["GT", "E"], w=["t1"])
                    S_.unit()
                    act(E[:, :], t1[:, :], AF.Exp, r=["t1", ktln], w=["E"])
                    V(lambda e: e.tensor_tensor(khT[:, :], kT[:, :], E[:, :], ALU.mult), r=["kT", "E"], w=["khT"])
                    act(E4_[:, :], GT[:, :], AF.Exp, r=["GT"], w=[E4n])
                    S_.unit()
                    V(lambda e: e.scalar_tensor_tensor(c4(qhA_)[:, :, 0, :], c4(qs_)[:, :, 0, :], QSC, c4(E4_)[:, :, 0, :],
                                                       ALU.mult, ALU.mult), r=[qsn, E4n], w=[qhAn])
                    V(lambda e: e.scalar_tensor_tensor(c4(qhB_)[:, :, 1, :], c4(qs_)[:, :, 1, :], QSC, c4(E4_)[:, :, 1, :],
                                                       ALU.mult, ALU.mult), r=[qsn, E4n], w=[qhBn])
                    for h in range(4):
                        tr(bankb(4)[:, h * 128:(h + 1) * 128], khT[:, h * 128:(h + 1) * 128], identB[:, :],
                           r=["khT", "identB"], w=[PS(4)])
                    act(khtok_[:, :], bankb(4)[:, 0:512], AF.Copy, r=[PS(4)], w=[khtokn])
                    S_.unit()

                def stageY2(t):
                    b = t % 2
                    b3 = t % 3
                    qsn, sgn_, gsn, vtn, hcn, c6n = "qs%d" % b, "sg%d" % b, "gs%d" % b3, "vtok%d" % b3, "hcur%d" % b, "cat6_%d" % b
                    qs_, sg_, gs_, vt_, c6 = qs[b], sg[b], gs[b3], vtok[b3], cat6[b]
                    qtl_, ktl_, khtok_, qhA_, qhB_, E4_ = qtl[b], ktl[b], khtok[b], qhA[b], qhB[b], E4[b]
                    cc_, stt2, rsb_ = cc[b], stt_[b], rs_b[b]
                    qtln, ktln, khtokn, qhAn, qhBn, E4n = "qtl%d" % b, "ktl%d" % b, "khtok%d" % b, "qhA%d" % b, "qhB%d" % b, "E4%d" % b
                    ccn, sttn, rsbn = "cc%d" % b, "stt%d" % b, "rs_b%d" % b
                    for h in range(4):
                        mm(bank(5)[:, h * 128:(h + 1) * 128], ktl_[:, h * 128:(h + 1) * 128],
                           qtl_[:, h * 128:(h + 1) * 128], True, True, r=[ktln, qtln], w=[PS(5)])
                    V(lambda e: e.copy_predicated(ATm[:, :], cmask[:, :], bank(5)), r=[PS(5), "cmask"], w=["ATm"])
                    S_.unit()
                    for h in range(4):
                        hs = slice(h * 128, (h + 1) * 128)
                        mm(bank(6)[:, hs], ATm[:, hs], vt_[:, hs], h == 0, False, r=["ATm", vtn], w=[PS(6)])
                        mm(bank(6)[:, hs], qhA_[:, hs], Sbf0[:, hs], False, False, r=[qhAn, "Sbf0"], w=[PS(6)])
                    for h in range(4):
                        hs = slice(h * 128, (h + 1) * 128)
                        mm(bank(7)[:, hs], khtok_[0:64, hs], vt_[0:64, hs], True, True, r=[khtokn, vtn], w=[PS(7)])
                    dec = lambda c: c4(E4_)[:, :, c, 63:64].broadcast_to([128, 4, 128])
                    V(lambda e: e.tensor_tensor(v3(Sst), v3(Sst), dec(0), ALU.mult), r=["Sst", E4n], w=["Sst"])
                    V(lambda e: e.tensor_tensor(Sst[:, :], Sst[:, :], bank(7), ALU.add), r=["Sst", PS(7)], w=["Sst"])
                    act(Sbf1[:, :], Sst[:, :], AF.Copy, r=["Sst"], w=["Sbf1"])
                    S_.unit()
                    for h in range(4):
                        hs = slice(h * 128, (h + 1) * 128)
                        mm(bank(6)[:, hs], qhB_[:, hs], Sbf1[:, hs], False, True, r=[qhBn, "Sbf1"], w=[PS(6)])
                    for h in range(4):
                        hs = slice(h * 128, (h + 1) * 128)
                        mm(bank(7)[:, hs], khtok_[64:128, hs], vt_[64:128, hs], True, True, r=[khtokn, vtn], w=[PS(7)])
                    V(lambda e: e.tensor_tensor(v3(Sst), v3(Sst), dec(1), ALU.mult), r=["Sst", E4n], w=["Sst"])
                    V(lambda e: e.tensor_tensor(Sst[:, :], Sst[:, :], bank(7), ALU.add), r=["Sst", PS(7)], w=["Sst"])
                    act(Sbf0[:, :], Sst[:, :], AF.Copy, r=["Sst"], w=["Sbf0"])
                    S_.unit()
                    for h in range(4):
                        act(oa[:, h * 128:(h + 1) * 128], bank(6)[:, h * 128:(h + 1) * 128], AF.Square,
                            r=[PS(6)], w=["oa", "ss4"], accum_out=ss4[:, h:h + 1])
                    V(lambda e: e.tensor_scalar(ss4[:, 4:8], ss4[:, 0:4], 1.0 / 128, EPS, ALU.mult, ALU.add),
                      r=["ss4"], w=["ss4b"])
                    act(ss4[:, 8:12], ss4[:, 4:8], AF.Ln, r=["ss4b"], w=["ss4c"])
                    act(ss4[:, 12:16], ss4[:, 8:12], AF.Exp, r=["ss4c"], w=["ss4d"], scale=-0.5)
                    V(lambda e: e.tensor_tensor(v3(oa), v3(bank(6)),
                                                ss4[:, 12:16].rearrange("p (h o) -> p h o", o=1).broadcast_to([128, 4, 128]),
                                                ALU.mult), r=[PS(6), "ss4d", "oa"], w=["oa"])
                    V(lambda e: e.tensor_tensor(oab[:, :], oa[:, :], gs_[:, :], ALU.mult), r=["oa", gsn], w=["oab"])
                    S_.unit()
                    for h in range(4):
                        tr(bankb(5)[:, 512 + h * 128:512 + (h + 1) * 128], oab[:, h * 128:(h + 1) * 128], identB[:, :],
                           r=["oab", "identB"], w=[PS(5)])
                    for h in range(4):
                        act(c6[:, h, :], bankb(5)[:, 512 + h * 128:512 + (h + 1) * 128], AF.Copy,
                            r=[PS(5), "onwT"], w=[c6n], scale=onwT[:, l * 4 + h:l * 4 + h + 1])
                    S_.unit()
                    y3 = yv[:, :].rearrange("p (c t) -> p c t", t=128)
                    V(lambda e: e.tensor_tensor(y3, cc_[:, :].rearrange("p (c t) -> p c t", t=128),
                                                stt2[:, 0:128].rearrange("p (o t) -> p o t", o=1).broadcast_to([128, 2, 128]),
                                                ALU.subtract), r=[ccn, sttn], w=["yv"])
                    V(lambda e: e.tensor_tensor(y3, y3,
                                                rsb_[:, :].rearrange("p (o t) -> p o t", o=1).broadcast_to([128, 2, 128]),
                                                ALU.mult), r=["yv", rsbn], w=["yv"])
                    for ct in range(2):
                        cs = slice(ct * 128, (ct + 1) * 128)
                        V(lambda e, cs=cs, ct=ct: e.tensor_scalar(yv[:, cs], yv[:, cs], lnw[:, l * 2 + ct:l * 2 + ct + 1],
                                                                  lnb[:, l * 2 + ct:l * 2 + ct + 1], ALU.mult, ALU.add),
                          r=["yv", "lnw", "lnb"], w=["yv"])
                    sigm(thy[:, :], yv[:, :], ["yv"], ["thy"])
                    V(lambda e: e.tensor_tensor(c6[:, 4:6, :].rearrange("p c t -> p (c t)"), thy[:, :], yv[:, :], ALU.mult),
                      r=["thy", "yv"], w=[c6n])
                    ld(cat_d[t, :, :], c6[:, :, :].rearrange("p k n -> p (k n)"), r=[c6n], w=["catd%d" % t])

                def cap1(fn, t):
                    S_.begin()
                    fn(t)
                    return S_.end()

                ld(xt[0][:, :], src_d[0:128, :], w=["xt0"])
                for step in range(NT + 2):
                    streams = []
                    if step - 2 >= 0:
                        streams.append(cap1(stageY2, step - 2))
                    if 0 <= step - 1 < NT:
                        streams.append(cap1(stageY1, step - 1))
                    if step < NT:
                        streams.append(cap1(stageX, step))
                    S_.run_merged(streams)
                    if step == 1:
                        prefetch_a2()
                S_.flush()
            if _STOP <= 1:
                return nc

            with ExitStack() as p2:
                KTc = sb(p2, "KTc", [128, 2, S], BF16)
                Vaug = sb(p2, "Vaug", [128, NT, 4, 65], BF16)
                kidx = sb(p2, "kidx", [128, S], BF16)
                score = [sb(p2, "score%d" % i, [128, S], F32) for i in range(3)]
                junkb = sb(p2, "junkb", [128, S], BF16)
                junk8 = sb(p2, "junk8", [128, S], U8)
                cntd = sb(p2, "cntd", [128, NIT], F32)
                vc = sb(p2, "vc", [128, NIT], F32)
                thrc = sb(p2, "thrc", [128, 4], F32)
                ones1 = sb(p2, "ones1", [128, 1], BF16)
                xt = [sb(p2, "x2t%d" % i, [128, D], F32) for i in range(3)]
                hT = [sb(p2, "h2T%d" % i, [128, 8, 128], BF16) for i in range(3)]
                catT = [sb(p2, "catT%d" % i, [128, 8, 128], BF16) for i in range(3)]
                sqT = [sb(p2, "sqT%d" % i, [128, 2, 256], BF16) for i in range(3)]
                identB2 = sb(p2, "identB2", [128, 256], BF16)
                iqT = sb(p2, "iqT", [128, 3, 128], BF16)
                wab = sb(p2, "wab", [128, 8], F32)
                wsg = sb(p2, "wsg", [128, 8], F32)
                Rb = [sb(p2, "Rb%d" % i, [128, 512], F32) for i in range(3)]
                Mb = [sb(p2, "Mb%d" % i, [128, 512], BF16) for i in range(3)]
                PT = [sb(p2, "PT%d" % i, [128, 512], BF16) for i in range(3)]
                ob = sb(p2, "ob", [128, 256], BF16)
                bis = sb(p2, "bis", [128, 16], F32)
                tabA = sb(p2, "tabA", [128, NIT], F32)
                tabB = sb(p2, "tabB", [128, NIT], F32)
                cntc = sb(p2, "cntc", [128, NIT], F32)
                uc = sb(p2, "uc", [128, NIT], F32)
                mid = sb(p2, "mid", [128, NIT + 1], F32)
                top8 = sb(p2, "top8", [128, 8], F32)
                rs4 = sb(p2, "rs4", [128, 4], F32)

                G(lambda e: e.memset(Vaug[:, :, :, 64:65], 1.0), w=["Vaug"])
                G(lambda e: e.memset(ones1[:, :], 1.0), w=["ones1"])
                for i in range(3):
                    G(lambda e, i=i: e.memset(sqT[i][:, :, :], 0.0), w=["sqT%d" % i])
                for c in range(2):
                    V(lambda e, c=c: e.tensor_copy(identB2[:, c * 128:(c + 1) * 128], identB[:, :]), r=["identB"], w=["identB2"])

                IDXC = (32.0 ** -0.5) * (8.0 ** -0.5)
                state = {"rbi": 0, "mbi": 0, "mgi": 0}

                def stagePS(t):
                    b = t % 3
                    xtn, hTn, cTn, scn, sqn = "x2t%d" % b, "h2T%d" % b, "catT%d" % b, "score%d" % b, "sqT%d" % b
                    sc = score[b]
                    N = (t + 1) * 128
                    tok = slice(t * 128, (t + 1) * 128)
                    ld(xt[b][:, :], src_d[tok, :], w=[xtn])
                    ld(hT[b][:, :, :].rearrange("p k n -> p (k n)"), hT_d[t, :, :], r=["hTd%d" % t], w=[hTn])
                    ld(catT[b][:, 0:4, :].rearrange("p k n -> p (k n)"), cat_d[t, :, 0:512], r=["catd%d" % t], w=[cTn])
                    ld(catT[b][:, 6:8, :].rearrange("p k n -> p (k n)"), cat_d[t, :, 512:768], r=["catd%d" % t], w=[cTn])
                    S_.unit()
                    for k in range(8):
                        mm(bank(0)[:, 0:256], hT[b][:, k, :], wB[:, k, 512:768], k == 0, k == 7, r=[hTn, "wB"], w=[PS(0)])
                    for k in range(8):
                        mm(bank(0)[:, 256:264], hT[b][:, k, :], wB[:, k, 1024:1032], k == 0, k == 7,
                           r=[hTn, "wB"], w=[PS(0)])
                    act(Vaug[:, t, :, 0:64], bank(0)[:, 0:256].rearrange("p (h d) -> p h d", d=64), AF.Copy,
                        r=[PS(0)], w=["Vaug"])
                    act(wab[:, :], bank(0)[:, 256:264], AF.Abs, r=[PS(0)], w=["wab"], scale=IDXC)
                    act(wsg[:, :], bank(0)[:, 256:264], AF.Sign, r=[PS(0)], w=["wsg"])
                    S_.unit()
                    for f in range(4):
                        cb = (0, 128, 256, 384)[f]
                        for k in range(8):
                            mm(bank(1)[:, f * 128:(f + 1) * 128], wB[:, k, cb:cb + 128], hT[b][:, k, :],
                               k == 0, k == 7, r=[hTn, "wB"], w=[PS(1)])
                    for hl in range(2):
                        rows = slice(64 * hl, 64 * hl + 64)
                        act(sqT[b][rows, :, hl * 128:(hl + 1) * 128],
                            bank(1)[rows, 0:256].rearrange("p (c t) -> p c t", t=128), AF.Copy,
                            r=[PS(1)], w=[sqn], scale=0.125)
                    V(lambda e: e.tensor_copy(KTc[:, :, tok], bank(1)[:, 256:512].rearrange("p (c t) -> p c t", t=128)),
                      r=[PS(1)], w=["KTc"])
                    S_.unit()
                    for f, (cb, m) in enumerate(((768, 96), (864, 96), (960, 64), (1032, 96))):
                        for k in range(8):
                            mm(bank(2)[0:m, f * 128:(f + 1) * 128], wB[:, k, cb:cb + m], hT[b][:, k, :],
                               k == 0, k == 7, r=[hTn, "wB"], w=[PS(2)])
                    act(iqT[0:96, :, :], bank(2)[0:96, 0:384].rearrange("p (c t) -> p c t", t=128), AF.Copy,
                        r=[PS(2)], w=["iqT"])
                    V(lambda e: e.tensor_copy(kidx[0:96, tok], bank(2)[0:96, 384:512]), r=[PS(2)], w=["kidx"])
                    S_.unit()
                    NB = (N + 511) // 512
                    prev_acc = [None]
                    for kb in range(NB):
                        wN = min(512, N - kb * 512)
                        ks_ = slice(kb * 512, kb * 512 + wN)
                        for hh in range(8):
                            g_, r_ = hh // 3, hh % 3
                            pb = 3 + (hh % 2)
                            mm(bank(pb)[:, 0:wN], iqT[32 * r_:32 * r_ + 32, g_, :], kidx[32 * r_:32 * r_ + 32, ks_],
                               True, True, r=["iqT", "kidx"], w=[PS(pb)])
                            R_ = Rb[state["rbi"] % 3]
                            Rn = "Rb%d" % (state["rbi"] % 3)
                            state["rbi"] += 1
                            act(R_[:, 0:wN], bank(pb)[:, 0:wN], AF.Relu, r=[PS(pb), "wab"], w=[Rn],
                                scale=wab[:, hh:hh + 1])

                            def acc(R_=R_, Rn=Rn, ks_=ks_, wN=wN, hh=hh):
                                if hh == 0:
                                    V(lambda e: e.tensor_scalar(sc[:, ks_], R_[:, 0:wN], wsg[:, 0:1], None, ALU.mult),
                                      r=[Rn, "wsg"], w=[scn])
                                else:
                                    V(lambda e: e.scalar_tensor_tensor(sc[:, ks_], R_[:, 0:wN], wsg[:, hh:hh + 1], sc[:, ks_],
                                                                       ALU.mult, ALU.add),
                                      r=[Rn, "wsg", scn], w=[scn])
                            if prev_acc[0] is not None:
                                prev_acc[0]()
                            prev_acc[0] = acc
                            S_.unit()
                    prev_acc[0]()
                    G(lambda e: e.affine_select(sc[:, tok], sc[:, tok], [[-1, 128]], ALU.is_ge, fillreg(e, -1e30),
                                                base=0, channel_multiplier=1), r=[scn], w=[scn])

                def stageBI(t):
                    b = t % 3
                    scn = "score%d" % b
                    sc = score[b]
                    N = (t + 1) * 128
                    thn = "thr%d" % b
                    if t * 128 < TOPK:
                        V(lambda e: e.tensor_copy(thrc[:, b:b + 1], thrneg[:, 0:1]), r=["thrneg"], w=[thn])
                        return
                    Ka = max(128, min(N - 128, int(round(0.6 * (t + 1))) * 128))
                    V(lambda e: e.max(top8[:, :], sc[:, 0:N]), r=[scn], w=["top8"])
                    S_.unit()
                    V(lambda e: e.tensor_reduce(bis[:, 0:1], sc[:, 0:TOPK], mybir.AxisListType.X, ALU.min),
                      r=[scn], w=["bis0"])
                    V(lambda e: e.tensor_tensor(bis[:, 1:2], top8[:, 0:1], bis[:, 0:1], ALU.subtract),
                      r=["top8", "bis0"], w=["bis1"])
                    V(lambda e: e.tensor_scalar(tabA[:, :], tabA0[:, :], bis[:, 1:2], None, ALU.mult),
                      r=["bis1", "tabA0"], w=["tabA"])
                    V(lambda e: e.tensor_scalar(tabB[:, :], tabB0[:, :], bis[:, 1:2], None, ALU.mult),
                      r=["bis1", "tabB0"], w=["tabB"])
                    V(lambda e: e.scalar_tensor_tensor(mid[:, 0:1], bis[:, 1:2], 0.5, bis[:, 0:1], ALU.mult, ALU.add),
                      r=["bis0", "bis1"], w=["mid"])
                    S_.unit()
                    for n in range(NIT):
                        act(junkb[:, 0:Ka], sc[:, 0:Ka], AF.Sign, r=[scn, "mid"], w=["junkb", "cntA"],
                            scale=-1.0, bias=mid[:, n:n + 1], accum_out=cntc[:, n:n + 1])
                        V(lambda e, n=n: e.scalar_tensor_tensor(junk8[:, Ka:N], sc[:, Ka:N], mid[:, n:n + 1],
                                                                ones1[:, 0:1].broadcast_to([128, N - Ka]),
                                                                ALU.is_ge, ALU.mult, accum_out=cntd[:, n:n + 1]),
                          r=[scn, "mid", "ones1"], w=["junk8", "cntD"])
                        V(lambda e, n=n: e.scalar_tensor_tensor(vc[:, n:n + 1], cntd[:, n:n + 1], 2.0, cntc[:, n:n + 1],
                                                                ALU.mult, ALU.subtract),
                          r=["cntA", "cntD"], w=["vc"])
                        V(lambda e, n=n: e.scalar_tensor_tensor(uc[:, n:n + 1], vc[:, n:n + 1], float(2 * TOPK - Ka - 1),
                                                                tabB[:, n:n + 1], ALU.is_gt, ALU.mult),
                          r=["vc", "tabB"], w=["uc"])
                        V(lambda e, n=n: e.scalar_tensor_tensor(mid[:, n + 1:n + 2], mid[:, n:n + 1], tabA[:, n:n + 1],
                                                                uc[:, n:n + 1], ALU.subtract, ALU.add),
                          r=["mid", "tabA", "uc"], w=["mid"])
                        S_.unit()
                    V(lambda e: e.tensor_copy(thrc[:, b:b + 1], mid[:, NIT:NIT + 1]), r=["mid"], w=[thn])

                def stageAT(t):
                    b = t % 3
                    xtn, cTn, scn, sqn, thn = "x2t%d" % b, "catT%d" % b, "score%d" % b, "sqT%d" % b, "thr%d" % b
                    sc = score[b]
                    tok = slice(t * 128, (t + 1) * 128)
                    thr = thrc[:, b:b + 1]
                    prev_pv = [None]
                    NG = (t + 4) // 4
                    Nq = (t + 1) * 128
                    mg0 = state["mgi"]
                    state["mgi"] += NG

                    def genmask(g):
                        if g >= NG:
                            return
                        w_ = min(512, Nq - g * 512)
                        gi = mg0 + g
                        M_ = Mb[gi % 3]
                        V(lambda e: e.tensor_scalar(M_[:, 0:w_], sc[:, g * 512:g * 512 + w_], thr, -30000.0, ALU.is_lt, ALU.mult),
                          r=[scn, thn], w=["Mb%d" % (gi % 3)])
                    genmask(0)
                    genmask(1)
                    for kb in range(t + 1):
                        kcs = slice(kb * 128, (kb + 1) * 128)
                        mbi = state["mbi"]
                        state["mbi"] += 1
                        g = kb // 4
                        if kb % 4 == 0:
                            genmask(g + 2)
                        gi = mg0 + g
                        M_ = Mb[gi % 3][:, (kb % 4) * 128:(kb % 4 + 1) * 128]
                        Mn = "Mb%d" % (gi % 3)
                        P_ = PT[mbi % 3]
                        Pn = "PT%d" % (mbi % 3)
                        pb = 5 + (mbi % 2)
                        for c in range(2):
                            mm(bank(pb)[:, c * 256:(c + 1) * 256], KTc[:, c, kcs], sqT[b][:, c, :], True, False,
                               r=["KTc", sqn], w=[PS(pb)])
                            mm(bank(pb)[:, c * 256:(c + 1) * 256], M_, identB2[:, :], False, True,
                               r=[Mn, "identB2"], w=[PS(pb)])
                        act(P_[:, :], bank(pb), AF.Exp, r=[PS(pb)], w=[Pn])

                        def pv(kb=kb, P_=P_, Pn=Pn):
                            for h in range(4):
                                mm(bank(7)[:, h * 65:(h + 1) * 65], P_[:, h * 128:(h + 1) * 128], Vaug[:, kb, h, :],
                                   kb == 0 and h == 0, kb == t, r=[Pn, "Vaug"], w=[PS(7)])
                        if prev_pv[0] is not None:
                            prev_pv[0]()
                        prev_pv[0] = pv
                        S_.unit()
                    prev_pv[0]()
                    o3 = bank(7)[:, 0:260].rearrange("p (h d) -> p h d", d=65)
                    V(lambda e: e.reciprocal(rs4[:, :].rearrange("p (h o) -> p h o", o=1), o3[:, :, 64:65]), r=[PS(7)], w=["rs4"])
                    V(lambda e: e.tensor_tensor(ob[:, :].rearrange("p (h d) -> p h d", d=64), o3[:, :, 0:64],
                                                rs4[:, :].rearrange("p (h o) -> p h o", o=1).broadcast_to([128, 4, 64]),
                                                ALU.mult), r=[PS(7), "rs4"], w=["ob"])
                    for c in range(2):
                        tr(bankb(7)[:, c * 128:(c + 1) * 128], ob[:, c * 128:(c + 1) * 128], identB[:, :],
                           r=["ob", "identB"], w=[PS(7)])
                    act(catT[b][:, 4:6, :], bankb(7)[:, 0:256].rearrange("p (c t) -> p c t", t=128), AF.Copy,
                        r=[PS(7)], w=[cTn])
                    S_.unit()
                    for hf in range(2):
                        for c in range(8):
                            mm(bank(5 + hf), catT[b][:, c, :], woS[:, c, hf * 512:(hf + 1) * 512], c == 0, c == 7,
                               r=[cTn, "woS"], w=[PS(5 + hf)])
                        V(lambda e, hf=hf: e.tensor_tensor(xt[b][:, hf * 512:(hf + 1) * 512],
                                                           xt[b][:, hf * 512:(hf + 1) * 512], bank(5 + hf), ALU.add),
                          r=[xtn, PS(5 + hf)], w=[xtn])
                        S_.unit()
                    ld(out_d[tok, :], xt[b][:, :], r=[xtn], w=["outd%d" % t])

                def cap_(fn, t):
                    S_.begin()
                    fn(t)
                    return S_.end()

                for step in range(NT + 2):
                    streams = []
                    if step - 2 >= 0:
                        streams.append(cap_(stageAT, step - 2))
                    if 0 <= step - 1 < NT:
                        streams.append(cap_(stageBI, step - 1))
                    if step < NT:
                        streams.append(cap_(stagePS, step))
                    S_.run_merged(streams)
                S_.flush()
            if _STOP <= 2:
                return nc

            pl.__exit__(None, None, None)
            with ExitStack() as p3:
                w1S = sb(p3, "w1S", [128, 8, DFF], BF16)
                w2S = sb(p3, "w2S", [128, 32, D], BF16)
                g2bc = sb(p3, "g2bc", [128, D], F32)
                dtmp = sb(p3, "dtmp3", [128, 128], F32)
                xtN = [sb(p3, "x3n%d" % i, [128, D], F32) for i in range(2)]
                xr = [sb(p3, "x3r%d" % i, [128, D], F32) for i in range(2)]
                xn = sb(p3, "xn3", [128, D], F32)
                ssq = sb(p3, "ssq3", [128, 4], F32)
                ssf = sb(p3, "ssf3", [128, 4], F32)
                hTb = sb(p3, "hTb", [128, 8, 512], BF16)
                h1raw = sb(p3, "h1raw", [128, 8192], F32)
                h1T = h1raw[:, :].bitcast(BF16).rearrange("p (f t) -> p f t", t=512)
                wst = [h1raw[:, 0:1024], h1raw[:, 1024:2048]]
                wst_b = [h1raw[:, 2048:3072], h1raw[:, 3072:4096]]
                rl = [sb(p3, "rl%d" % i, [128, 512], F32) for i in range(2)]
                last = (l == L - 1)
                NBLK = S // 512
                stB = {"ri": 0, "ni": 0, "xi": 0}

                def stageN(blk):
                    for i in range(4):
                        t = blk * 4 + i
                        j = stB["ni"] % 2
                        stB["ni"] += 1
                        ld(xtN[j][:, :], out_d[t * 128:(t + 1) * 128, :], r=["outd%d" % t], w=["x3n%d" % j])
                        norm_tile(xtN[j][:, :], "x3n%d" % j, xn, ssq, hTb, "hTb", G2T, l, 3, (0, 1),
                                  ncols=128, coff=i * 128)
                        S_.unit()

                w1i = 0
                for cb in range(4):
                    for k in range(8):
                        stg, stn = wst_b[w1i % 2], "w1st%d" % (w1i % 2)
                        w1i += 1
                        ld(stg, w1_d[l, k * 128:(k + 1) * 128, cb * 1024:(cb + 1) * 1024], w=[stn])
                        act(w1S[:, k, cb * 1024:(cb + 1) * 1024], stg, AF.Copy, r=[stn], w=["w1S_%d" % cb])
                stageN(0)
                build_bc(g2bc, "g2bc", l, 5, dtmp, "dtmp3", (0, 1))
                for k in range(32):
                    ld(wst[k % 2], w2_d[l, k * 128:(k + 1) * 128, :], w=["w2st%d" % (k % 2)])
                    eng = V if k % 2 == 0 else G
                    eng(lambda e, k=k: e.tensor_tensor(w2S[:, k, :], wst[k % 2], g2bc[:, :], ALU.mult),
                        r=["w2st%d" % (k % 2), "g2bc"], w=["w2S_%d" % k])
                if last:
                    ld(g2bc[:, :], fnw_d[0:1, :].partition_broadcast(128), r=["w2S_%d" % k for k in range(32)], w=["g2bc"])
                def stageM1(blk):
                    for f in range(32):
                        pb = 2 + (f % 2)
                        for k in range(8):
                            mm(bank(pb), w1S[:, k, f * 128:(f + 1) * 128], hTb[:, k, :], k == 0, k == 7,
                               r=["w1S_%d" % (f // 8), "hTb"], w=[PS(pb)])
                        r_ = rl[stB["ri"] % 2]
                        rn = "rl%d" % (stB["ri"] % 2)
                        stB["ri"] += 1
                        act(r_[:, :], bank(pb), AF.Relu, r=[PS(pb)], w=[rn])
                        G(lambda e, r_=r_, f=f: e.tensor_tensor(h1T[:, f, :], r_[:, :], r_[:, :], ALU.mult),
                          r=[rn], w=["h1T", "w2st0", "w2st1", "w1st0", "w1st1"])

                def stageM2(blk):
                    for i in range(4):
                        t = blk * 4 + i
                        j = stB["xi"] % 2
                        stB["xi"] += 1
                        xb, xbn = xr[j], "x3r%d" % j
                        ld(xb[:, :], out_d[t * 128:(t + 1) * 128, :], r=["outd%d" % t], w=[xbn])
                        for hf in range(2):
                            pb = 4 + 2 * (i % 2) + hf
                            for f in range(32):
                                mm(bank(pb), h1T[:, f, i * 128:(i + 1) * 128], w2S[:, f, hf * 512:(hf + 1) * 512],
                                   f == 0, f == 31, r=["h1T", "w2S_%d" % f], w=[PS(pb)])
                                if f % 8 == 7:
                                    S_.unit()
                            V(lambda e, xb=xb, hf=hf, pb=pb: e.tensor_tensor(xb[:, hf * 512:(hf + 1) * 512],
                                                                             xb[:, hf * 512:(hf + 1) * 512], bank(pb), ALU.add),
                              r=[xbn, PS(pb)], w=[xbn])
                        if last:
                            act(xn[:, :], xb[:, :], AF.Square, r=[xbn], w=["xn", "ssf"], accum_out=ssf[:, 0:1])
                            V(lambda e: e.tensor_scalar(ssf[:, 1:2], ssf[:, 0:1], 1.0 / D, EPS, ALU.mult, ALU.add),
                              r=["ssf"], w=["ssf1"])
                            act(ssf[:, 2:3], ssf[:, 1:2], AF.Ln, r=["ssf1"], w=["ssf2"])
                            act(ssf[:, 3:4], ssf[:, 2:3], AF.Exp, r=["ssf2"], w=["ssf3"], scale=-0.5)
                            V(lambda e, xb=xb: e.scalar_tensor_tensor(xb[:, :], xb[:, :], ssf[:, 3:4], g2bc[:, :],
                                                                      ALU.mult, ALU.mult),
                              r=[xbn, "ssf3", "g2bc"], w=[xbn])
                        ld(out_d[t * 128:(t + 1) * 128, :], xb[:, :], r=[xbn], w=["outd%d" % t])
                        S_.unit()

                def capB(fn, blk):
                    S_.begin()
                    fn(blk)
                    return S_.end()

                for blk in range(NBLK):
                    stageM1(blk)
                    streams = [capB(stageM2, blk)]
                    if blk + 1 < NBLK:
                        streams.append(capB(stageN, blk + 1))
                    S_.run_merged(streams)
                S_.flush()
    return nc


def _colsT(v, L, n):
    v = np.asarray(v, np.float32).reshape(L, n, 128)
    return np.ascontiguousarray(v.transpose(2, 0, 1).reshape(128, L * n))


def make_in_maps(inp, L, nb):
    f = lambda a: np.ascontiguousarray(np.asarray(a, np.float32))
    shared = {
        "ada_w": f(inp["ada_w"]),
        "ada_bT": _colsT(inp["ada_b"], L, 48),
        "nmixT": _colsT(inp["norm_mix_w"], L, 8),
        "nmlpT": _colsT(inp["norm_mlp_w"], L, 8),
        "fnw": f(inp["final_norm_w"]).reshape(1, D),
        "w_in": f(inp["w_in"]), "w_out": f(inp["w_out"]), "w1": f(inp["mlp_w1"]), "w2": f(inp["mlp_w2"]),
        "lbT": _colsT(inp["hg_lb_logits"], L, 4),
        "onwT": _colsT(inp["hg_onorm_w"], L, 4),
        "cvw": np.ascontiguousarray(np.asarray(inp["cv_w"], np.float32).reshape(L, 31, 2, 128)
                                    .transpose(3, 0, 2, 1).reshape(128, L * 62)),
        "cvb": _colsT(inp["cv_b"], L, 2),
        "lnw": _colsT(inp["cv_ln_w"], L, 2),
        "lnb": _colsT(inp["cv_ln_b"], L, 2),
    }
    maps = []
    x = np.asarray(inp["x"], np.float32)
    c = np.asarray(inp["c"], np.float32)
    for b in range(nb):
        m = dict(shared)
        m["x"] = np.ascontiguousarray(x[b])
        m["cT"] = np.ascontiguousarray(c[b].reshape(8, 128).T)
        maps.append(m)
    return maps


_NC_CACHE = {}


def kernel(x, c, ada_w, ada_b, norm_mix_w, norm_mlp_w, w_in, hg_lb_logits, hg_onorm_w,
           cv_w, cv_b, cv_ln_w, cv_ln_b, w_out, mlp_w1, mlp_w2, final_norm_w):
    inp = dict(x=x, c=c, ada_w=ada_w, ada_b=ada_b, norm_mix_w=norm_mix_w, norm_mlp_w=norm_mlp_w, w_in=w_in,
               hg_lb_logits=hg_lb_logits, hg_onorm_w=hg_onorm_w, cv_w=cv_w, cv_b=cv_b, cv_ln_w=cv_ln_w,
               cv_ln_b=cv_ln_b, w_out=w_out, mlp_w1=mlp_w1, mlp_w2=mlp_w2, final_norm_w=final_norm_w)
    B, S, _ = np.asarray(x).shape
    L = np.asarray(w_in).shape[0]
    topk = min(256, S // 4)
    key = (S, L, topk)
    if key not in _NC_CACHE:
        _NC_CACHE[key] = build_nc(S, L, topk)
    nc = _NC_CACHE[key]
    maps = make_in_maps(inp, L, B)
    res = run_bass_kernel_spmd(nc, maps, core_ids=list(range(B)))
    return np.stack([np.asarray(r["out"], np.float32) for r in res.results], axis=0)
```

```python
import numpy as np
from contextlib import ExitStack
import concourse.bass as bass
import concourse.mybir as mybir
from concourse.bass_utils import run_bass_kernel_spmd

F32 = mybir.dt.float32
BF16 = mybir.dt.bfloat16
U8 = mybir.dt.uint8
AF = mybir.ActivationFunctionType
ALU = mybir.AluOpType

D = 1024
DIN = 3624
DFF = 4096
EPS = 1e-6
NIT = 16
ENGS = ("tensor", "vector", "scalar", "gpsimd", "sync")
NDMA_SEMS = 24
import os as _os
_STOP = int(_os.environ.get('KSTOP', '99'))
_CUT = int(_os.environ.get('KCUT', '99'))
_SUB = int(_os.environ.get('KSUB', '99'))


class _Op:
    __slots__ = ("eng", "fn", "deps", "signals", "sem", "val", "is_dma")

    def __init__(self, eng, fn, is_dma=False):
        self.eng = eng
        self.fn = fn
        self.deps = []
        self.signals = False
        self.sem = None
        self.val = 0
        self.is_dma = is_dma


class _Slot:
    __slots__ = ("writer", "readers")

    def __init__(self):
        self.writer = None
        self.readers = []


class Sched:
    def __init__(self, nc, es):
        self.nc = nc
        self.q = {e: [] for e in ENGS}
        self.slots = {}
        self.phase_dmas = []
        self.esem = {e: es.enter_context(nc.semaphore("es_" + e)) for e in ENGS}
        self.dsem = {e: [es.enter_context(nc.semaphore("ds_%s_%d" % (e, i))) for i in range(NDMA_SEMS)]
                     for e in ("sync", "gpsimd")}
        self.cnt = {e: 0 for e in ENGS}
        self.dcnt = {e: [0] * NDMA_SEMS for e in self.dsem}
        self.drr = {e: 0 for e in self.dsem}
        self.dprev = {e: [None] * NDMA_SEMS for e in self.dsem}
        self.waited = {e: {} for e in ENGS}
        self.nops = 0

    def _slot(self, k):
        s = self.slots.get(k)
        if s is None:
            s = self.slots[k] = _Slot()
        return s

    def _add(self, op, reads, writes):
        deps = set()
        for k in reads:
            s = self._slot(k)
            if s.writer is not None:
                deps.add(s.writer)
            if k.startswith("ps"):
                for r in s.readers:
                    if r.eng != op.eng:
                        deps.add(r)
        for k in writes:
            s = self._slot(k)
            if s.writer is not None:
                deps.add(s.writer)
            for r in s.readers:
                deps.add(r)
        deps.discard(op)
        for d in deps:
            if d.eng == "tensor" and op.eng == "tensor":
                continue
            op.deps.append(d)
            d.signals = True
        for k in writes:
            s = self._slot(k)
            s.writer = op
            s.readers = []
        for k in reads:
            if k not in writes:
                self._slot(k).readers.append(op)
        self.q[op.eng].append(op)
        self.nops += 1
        return op

    cap = None

    def begin(self):
        self.cap = [[]]

    def unit(self):
        if self.cap is not None and self.cap[-1]:
            self.cap.append([])

    def end(self):
        u = [x for x in self.cap if x]
        self.cap = None
        return u

    def run_merged(self, streams):
        pos = [0] * len(streams)
        while True:
            best, bf = -1, 2.0
            for i, s in enumerate(streams):
                if pos[i] < len(s):
                    f = (pos[i] + 0.5) / len(s)
                    if f < bf:
                        best, bf = i, f
            if best < 0:
                break
            for item in streams[best][pos[best]]:
                if item[0] == "op":
                    self.op(*item[1:])
                else:
                    self.dma(item[1], item[2], item[3], item[4], item[5], **item[6])
            pos[best] += 1

    def op(self, eng, fn, reads=(), writes=()):
        if self.cap is not None:
            self.cap[-1].append(("op", eng, fn, tuple(reads), tuple(writes)))
            return None
        return self._add(_Op(eng, fn), list(reads), list(writes))

    def dma(self, eng, out, in_, reads=(), writes=(), **kw):
        if self.cap is not None:
            self.cap[-1].append(("dma", eng, out, in_, tuple(reads), tuple(writes), kw))
            return None
        fn = lambda e: e.dma_start(out=out, in_=in_, **kw)
        op = _Op(eng, fn, is_dma=True)
        op.signals = True
        self._add(op, list(reads), list(writes))
        self.phase_dmas.append(op)
        return op

    def flush(self):
        nc = self.nc
        fin = _Op("sync", None)
        fin.deps = list(self.phase_dmas)
        self.phase_dmas = []
        self.q["sync"].append(fin)
        for e in ENGS:
            for op in self.q[e]:
                if op.fn is None:
                    continue
                if op.is_dma:
                    j = self.drr[e]
                    self.drr[e] = (j + 1) % NDMA_SEMS
                    self.dcnt[e][j] += 16
                    op.sem = self.dsem[e][j]
                    op.val = self.dcnt[e][j]
                    prev = self.dprev[e][j]
                    if prev is not None:
                        op.deps.append(prev)
                    self.dprev[e][j] = op
                elif op.signals:
                    self.cnt[e] += 1
                    op.sem = self.esem[e]
                    op.val = self.cnt[e]
        q = self.q
        waited_all = self.waited

        def run(e):
            def body(eng):
                waited = waited_all[e]
                for op in q[e]:
                    for d in op.deps:
                        if d.sem is None:
                            continue
                        key = id(d.sem)
                        if waited.get(key, 0) < d.val:
                            eng.wait_ge(d.sem, d.val)
                            waited[key] = d.val
                    if op.fn is None:
                        continue
                    ins = op.fn(eng)
                    if op.signals:
                        ins.then_inc(op.sem, 16 if op.is_dma else 1)
            return body

        with nc.Block() as block:
            if q["tensor"]:
                block.tensor(run("tensor"))
            if q["vector"]:
                block.vector(run("vector"))
            if q["scalar"]:
                block.scalar(run("scalar"))
            if q["gpsimd"]:
                block.gpsimd(run("gpsimd"))
            block.sync(run("sync"))
        self.q = {e: [] for e in ENGS}


def build_nc(S, L, TOPK, dbg=False):
    NT = S // 128
    nc = bass.Bass("TRN2", target_bir_lowering=False)

    def din(name, shape, dt=F32):
        return nc.dram_tensor(name, list(shape), dt, kind="ExternalInput").ap()

    x_d = din("x", [S, D])
    cT_d = din("cT", [128, 8])
    adaw_d = din("ada_w", [L, D, 6 * D])
    adabT_d = din("ada_bT", [128, L * 48])
    nmixT_d = din("nmixT", [128, L * 8])
    nmlpT_d = din("nmlpT", [128, L * 8])
    fnw_d = din("fnw", [1, D])
    win_d = din("w_in", [L, D, DIN])
    wout_d = din("w_out", [L, D, D])
    w1_d = din("w1", [L, D, DFF])
    w2_d = din("w2", [L, DFF, D])
    lbT_d = din("lbT", [128, L * 4])
    onwT_d = din("onwT", [128, L * 4])
    cvw_d = din("cvw", [128, L * 62])
    cvb_d = din("cvb", [128, L * 2])
    lnw_d = din("lnw", [128, L * 2])
    lnb_d = din("lnb", [128, L * 2])
    out_d = nc.dram_tensor("out", [S, D], F32, kind="ExternalOutput").ap()
    hT_d = nc.dram_tensor("hT_scr", [NT, 128, 1024], BF16, kind="Internal").ap()
    cat_d = nc.dram_tensor("cat_scr", [NT, 128, 768], BF16, kind="Internal").ap()

    with ExitStack() as es:
        S_ = Sched(nc, es)

        def V(fn, r=(), w=()):
            return S_.op("vector", fn, r, w)

        def A(fn, r=(), w=()):
            return S_.op("scalar", fn, r, w)

        def G(fn, r=(), w=()):
            return S_.op("gpsimd", fn, r, w)

        def T(fn, r=(), w=()):
            return S_.op("tensor", fn, r, w)

        _fillregs = {}

        def fillreg(e, val):
            if val not in _fillregs:
                _fillregs[val] = e.to_reg(val)
            return _fillregs[val]

        def mm(out, lhsT, rhs, start, stop, r, w):
            return T(lambda e: e.matmul(out, lhsT, rhs, start=start, stop=stop, skip_group_check=True), r, w)

        def tr(out, in_, ident, r, w):
            return T(lambda e: e.transpose(out, in_, ident), r, w)

        def act(out, in_, func, r, w, **kw):
            return A(lambda e: e.activation(out, in_, func, **kw), r, w)

        def ld(out, in_, w, r=(), **kw):
            return S_.dma("sync", out, in_, reads=r, writes=w, **kw)

        def ldc(out, in_, w, r=()):
            return S_.dma("gpsimd", out, in_, reads=r, writes=w, max_dma_last_dim=2048)

        _uid = [0]

        def sb(stack, name, shape, dt):
            _uid[0] += 1
            return stack.enter_context(nc.sbuf_tensor("s%d_%s" % (_uid[0], name), list(shape), dt))

        pst = [es.enter_context(nc.psum_tensor("pst%d" % i, [128, 1024], F32)) for i in range(4)]

        def bank(i):
            return pst[i // 2][:, (i % 2) * 512:(i % 2 + 1) * 512]

        def bankb(i):
            return bank(i).bitcast(BF16)

        PS = lambda i: "ps%d" % i

        identF = sb(es, "identF", [128, 128], F32)
        identB = sb(es, "identB", [128, 128], BF16)
        onesM = sb(es, "onesM", [128, 128], F32)
        cT = sb(es, "cTs", [128, 8], F32)
        cact = sb(es, "cact", [128, 8], F32)
        modT = sb(es, "modT", [128, L * 48], F32)
        adabT = sb(es, "adabT", [128, L * 48], F32)
        nmixT = sb(es, "nmixT", [128, L * 8], F32)
        nmlpT = sb(es, "nmlpT", [128, L * 8], F32)
        G1T = sb(es, "G1T", [128, L * 8], F32)
        G2T = sb(es, "G2T", [128, L * 8], F32)
        lbT = sb(es, "lbT", [128, L * 4], F32)
        omlT = sb(es, "omlT", [128, L * 4], F32)
        lbtmp = sb(es, "lbtmp", [128, 8], F32)
        onwT = sb(es, "onwT", [128, L * 4], F32)
        cvw = sb(es, "cvw", [128, L * 62], F32)
        cvb = sb(es, "cvb", [128, L * 2], F32)
        lnw = sb(es, "lnw", [128, L * 2], F32)
        lnb = sb(es, "lnb", [128, L * 2], F32)
        tabA0 = sb(es, "tabA0", [128, NIT], F32)
        tabB0 = sb(es, "tabB0", [128, NIT], F32)
        thrneg = sb(es, "thrneg", [128, 1], F32)
        evt = sb(es, "evt", [128, 512], F32)
        negh = sb(es, "negh", [128, 128], F32)
        omlh = sb(es, "omlh", [128, L * 4], F32)
        lbp = sb(es, "lbp", [128, L * 4], F32)
        onwh = sb(es, "onwh", [128, L * 4], F32)
        lnwh = sb(es, "lnwh", [128, L * 2], F32)
        lnbh = sb(es, "lnbh", [128, L * 2], F32)

        def modcol(l, j, k):
            c = l * 48 + j * 8 + k
            return modT[:, c:c + 1]

        with ExitStack() as ps_:
            stage = [sb(ps_, "adast%d" % i, [128, 8, 512], F32) for i in range(2)]
            rowS = [sb(ps_, "rowS%d" % i, [1, 512], F32) for i in range(2)]
            one1f = sb(ps_, "one1f", [1, 1], F32)
            G(lambda e: e.memset(one1f[:, :], 1.0), w=["one1f"])
            G(lambda e: e.memset(identF[:, :], 1.0), w=["identF"])
            G(lambda e: e.affine_select(identF[:, :], identF[:, :], [[-1, 128]], ALU.is_equal, fillreg(e, 0.0),
                                        base=0, channel_multiplier=1), r=["identF"], w=["identF"])
            V(lambda e: e.tensor_copy(identB[:, :], identF[:, :]), r=["identF"], w=["identB"])
            G(lambda e: e.memset(onesM[:, :], 1.0 / 256.0), w=["onesM"])
            G(lambda e: e.memset(thrneg[:, :], -1e29), w=["thrneg"])
            for n in range(NIT):
                a_n = 2.0 ** -(n + 2) if n < NIT - 1 else 2.0 ** -(NIT)
                b_n = 2.0 ** -(n + 1)
                G(lambda e, n=n, a_n=a_n: e.memset(tabA0[:, n:n + 1], a_n), w=["tabA0"])
                G(lambda e, n=n, b_n=b_n: e.memset(tabB0[:, n:n + 1], b_n), w=["tabB0"])
            for (dst, src, nm) in ((cT, cT_d, "cT"), (adabT, adabT_d, "adabT"), (nmixT, nmixT_d, "nmixT"),
                                   (nmlpT, nmlpT_d, "nmlpT"), (lbT, lbT_d, "lbT"), (onwT, onwT_d, "onwT"),
                                   (cvw, cvw_d, "cvw"), (cvb, cvb_d, "cvb"), (lnw, lnw_d, "lnw"),
                                   (lnb, lnb_d, "lnb")):
                ld(dst[:, :], src[:, :], w=[nm])
            act(cact[:, :], cT[:, :], AF.Silu, r=["cT"], w=["cact"])
            lb3 = lbT[:, :].rearrange("p (l h) -> p l h", h=4)
            act(lbT[:, :], lbT[:, :], AF.Exp, r=["lbT"], w=["lbT"])
            V(lambda e: e.tensor_copy(lbtmp[:, 0:4], lb3[:, 0, :]), r=["lbT"], w=["lbtmp"])
            for l in range(1, L):
                V(lambda e, l=l: e.tensor_tensor(lbtmp[:, 0:4], lbtmp[:, 0:4], lb3[:, l, :], ALU.add),
                  r=["lbT", "lbtmp"], w=["lbtmp"])
            V(lambda e: e.reciprocal(lbtmp[:, 4:8], lbtmp[:, 0:4]), r=["lbtmp"], w=["lbtmp2"])
            for l in range(L):
                V(lambda e, l=l: e.tensor_tensor(lb3[:, l, :], lb3[:, l, :], lbtmp[:, 4:8], ALU.mult),
                  r=["lbT", "lbtmp2"], w=["lbT"])
            V(lambda e: e.memset(lb3[:, 0, :], 0.0), r=["lbT"], w=["lbT"])
            for l in range(2, L):
                V(lambda e, l=l: e.tensor_tensor(lb3[:, l, :], lb3[:, l, :], lb3[:, l - 1, :], ALU.add),
                  r=["lbT"], w=["lbT"])
            V(lambda e: e.tensor_scalar(omlT[:, :], lbT[:, :], -1.0, 1.0, ALU.mult, ALU.add),
              r=["lbT"], w=["omlT"])
            V(lambda e: e.tensor_scalar(omlh[:, :], omlT[:, :], 0.5, None, ALU.mult), r=["omlT"], w=["omlh"])
            V(lambda e: e.tensor_tensor(lbp[:, :], lbT[:, :], omlh[:, :], ALU.add), r=["lbT", "omlh"], w=["lbp"])
            V(lambda e: e.tensor_scalar(onwh[:, :], onwT[:, :], 0.5, None, ALU.mult), r=["onwT"], w=["onwh"])
            V(lambda e: e.tensor_scalar(lnwh[:, :], lnw[:, :], 0.5, None, ALU.mult), r=["lnw"], w=["lnwh"])
            V(lambda e: e.tensor_scalar(lnbh[:, :], lnb[:, :], 0.5, None, ALU.mult), r=["lnb"], w=["lnbh"])
            G(lambda e: e.memset(negh[:, :], -0.5), w=["negh"])
            piece = 0
            for l in range(L):
                for pc in range(12):
                    st = stage[piece % 2]
                    sn = "adast%d" % (piece % 2)
                    ld(st[:, :, :], adaw_d[l, :, pc * 512:(pc + 1) * 512].rearrange("(k p) n -> p k n", p=128),
                       w=[sn])
                    rb = 2 + (piece % 2)
                    for k in range(8):
                        mm(bank(rb)[0:1, :], cact[:, k:k + 1], st[:, k, :], k == 0, k == 7, r=[sn, "cact"], w=[PS(rb)])
                    rw = rowS[piece % 2]
                    rwn = "rowS%d" % (piece % 2)
                    act(rw[0:1, :], bank(rb)[0:1, :], AF.Copy, r=[PS(rb)], w=[rwn])
                    for f in range(4):
                        col = l * 48 + pc * 4 + f
                        mm(bank(0)[:, col:col + 1], rw[0:1, f * 128:(f + 1) * 128], one1f[0:1, 0:1],
                           piece == 0 and f == 0, True, r=[rwn, "one1f"], w=[PS(0)])
                    piece += 1
            V(lambda e: e.tensor_tensor(modT[:, :], bank(0)[:, 0:L * 48], adabT[:, :], ALU.add),
              r=[PS(0), "adabT"], w=["modT"])
            m4 = modT[:, :].rearrange("p (l j k) -> p l j k", j=6, k=8)
            V(lambda e: e.scalar_tensor_tensor(G1T[:, :].rearrange("p (l k) -> p l k", k=8), m4[:, :, 1, :], 1.0,
                                               nmixT[:, :].rearrange("p (l k) -> p l k", k=8), ALU.add, ALU.mult),
              r=["modT", "nmixT"], w=["G1T"])
            V(lambda e: e.scalar_tensor_tensor(G2T[:, :].rearrange("p (l k) -> p l k", k=8), m4[:, :, 4, :], 1.0,
                                               nmlpT[:, :].rearrange("p (l k) -> p l k", k=8), ALU.add, ALU.mult),
              r=["modT", "nmlpT"], w=["G2T"])
            S_.flush()
        if _STOP <= 0:
            return nc

        def build_bc(bc, bcname, l, j, dtmp, dname, pbanks):
            for k in range(8):
                V(lambda e, k=k: e.tensor_scalar(dtmp[:, :], identF[:, :], modcol(l, j, k), None, ALU.mult),
                  r=["identF", "modT", dname], w=[dname])
                bk = pbanks[k // 4]
                mm(bank(bk)[:, (k % 4) * 128:(k % 4 + 1) * 128], onesM[:, :], dtmp[:, :], True, True,
                   r=["onesM", dname], w=[PS(bk)])
            for hh in range(2):
                A(lambda e, hh=hh: e.activation(bc[:, hh * 512:(hh + 1) * 512], bank(pbanks[hh]), AF.Copy, scale=256.0),
                  r=[PS(pbanks[hh])], w=[bcname])

        def norm_tile(xt_ap, xtname, xn, ssq, hT_out, hTname, GT_, l, jsh, pb, ncols=128, coff=0):
            act(xn[:, :], xt_ap, AF.Square, r=[xtname], w=["xn", "ssq"], accum_out=ssq[:, 0:1])
            V(lambda e: e.tensor_scalar(ssq[:, 1:2], ssq[:, 0:1], 1.0 / D, EPS, ALU.mult, ALU.add),
              r=["ssq"], w=["ssq1"])
            act(ssq[:, 2:3], ssq[:, 1:2], AF.Ln, r=["ssq1"], w=["ssq2"])
            act(ssq[:, 3:4], ssq[:, 2:3], AF.Exp, r=["ssq2"], w=["ssq3"], scale=-0.5)
            V(lambda e: e.tensor_scalar(xn[:, :], xt_ap, ssq[:, 3:4], None, ALU.mult),
              r=[xtname, "ssq3", "xn"], w=["xn"])
            for half in range(2):
                bk = pb[half]
                for k in range(4 * half, 4 * half + 4):
                    tr(bank(bk)[:, (k % 4) * 128:(k % 4 + 1) * 128], xn[:, k * 128:(k + 1) * 128], identF[:, :],
                       r=["xn", "identF"], w=[PS(bk)])
                k0 = 4 * half
                g_b = GT_[:, l * 8 + k0:l * 8 + k0 + 4].rearrange("p (k o) -> p k o", o=1).broadcast_to([128, 4, 128])
                c0 = l * 48 + jsh * 8 + k0
                s_b = modT[:, c0:c0 + 4].rearrange("p (k o) -> p k o", o=1).broadcast_to([128, 4, 128])
                V(lambda e, bk=bk, g_b=g_b: e.tensor_tensor(evt[:, :].rearrange("p (k t) -> p k t", t=128),
                                                            bank(bk).rearrange("p (k t) -> p k t", t=128), g_b, ALU.mult),
                  r=[PS(bk), "G1T", "G2T"], w=["evt"])
                V(lambda e, k0=k0, s_b=s_b: e.tensor_tensor(hT_out[:, k0:k0 + 4, coff:coff + ncols],
                                                            evt[:, :].rearrange("p (k t) -> p k t", t=128), s_b, ALU.add),
                  r=["evt", "modT"], w=[hTname])

        for l in range(L):
            src_d = x_d if l == 0 else out_d
            pl = ExitStack()
            pl.__enter__()
            WB = 1128
            wB = sb(pl, "wB", [128, 8, WB], BF16)
            woS = sb(pl, "woS", [128, 8, D], BF16)
            with ExitStack() as p1:
                WA = 2560
                wA = sb(p1, "wA", [128, 8, WA], BF16)
                dg = sb(p1, "dg", [128, 2, 31, 128], BF16)
                xt = [sb(p1, "xt%d" % i, [128, D], F32) for i in range(2)]
                xn = sb(p1, "xn", [128, D], F32)
                ssq = sb(p1, "ssq", [128, 4], F32)
                hT = [sb(p1, "hT%d" % i, [128, 8, 128], BF16) for i in range(2)]
                tht = [sb(p1, "tht%d" % i, [128, 512], F32) for i in range(2)]
                qs = [sb(p1, "qs%d" % i, [128, 512], F32) for i in range(2)]
                sg = [sb(p1, "sg%d" % i, [128, 512], F32) for i in range(2)]
                gs = [sb(p1, "gs%d" % i, [128, 512], F32) for i in range(3)]
                vtok = [sb(p1, "vtok%d" % i, [128, 512], BF16) for i in range(3)]
                hcur = [sb(p1, "hcur%d" % i, [128, 2, 128], BF16) for i in range(2)]
                kT = sb(p1, "kT", [128, 512], F32)
                fT = sb(p1, "fT", [128, 512], F32)
                GT = sb(p1, "GT", [128, 512], F32)
                t1 = sb(p1, "t1", [128, 512], F32)
                E = sb(p1, "E", [128, 512], F32)
                E4 = [sb(p1, "E4%d" % i, [128, 512], F32) for i in range(2)]
                qtl = [sb(p1, "qtl%d" % i, [128, 512], BF16) for i in range(2)]
                ktl = [sb(p1, "ktl%d" % i, [128, 512], BF16) for i in range(2)]
                khT = sb(p1, "khT", [128, 512], BF16)
                qhA = [sb(p1, "qhA%d" % i, [128, 512], BF16) for i in range(2)]
                qhB = [sb(p1, "qhB%d" % i, [128, 512], BF16) for i in range(2)]
                khtok = [sb(p1, "khtok%d" % i, [128, 512], BF16) for i in range(2)]
                ATm = sb(p1, "ATm", [128, 512], BF16)
                Sst = sb(p1, "Sst", [128, 512], F32)
                Sbf0 = sb(p1, "Sbf0", [128, 512], BF16)
                Sbf1 = sb(p1, "Sbf1", [128, 512], BF16)
                oa = sb(p1, "oa", [128, 512], F32)
                oab = sb(p1, "oab", [128, 512], BF16)
                ss4 = sb(p1, "ss4", [128, 16], F32)
                scanm = sb(p1, "scanm", [128, 512], F32)
                cmask = sb(p1, "cmask", [128, 512], U8)
                cmf = sb(p1, "cmf", [128, 512], F32)
                hbuf = sb(p1, "hbuf", [128, 2, 160], BF16)
                cc = [sb(p1, "cc%d" % i, [128, 256], F32) for i in range(2)]
                csq = sb(p1, "csq", [128, 256], F32)
                stt_ = [sb(p1, "stt%d" % i, [128, 256], F32) for i in range(2)]
                yv = sb(p1, "yv", [128, 256], F32)
                thy = sb(p1, "thy", [128, 256], F32)
                rs_b = [sb(p1, "rs_b%d" % i, [128, 128], F32) for i in range(2)]
                cat6 = [sb(p1, "cat6_%d" % i, [128, 6, 128], BF16) for i in range(2)]

                wst1 = [sb(p1, "wst%d" % i, [128, D], F32) for i in range(2)]
                wi = 0
                for (gn, d0, s0) in (("v", 1024, 1024), ("g", 1536, 1536), ("q", 0, 0), ("f", 512, 512), ("cu", 2048, 3112)):
                    for k2 in range(4):
                        stg, stn = wst1[wi % 2], "wst%d" % (wi % 2)
                        wi += 1
                        ld(stg[:, :].rearrange("p (k n) -> p k n", n=512),
                           win_d[l, k2 * 256:(k2 + 1) * 256, s0:s0 + 512].rearrange("(k p) n -> p k n", p=128), w=[stn])
                        act(wA[:, 2 * k2:2 * k2 + 2, d0:d0 + 512], stg[:, :].rearrange("p (k n) -> p k n", n=512), AF.Copy,
                            r=[stn], w=["wA_" + gn])
                g1bc = sb(p1, "g1bc", [128, D], F32)
                dtmp1 = sb(p1, "dtmp", [128, 128], F32)
                def prefetch_a2():
                    for k in range(8):
                        rows = slice(k * 128, (k + 1) * 128)
                        ldc(wB[:, k, 0:1024], win_d[l, rows, 2048:3072], w=["wB"])
                        ldc(wB[:, k, 1024:1032], win_d[l, rows, 3104:3112], w=["wB"])
                        for r3 in range(3):
                            ldc(wB[:, k, 1032 + 32 * r3:1064 + 32 * r3], win_d[l, rows, 3072:3104], w=["wB"])
                    build_bc(g1bc, "g1bc", l, 2, dtmp1, "dtmp", (0, 1))
                    for k in range(8):
                        ld(wst1[k % 2][:, :], wout_d[l, k * 128:(k + 1) * 128, :], w=["wst%d" % (k % 2)])
                        V(lambda e, k=k: e.tensor_tensor(woS[:, k, :], wst1[k % 2][:, :], g1bc[:, :], ALU.mult),
                          r=["wst%d" % (k % 2), "g1bc"], w=["woS"])
                for ct in range(2):
                    for j in range(31):
                        c = l * 62 + ct * 31 + j
                        V(lambda e, ct=ct, j=j, c=c: e.tensor_scalar(dg[:, ct, j, :], identF[:, :], cvw[:, c:c + 1],
                                                                     None, ALU.mult),
                          r=["identF", "cvw"], w=["dg"])
                G(lambda e: e.memset(scanm[:, :], 1.0), w=["scanm"])
                sm3 = scanm[:, :].rearrange("p (c j) -> p c j", j=64)
                G(lambda e: e.memset(sm3[:, :, 0:1], 0.0), r=["scanm"], w=["scanm"])
                G(lambda e: e.memset(cmf[:, :], 1.0), w=["cmf"])
                for h in range(4):
                    G(lambda e, h=h: e.affine_select(cmf[:, h * 128:(h + 1) * 128], cmf[:, h * 128:(h + 1) * 128],
                                                     [[1, 128]], ALU.is_ge, fillreg(e, 0.0), base=0, channel_multiplier=-1),
                      r=["cmf"], w=["cmf"])
                    G(lambda e, h=h: e.memset(cmf[0:64, h * 128 + 64:(h + 1) * 128], 0.0), r=["cmf"], w=["cmf"])
                V(lambda e: e.tensor_copy(cmask[:, :], cmf[:, :]), r=["cmf"], w=["cmask"])
                for (tl, nm) in ((ATm, "ATm"), (qhA[0], "qhA0"), (qhA[1], "qhA1"), (qhB[0], "qhB0"), (qhB[1], "qhB1"),
                                 (Sst, "Sst"), (Sbf0, "Sbf0"), (Sbf1, "Sbf1")):
                    G(lambda e, tl=tl: e.memset(tl[:, :], 0.0), w=[nm])
                G(lambda e: e.memset(hbuf[:, :, :], 0.0), w=["hbuf"])

                bc4 = lambda tl_: tl_[:, l * 4:(l + 1) * 4].rearrange("p (h o) -> p h o", o=1).broadcast_to([128, 4, 128])
                lb_b, oml_b = bc4(lbT), bc4(omlT)
                v3 = lambda t_: t_[:, :].rearrange("p (h t) -> p h t", t=128)
                c3 = lambda t_: t_[:, :].rearrange("p (c j) -> p c j", j=64)
                c4 = lambda t_: t_[:, :].rearrange("p (h c j) -> p h c j", c=2, j=64)
                QSC = 128.0 ** -0.5
                st1 = {"th": 0}

                def nxt_th():
                    i = st1["th"] % 2
                    st1["th"] += 1
                    return tht[i], "tht%d" % i

                def sigm(dst, src_ap, r, w):
                    act(dst, src_ap, AF.Exp, r=r, w=w, scale=-1.0)
                    act(dst, dst, AF.Ln, r=w, w=w, bias=1.0)
                    act(dst, dst, AF.Exp, r=w, w=w, scale=-1.0)

                def stageX(t):
                    b = t % 2
                    b3 = t % 3
                    xtn, hTn = "xt%d" % b, "hT%d" % b
                    if t + 1 < NT:
                        ld(xt[1 - b][:, :], src_d[(t + 1) * 128:(t + 2) * 128, :], w=["xt%d" % (1 - b)])
                    norm_tile(xt[b][:, :], xtn, xn, ssq, hT[b], hTn, G1T, l, 0, (0, 0))
                    ld(hT_d[t, :, :], hT[b][:, :, :].rearrange("p k n -> p (k n)"), r=[hTn], w=["hTd%d" % t])
                    S_.unit()
                    for k in range(8):
                        mm(bank(1), hT[b][:, k, :], wA[:, k, 1024:1536], k == 0, k == 7, r=[hTn, "wA_v"], w=[PS(1)])
                    act(vtok[b3][:, :], bank(1), AF.Copy, r=[PS(1)], w=["vtok%d" % b3])
                    S_.unit()
                    for k in range(8):
                        mm(bank(2), hT[b][:, k, :], wA[:, k, 1536:2048], k == 0, k == 7, r=[hTn, "wA_g"], w=[PS(2)])
                    th_, thn_ = nxt_th()
                    sigm(th_[:, :], bank(2), [PS(2)], [thn_])
                    V(lambda e, th_=th_: e.tensor_tensor(gs[b3][:, :], th_[:, :], bank(2), ALU.mult),
                      r=[thn_, PS(2)], w=["gs%d" % b3])
                    S_.unit()
                    for f in range(4):
                        for k in range(8):
                            mm(bank(1)[:, f * 128:(f + 1) * 128], wA[:, k, f * 128:(f + 1) * 128], hT[b][:, k, :],
                               k == 0, k == 7, r=[hTn, "wA_q"], w=[PS(1)])
                    th_, thn_ = nxt_th()
                    sigm(th_[:, :], bank(1), [PS(1)], [thn_])
                    V(lambda e, th_=th_: e.tensor_tensor(qs[b][:, :], th_[:, :], bank(1), ALU.mult),
                      r=[thn_, PS(1)], w=["qs%d" % b])
                    S_.unit()
                    for f in range(4):
                        for k in range(8):
                            mm(bank(2)[:, f * 128:(f + 1) * 128], wA[:, k, 512 + f * 128:512 + (f + 1) * 128], hT[b][:, k, :],
                               k == 0, k == 7, r=[hTn, "wA_f"], w=[PS(2)])
                    sigm(sg[b][:, :], bank(2), [PS(2)], ["sg%d" % b])
                    S_.unit()
                    for f in range(4):
                        for k in range(8):
                            mm(bank(1)[:, f * 128:(f + 1) * 128], wA[:, k, 2048 + f * 128:2048 + (f + 1) * 128], hT[b][:, k, :],
                               k == 0, k == 7, r=[hTn, "wA_cu"], w=[PS(1)])
                    th_, thn_ = nxt_th()
                    sigm(th_[:, 0:256], bank(1)[:, 256:512], [PS(1)], [thn_])
                    V(lambda e, th_=th_: e.tensor_tensor(hcur[b][:, :, :].rearrange("p c t -> p (c t)"), th_[:, 0:256],
                                                         bank(1)[:, 0:256], ALU.mult),
                      r=[thn_, PS(1)], w=["hcur%d" % b])

                def stageY1(t):
                    b = t % 2
                    b3 = t % 3
                    qsn, sgn_, gsn, vtn, hcn, c6n = "qs%d" % b, "sg%d" % b, "gs%d" % b3, "vtok%d" % b3, "hcur%d" % b, "cat6_%d" % b
                    qs_, sg_, gs_, vt_, c6 = qs[b], sg[b], gs[b3], vtok[b3], cat6[b]
                    qtl_, ktl_, khtok_, qhA_, qhB_, E4_ = qtl[b], ktl[b], khtok[b], qhA[b], qhB[b], E4[b]
                    cc_, stt2, rsb_ = cc[b], stt_[b], rs_b[b]
                    qtln, ktln, khtokn, qhAn, qhBn, E4n = "qtl%d" % b, "ktl%d" % b, "khtok%d" % b, "qhA%d" % b, "qhB%d" % b, "E4%d" % b
                    ccn, sttn, rsbn = "cc%d" % b, "stt%d" % b, "rs_b%d" % b
                    G(lambda e: e.tensor_copy(hbuf[:, :, 32:160], hcur[b][:, :, :]), r=[hcn, "hbuf"], w=["hbuf"])
                    for ct in range(2):
                        for j in range(31):
                            mm(bank(3)[:, ct * 128:(ct + 1) * 128], dg[:, ct, j, :],
                               hbuf[:, ct, 2 + j:2 + j + 128], j == 0, j == 30, r=["dg", "hbuf"], w=[PS(3)])
                        S_.unit()
                    for ct in range(2):
                        cs = slice(ct * 128, (ct + 1) * 128)
                        act(cc_[:, cs], bank(3)[:, cs], AF.Identity,
                            r=[PS(3), "cvb"], w=[ccn], bias=cvb[:, l * 2 + ct:l * 2 + ct + 1])
                    act(csq[:, :], cc_[:, :], AF.Square, r=[ccn], w=["csq"])
                    G(lambda e: e.tensor_copy(hbuf[:, :, 0:32], hbuf[:, :, 128:160]), r=["hbuf", PS(3)], w=["hbuf"])
                    S_.unit()
                    for (si, src_, sn) in ((0, cc_, ccn), (1, csq, "csq")):
                        for ct in range(2):
                            mm(bank(4)[:, si * 128:(si + 1) * 128], onesM[:, :], src_[:, ct * 128:(ct + 1) * 128],
                               ct == 0, ct == 1, r=["onesM", sn], w=[PS(4)])
                    act(stt2[:, :], bank(4)[:, 0:256], AF.Copy, r=[PS(4)], w=[sttn])
                    V(lambda e: e.tensor_tensor(rsb_[:, :], stt2[:, 0:128], stt2[:, 0:128], ALU.mult), r=[sttn], w=[rsbn])
                    V(lambda e: e.tensor_tensor(rsb_[:, :], stt2[:, 128:256], rsb_[:, :], ALU.subtract),
                      r=[sttn, rsbn], w=[rsbn])
                    V(lambda e: e.tensor_scalar(rsb_[:, :], rsb_[:, :], EPS, None, ALU.add), r=[rsbn], w=[rsbn])
                    S_.unit()
                    V(lambda e: e.tensor_tensor(v3(t1), v3(sg_), oml_b, ALU.mult), r=[sgn_, "omlT"], w=["t1"])
                    V(lambda e: e.tensor_tensor(v3(fT), v3(t1), lb_b, ALU.add), r=["t1", "lbT"], w=["fT"])
                    V(lambda e: e.tensor_tensor(v3(kT), oml_b, v3(t1), ALU.subtract), r=["t1", "omlT"], w=["kT"])
                    V(lambda e: e.tensor_scalar(fT[:, :], fT[:, :], 1e-30, None, ALU.max), r=["fT"], w=["fT"])
                    act(fT[:, :], fT[:, :], AF.Ln, r=["fT"], w=["fT"])
                    act(rsb_[:, :], rsb_[:, :], AF.Ln, r=[rsbn], w=[rsbn])
                    act(rsb_[:, :], rsb_[:, :], AF.Exp, r=[rsbn], w=[rsbn], scale=-0.5)
                    S_.unit()
                    V(lambda e: e.tensor_tensor_scan(GT[:, :], scanm[:, :], fT[:, :], 0.0, ALU.mult, ALU.add),
                      r=["scanm", "fT"], w=["GT"])
                    V(lambda e: e.tensor_tensor(c3(t1), c3(GT), c3(GT)[:, :, 32:33].broadcast_to([128, 8, 64]),
                                                ALU.subtract), r=["GT"], w=["t1"])
                    act(E[:, :], t1[:, :], AF.Exp, r=["t1"], w=["E"])
                    V(lambda e: e.scalar_tensor_tensor(qtl_[:, :], qs_[:, :], QSC, E[:, :], ALU.mult, ALU.mult),
                      r=[qsn, "E"], w=[qtln])
                    S_.unit()
                    act(E[:, :], t1[:, :], AF.Exp, r=["t1", qtln], w=["E"], scale=-1.0)
                    V(lambda e: e.tensor_tensor(ktl_[:, :], kT[:, :], E[:, :], ALU.mult), r=["kT", "E"], w=[ktln])
                    V(lambda e: e.tensor_tensor(c3(t1), c3(GT)[:, :, 63:64].broadcast_to([128, 8, 64]), c3(GT),
                                                ALU.subtract), r=["GT", "E"], w=["t1"])
                    S_.unit()
                    act(E[:, :], t1[:, :], AF.Exp, r=["t1", ktln], w=["E"])
                    V(lambda e: e.tensor_tensor(khT[:, :], kT[:, :], E[:, :], ALU.mult), r=["kT", "E"], w=["khT"])
                    act(E4_[:, :], GT[:, :], AF.Exp, r=["GT"], w=[E4n])
                    S_.unit()
                    V(lambda e: e.scalar_tensor_tensor(c4(qhA_)[:, :, 0, :], c4(qs_)[:, :, 0, :], QSC, c4(E4_)[:, :, 0, :],
                                                       ALU.mult, ALU.mult), r=[qsn, E4n], w=[qhAn])
                    V(lambda e: e.scalar_tensor_tensor(c4(qhB_)[:, :, 1, :], c4(qs_)[:, :, 1, :], QSC, c4(E4_)[:, :, 1, :],
                                                       ALU.mult, ALU.mult), r=[qsn, E4n], w=[qhBn])
                    for h in range(4):
                        tr(bankb(4)[:, h * 128:(h + 1) * 128], khT[:, h * 128:(h + 1) * 128], identB[:, :],
                           r=["khT", "identB"], w=[PS(4)])
                    act(khtok_[:, :], bankb(4)[:, 0:512], AF.Copy, r=[PS(4)], w=[khtokn])
                    S_.unit()

                def stageY2(t):
                    b = t % 2
                    b3 = t % 3
                    qsn, sgn_, gsn, vtn, hcn, c6n = "qs%d" % b, "sg%d" % b, "gs%d" % b3, "vtok%d" % b3, "hcur%d" % b, "cat6_%d" % b
                    qs_, sg_, gs_, vt_, c6 = qs[b], sg[b], gs[b3], vtok[b3], cat6[b]
                    qtl_, ktl_, khtok_, qhA_, qhB_, E4_ = qtl[b], ktl[b], khtok[b], qhA[b], qhB[b], E4[b]
                    cc_, stt2, rsb_ = cc[b], stt_[b], rs_b[b]
                    qtln, ktln, khtokn, qhAn, qhBn, E4n = "qtl%d" % b, "ktl%d" % b, "khtok%d" % b, "qhA%d" % b, "qhB%d" % b, "E4%d" % b
                    ccn, sttn, rsbn = "cc%d" % b, "stt%d" % b, "rs_b%d" % b
                    for h in range(4):
                        mm(bank(5)[:, h * 128:(h + 1) * 128], ktl_[:, h * 128:(h + 1) * 128],
                           qtl_[:, h * 128:(h + 1) * 128], True, True, r=[ktln, qtln], w=[PS(5)])
                    V(lambda e: e.copy_predicated(ATm[:, :], cmask[:, :], bank(5)), r=[PS(5), "cmask"], w=["ATm"])
                    S_.unit()
                    for h in range(4):
                        hs = slice(h * 128, (h + 1) * 128)
                        mm(bank(6)[:, hs], ATm[:, hs], vt_[:, hs], h == 0, False, r=["ATm", vtn], w=[PS(6)])
                        mm(bank(6)[:, hs], qhA_[:, hs], Sbf0[:, hs], False, False, r=[qhAn, "Sbf0"], w=[PS(6)])
                    for h in range(4):
                        hs = slice(h * 128, (h + 1) * 128)
                        mm(bank(7)[:, hs], khtok_[0:64, hs], vt_[0:64, hs], True, True, r=[khtokn, vtn], w=[PS(7)])
                    dec = lambda c: c4(E4_)[:, :, c, 63:64].broadcast_to([128, 4, 128])
                    V(lambda e: e.tensor_tensor(v3(Sst), v3(Sst), dec(0), ALU.mult), r=["Sst", E4n], w=["Sst"])
                    V(lambda e: e.tensor_tensor(Sst[:, :], Sst[:, :], bank(7), ALU.add), r=["Sst", PS(7)], w=["Sst"])
                    act(Sbf1[:, :], Sst[:, :], AF.Copy, r=["Sst"], w=["Sbf1"])
                    S_.unit()
                    for h in range(4):
                        hs = slice(h * 128, (h + 1) * 128)
                        mm(bank(6)[:, hs], qhB_[:, hs], Sbf1[:, hs], False, True, r=[qhBn, "Sbf1"], w=[PS(6)])
                    for h in range(4):
                        hs = slice(h * 128, (h + 1) * 128)
                        mm(bank(7)[:, hs], khtok_[64:128, hs], vt_[64:128, hs], True, True, r=[khtokn, vtn], w=[PS(7)])
                    V(lambda e: e.tensor_tensor(v3(Sst), v3(Sst), dec(1), ALU.mult), r=["Sst", E4n], w=["Sst"])
                    V(lambda e: e.tensor_tensor(Sst[:, :], Sst[:, :], bank(7), ALU.add), r=["Sst", PS(7)], w=["Sst"])
                    act(Sbf0[:, :], Sst[:, :], AF.Copy, r=["Sst"], w=["Sbf0"])
                    S_.unit()
                    for h in range(4):
                        act(oa[:, h * 128:(h + 1) * 128], bank(6)[:, h * 128:(h + 1) * 128], AF.Square,
                            r=[PS(6)], w=["oa", "ss4"], accum_out=ss4[:, h:h + 1])
                    V(lambda e: e.tensor_scalar(ss4[:, 4:8], ss4[:, 0:4], 1.0 / 128, EPS, ALU.mult, ALU.add),
                      r=["ss4"], w=["ss4b"])
                    act(ss4[:, 8:12], ss4[:, 4:8], AF.Ln, r=["ss4b"], w=["ss4c"])
                    act(ss4[:, 12:16], ss4[:, 8:12], AF.Exp, r=["ss4c"], w=["ss4d"], scale=-0.5)
                    V(lambda e: e.tensor_tensor(v3(oa), v3(bank(6)),
                                                ss4[:, 12:16].rearrange("p (h o) -> p h o", o=1).broadcast_to([128, 4, 128]),
                                                ALU.mult), r=[PS(6), "ss4d", "oa"], w=["oa"])
                    V(lambda e: e.tensor_tensor(oab[:, :], oa[:, :], gs_[:, :], ALU.mult), r=["oa", gsn], w=["oab"])
                    S_.unit()
                    for h in range(4):
                        tr(bankb(5)[:, 512 + h * 128:512 + (h + 1) * 128], oab[:, h * 128:(h + 1) * 128], identB[:, :],
                           r=["oab", "identB"], w=[PS(5)])
                    for h in range(4):
                        act(c6[:, h, :], bankb(5)[:, 512 + h * 128:512 + (h + 1) * 128], AF.Copy,
                            r=[PS(5), "onwT"], w=[c6n], scale=onwT[:, l * 4 + h:l * 4 + h + 1])
                    S_.unit()
                    y3 = yv[:, :].rearrange("p (c t) -> p c t", t=128)
                    V(lambda e: e.tensor_tensor(y3, cc_[:, :].rearrange("p (c t) -> p c t", t=128),
                                                stt2[:, 0:128].rearrange("p (o t) -> p o t", o=1).broadcast_to([128, 2, 128]),
                                                ALU.subtract), r=[ccn, sttn], w=["yv"])
                    V(lambda e: e.tensor_tensor(y3, y3,
                                                rsb_[:, :].rearrange("p (o t) -> p o t", o=1).broadcast_to([128, 2, 128]),
                                                ALU.mult), r=["yv", rsbn], w=["yv"])
                    for ct in range(2):
                        cs = slice(ct * 128, (ct + 1) * 128)
                        V(lambda e, cs=cs, ct=ct: e.tensor_scalar(yv[:, cs], yv[:, cs], lnw[:, l * 2 + ct:l * 2 + ct + 1],
                                                                  lnb[:, l * 2 + ct:l * 2 + ct + 1], ALU.mult, ALU.add),
                          r=["yv", "lnw", "lnb"], w=["yv"])
                    sigm(thy[:, :], yv[:, :], ["yv"], ["thy"])
                    V(lambda e: e.tensor_tensor(c6[:, 4:6, :].rearrange("p c t -> p (c t)"), thy[:, :], yv[:, :], ALU.mult),
                      r=["thy", "yv"], w=[c6n])
                    ld(cat_d[t, :, :], c6[:, :, :].rearrange("p k n -> p (k n)"), r=[c6n], w=["catd%d" % t])

                def cap1(fn, t):
                    S_.begin()
                    fn(t)
                    return S_.end()

                ld(xt[0][:, :], src_d[0:128, :], w=["xt0"])
                for step in range(NT + 2):
                    streams = []
                    if step - 2 >= 0:
                        streams.append(cap1(stageY2, step - 2))
                    if 0 <= step - 1 < NT:
                        streams.append(cap1(stageY1, step - 1))
                    if step < NT:
                        streams.append(cap1(stageX, step))
                    S_.run_merged(streams)
                    if step == 1:
                        prefetch_a2()
                S_.flush()
            if _STOP <= 1:
                return nc

            with ExitStack() as p2:
                KTc = sb(p2, "KTc", [128, 2, S], BF16)
                Vaug = sb(p2, "Vaug", [128, NT, 4, 65], BF16)
                kidx = sb(p2, "kidx", [128, S], BF16)
                score = [sb(p2, "score%d" % i, [128, S], F32) for i in range(3)]
                junkb = sb(p2, "junkb", [128, S], BF16)
                junk8 = sb(p2, "junk8", [128, S], U8)
                cntd = sb(p2, "cntd", [128, NIT], F32)
                vc = sb(p2, "vc", [128, NIT], F32)
                thrc = sb(p2, "thrc", [128, 4], F32)
                ones1 = sb(p2, "ones1", [128, 1], BF16)
                xt = [sb(p2, "x2t%d" % i, [128, D], F32) for i in range(3)]
                hT = [sb(p2, "h2T%d" % i, [128, 8, 128], BF16) for i in range(3)]
                catT = [sb(p2, "catT%d" % i, [128, 8, 128], BF16) for i in range(3)]
                sqT = [sb(p2, "sqT%d" % i, [128, 2, 256], BF16) for i in range(3)]
                identB2 = sb(p2, "identB2", [128, 256], BF16)
                iqT = sb(p2, "iqT", [128, 3, 128], BF16)
                wab = sb(p2, "wab", [128, 8], F32)
                wsg = sb(p2, "wsg", [128, 8], F32)
                Rb = [sb(p2, "Rb%d" % i, [128, 512], F32) for i in range(3)]
                Mb = [sb(p2, "Mb%d" % i, [128, 512], BF16) for i in range(3)]
                PT = [sb(p2, "PT%d" % i, [128, 512], BF16) for i in range(3)]
                ob = sb(p2, "ob", [128, 256], BF16)
                bis = sb(p2, "bis", [128, 16], F32)
                tabA = sb(p2, "tabA", [128, NIT], F32)
                tabB = sb(p2, "tabB", [128, NIT], F32)
                cntc = sb(p2, "cntc", [128, NIT], F32)
                uc = sb(p2, "uc", [128, NIT], F32)
                mid = sb(p2, "mid", [128, NIT + 1], F32)
                top8 = sb(p2, "top8", [128, 8], F32)
                rs4 = sb(p2, "rs4", [128, 4], F32)

                G(lambda e: e.memset(Vaug[:, :, :, 64:65], 1.0), w=["Vaug"])
                G(lambda e: e.memset(ones1[:, :], 1.0), w=["ones1"])
                for i in range(3):
                    G(lambda e, i=i: e.memset(sqT[i][:, :, :], 0.0), w=["sqT%d" % i])
                for c in range(2):
                    V(lambda e, c=c: e.tensor_copy(identB2[:, c * 128:(c + 1) * 128], identB[:, :]), r=["identB"], w=["identB2"])

                IDXC = (32.0 ** -0.5) * (8.0 ** -0.5)
                state = {"rbi": 0, "mbi": 0, "mgi": 0}

                def stagePS(t):
                    b = t % 3
                    xtn, hTn, cTn, scn, sqn = "x2t%d" % b, "h2T%d" % b, "catT%d" % b, "score%d" % b, "sqT%d" % b
                    sc = score[b]
                    N = (t + 1) * 128
                    tok = slice(t * 128, (t + 1) * 128)
                    ld(xt[b][:, :], src_d[tok, :], w=[xtn])
                    ld(hT[b][:, :, :].rearrange("p k n -> p (k n)"), hT_d[t, :, :], r=["hTd%d" % t], w=[hTn])
                    ld(catT[b][:, 0:4, :].rearrange("p k n -> p (k n)"), cat_d[t, :, 0:512], r=["catd%d" % t], w=[cTn])
                    ld(catT[b][:, 6:8, :].rearrange("p k n -> p (k n)"), cat_d[t, :, 512:768], r=["catd%d" % t], w=[cTn])
                    S_.unit()
                    for k in range(8):
                        mm(bank(0)[:, 0:256], hT[b][:, k, :], wB[:, k, 512:768], k == 0, k == 7, r=[hTn, "wB"], w=[PS(0)])
                    for k in range(8):
                        mm(bank(0)[:, 256:264], hT[b][:, k, :], wB[:, k, 1024:1032], k == 0, k == 7,
                           r=[hTn, "wB"], w=[PS(0)])
                    act(Vaug[:, t, :, 0:64], bank(0)[:, 0:256].rearrange("p (h d) -> p h d", d=64), AF.Copy,
                        r=[PS(0)], w=["Vaug"])
                    act(wab[:, :], bank(0)[:, 256:264], AF.Abs, r=[PS(0)], w=["wab"], scale=IDXC)
                    act(wsg[:, :], bank(0)[:, 256:264], AF.Sign, r=[PS(0)], w=["wsg"])
                    S_.unit()
                    for f in range(4):
                        cb = (0, 128, 256, 384)[f]
                        for k in range(8):
                            mm(bank(1)[:, f * 128:(f + 1) * 128], wB[:, k, cb:cb + 128], hT[b][:, k, :],
                               k == 0, k == 7, r=[hTn, "wB"], w=[PS(1)])
                    for hl in range(2):
                        rows = slice(64 * hl, 64 * hl + 64)
                        act(sqT[b][rows, :, hl * 128:(hl + 1) * 128],
                            bank(1)[rows, 0:256].rearrange("p (c t) -> p c t", t=128), AF.Copy,
                            r=[PS(1)], w=[sqn], scale=0.125)
                    V(lambda e: e.tensor_copy(KTc[:, :, tok], bank(1)[:, 256:512].rearrange("p (c t) -> p c t", t=128)),
                      r=[PS(1)], w=["KTc"])
                    S_.unit()
                    for f, (cb, m) in enumerate(((768, 96), (864, 96), (960, 64), (1032, 96))):
                        for k in range(8):
                            mm(bank(2)[0:m, f * 128:(f + 1) * 128], wB[:, k, cb:cb + m], hT[b][:, k, :],
                               k == 0, k == 7, r=[hTn, "wB"], w=[PS(2)])
                    act(iqT[0:96, :, :], bank(2)[0:96, 0:384].rearrange("p (c t) -> p c t", t=128), AF.Copy,
                        r=[PS(2)], w=["iqT"])
                    V(lambda e: e.tensor_copy(kidx[0:96, tok], bank(2)[0:96, 384:512]), r=[PS(2)], w=["kidx"])
                    S_.unit()
                    NB = (N + 511) // 512
                    prev_acc = [None]
                    for kb in range(NB):
                        wN = min(512, N - kb * 512)
                        ks_ = slice(kb * 512, kb * 512 + wN)
                        for hh in range(8):
                            g_, r_ = hh // 3, hh % 3
                            pb = 3 + (hh % 2)
                            mm(bank(pb)[:, 0:wN], iqT[32 * r_:32 * r_ + 32, g_, :], kidx[32 * r_:32 * r_ + 32, ks_],
                               True, True, r=["iqT", "kidx"], w=[PS(pb)])
                            R_ = Rb[state["rbi"] % 3]
                            Rn = "Rb%d" % (state["rbi"] % 3)
                            state["rbi"] += 1
                            act(R_[:, 0:wN], bank(pb)[:, 0:wN], AF.Relu, r=[PS(pb), "wab"], w=[Rn],
                                scale=wab[:, hh:hh + 1])

                            def acc(R_=R_, Rn=Rn, ks_=ks_, wN=wN, hh=hh):
                                if hh == 0:
                                    V(lambda e: e.tensor_scalar(sc[:, ks_], R_[:, 0:wN], wsg[:, 0:1], None, ALU.mult),
                                      r=[Rn, "wsg"], w=[scn])
                                else:
                                    V(lambda e: e.scalar_tensor_tensor(sc[:, ks_], R_[:, 0:wN], wsg[:, hh:hh + 1], sc[:, ks_],
                                                                       ALU.mult, ALU.add),
                                      r=[Rn, "wsg", scn], w=[scn])
                            if prev_acc[0] is not None:
                                prev_acc[0]()
                            prev_acc[0] = acc
                            S_.unit()
                    prev_acc[0]()
                    G(lambda e: e.affine_select(sc[:, tok], sc[:, tok], [[-1, 128]], ALU.is_ge, fillreg(e, -1e30),
                                                base=0, channel_multiplier=1), r=[scn], w=[scn])

                def stageBI(t):
                    b = t % 3
                    scn = "score%d" % b
                    sc = score[b]
                    N = (t + 1) * 128
                    thn = "thr%d" % b
                    if t * 128 < TOPK:
                        V(lambda e: e.tensor_copy(thrc[:, b:b + 1], thrneg[:, 0:1]), r=["thrneg"], w=[thn])
                        return
                    Ka = max(128, min(N - 128, int(round(0.66 * (t + 1))) * 128))
                    V(lambda e: e.max(top8[:, :], sc[:, 0:N]), r=[scn], w=["top8"])
                    S_.unit()
                    V(lambda e: e.tensor_reduce(bis[:, 0:1], sc[:, 0:TOPK], mybir.AxisListType.X, ALU.min),
                      r=[scn], w=["bis0"])
                    V(lambda e: e.tensor_tensor(bis[:, 1:2], top8[:, 0:1], bis[:, 0:1], ALU.subtract),
                      r=["top8", "bis0"], w=["bis1"])
                    V(lambda e: e.tensor_scalar(tabA[:, :], tabA0[:, :], bis[:, 1:2], None, ALU.mult),
                      r=["bis1", "tabA0"], w=["tabA"])
                    V(lambda e: e.tensor_scalar(tabB[:, :], tabB0[:, :], bis[:, 1:2], None, ALU.mult),
                      r=["bis1", "tabB0"], w=["tabB"])
                    V(lambda e: e.scalar_tensor_tensor(mid[:, 0:1], bis[:, 1:2], 0.5, bis[:, 0:1], ALU.mult, ALU.add),
                      r=["bis0", "bis1"], w=["mid"])
                    S_.unit()
                    for n in range(NIT):
                        act(junkb[:, 0:Ka], sc[:, 0:Ka], AF.Sign, r=[scn, "mid"], w=["junkb", "cntA"],
                            scale=-1.0, bias=mid[:, n:n + 1], accum_out=cntc[:, n:n + 1])
                        V(lambda e, n=n: e.scalar_tensor_tensor(junk8[:, Ka:N], sc[:, Ka:N], mid[:, n:n + 1],
                                                                ones1[:, 0:1].broadcast_to([128, N - Ka]),
                                                                ALU.is_ge, ALU.mult, accum_out=cntd[:, n:n + 1]),
                          r=[scn, "mid", "ones1"], w=["junk8", "cntD"])
                        V(lambda e, n=n: e.scalar_tensor_tensor(vc[:, n:n + 1], cntd[:, n:n + 1], 2.0, cntc[:, n:n + 1],
                                                                ALU.mult, ALU.subtract),
                          r=["cntA", "cntD"], w=["vc"])
                        V(lambda e, n=n: e.scalar_tensor_tensor(uc[:, n:n + 1], vc[:, n:n + 1], float(2 * TOPK - Ka - 1),
                                                                tabB[:, n:n + 1], ALU.is_gt, ALU.mult),
                          r=["vc", "tabB"], w=["uc"])
                        V(lambda e, n=n: e.scalar_tensor_tensor(mid[:, n + 1:n + 2], mid[:, n:n + 1], tabA[:, n:n + 1],
                                                                uc[:, n:n + 1], ALU.subtract, ALU.add),
                          r=["mid", "tabA", "uc"], w=["mid"])
                        S_.unit()
                    V(lambda e: e.tensor_copy(thrc[:, b:b + 1], mid[:, NIT:NIT + 1]), r=["mid"], w=[thn])

                def stageAT(t):
                    b = t % 3
                    xtn, cTn, scn, sqn, thn = "x2t%d" % b, "catT%d" % b, "score%d" % b, "sqT%d" % b, "thr%d" % b
                    sc = score[b]
                    tok = slice(t * 128, (t + 1) * 128)
                    thr = thrc[:, b:b + 1]
                    prev_pv = [None]
                    NG = (t + 4) // 4
                    Nq = (t + 1) * 128
                    mg0 = state["mgi"]
                    state["mgi"] += NG

                    def genmask(g):
                        if g >= NG:
                            return
                        w_ = min(512, Nq - g * 512)
                        gi = mg0 + g
                        M_ = Mb[gi % 3]
                        V(lambda e: e.tensor_scalar(M_[:, 0:w_], sc[:, g * 512:g * 512 + w_], thr, -30000.0, ALU.is_lt, ALU.mult),
                          r=[scn, thn], w=["Mb%d" % (gi % 3)])
                    genmask(0)
                    genmask(1)
                    for kb in range(t + 1):
                        kcs = slice(kb * 128, (kb + 1) * 128)
                        mbi = state["mbi"]
                        state["mbi"] += 1
                        g = kb // 4
                        if kb % 4 == 0:
                            genmask(g + 2)
                        gi = mg0 + g
                        M_ = Mb[gi % 3][:, (kb % 4) * 128:(kb % 4 + 1) * 128]
                        Mn = "Mb%d" % (gi % 3)
                        P_ = PT[mbi % 3]
                        Pn = "PT%d" % (mbi % 3)
                        pb = 5 + (mbi % 2)
                        for c in range(2):
                            mm(bank(pb)[:, c * 256:(c + 1) * 256], KTc[:, c, kcs], sqT[b][:, c, :], True, False,
                               r=["KTc", sqn], w=[PS(pb)])
                            mm(bank(pb)[:, c * 256:(c + 1) * 256], M_, identB2[:, :], False, True,
                               r=[Mn, "identB2"], w=[PS(pb)])
                        act(P_[:, :], bank(pb), AF.Exp, r=[PS(pb)], w=[Pn])

                        def pv(kb=kb, P_=P_, Pn=Pn):
                            for h in range(4):
                                mm(bank(7)[:, h * 65:(h + 1) * 65], P_[:, h * 128:(h + 1) * 128], Vaug[:, kb, h, :],
                                   kb == 0 and h == 0, kb == t, r=[Pn, "Vaug"], w=[PS(7)])
                        if prev_pv[0] is not None:
                            prev_pv[0]()
                        prev_pv[0] = pv
                        S_.unit()
                    prev_pv[0]()
                    o3 = bank(7)[:, 0:260].rearrange("p (h d) -> p h d", d=65)
                    V(lambda e: e.reciprocal(rs4[:, :].rearrange("p (h o) -> p h o", o=1), o3[:, :, 64:65]), r=[PS(7)], w=["rs4"])
                    V(lambda e: e.tensor_tensor(ob[:, :].rearrange("p (h d) -> p h d", d=64), o3[:, :, 0:64],
                                                rs4[:, :].rearrange("p (h o) -> p h o", o=1).broadcast_to([128, 4, 64]),
                                                ALU.mult), r=[PS(7), "rs4"], w=["ob"])
                    for c in range(2):
                        tr(bankb(7)[:, c * 128:(c + 1) * 128], ob[:, c * 128:(c + 1) * 128], identB[:, :],
                           r=["ob", "identB"], w=[PS(7)])
                    act(catT[b][:, 4:6, :], bankb(7)[:, 0:256].rearrange("p (c t) -> p c t", t=128), AF.Copy,
                        r=[PS(7)], w=[cTn])
                    S_.unit()
                    for hf in range(2):
                        for c in range(8):
                            mm(bank(5 + hf), catT[b][:, c, :], woS[:, c, hf * 512:(hf + 1) * 512], c == 0, c == 7,
                               r=[cTn, "woS"], w=[PS(5 + hf)])
                        V(lambda e, hf=hf: e.tensor_tensor(xt[b][:, hf * 512:(hf + 1) * 512],
                                                           xt[b][:, hf * 512:(hf + 1) * 512], bank(5 + hf), ALU.add),
                          r=[xtn, PS(5 + hf)], w=[xtn])
                        S_.unit()
                    ld(out_d[tok, :], xt[b][:, :], r=[xtn], w=["outd%d" % t])

                def cap_(fn, t):
                    S_.begin()
                    fn(t)
                    return S_.end()

                for step in range(NT + 2):
                    streams = []
                    if step - 2 >= 0:
                        streams.append(cap_(stageAT, step - 2))
                    if 0 <= step - 1 < NT:
                        streams.append(cap_(stageBI, step - 1))
                    if step < NT:
                        streams.append(cap_(stagePS, step))
                    S_.run_merged(streams)
                S_.flush()
            if _STOP <= 2:
                return nc

            pl.__exit__(None, None, None)
            with ExitStack() as p3:
                w1S = sb(p3, "w1S", [128, 8, DFF], BF16)
                w2S = sb(p3, "w2S", [128, 32, D], BF16)
                g2bc = sb(p3, "g2bc", [128, D], F32)
                dtmp = sb(p3, "dtmp3", [128, 128], F32)
                xtN = [sb(p3, "x3n%d" % i, [128, D], F32) for i in range(2)]
                xr = [sb(p3, "x3r%d" % i, [128, D], F32) for i in range(2)]
                xn = sb(p3, "xn3", [128, D], F32)
                ssq = sb(p3, "ssq3", [128, 4], F32)
                ssf = sb(p3, "ssf3", [128, 4], F32)
                hTb = sb(p3, "hTb", [128, 8, 512], BF16)
                h1raw = sb(p3, "h1raw", [128, 8192], F32)
                h1T = h1raw[:, :].bitcast(BF16).rearrange("p (f t) -> p f t", t=512)
                wst = [h1raw[:, 0:1024], h1raw[:, 1024:2048]]
                wst_b = [h1raw[:, 2048:3072], h1raw[:, 3072:4096]]
                rl = [sb(p3, "rl%d" % i, [128, 512], F32) for i in range(2)]
                last = (l == L - 1)
                NBLK = S // 512
                stB = {"ri": 0, "ni": 0, "xi": 0}

                def stageN(blk):
                    for i in range(4):
                        t = blk * 4 + i
                        j = stB["ni"] % 2
                        stB["ni"] += 1
                        ld(xtN[j][:, :], out_d[t * 128:(t + 1) * 128, :], r=["outd%d" % t], w=["x3n%d" % j])
                        norm_tile(xtN[j][:, :], "x3n%d" % j, xn, ssq, hTb, "hTb", G2T, l, 3, (0, 1),
                                  ncols=128, coff=i * 128)
                        S_.unit()

                w1i = 0
                for cb in range(4):
                    for k in range(8):
                        stg, stn = wst_b[w1i % 2], "w1st%d" % (w1i % 2)
                        w1i += 1
                        ld(stg, w1_d[l, k * 128:(k + 1) * 128, cb * 1024:(cb + 1) * 1024], w=[stn])
                        act(w1S[:, k, cb * 1024:(cb + 1) * 1024], stg, AF.Copy, r=[stn], w=["w1S_%d" % cb])
                stageN(0)
                build_bc(g2bc, "g2bc", l, 5, dtmp, "dtmp3", (0, 1))
                for k in range(32):
                    ld(wst[k % 2], w2_d[l, k * 128:(k + 1) * 128, :], w=["w2st%d" % (k % 2)])
                    eng = V if k % 2 == 0 else G
                    eng(lambda e, k=k: e.tensor_tensor(w2S[:, k, :], wst[k % 2], g2bc[:, :], ALU.mult),
                        r=["w2st%d" % (k % 2), "g2bc"], w=["w2S_%d" % k])
                if last:
                    ld(g2bc[:, :], fnw_d[0:1, :].partition_broadcast(128), r=["w2S_%d" % k for k in range(32)], w=["g2bc"])
                def stageM1(blk):
                    for f in range(32):
                        pb = 2 + (f % 2)
                        for k in range(8):
                            mm(bank(pb), w1S[:, k, f * 128:(f + 1) * 128], hTb[:, k, :], k == 0, k == 7,
                               r=["w1S_%d" % (f // 8), "hTb"], w=[PS(pb)])
                        r_ = rl[stB["ri"] % 2]
                        rn = "rl%d" % (stB["ri"] % 2)
                        stB["ri"] += 1
                        act(r_[:, :], bank(pb), AF.Relu, r=[PS(pb)], w=[rn])
                        G(lambda e, r_=r_, f=f: e.tensor_tensor(h1T[:, f, :], r_[:, :], r_[:, :], ALU.mult),
                          r=[rn], w=["h1T", "w2st0", "w2st1", "w1st0", "w1st1"])

                def stageM2(blk):
                    for i in range(4):
                        t = blk * 4 + i
                        j = stB["xi"] % 2
                        stB["xi"] += 1
                        xb, xbn = xr[j], "x3r%d" % j
                        ld(xb[:, :], out_d[t * 128:(t + 1) * 128, :], r=["outd%d" % t], w=[xbn])
                        for hf in range(2):
                            pb = 4 + 2 * (i % 2) + hf
                            for f in range(32):
                                mm(bank(pb), h1T[:, f, i * 128:(i + 1) * 128], w2S[:, f, hf * 512:(hf + 1) * 512],
                                   f == 0, f == 31, r=["h1T", "w2S_%d" % f], w=[PS(pb)])
                                if f % 8 == 7:
                                    S_.unit()
                            V(lambda e, xb=xb, hf=hf, pb=pb: e.tensor_tensor(xb[:, hf * 512:(hf + 1) * 512],
                                                                             xb[:, hf * 512:(hf + 1) * 512], bank(pb), ALU.add),
                              r=[xbn, PS(pb)], w=[xbn])
                        if last:
                            act(xn[:, :], xb[:, :], AF.Square, r=[xbn], w=["xn", "ssf"], accum_out=ssf[:, 0:1])
                            V(lambda e: e.tensor_scalar(ssf[:, 1:2], ssf[:, 0:1], 1.0 / D, EPS, ALU.mult, ALU.add),
                              r=["ssf"], w=["ssf1"])
                            act(ssf[:, 2:3], ssf[:, 1:2], AF.Ln, r=["ssf1"], w=["ssf2"])
                            act(ssf[:, 3:4], ssf[:, 2:3], AF.Exp, r=["ssf2"], w=["ssf3"], scale=-0.5)
                            V(lambda e, xb=xb: e.scalar_tensor_tensor(xb[:, :], xb[:, :], ssf[:, 3:4], g2bc[:, :],
                                                                      ALU.mult, ALU.mult),
                              r=[xbn, "ssf3", "g2bc"], w=[xbn])
                        ld(out_d[t * 128:(t + 1) * 128, :], xb[:, :], r=[xbn], w=["outd%d" % t])
                        S_.unit()

                def capB(fn, blk):
                    S_.begin()
                    fn(blk)
                    return S_.end()

                for blk in range(NBLK):
                    stageM1(blk)
                    streams = [capB(stageM2, blk)]
                    if blk + 1 < NBLK:
                        streams.append(capB(stageN, blk + 1))
                    S_.run_merged(streams)
                S_.flush()
    return nc


def _colsT(v, L, n):
    v = np.asarray(v, np.float32).reshape(L, n, 128)
    return np.ascontiguousarray(v.transpose(2, 0, 1).reshape(128, L * n))


def make_in_maps(inp, L, nb):
    f = lambda a: np.ascontiguousarray(np.asarray(a, np.float32))
    shared = {
        "ada_w": f(inp["ada_w"]),
        "ada_bT": _colsT(inp["ada_b"], L, 48),
        "nmixT": _colsT(inp["norm_mix_w"], L, 8),
        "nmlpT": _colsT(inp["norm_mlp_w"], L, 8),
        "fnw": f(inp["final_norm_w"]).reshape(1, D),
        "w_in": f(inp["w_in"]), "w_out": f(inp["w_out"]), "w1": f(inp["mlp_w1"]), "w2": f(inp["mlp_w2"]),
        "lbT": _colsT(inp["hg_lb_logits"], L, 4),
        "onwT": _colsT(inp["hg_onorm_w"], L, 4),
        "cvw": np.ascontiguousarray(np.asarray(inp["cv_w"], np.float32).reshape(L, 31, 2, 128)
                                    .transpose(3, 0, 2, 1).reshape(128, L * 62)),
        "cvb": _colsT(inp["cv_b"], L, 2),
        "lnw": _colsT(inp["cv_ln_w"], L, 2),
        "lnb": _colsT(inp["cv_ln_b"], L, 2),
    }
    maps = []
    x = np.asarray(inp["x"], np.float32)
    c = np.asarray(inp["c"], np.float32)
    for b in range(nb):
        m = dict(shared)
        m["x"] = np.ascontiguousarray(x[b])
        m["cT"] = np.ascontiguousarray(c[b].reshape(8, 128).T)
        maps.append(m)
    return maps


_NC_CACHE = {}


def kernel(x, c, ada_w, ada_b, norm_mix_w, norm_mlp_w, w_in, hg_lb_logits, hg_onorm_w,
           cv_w, cv_b, cv_ln_w, cv_ln_b, w_out, mlp_w1, mlp_w2, final_norm_w):
    inp = dict(x=x, c=c, ada_w=ada_w, ada_b=ada_b, norm_mix_w=norm_mix_w, norm_mlp_w=norm_mlp_w, w_in=w_in,
               hg_lb_logits=hg_lb_logits, hg_onorm_w=hg_onorm_w, cv_w=cv_w, cv_b=cv_b, cv_ln_w=cv_ln_w,
               cv_ln_b=cv_ln_b, w_out=w_out, mlp_w1=mlp_w1, mlp_w2=mlp_w2, final_norm_w=final_norm_w)
    B, S, _ = np.asarray(x).shape
    L = np.asarray(w_in).shape[0]
    topk = min(256, S // 4)
    key = (S, L, topk)
    if key not in _NC_CACHE:
        _NC_CACHE[key] = build_nc(S, L, topk)
    nc = _NC_CACHE[key]
    maps = make_in_maps(inp, L, B)
    res = run_bass_kernel_spmd(nc, maps, core_ids=list(range(B)))
    return np.stack([np.asarray(r["out"], np.float32) for r in res.results], axis=0)
```

```python
import numpy as np
from contextlib import ExitStack
import concourse.bass as bass
import concourse.mybir as mybir
from concourse.bass_utils import run_bass_kernel_spmd

F32 = mybir.dt.float32
BF16 = mybir.dt.bfloat16
U8 = mybir.dt.uint8
AF = mybir.ActivationFunctionType
ALU = mybir.AluOpType

D = 1024
DIN = 3624
DFF = 4096
EPS = 1e-6
NIT = 16
ENGS = ("tensor", "vector", "scalar", "gpsimd", "sync")
NDMA_SEMS = 24
import os as _os
_STOP = int(_os.environ.get('KSTOP', '99'))
_CUT = int(_os.environ.get('KCUT', '99'))
_SUB = int(_os.environ.get('KSUB', '99'))


class _Op:
    __slots__ = ("eng", "fn", "deps", "signals", "sem", "val", "is_dma")

    def __init__(self, eng, fn, is_dma=False):
        self.eng = eng
        self.fn = fn
        self.deps = []
        self.signals = False
        self.sem = None
        self.val = 0
        self.is_dma = is_dma


class _Slot:
    __slots__ = ("writer", "readers")

    def __init__(self):
        self.writer = None
        self.readers = []


class Sched:
    def __init__(self, nc, es):
        self.nc = nc
        self.q = {e: [] for e in ENGS}
        self.slots = {}
        self.phase_dmas = []
        self.esem = {e: es.enter_context(nc.semaphore("es_" + e)) for e in ENGS}
        self.dsem = {e: [es.enter_context(nc.semaphore("ds_%s_%d" % (e, i))) for i in range(NDMA_SEMS)]
                     for e in ("sync", "gpsimd")}
        self.cnt = {e: 0 for e in ENGS}
        self.dcnt = {e: [0] * NDMA_SEMS for e in self.dsem}
        self.drr = {e: 0 for e in self.dsem}
        self.dprev = {e: [None] * NDMA_SEMS for e in self.dsem}
        self.waited = {e: {} for e in ENGS}
        self.nops = 0

    def _slot(self, k):
        s = self.slots.get(k)
        if s is None:
            s = self.slots[k] = _Slot()
        return s

    def _add(self, op, reads, writes):
        deps = set()
        for k in reads:
            s = self._slot(k)
            if s.writer is not None:
                deps.add(s.writer)
            if k.startswith("ps"):
                for r in s.readers:
                    if r.eng != op.eng:
                        deps.add(r)
        for k in writes:
            s = self._slot(k)
            if s.writer is not None:
                deps.add(s.writer)
            for r in s.readers:
                deps.add(r)
        deps.discard(op)
        for d in deps:
            if d.eng == "tensor" and op.eng == "tensor":
                continue
            op.deps.append(d)
            d.signals = True
        for k in writes:
            s = self._slot(k)
            s.writer = op
            s.readers = []
        for k in reads:
            if k not in writes:
                self._slot(k).readers.append(op)
        self.q[op.eng].append(op)
        self.nops += 1
        return op

    cap = None

    def begin(self):
        self.cap = [[]]

    def unit(self):
        if self.cap is not None and self.cap[-1]:
            self.cap.append([])

    def end(self):
        u = [x for x in self.cap if x]
        self.cap = None
        return u

    def run_merged(self, streams):
        pos = [0] * len(streams)
        while True:
            best, bf = -1, 2.0
            for i, s in enumerate(streams):
                if pos[i] < len(s):
                    f = (pos[i] + 0.5) / len(s)
                    if f < bf:
                        best, bf = i, f
            if best < 0:
                break
            for item in streams[best][pos[best]]:
                if item[0] == "op":
                    self.op(*item[1:])
                else:
                    self.dma(item[1], item[2], item[3], item[4], item[5], **item[6])
            pos[best] += 1

    def op(self, eng, fn, reads=(), writes=()):
        if self.cap is not None:
            self.cap[-1].append(("op", eng, fn, tuple(reads), tuple(writes)))
            return None
        return self._add(_Op(eng, fn), list(reads), list(writes))

    def dma(self, eng, out, in_, reads=(), writes=(), **kw):
        if self.cap is not None:
            self.cap[-1].append(("dma", eng, out, in_, tuple(reads), tuple(writes), kw))
            return None
        fn = lambda e: e.dma_start(out=out, in_=in_, **kw)
        op = _Op(eng, fn, is_dma=True)
        op.signals = True
        self._add(op, list(reads), list(writes))
        self.phase_dmas.append(op)
        return op

    def flush(self):
        nc = self.nc
        fin = _Op("sync", None)
        fin.deps = list(self.phase_dmas)
        self.phase_dmas = []
        self.q["sync"].append(fin)
        for e in ENGS:
            for op in self.q[e]:
                if op.fn is None:
                    continue
                if op.is_dma:
                    j = self.drr[e]
                    self.drr[e] = (j + 1) % NDMA_SEMS
                    self.dcnt[e][j] += 16
                    op.sem = self.dsem[e][j]
                    op.val = self.dcnt[e][j]
                    prev = self.dprev[e][j]
                    if prev is not None:
                        op.deps.append(prev)
                    self.dprev[e][j] = op
                elif op.signals:
                    self.cnt[e] += 1
                    op.sem = self.esem[e]
                    op.val = self.cnt[e]
        q = self.q
        waited_all = self.waited

        def run(e):
            def body(eng):
                waited = waited_all[e]
                for op in q[e]:
                    for d in op.deps:
                        if d.sem is None:
                            continue
                        key = id(d.sem)
                        if waited.get(key, 0) < d.val:
                            eng.wait_ge(d.sem, d.val)
                            waited[key] = d.val
                    if op.fn is None:
                        continue
                    ins = op.fn(eng)
                    if op.signals:
                        ins.then_inc(op.sem, 16 if op.is_dma else 1)
            return body

        with nc.Block() as block:
            if q["tensor"]:
                block.tensor(run("tensor"))
            if q["vector"]:
                block.vector(run("vector"))
            if q["scalar"]:
                block.scalar(run("scalar"))
            if q["gpsimd"]:
                block.gpsimd(run("gpsimd"))
            block.sync(run("sync"))
        self.q = {e: [] for e in ENGS}


def build_nc(S, L, TOPK, dbg=False):
    NT = S // 128
    nc = bass.Bass("TRN2", target_bir_lowering=False)

    def din(name, shape, dt=F32):
        return nc.dram_tensor(name, list(shape), dt, kind="ExternalInput").ap()

    x_d = din("x", [S, D])
    cT_d = din("cT", [128, 8])
    adaw_d = din("ada_w", [L, D, 6 * D])
    adabT_d = din("ada_bT", [128, L * 48])
    nmixT_d = din("nmixT", [128, L * 8])
    nmlpT_d = din("nmlpT", [128, L * 8])
    fnw_d = din("fnw", [1, D])
    win_d = din("w_in", [L, D, DIN])
    wout_d = din("w_out", [L, D, D])
    w1_d = din("w1", [L, D, DFF])
    w2_d = din("w2", [L, DFF, D])
    lbT_d = din("lbT", [128, L * 4])
    onwT_d = din("onwT", [128, L * 4])
    cvw_d = din("cvw", [128, L * 62])
    cvb_d = din("cvb", [128, L * 2])
    lnw_d = din("lnw", [128, L * 2])
    lnb_d = din("lnb", [128, L * 2])
    out_d = nc.dram_tensor("out", [S, D], F32, kind="ExternalOutput").ap()
    hT_d = nc.dram_tensor("hT_scr", [NT, 128, 1024], BF16, kind="Internal").ap()
    cat_d = nc.dram_tensor("cat_scr", [NT, 128, 768], BF16, kind="Internal").ap()

    with ExitStack() as es:
        S_ = Sched(nc, es)

        def V(fn, r=(), w=()):
            return S_.op("vector", fn, r, w)

        def A(fn, r=(), w=()):
            return S_.op("scalar", fn, r, w)

        def G(fn, r=(), w=()):
            return S_.op("gpsimd", fn, r, w)

        def T(fn, r=(), w=()):
            return S_.op("tensor", fn, r, w)

        _fillregs = {}

        def fillreg(e, val):
            if val not in _fillregs:
                _fillregs[val] = e.to_reg(val)
            return _fillregs[val]

        def mm(out, lhsT, rhs, start, stop, r, w):
            return T(lambda e: e.matmul(out, lhsT, rhs, start=start, stop=stop, skip_group_check=True), r, w)

        def tr(out, in_, ident, r, w):
            return T(lambda e: e.transpose(out, in_, ident), r, w)

        def act(out, in_, func, r, w, **kw):
            return A(lambda e: e.activation(out, in_, func, **kw), r, w)

        def ld(out, in_, w, r=(), **kw):
            return S_.dma("sync", out, in_, reads=r, writes=w, **kw)

        def ldc(out, in_, w, r=()):
            return S_.dma("gpsimd", out, in_, reads=r, writes=w, max_dma_last_dim=2048)

        _uid = [0]

        def sb(stack, name, shape, dt):
            _uid[0] += 1
            return stack.enter_context(nc.sbuf_tensor("s%d_%s" % (_uid[0], name), list(shape), dt))

        pst = [es.enter_context(nc.psum_tensor("pst%d" % i, [128, 1024], F32)) for i in range(4)]

        def bank(i):
            return pst[i // 2][:, (i % 2) * 512:(i % 2 + 1) * 512]

        def bankb(i):
            return bank(i).bitcast(BF16)

        PS = lambda i: "ps%d" % i

        identF = sb(es, "identF", [128, 128], F32)
        identB = sb(es, "identB", [128, 128], BF16)
        onesM = sb(es, "onesM", [128, 128], F32)
        cT = sb(es, "cTs", [128, 8], F32)
        cact = sb(es, "cact", [128, 8], F32)
        modT = sb(es, "modT", [128, L * 48], F32)
        adabT = sb(es, "adabT", [128, L * 48], F32)
        nmixT = sb(es, "nmixT", [128, L * 8], F32)
        nmlpT = sb(es, "nmlpT", [128, L * 8], F32)
        G1T = sb(es, "G1T", [128, L * 8], F32)
        G2T = sb(es, "G2T", [128, L * 8], F32)
        lbT = sb(es, "lbT", [128, L * 4], F32)
        omlT = sb(es, "omlT", [128, L * 4], F32)
        lbtmp = sb(es, "lbtmp", [128, 8], F32)
        onwT = sb(es, "onwT", [128, L * 4], F32)
        cvw = sb(es, "cvw", [128, L * 62], F32)
        cvb = sb(es, "cvb", [128, L * 2], F32)
        lnw = sb(es, "lnw", [128, L * 2], F32)
        lnb = sb(es, "lnb", [128, L * 2], F32)
        tabA0 = sb(es, "tabA0", [128, NIT], F32)
        tabB0 = sb(es, "tabB0", [128, NIT], F32)
        thrneg = sb(es, "thrneg", [128, 1], F32)
        evt = sb(es, "evt", [128, 512], F32)
        negh = sb(es, "negh", [128, 128], F32)
        omlh = sb(es, "omlh", [128, L * 4], F32)
        lbp = sb(es, "lbp", [128, L * 4], F32)
        onwh = sb(es, "onwh", [128, L * 4], F32)
        lnwh = sb(es, "lnwh", [128, L * 2], F32)
        lnbh = sb(es, "lnbh", [128, L * 2], F32)

        def modcol(l, j, k):
            c = l * 48 + j * 8 + k
            return modT[:, c:c + 1]

        with ExitStack() as ps_:
            stage = [sb(ps_, "adast%d" % i, [128, 8, 512], F32) for i in range(2)]
            rowS = [sb(ps_, "rowS%d" % i, [1, 512], F32) for i in range(2)]
            one1f = sb(ps_, "one1f", [1, 1], F32)
            G(lambda e: e.memset(one1f[:, :], 1.0), w=["one1f"])
            G(lambda e: e.memset(identF[:, :], 1.0), w=["identF"])
            G(lambda e: e.affine_select(identF[:, :], identF[:, :], [[-1, 128]], ALU.is_equal, fillreg(e, 0.0),
                                        base=0, channel_multiplier=1), r=["identF"], w=["identF"])
            V(lambda e: e.tensor_copy(identB[:, :], identF[:, :]), r=["identF"], w=["identB"])
            G(lambda e: e.memset(onesM[:, :], 1.0 / 256.0), w=["onesM"])
            G(lambda e: e.memset(thrneg[:, :], -1e29), w=["thrneg"])
            for n in range(NIT):
                a_n = 2.0 ** -(n + 2) if n < NIT - 1 else 2.0 ** -(NIT)
                b_n = 2.0 ** -(n + 1)
                G(lambda e, n=n, a_n=a_n: e.memset(tabA0[:, n:n + 1], a_n), w=["tabA0"])
                G(lambda e, n=n, b_n=b_n: e.memset(tabB0[:, n:n + 1], b_n), w=["tabB0"])
            for (dst, src, nm) in ((cT, cT_d, "cT"), (adabT, adabT_d, "adabT"), (nmixT, nmixT_d, "nmixT"),
                                   (nmlpT, nmlpT_d, "nmlpT"), (lbT, lbT_d, "lbT"), (onwT, onwT_d, "onwT"),
                                   (cvw, cvw_d, "cvw"), (cvb, cvb_d, "cvb"), (lnw, lnw_d, "lnw"),
                                   (lnb, lnb_d, "lnb")):
                ld(dst[:, :], src[:, :], w=[nm])
            act(cact[:, :], cT[:, :], AF.Silu, r=["cT"], w=["cact"])
            lb3 = lbT[:, :].rearrange("p (l h) -> p l h", h=4)
            act(lbT[:, :], lbT[:, :], AF.Exp, r=["lbT"], w=["lbT"])
            V(lambda e: e.tensor_copy(lbtmp[:, 0:4], lb3[:, 0, :]), r=["lbT"], w=["lbtmp"])
            for l in range(1, L):
                V(lambda e, l=l: e.tensor_tensor(lbtmp[:, 0:4], lbtmp[:, 0:4], lb3[:, l, :], ALU.add),
                  r=["lbT", "lbtmp"], w=["lbtmp"])
            V(lambda e: e.reciprocal(lbtmp[:, 4:8], lbtmp[:, 0:4]), r=["lbtmp"], w=["lbtmp2"])
            for l in range(L):
                V(lambda e, l=l: e.tensor_tensor(lb3[:, l, :], lb3[:, l, :], lbtmp[:, 4:8], ALU.mult),
                  r=["lbT", "lbtmp2"], w=["lbT"])
            V(lambda e: e.memset(lb3[:, 0, :], 0.0), r=["lbT"], w=["lbT"])
            for l in range(2, L):
                V(lambda e, l=l: e.tensor_tensor(lb3[:, l, :], lb3[:, l, :], lb3[:, l - 1, :], ALU.add),
                  r=["lbT"], w=["lbT"])
            V(lambda e: e.tensor_scalar(omlT[:, :], lbT[:, :], -1.0, 1.0, ALU.mult, ALU.add),
              r=["lbT"], w=["omlT"])
            V(lambda e: e.tensor_scalar(omlh[:, :], omlT[:, :], 0.5, None, ALU.mult), r=["omlT"], w=["omlh"])
            V(lambda e: e.tensor_tensor(lbp[:, :], lbT[:, :], omlh[:, :], ALU.add), r=["lbT", "omlh"], w=["lbp"])
            V(lambda e: e.tensor_scalar(onwh[:, :], onwT[:, :], 0.5, None, ALU.mult), r=["onwT"], w=["onwh"])
            V(lambda e: e.tensor_scalar(lnwh[:, :], lnw[:, :], 0.5, None, ALU.mult), r=["lnw"], w=["lnwh"])
            V(lambda e: e.tensor_scalar(lnbh[:, :], lnb[:, :], 0.5, None, ALU.mult), r=["lnb"], w=["lnbh"])
            G(lambda e: e.memset(negh[:, :], -0.5), w=["negh"])
            piece = 0
            for l in range(L):
                for pc in range(12):
                    st = stage[piece % 2]
                    sn = "adast%d" % (piece % 2)
                    ld(st[:, :, :], adaw_d[l, :, pc * 512:(pc + 1) * 512].rearrange("(k p) n -> p k n", p=128),
                       w=[sn])
                    rb = 2 + (piece % 2)
                    for k in range(8):
                        mm(bank(rb)[0:1, :], cact[:, k:k + 1], st[:, k, :], k == 0, k == 7, r=[sn, "cact"], w=[PS(rb)])
                    rw = rowS[piece % 2]
                    rwn = "rowS%d" % (piece % 2)
                    act(rw[0:1, :], bank(rb)[0:1, :], AF.Copy, r=[PS(rb)], w=[rwn])
                    for f in range(4):
                        col = l * 48 + pc * 4 + f
                        mm(bank(0)[:, col:col + 1], rw[0:1, f * 128:(f + 1) * 128], one1f[0:1, 0:1],
                           piece == 0 and f == 0, True, r=[rwn, "one1f"], w=[PS(0)])
                    piece += 1
            V(lambda e: e.tensor_tensor(modT[:, :], bank(0)[:, 0:L * 48], adabT[:, :], ALU.add),
              r=[PS(0), "adabT"], w=["modT"])
            m4 = modT[:, :].rearrange("p (l j k) -> p l j k", j=6, k=8)
            V(lambda e: e.scalar_tensor_tensor(G1T[:, :].rearrange("p (l k) -> p l k", k=8), m4[:, :, 1, :], 1.0,
                                               nmixT[:, :].rearrange("p (l k) -> p l k", k=8), ALU.add, ALU.mult),
              r=["modT", "nmixT"], w=["G1T"])
            V(lambda e: e.scalar_tensor_tensor(G2T[:, :].rearrange("p (l k) -> p l k", k=8), m4[:, :, 4, :], 1.0,
                                               nmlpT[:, :].rearrange("p (l k) -> p l k", k=8), ALU.add, ALU.mult),
              r=["modT", "nmlpT"], w=["G2T"])
            S_.flush()
        if _STOP <= 0:
            return nc

        def build_bc(bc, bcname, l, j, dtmp, dname, pbanks):
            for k in range(8):
                V(lambda e, k=k: e.tensor_scalar(dtmp[:, :], identF[:, :], modcol(l, j, k), None, ALU.mult),
                  r=["identF", "modT", dname], w=[dname])
                bk = pbanks[k // 4]
                mm(bank(bk)[:, (k % 4) * 128:(k % 4 + 1) * 128], onesM[:, :], dtmp[:, :], True, True,
                   r=["onesM", dname], w=[PS(bk)])
            for hh in range(2):
                A(lambda e, hh=hh: e.activation(bc[:, hh * 512:(hh + 1) * 512], bank(pbanks[hh]), AF.Copy, scale=256.0),
                  r=[PS(pbanks[hh])], w=[bcname])

        def norm_tile(xt_ap, xtname, xn, ssq, hT_out, hTname, GT_, l, jsh, pb, ncols=128, coff=0):
            act(xn[:, :], xt_ap, AF.Square, r=[xtname], w=["xn", "ssq"], accum_out=ssq[:, 0:1])
            V(lambda e: e.tensor_scalar(ssq[:, 1:2], ssq[:, 0:1], 1.0 / D, EPS, ALU.mult, ALU.add),
              r=["ssq"], w=["ssq1"])
            act(ssq[:, 2:3], ssq[:, 1:2], AF.Ln, r=["ssq1"], w=["ssq2"])
            act(ssq[:, 3:4], ssq[:, 2:3], AF.Exp, r=["ssq2"], w=["ssq3"], scale=-0.5)
            V(lambda e: e.tensor_scalar(xn[:, :], xt_ap, ssq[:, 3:4], None, ALU.mult),
              r=[xtname, "ssq3", "xn"], w=["xn"])
            for half in range(2):
                bk = pb[half]
                for k in range(4 * half, 4 * half + 4):
                    tr(bank(bk)[:, (k % 4) * 128:(k % 4 + 1) * 128], xn[:, k * 128:(k + 1) * 128], identF[:, :],
                       r=["xn", "identF"], w=[PS(bk)])
                k0 = 4 * half
                g_b = GT_[:, l * 8 + k0:l * 8 + k0 + 4].rearrange("p (k o) -> p k o", o=1).broadcast_to([128, 4, 128])
                c0 = l * 48 + jsh * 8 + k0
                s_b = modT[:, c0:c0 + 4].rearrange("p (k o) -> p k o", o=1).broadcast_to([128, 4, 128])
                V(lambda e, bk=bk, g_b=g_b: e.tensor_tensor(evt[:, :].rearrange("p (k t) -> p k t", t=128),
                                                            bank(bk).rearrange("p (k t) -> p k t", t=128), g_b, ALU.mult),
                  r=[PS(bk), "G1T", "G2T"], w=["evt"])
                V(lambda e, k0=k0, s_b=s_b: e.tensor_tensor(hT_out[:, k0:k0 + 4, coff:coff + ncols],
                                                            evt[:, :].rearrange("p (k t) -> p k t", t=128), s_b, ALU.add),
                  r=["evt", "modT"], w=[hTname])

        for l in range(L):
            src_d = x_d if l == 0 else out_d
            pl = ExitStack()
            pl.__enter__()
            WB = 1128
            wB = sb(pl, "wB", [128, 8, WB], BF16)
            woS = sb(pl, "woS", [128, 8, D], BF16)
            with ExitStack() as p1:
                WA = 2560
                wA = sb(p1, "wA", [128, 8, WA], BF16)
                dg = sb(p1, "dg", [128, 2, 31, 128], BF16)
                xt = [sb(p1, "xt%d" % i, [128, D], F32) for i in range(2)]
                xn = sb(p1, "xn", [128, D], F32)
                ssq = sb(p1, "ssq", [128, 4], F32)
                hT = [sb(p1, "hT%d" % i, [128, 8, 128], BF16) for i in range(2)]
                tht = [sb(p1, "tht%d" % i, [128, 512], F32) for i in range(2)]
                qs = [sb(p1, "qs%d" % i, [128, 512], F32) for i in range(2)]
                sg = [sb(p1, "sg%d" % i, [128, 512], F32) for i in range(2)]
                gs = [sb(p1, "gs%d" % i, [128, 512], F32) for i in range(3)]
                vtok = [sb(p1, "vtok%d" % i, [128, 512], BF16) for i in range(3)]
                hcur = [sb(p1, "hcur%d" % i, [128, 2, 128], BF16) for i in range(2)]
                kT = sb(p1, "kT", [128, 512], F32)
                fT = sb(p1, "fT", [128, 512], F32)
                GT = sb(p1, "GT", [128, 512], F32)
                t1 = sb(p1, "t1", [128, 512], F32)
                E = sb(p1, "E", [128, 512], F32)
                E4 = [sb(p1, "E4%d" % i, [128, 512], F32) for i in range(2)]
                qtl = [sb(p1, "qtl%d" % i, [128, 512], BF16) for i in range(2)]
                ktl = [sb(p1, "ktl%d" % i, [128, 512], BF16) for i in range(2)]
                khT = sb(p1, "khT", [128, 512], BF16)
                qhA = [sb(p1, "qhA%d" % i, [128, 512], BF16) for i in range(2)]
                qhB = [sb(p1, "qhB%d" % i, [128, 512], BF16) for i in range(2)]
                khtok = [sb(p1, "khtok%d" % i, [128, 512], BF16) for i in range(2)]
                ATm = sb(p1, "ATm", [128, 512], BF16)
                Sst = sb(p1, "Sst", [128, 512], F32)
                Sbf0 = sb(p1, "Sbf0", [128, 512], BF16)
                Sbf1 = sb(p1, "Sbf1", [128, 512], BF16)
                oa = sb(p1, "oa", [128, 512], F32)
                oab = sb(p1, "oab", [128, 512], BF16)
                ss4 = sb(p1, "ss4", [128, 16], F32)
                scanm = sb(p1, "scanm", [128, 512], F32)
                cmask = sb(p1, "cmask", [128, 512], U8)
                cmf = sb(p1, "cmf", [128, 512], F32)
                hbuf = sb(p1, "hbuf", [128, 2, 160], BF16)
                cc = [sb(p1, "cc%d" % i, [128, 256], F32) for i in range(2)]
                csq = sb(p1, "csq", [128, 256], F32)
                stt_ = [sb(p1, "stt%d" % i, [128, 256], F32) for i in range(2)]
                yv = sb(p1, "yv", [128, 256], F32)
                thy = sb(p1, "thy", [128, 256], F32)
                rs_b = [sb(p1, "rs_b%d" % i, [128, 128], F32) for i in range(2)]
                cat6 = [sb(p1, "cat6_%d" % i, [128, 6, 128], BF16) for i in range(2)]

                wst1 = [sb(p1, "wst%d" % i, [128, D], F32) for i in range(2)]
                wi = 0
                for (gn, d0, s0) in (("v", 1024, 1024), ("g", 1536, 1536), ("q", 0, 0), ("f", 512, 512), ("cu", 2048, 3112)):
                    for k2 in range(4):
                        stg, stn = wst1[wi % 2], "wst%d" % (wi % 2)
                        wi += 1
                        ld(stg[:, :].rearrange("p (k n) -> p k n", n=512),
                           win_d[l, k2 * 256:(k2 + 1) * 256, s0:s0 + 512].rearrange("(k p) n -> p k n", p=128), w=[stn])
                        act(wA[:, 2 * k2:2 * k2 + 2, d0:d0 + 512], stg[:, :].rearrange("p (k n) -> p k n", n=512), AF.Copy,
                            r=[stn], w=["wA_" + gn])
                g1bc = sb(p1, "g1bc", [128, D], F32)
                stg2 = sb(p1, "stg2", [128, 8, 40], F32)
                dtmp1 = sb(p1, "dtmp", [128, 128], F32)
                def prefetch_a2():
                    ld(stg2[:, :, :], win_d[l, :, 3072:3112].rearrange("(k p) n -> p k n", p=128), w=["stg2"])
                    for k in range(8):
                        rows = slice(k * 128, (k + 1) * 128)
                        stg, stn = wst1[k % 2], "wst%d" % (k % 2)
                        ld(stg[:, :], win_d[l, rows, 2048:3072], w=[stn])
                        act(wB[:, k, 0:1024], stg[:, :], AF.Copy, r=[stn], w=["wB"])
                    act(wB[:, :, 1024:1032], stg2[:, :, 32:40], AF.Copy, r=["stg2"], w=["wB"])
                    for r3 in range(3):
                        act(wB[:, :, 1032 + 32 * r3:1064 + 32 * r3], stg2[:, :, 0:32], AF.Copy, r=["stg2"], w=["wB"])
                    build_bc(g1bc, "g1bc", l, 2, dtmp1, "dtmp", (0, 1))
                    for k in range(8):
                        ld(wst1[k % 2][:, :], wout_d[l, k * 128:(k + 1) * 128, :], w=["wst%d" % (k % 2)])
                        V(lambda e, k=k: e.tensor_tensor(woS[:, k, :], wst1[k % 2][:, :], g1bc[:, :], ALU.mult),
                          r=["wst%d" % (k % 2), "g1bc"], w=["woS"])
                for ct in range(2):
                    for j in range(31):
                        c = l * 62 + ct * 31 + j
                        V(lambda e, ct=ct, j=j, c=c: e.tensor_scalar(dg[:, ct, j, :], identF[:, :], cvw[:, c:c + 1],
                                                                     None, ALU.mult),
                          r=["identF", "cvw"], w=["dg"])
                G(lambda e: e.memset(scanm[:, :], 1.0), w=["scanm"])
                sm3 = scanm[:, :].rearrange("p (c j) -> p c j", j=64)
                G(lambda e: e.memset(sm3[:, :, 0:1], 0.0), r=["scanm"], w=["scanm"])
                G(lambda e: e.memset(cmf[:, :], 1.0), w=["cmf"])
                for h in range(4):
                    G(lambda e, h=h: e.affine_select(cmf[:, h * 128:(h + 1) * 128], cmf[:, h * 128:(h + 1) * 128],
                                                     [[1, 128]], ALU.is_ge, fillreg(e, 0.0), base=0, channel_multiplier=-1),
                      r=["cmf"], w=["cmf"])
                    G(lambda e, h=h: e.memset(cmf[0:64, h * 128 + 64:(h + 1) * 128], 0.0), r=["cmf"], w=["cmf"])
                V(lambda e: e.tensor_copy(cmask[:, :], cmf[:, :]), r=["cmf"], w=["cmask"])
                for (tl, nm) in ((ATm, "ATm"), (qhA[0], "qhA0"), (qhA[1], "qhA1"), (qhB[0], "qhB0"), (qhB[1], "qhB1"),
                                 (Sst, "Sst"), (Sbf0, "Sbf0"), (Sbf1, "Sbf1")):
                    G(lambda e, tl=tl: e.memset(tl[:, :], 0.0), w=[nm])
                G(lambda e: e.memset(hbuf[:, :, :], 0.0), w=["hbuf"])

                bc4 = lambda tl_: tl_[:, l * 4:(l + 1) * 4].rearrange("p (h o) -> p h o", o=1).broadcast_to([128, 4, 128])
                lb_b, oml_b = bc4(lbT), bc4(omlT)
                v3 = lambda t_: t_[:, :].rearrange("p (h t) -> p h t", t=128)
                c3 = lambda t_: t_[:, :].rearrange("p (c j) -> p c j", j=64)
                c4 = lambda t_: t_[:, :].rearrange("p (h c j) -> p h c j", c=2, j=64)
                QSC = 128.0 ** -0.5
                st1 = {"th": 0}

                def nxt_th():
                    i = st1["th"] % 2
                    st1["th"] += 1
                    return tht[i], "tht%d" % i

                def sigm(dst, src_ap, r, w):
                    act(dst, src_ap, AF.Exp, r=r, w=w, scale=-1.0)
                    act(dst, dst, AF.Ln, r=w, w=w, bias=1.0)
                    act(dst, dst, AF.Exp, r=w, w=w, scale=-1.0)

                def stageX(t):
                    b = t % 2
                    b3 = t % 3
                    xtn, hTn = "xt%d" % b, "hT%d" % b
                    if t + 1 < NT:
                        ld(xt[1 - b][:, :], src_d[(t + 1) * 128:(t + 2) * 128, :], w=["xt%d" % (1 - b)])
                    norm_tile(xt[b][:, :], xtn, xn, ssq, hT[b], hTn, G1T, l, 0, (0, 0))
                    ld(hT_d[t, :, :], hT[b][:, :, :].rearrange("p k n -> p (k n)"), r=[hTn], w=["hTd%d" % t])
                    S_.unit()
                    for k in range(8):
                        mm(bank(1), hT[b][:, k, :], wA[:, k, 1024:1536], k == 0, k == 7, r=[hTn, "wA_v"], w=[PS(1)])
                    act(vtok[b3][:, :], bank(1), AF.Copy, r=[PS(1)], w=["vtok%d" % b3])
                    S_.unit()
                    for k in range(8):
                        mm(bank(2), hT[b][:, k, :], wA[:, k, 1536:2048], k == 0, k == 7, r=[hTn, "wA_g"], w=[PS(2)])
                    th_, thn_ = nxt_th()
                    sigm(th_[:, :], bank(2), [PS(2)], [thn_])
                    V(lambda e, th_=th_: e.tensor_tensor(gs[b3][:, :], th_[:, :], bank(2), ALU.mult),
                      r=[thn_, PS(2)], w=["gs%d" % b3])
                    S_.unit()
                    for f in range(4):
                        for k in range(8):
                            mm(bank(1)[:, f * 128:(f + 1) * 128], wA[:, k, f * 128:(f + 1) * 128], hT[b][:, k, :],
                               k == 0, k == 7, r=[hTn, "wA_q"], w=[PS(1)])
                    th_, thn_ = nxt_th()
                    sigm(th_[:, :], bank(1), [PS(1)], [thn_])
                    V(lambda e, th_=th_: e.tensor_tensor(qs[b][:, :], th_[:, :], bank(1), ALU.mult),
                      r=[thn_, PS(1)], w=["qs%d" % b])
                    S_.unit()
                    for f in range(4):
                        for k in range(8):
                            mm(bank(2)[:, f * 128:(f + 1) * 128], wA[:, k, 512 + f * 128:512 + (f + 1) * 128], hT[b][:, k, :],
                               k == 0, k == 7, r=[hTn, "wA_f"], w=[PS(2)])
                    sigm(sg[b][:, :], bank(2), [PS(2)], ["sg%d" % b])
                    S_.unit()
                    for f in range(4):
                        for k in range(8):
                            mm(bank(1)[:, f * 128:(f + 1) * 128], wA[:, k, 2048 + f * 128:2048 + (f + 1) * 128], hT[b][:, k, :],
                               k == 0, k == 7, r=[hTn, "wA_cu"], w=[PS(1)])
                    th_, thn_ = nxt_th()
                    sigm(th_[:, 0:256], bank(1)[:, 256:512], [PS(1)], [thn_])
                    V(lambda e, th_=th_: e.tensor_tensor(hcur[b][:, :, :].rearrange("p c t -> p (c t)"), th_[:, 0:256],
                                                         bank(1)[:, 0:256], ALU.mult),
                      r=[thn_, PS(1)], w=["hcur%d" % b])

                def stageY1(t):
                    b = t % 2
                    b3 = t % 3
                    qsn, sgn_, gsn, vtn, hcn, c6n = "qs%d" % b, "sg%d" % b, "gs%d" % b3, "vtok%d" % b3, "hcur%d" % b, "cat6_%d" % b
                    qs_, sg_, gs_, vt_, c6 = qs[b], sg[b], gs[b3], vtok[b3], cat6[b]
                    qtl_, ktl_, khtok_, qhA_, qhB_, E4_ = qtl[b], ktl[b], khtok[b], qhA[b], qhB[b], E4[b]
                    cc_, stt2, rsb_ = cc[b], stt_[b], rs_b[b]
                    qtln, ktln, khtokn, qhAn, qhBn, E4n = "qtl%d" % b, "ktl%d" % b, "khtok%d" % b, "qhA%d" % b, "qhB%d" % b, "E4%d" % b
                    ccn, sttn, rsbn = "cc%d" % b, "stt%d" % b, "rs_b%d" % b
                    G(lambda e: e.tensor_copy(hbuf[:, :, 32:160], hcur[b][:, :, :]), r=[hcn, "hbuf"], w=["hbuf"])
                    for ct in range(2):
                        for j in range(31):
                            mm(bank(3)[:, ct * 128:(ct + 1) * 128], dg[:, ct, j, :],
                               hbuf[:, ct, 2 + j:2 + j + 128], j == 0, j == 30, r=["dg", "hbuf"], w=[PS(3)])
                        S_.unit()
                    for ct in range(2):
                        cs = slice(ct * 128, (ct + 1) * 128)
                        act(cc_[:, cs], bank(3)[:, cs], AF.Identity,
                            r=[PS(3), "cvb"], w=[ccn], bias=cvb[:, l * 2 + ct:l * 2 + ct + 1])
                    act(csq[:, :], cc_[:, :], AF.Square, r=[ccn], w=["csq"])
                    G(lambda e: e.tensor_copy(hbuf[:, :, 0:32], hbuf[:, :, 128:160]), r=["hbuf", PS(3)], w=["hbuf"])
                    S_.unit()
                    for (si, src_, sn) in ((0, cc_, ccn), (1, csq, "csq")):
                        for ct in range(2):
                            mm(bank(4)[:, si * 128:(si + 1) * 128], onesM[:, :], src_[:, ct * 128:(ct + 1) * 128],
                               ct == 0, ct == 1, r=["onesM", sn], w=[PS(4)])
                    act(stt2[:, :], bank(4)[:, 0:256], AF.Copy, r=[PS(4)], w=[sttn])
                    V(lambda e: e.tensor_tensor(rsb_[:, :], stt2[:, 0:128], stt2[:, 0:128], ALU.mult), r=[sttn], w=[rsbn])
                    V(lambda e: e.tensor_tensor(rsb_[:, :], stt2[:, 128:256], rsb_[:, :], ALU.subtract),
                      r=[sttn, rsbn], w=[rsbn])
                    V(lambda e: e.tensor_scalar(rsb_[:, :], rsb_[:, :], EPS, None, ALU.add), r=[rsbn], w=[rsbn])
                    S_.unit()
                    V(lambda e: e.tensor_tensor(v3(t1), v3(sg_), oml_b, ALU.mult), r=[sgn_, "omlT"], w=["t1"])
                    V(lambda e: e.tensor_tensor(v3(fT), v3(t1), lb_b, ALU.add), r=["t1", "lbT"], w=["fT"])
                    V(lambda e: e.tensor_tensor(v3(kT), oml_b, v3(t1), ALU.subtract), r=["t1", "omlT"], w=["kT"])
                    V(lambda e: e.tensor_scalar(fT[:, :], fT[:, :], 1e-30, None, ALU.max), r=["fT"], w=["fT"])
                    act(fT[:, :], fT[:, :], AF.Ln, r=["fT"], w=["fT"])
                    act(rsb_[:, :], rsb_[:, :], AF.Ln, r=[rsbn], w=[rsbn])
                    act(rsb_[:, :], rsb_[:, :], AF.Exp, r=[rsbn], w=[rsbn], scale=-0.5)
                    S_.unit()
                    V(lambda e: e.tensor_tensor_scan(GT[:, :], scanm[:, :], fT[:, :], 0.0, ALU.mult, ALU.add),
                      r=["scanm", "fT"], w=["GT"])
                    V(lambda e: e.tensor_tensor(c3(t1), c3(GT), c3(GT)[:, :, 32:33].broadcast_to([128, 8, 64]),
                                                ALU.subtract), r=["GT"], w=["t1"])
                    act(E[:, :], t1[:, :], AF.Exp, r=["t1"], w=["E"])
                    V(lambda e: e.scalar_tensor_tensor(qtl_[:, :], qs_[:, :], QSC, E[:, :], ALU.mult, ALU.mult),
                      r=[qsn, "E"], w=[qtln])
                    S_.unit()
                    act(E[:, :], t1[:, :], AF.Exp, r=["t1", qtln], w=["E"], scale=-1.0)
                    V(lambda e: e.tensor_tensor(ktl_[:, :], kT[:, :], E[:, :], ALU.mult), r=["kT", "E"], w=[ktln])
                    V(lambda e: e.tensor_tensor(c3(t1), c3(GT)[:, :, 63:64].broadcast_to([128, 8, 64]), c3(GT),
                                                ALU.subtract), r=["GT", "E"], w=["t1"])
                    S_.unit()
                    act(E[:, :], t1[:, :], AF.Exp, r=["t1", ktln], w=["E"])
                    V(lambda e: e.tensor_tensor(khT[:, :], kT[:, :], E[:, :], ALU.mult), r=["kT", "E"], w=["khT"])
                    act(E4_[:, :], GT[:, :], AF.Exp, r=["GT"], w=[E4n])
                    S_.unit()
                    V(lambda e: e.scalar_tensor_tensor(c4(qhA_)[:, :, 0, :], c4(qs_)[:, :, 0, :], QSC, c4(E4_)[:, :, 0, :],
                                                       ALU.mult, ALU.mult), r=[qsn, E4n], w=[qhAn])
                    V(lambda e: e.scalar_tensor_tensor(c4(qhB_)[:, :, 1, :], c4(qs_)[:, :, 1, :], QSC, c4(E4_)[:, :, 1, :],
                                                       ALU.mult, ALU.mult), r=[qsn, E4n], w=[qhBn])
                    for h in range(4):
                        tr(bankb(4)[:, h * 128:(h + 1) * 128], khT[:, h * 128:(h + 1) * 128], identB[:, :],
                           r=["khT", "identB"], w=[PS(4)])
                    act(khtok_[:, :], bankb(4)[:, 0:512], AF.Copy, r=[PS(4)], w=[khtokn])
                    S_.unit()

                def stageY2(t):
                    b = t % 2
                    b3 = t % 3
                    qsn, sgn_, gsn, vtn, hcn, c6n = "qs%d" % b, "sg%d" % b, "gs%d" % b3, "vtok%d" % b3, "hcur%d" % b, "cat6_%d" % b
                    qs_, sg_, gs_, vt_, c6 = qs[b], sg[b], gs[b3], vtok[b3], cat6[b]
                    qtl_, ktl_, khtok_, qhA_, qhB_, E4_ = qtl[b], ktl[b], khtok[b], qhA[b], qhB[b], E4[b]
                    cc_, stt2, rsb_ = cc[b], stt_[b], rs_b[b]
                    qtln, ktln, khtokn, qhAn, qhBn, E4n = "qtl%d" % b, "ktl%d" % b, "khtok%d" % b, "qhA%d" % b, "qhB%d" % b, "E4%d" % b
                    ccn, sttn, rsbn = "cc%d" % b, "stt%d" % b, "rs_b%d" % b
                    for h in range(4):
                        mm(bank(5)[:, h * 128:(h + 1) * 128], ktl_[:, h * 128:(h + 1) * 128],
                           qtl_[:, h * 128:(h + 1) * 128], True, True, r=[ktln, qtln], w=[PS(5)])
                    V(lambda e: e.copy_predicated(ATm[:, :], cmask[:, :], bank(5)), r=[PS(5), "cmask"], w=["ATm"])
                    S_.unit()
                    for h in range(4):
                        hs = slice(h * 128, (h + 1) * 128)
                        mm(bank(6)[:, hs], ATm[:, hs], vt_[:, hs], h == 0, False, r=["ATm", vtn], w=[PS(6)])
                        mm(bank(6)[:, hs], qhA_[:, hs], Sbf0[:, hs], False, False, r=[qhAn, "Sbf0"], w=[PS(6)])
                    for h in range(4):
                        hs = slice(h * 128, (h + 1) * 128)
                        mm(bank(7)[:, hs], khtok_[0:64, hs], vt_[0:64, hs], True, True, r=[khtokn, vtn], w=[PS(7)])
                    dec = lambda c: c4(E4_)[:, :, c, 63:64].broadcast_to([128, 4, 128])
                    V(lambda e: e.tensor_tensor(v3(Sst), v3(Sst), dec(0), ALU.mult), r=["Sst", E4n], w=["Sst"])
                    V(lambda e: e.tensor_tensor(Sst[:, :], Sst[:, :], bank(7), ALU.add), r=["Sst", PS(7)], w=["Sst"])
                    act(Sbf1[:, :], Sst[:, :], AF.Copy, r=["Sst"], w=["Sbf1"])
                    S_.unit()
                    for h in range(4):
                        hs = slice(h * 128, (h + 1) * 128)
                        mm(bank(6)[:, hs], qhB_[:, hs], Sbf1[:, hs], False, True, r=[qhBn, "Sbf1"], w=[PS(6)])
                    for h in range(4):
                        hs = slice(h * 128, (h + 1) * 128)
                        mm(bank(7)[:, hs], khtok_[64:128, hs], vt_[64:128, hs], True, True, r=[khtokn, vtn], w=[PS(7)])
                    V(lambda e: e.tensor_tensor(v3(Sst), v3(Sst), dec(1), ALU.mult), r=["Sst", E4n], w=["Sst"])
                    V(lambda e: e.tensor_tensor(Sst[:, :], Sst[:, :], bank(7), ALU.add), r=["Sst", PS(7)], w=["Sst"])
                    act(Sbf0[:, :], Sst[:, :], AF.Copy, r=["Sst"], w=["Sbf0"])
                    S_.unit()
                    for h in range(4):
                        act(oa[:, h * 128:(h + 1) * 128], bank(6)[:, h * 128:(h + 1) * 128], AF.Square,
                            r=[PS(6)], w=["oa", "ss4"], accum_out=ss4[:, h:h + 1])
                    V(lambda e: e.tensor_scalar(ss4[:, 4:8], ss4[:, 0:4], 1.0 / 128, EPS, ALU.mult, ALU.add),
                      r=["ss4"], w=["ss4b"])
                    act(ss4[:, 8:12], ss4[:, 4:8], AF.Ln, r=["ss4b"], w=["ss4c"])
                    act(ss4[:, 12:16], ss4[:, 8:12], AF.Exp, r=["ss4c"], w=["ss4d"], scale=-0.5)
                    V(lambda e: e.tensor_tensor(v3(oa), v3(bank(6)),
                                                ss4[:, 12:16].rearrange("p (h o) -> p h o", o=1).broadcast_to([128, 4, 128]),
                                                ALU.mult), r=[PS(6), "ss4d", "oa"], w=["oa"])
                    V(lambda e: e.tensor_tensor(oab[:, :], oa[:, :], gs_[:, :], ALU.mult), r=["oa", gsn], w=["oab"])
                    S_.unit()
                    for h in range(4):
                        tr(bankb(5)[:, 512 + h * 128:512 + (h + 1) * 128], oab[:, h * 128:(h + 1) * 128], identB[:, :],
                           r=["oab", "identB"], w=[PS(5)])
                    for h in range(4):
                        act(c6[:, h, :], bankb(5)[:, 512 + h * 128:512 + (h + 1) * 128], AF.Copy,
                            r=[PS(5), "onwT"], w=[c6n], scale=onwT[:, l * 4 + h:l * 4 + h + 1])
                    S_.unit()
                    y3 = yv[:, :].rearrange("p (c t) -> p c t", t=128)
                    V(lambda e: e.tensor_tensor(y3, cc_[:, :].rearrange("p (c t) -> p c t", t=128),
                                                stt2[:, 0:128].rearrange("p (o t) -> p o t", o=1).broadcast_to([128, 2, 128]),
                                                ALU.subtract), r=[ccn, sttn], w=["yv"])
                    V(lambda e: e.tensor_tensor(y3, y3,
                                                rsb_[:, :].rearrange("p (o t) -> p o t", o=1).broadcast_to([128, 2, 128]),
                                                ALU.mult), r=["yv", rsbn], w=["yv"])
                    for ct in range(2):
                        cs = slice(ct * 128, (ct + 1) * 128)
                        V(lambda e, cs=cs, ct=ct: e.tensor_scalar(yv[:, cs], yv[:, cs], lnw[:, l * 2 + ct:l * 2 + ct + 1],
                                                                  lnb[:, l * 2 + ct:l * 2 + ct + 1], ALU.mult, ALU.add),
                          r=["yv", "lnw", "lnb"], w=["yv"])
                    sigm(thy[:, :], yv[:, :], ["yv"], ["thy"])
                    V(lambda e: e.tensor_tensor(c6[:, 4:6, :].rearrange("p c t -> p (c t)"), thy[:, :], yv[:, :], ALU.mult),
                      r=["thy", "yv"], w=[c6n])
                    ld(cat_d[t, :, :], c6[:, :, :].rearrange("p k n -> p (k n)"), r=[c6n], w=["catd%d" % t])

                def cap1(fn, t):
                    S_.begin()
                    fn(t)
                    return S_.end()

                ld(xt[0][:, :], src_d[0:128, :], w=["xt0"])
                for step in range(NT + 2):
                    streams = []
                    if step - 2 >= 0:
                        streams.append(cap1(stageY2, step - 2))
                    if 0 <= step - 1 < NT:
                        streams.append(cap1(stageY1, step - 1))
                    if step < NT:
                        streams.append(cap1(stageX, step))
                    S_.run_merged(streams)
                    if step == 1:
                        prefetch_a2()
                S_.flush()
            if _STOP <= 1:
                return nc

            with ExitStack() as p2:
                KTc = sb(p2, "KTc", [128, 2, S], BF16)
                Vaug = sb(p2, "Vaug", [128, NT, 4, 65], BF16)
                kidx = sb(p2, "kidx", [128, S], BF16)
                score = [sb(p2, "score%d" % i, [128, S], F32) for i in range(3)]
                junkb = sb(p2, "junkb", [128, S], BF16)
                junk8 = sb(p2, "junk8", [128, S], U8)
                cntd = sb(p2, "cntd", [128, NIT], F32)
                vc = sb(p2, "vc", [128, NIT], F32)
                thrc = sb(p2, "thrc", [128, 4], F32)
                ones1 = sb(p2, "ones1", [128, 1], BF16)
                xt = [sb(p2, "x2t%d" % i, [128, D], F32) for i in range(3)]
                hT = [sb(p2, "h2T%d" % i, [128, 8, 128], BF16) for i in range(3)]
                catT = [sb(p2, "catT%d" % i, [128, 8, 128], BF16) for i in range(3)]
                sqT = [sb(p2, "sqT%d" % i, [128, 2, 256], BF16) for i in range(3)]
                identB2 = sb(p2, "identB2", [128, 256], BF16)
                iqT = sb(p2, "iqT", [128, 3, 128], BF16)
                wab = sb(p2, "wab", [128, 8], F32)
                wsg = sb(p2, "wsg", [128, 8], F32)
                Rb = [sb(p2, "Rb%d" % i, [128, 512], F32) for i in range(3)]
                Mb = [sb(p2, "Mb%d" % i, [128, 512], BF16) for i in range(3)]
                PT = [sb(p2, "PT%d" % i, [128, 512], BF16) for i in range(3)]
                ob = sb(p2, "ob", [128, 256], BF16)
                bis = sb(p2, "bis", [128, 16], F32)
                tabA = sb(p2, "tabA", [128, NIT], F32)
                tabB = sb(p2, "tabB", [128, NIT], F32)
                cntc = sb(p2, "cntc", [128, NIT], F32)
                uc = sb(p2, "uc", [128, NIT], F32)
                mid = sb(p2, "mid", [128, NIT + 1], F32)
                top8 = sb(p2, "top8", [128, 8], F32)
                rs4 = sb(p2, "rs4", [128, 4], F32)

                G(lambda e: e.memset(Vaug[:, :, :, 64:65], 1.0), w=["Vaug"])
                G(lambda e: e.memset(ones1[:, :], 1.0), w=["ones1"])
                for i in range(3):
                    G(lambda e, i=i: e.memset(sqT[i][:, :, :], 0.0), w=["sqT%d" % i])
                for c in range(2):
                    V(lambda e, c=c: e.tensor_copy(identB2[:, c * 128:(c + 1) * 128], identB[:, :]), r=["identB"], w=["identB2"])

                IDXC = (32.0 ** -0.5) * (8.0 ** -0.5)
                state = {"rbi": 0, "mbi": 0, "mgi": 0}

                def stagePS(t):
                    b = t % 3
                    xtn, hTn, cTn, scn, sqn = "x2t%d" % b, "h2T%d" % b, "catT%d" % b, "score%d" % b, "sqT%d" % b
                    sc = score[b]
                    N = (t + 1) * 128
                    tok = slice(t * 128, (t + 1) * 128)
                    ld(xt[b][:, :], src_d[tok, :], w=[xtn])
                    ld(hT[b][:, :, :].rearrange("p k n -> p (k n)"), hT_d[t, :, :], r=["hTd%d" % t], w=[hTn])
                    ld(catT[b][:, 0:4, :].rearrange("p k n -> p (k n)"), cat_d[t, :, 0:512], r=["catd%d" % t], w=[cTn])
                    ld(catT[b][:, 6:8, :].rearrange("p k n -> p (k n)"), cat_d[t, :, 512:768], r=["catd%d" % t], w=[cTn])
                    S_.unit()
                    for k in range(8):
                        mm(bank(0)[:, 0:256], hT[b][:, k, :], wB[:, k, 512:768], k == 0, k == 7, r=[hTn, "wB"], w=[PS(0)])
                    for k in range(8):
                        mm(bank(0)[:, 256:264], hT[b][:, k, :], wB[:, k, 1024:1032], k == 0, k == 7,
                           r=[hTn, "wB"], w=[PS(0)])
                    act(Vaug[:, t, :, 0:64], bank(0)[:, 0:256].rearrange("p (h d) -> p h d", d=64), AF.Copy,
                        r=[PS(0)], w=["Vaug"])
                    act(wab[:, :], bank(0)[:, 256:264], AF.Abs, r=[PS(0)], w=["wab"], scale=IDXC)
                    act(wsg[:, :], bank(0)[:, 256:264], AF.Sign, r=[PS(0)], w=["wsg"])
                    S_.unit()
                    for f in range(4):
                        cb = (0, 128, 256, 384)[f]
                        for k in range(8):
                            mm(bank(1)[:, f * 128:(f + 1) * 128], wB[:, k, cb:cb + 128], hT[b][:, k, :],
                               k == 0, k == 7, r=[hTn, "wB"], w=[PS(1)])
                    for hl in range(2):
                        rows = slice(64 * hl, 64 * hl + 64)
                        act(sqT[b][rows, :, hl * 128:(hl + 1) * 128],
                            bank(1)[rows, 0:256].rearrange("p (c t) -> p c t", t=128), AF.Copy,
                            r=[PS(1)], w=[sqn], scale=0.125)
                    V(lambda e: e.tensor_copy(KTc[:, :, tok], bank(1)[:, 256:512].rearrange("p (c t) -> p c t", t=128)),
                      r=[PS(1)], w=["KTc"])
                    S_.unit()
                    for f, (cb, m) in enumerate(((768, 96), (864, 96), (960, 64), (1032, 96))):
                        for k in range(8):
                            mm(bank(2)[0:m, f * 128:(f + 1) * 128], wB[:, k, cb:cb + m], hT[b][:, k, :],
                               k == 0, k == 7, r=[hTn, "wB"], w=[PS(2)])
                    act(iqT[0:96, :, :], bank(2)[0:96, 0:384].rearrange("p (c t) -> p c t", t=128), AF.Copy,
                        r=[PS(2)], w=["iqT"])
                    V(lambda e: e.tensor_copy(kidx[0:96, tok], bank(2)[0:96, 384:512]), r=[PS(2)], w=["kidx"])
                    S_.unit()
                    NB = (N + 511) // 512
                    prev_acc = [None]
                    for kb in range(NB):
                        wN = min(512, N - kb * 512)
                        ks_ = slice(kb * 512, kb * 512 + wN)
                        for hh in range(8):
                            g_, r_ = hh // 3, hh % 3
                            pb = 3 + (hh % 2)
                            mm(bank(pb)[:, 0:wN], iqT[32 * r_:32 * r_ + 32, g_, :], kidx[32 * r_:32 * r_ + 32, ks_],
                               True, True, r=["iqT", "kidx"], w=[PS(pb)])
                            R_ = Rb[state["rbi"] % 3]
                            Rn = "Rb%d" % (state["rbi"] % 3)
                            state["rbi"] += 1
                            act(R_[:, 0:wN], bank(pb)[:, 0:wN], AF.Relu, r=[PS(pb), "wab"], w=[Rn],
                                scale=wab[:, hh:hh + 1])

                            def acc(R_=R_, Rn=Rn, ks_=ks_, wN=wN, hh=hh):
                                if hh == 0:
                                    V(lambda e: e.tensor_scalar(sc[:, ks_], R_[:, 0:wN], wsg[:, 0:1], None, ALU.mult),
                                      r=[Rn, "wsg"], w=[scn])
                                else:
                                    V(lambda e: e.scalar_tensor_tensor(sc[:, ks_], R_[:, 0:wN], wsg[:, hh:hh + 1], sc[:, ks_],
                                                                       ALU.mult, ALU.add),
                                      r=[Rn, "wsg", scn], w=[scn])
                            if prev_acc[0] is not None:
                                prev_acc[0]()
                            prev_acc[0] = acc
                            S_.unit()
                    prev_acc[0]()
                    G(lambda e: e.affine_select(sc[:, tok], sc[:, tok], [[-1, 128]], ALU.is_ge, fillreg(e, -1e30),
                                                base=0, channel_multiplier=1), r=[scn], w=[scn])

                def stageBI(t):
                    b = t % 3
                    scn = "score%d" % b
                    sc = score[b]
                    N = (t + 1) * 128
                    thn = "thr%d" % b
                    if t * 128 < TOPK:
                        V(lambda e: e.tensor_copy(thrc[:, b:b + 1], thrneg[:, 0:1]), r=["thrneg"], w=[thn])
                        return
                    Ka = max(128, min(N - 128, int(round(0.66 * (t + 1))) * 128))
                    V(lambda e: e.max(top8[:, :], sc[:, 0:N]), r=[scn], w=["top8"])
                    S_.unit()
                    V(lambda e: e.tensor_reduce(bis[:, 0:1], sc[:, 0:TOPK], mybir.AxisListType.X, ALU.min),
                      r=[scn], w=["bis0"])
                    V(lambda e: e.tensor_tensor(bis[:, 1:2], top8[:, 0:1], bis[:, 0:1], ALU.subtract),
                      r=["top8", "bis0"], w=["bis1"])
                    V(lambda e: e.tensor_scalar(tabA[:, :], tabA0[:, :], bis[:, 1:2], None, ALU.mult),
                      r=["bis1", "tabA0"], w=["tabA"])
                    V(lambda e: e.tensor_scalar(tabB[:, :], tabB0[:, :], bis[:, 1:2], None, ALU.mult),
                      r=["bis1", "tabB0"], w=["tabB"])
                    V(lambda e: e.scalar_tensor_tensor(mid[:, 0:1], bis[:, 1:2], 0.5, bis[:, 0:1], ALU.mult, ALU.add),
                      r=["bis0", "bis1"], w=["mid"])
                    S_.unit()
                    for n in range(NIT):
                        act(junkb[:, 0:Ka], sc[:, 0:Ka], AF.Sign, r=[scn, "mid"], w=["junkb", "cntA"],
                            scale=-1.0, bias=mid[:, n:n + 1], accum_out=cntc[:, n:n + 1])
                        V(lambda e, n=n: e.scalar_tensor_tensor(junk8[:, Ka:N], sc[:, Ka:N], mid[:, n:n + 1],
                                                                ones1[:, 0:1].broadcast_to([128, N - Ka]),
                                                                ALU.is_ge, ALU.mult, accum_out=cntd[:, n:n + 1]),
                          r=[scn, "mid", "ones1"], w=["junk8", "cntD"])
                        V(lambda e, n=n: e.scalar_tensor_tensor(vc[:, n:n + 1], cntd[:, n:n + 1], 2.0, cntc[:, n:n + 1],
                                                                ALU.mult, ALU.subtract),
                          r=["cntA", "cntD"], w=["vc"])
                        V(lambda e, n=n: e.scalar_tensor_tensor(uc[:, n:n + 1], vc[:, n:n + 1], float(2 * TOPK - Ka - 1),
                                                                tabB[:, n:n + 1], ALU.is_gt, ALU.mult),
                          r=["vc", "tabB"], w=["uc"])
                        V(lambda e, n=n: e.scalar_tensor_tensor(mid[:, n + 1:n + 2], mid[:, n:n + 1], tabA[:, n:n + 1],
                                                                uc[:, n:n + 1], ALU.subtract, ALU.add),
                          r=["mid", "tabA", "uc"], w=["mid"])
                        S_.unit()
                    V(lambda e: e.tensor_copy(thrc[:, b:b + 1], mid[:, NIT:NIT + 1]), r=["mid"], w=[thn])

                def stageAT(t):
                    b = t % 3
                    xtn, cTn, scn, sqn, thn = "x2t%d" % b, "catT%d" % b, "score%d" % b, "sqT%d" % b, "thr%d" % b
                    sc = score[b]
                    tok = slice(t * 128, (t + 1) * 128)
                    thr = thrc[:, b:b + 1]
                    prev_pv = [None]
                    NG = (t + 4) // 4
                    Nq = (t + 1) * 128
                    mg0 = state["mgi"]
                    state["mgi"] += NG

                    def genmask(g):
                        if g >= NG:
                            return
                        w_ = min(512, Nq - g * 512)
                        gi = mg0 + g
                        M_ = Mb[gi % 3]
                        V(lambda e: e.tensor_scalar(M_[:, 0:w_], sc[:, g * 512:g * 512 + w_], thr, -30000.0, ALU.is_lt, ALU.mult),
                          r=[scn, thn], w=["Mb%d" % (gi % 3)])
                    genmask(0)
                    genmask(1)
                    for kb in range(t + 1):
                        kcs = slice(kb * 128, (kb + 1) * 128)
                        mbi = state["mbi"]
                        state["mbi"] += 1
                        g = kb // 4
                        if kb % 4 == 0:
                            genmask(g + 2)
                        gi = mg0 + g
                        M_ = Mb[gi % 3][:, (kb % 4) * 128:(kb % 4 + 1) * 128]
                        Mn = "Mb%d" % (gi % 3)
                        P_ = PT[mbi % 3]
                        Pn = "PT%d" % (mbi % 3)
                        pb = 5 + (mbi % 2)
                        for c in range(2):
                            mm(bank(pb)[:, c * 256:(c + 1) * 256], KTc[:, c, kcs], sqT[b][:, c, :], True, False,
                               r=["KTc", sqn], w=[PS(pb)])
                            mm(bank(pb)[:, c * 256:(c + 1) * 256], M_, identB2[:, :], False, True,
                               r=[Mn, "identB2"], w=[PS(pb)])
                        act(P_[:, :], bank(pb), AF.Exp, r=[PS(pb)], w=[Pn])

                        def pv(kb=kb, P_=P_, Pn=Pn):
                            for h in range(4):
                                mm(bank(7)[:, h * 65:(h + 1) * 65], P_[:, h * 128:(h + 1) * 128], Vaug[:, kb, h, :],
                                   kb == 0 and h == 0, kb == t, r=[Pn, "Vaug"], w=[PS(7)])
                        if prev_pv[0] is not None:
                            prev_pv[0]()
                        prev_pv[0] = pv
                        S_.unit()
                    prev_pv[0]()
                    o3 = bank(7)[:, 0:260].rearrange("p (h d) -> p h d", d=65)
                    V(lambda e: e.reciprocal(rs4[:, :].rearrange("p (h o) -> p h o", o=1), o3[:, :, 64:65]), r=[PS(7)], w=["rs4"])
                    V(lambda e: e.tensor_tensor(ob[:, :].rearrange("p (h d) -> p h d", d=64), o3[:, :, 0:64],
                                                rs4[:, :].rearrange("p (h o) -> p h o", o=1).broadcast_to([128, 4, 64]),
                                                ALU.mult), r=[PS(7), "rs4"], w=["ob"])
                    for c in range(2):
                        tr(bankb(7)[:, c * 128:(c + 1) * 128], ob[:, c * 128:(c + 1) * 128], identB[:, :],
                           r=["ob", "identB"], w=[PS(7)])
                    act(catT[b][:, 4:6, :], bankb(7)[:, 0:256].rearrange("p (c t) -> p c t", t=128), AF.Copy,
                        r=[PS(7)], w=[cTn])
                    S_.unit()
                    for hf in range(2):
                        for c in range(8):
                            mm(bank(5 + hf), catT[b][:, c, :], woS[:, c, hf * 512:(hf + 1) * 512], c == 0, c == 7,
                               r=[cTn, "woS"], w=[PS(5 + hf)])
                        V(lambda e, hf=hf: e.tensor_tensor(xt[b][:, hf * 512:(hf + 1) * 512],
                                                           xt[b][:, hf * 512:(hf + 1) * 512], bank(5 + hf), ALU.add),
                          r=[xtn, PS(5 + hf)], w=[xtn])
                        S_.unit()
                    ld(out_d[tok, :], xt[b][:, :], r=[xtn], w=["outd%d" % t])

                def cap_(fn, t):
                    S_.begin()
                    fn(t)
                    return S_.end()

                for step in range(NT + 2):
                    streams = []
                    if step - 2 >= 0:
                        streams.append(cap_(stageAT, step - 2))
                    if 0 <= step - 1 < NT:
                        streams.append(cap_(stageBI, step - 1))
                    if step < NT:
                        streams.append(cap_(stagePS, step))
                    S_.run_merged(streams)
                S_.flush()
            if _STOP <= 2:
                return nc

            pl.__exit__(None, None, None)
            with ExitStack() as p3:
                w1S = sb(p3, "w1S", [128, 8, DFF], BF16)
                w2S = sb(p3, "w2S", [128, 32, D], BF16)
                g2bc = sb(p3, "g2bc", [128, D], F32)
                dtmp = sb(p3, "dtmp3", [128, 128], F32)
                xtN = [sb(p3, "x3n%d" % i, [128, D], F32) for i in range(2)]
                xr = [sb(p3, "x3r%d" % i, [128, D], F32) for i in range(2)]
                xn = sb(p3, "xn3", [128, D], F32)
                ssq = sb(p3, "ssq3", [128, 4], F32)
                ssf = sb(p3, "ssf3", [128, 4], F32)
                hTb = sb(p3, "hTb", [128, 8, 512], BF16)
                h1raw = sb(p3, "h1raw", [128, 8192], F32)
                h1T = h1raw[:, :].bitcast(BF16).rearrange("p (f t) -> p f t", t=512)
                wst = [h1raw[:, 0:1024], h1raw[:, 1024:2048]]
                wst_b = [h1raw[:, 2048:3072], h1raw[:, 3072:4096]]
                rl = [sb(p3, "rl%d" % i, [128, 512], F32) for i in range(2)]
                last = (l == L - 1)
                NBLK = S // 512
                stB = {"ri": 0, "ni": 0, "xi": 0}

                def stageN(blk):
                    for i in range(4):
                        t = blk * 4 + i
                        j = stB["ni"] % 2
                        stB["ni"] += 1
                        ld(xtN[j][:, :], out_d[t * 128:(t + 1) * 128, :], r=["outd%d" % t], w=["x3n%d" % j])
                        norm_tile(xtN[j][:, :], "x3n%d" % j, xn, ssq, hTb, "hTb", G2T, l, 3, (0, 1),
                                  ncols=128, coff=i * 128)
                        S_.unit()

                w1i = 0
                for cb in range(4):
                    for k in range(8):
                        stg, stn = wst_b[w1i % 2], "w1st%d" % (w1i % 2)
                        w1i += 1
                        ld(stg, w1_d[l, k * 128:(k + 1) * 128, cb * 1024:(cb + 1) * 1024], w=[stn])
                        act(w1S[:, k, cb * 1024:(cb + 1) * 1024], stg, AF.Copy, r=[stn], w=["w1S_%d" % cb])
                stageN(0)
                build_bc(g2bc, "g2bc", l, 5, dtmp, "dtmp3", (0, 1))
                for k in range(32):
                    ld(wst[k % 2], w2_d[l, k * 128:(k + 1) * 128, :], w=["w2st%d" % (k % 2)])
                    eng = V if k % 2 == 0 else G
                    eng(lambda e, k=k: e.tensor_tensor(w2S[:, k, :], wst[k % 2], g2bc[:, :], ALU.mult),
                        r=["w2st%d" % (k % 2), "g2bc"], w=["w2S_%d" % k])
                if last:
                    ld(g2bc[:, :], fnw_d[0:1, :].partition_broadcast(128), r=["w2S_%d" % k for k in range(32)], w=["g2bc"])
                def stageM1(blk):
                    for f in range(32):
                        pb = 2 + (f % 2)
                        for k in range(8):
                            mm(bank(pb), w1S[:, k, f * 128:(f + 1) * 128], hTb[:, k, :], k == 0, k == 7,
                               r=["w1S_%d" % (f // 8), "hTb"], w=[PS(pb)])
                        r_ = rl[stB["ri"] % 2]
                        rn = "rl%d" % (stB["ri"] % 2)
                        stB["ri"] += 1
                        act(r_[:, :], bank(pb), AF.Relu, r=[PS(pb)], w=[rn])
                        G(lambda e, r_=r_, f=f: e.tensor_tensor(h1T[:, f, :], r_[:, :], r_[:, :], ALU.mult),
                          r=[rn], w=["h1T", "w2st0", "w2st1", "w1st0", "w1st1"])

                def stageM2(blk):
                    for i in range(4):
                        t = blk * 4 + i
                        j = stB["xi"] % 2
                        stB["xi"] += 1
                        xb, xbn = xr[j], "x3r%d" % j
                        ld(xb[:, :], out_d[t * 128:(t + 1) * 128, :], r=["outd%d" % t], w=[xbn])
                        for hf in range(2):
                            pb = 4 + 2 * (i % 2) + hf
                            for f in range(32):
                                mm(bank(pb), h1T[:, f, i * 128:(i + 1) * 128], w2S[:, f, hf * 512:(hf + 1) * 512],
                                   f == 0, f == 31, r=["h1T", "w2S_%d" % f], w=[PS(pb)])
                                if f % 8 == 7:
                                    S_.unit()
                            V(lambda e, xb=xb, hf=hf, pb=pb: e.tensor_tensor(xb[:, hf * 512:(hf + 1) * 512],
                                                                             xb[:, hf * 512:(hf + 1) * 512], bank(pb), ALU.add),
                              r=[xbn, PS(pb)], w=[xbn])
                        if last:
                            act(xn[:, :], xb[:, :], AF.Square, r=[xbn], w=["xn", "ssf"], accum_out=ssf[:, 0:1])
                            V(lambda e: e.tensor_scalar(ssf[:, 1:2], ssf[:, 0:1], 1.0 / D, EPS, ALU.mult, ALU.add),
                              r=["ssf"], w=["ssf1"])
                            act(ssf[:, 2:3], ssf[:, 1:2], AF.Ln, r=["ssf1"], w=["ssf2"])
                            act(ssf[:, 3:4], ssf[:, 2:3], AF.Exp, r=["ssf2"], w=["ssf3"], scale=-0.5)
                            V(lambda e, xb=xb: e.scalar_tensor_tensor(xb[:, :], xb[:, :], ssf[:, 3:4], g2bc[:, :],
                                                                      ALU.mult, ALU.mult),
                              r=[xbn, "ssf3", "g2bc"], w=[xbn])
                        ld(out_d[t * 128:(t + 1) * 128, :], xb[:, :], r=[xbn], w=["outd%d" % t])
                        S_.unit()

                def capB(fn, blk):
                    S_.begin()
                    fn(blk)
                    return S_.end()

                for blk in range(NBLK):
                    stageM1(blk)
                    streams = [capB(stageM2, blk)]
                    if blk + 1 < NBLK:
                        streams.append(capB(stageN, blk + 1))
                    S_.run_merged(streams)
                S_.flush()
    return nc


def _colsT(v, L, n):
    v = np.asarray(v, np.float32).reshape(L, n, 128)
    return np.ascontiguousarray(v.transpose(2, 0, 1).reshape(128, L * n))


def make_in_maps(inp, L, nb):
    f = lambda a: np.ascontiguousarray(np.asarray(a, np.float32))
    shared = {
        "ada_w": f(inp["ada_w"]),
        "ada_bT": _colsT(inp["ada_b"], L, 48),
        "nmixT": _colsT(inp["norm_mix_w"], L, 8),
        "nmlpT": _colsT(inp["norm_mlp_w"], L, 8),
        "fnw": f(inp["final_norm_w"]).reshape(1, D),
        "w_in": f(inp["w_in"]), "w_out": f(inp["w_out"]), "w1": f(inp["mlp_w1"]), "w2": f(inp["mlp_w2"]),
        "lbT": _colsT(inp["hg_lb_logits"], L, 4),
        "onwT": _colsT(inp["hg_onorm_w"], L, 4),
        "cvw": np.ascontiguousarray(np.asarray(inp["cv_w"], np.float32).reshape(L, 31, 2, 128)
                                    .transpose(3, 0, 2, 1).reshape(128, L * 62)),
        "cvb": _colsT(inp["cv_b"], L, 2),
        "lnw": _colsT(inp["cv_ln_w"], L, 2),
        "lnb": _colsT(inp["cv_ln_b"], L, 2),
    }
    maps = []
    x = np.asarray(inp["x"], np.float32)
    c = np.asarray(inp["c"], np.float32)
    for b in range(nb):
        m = dict(shared)
        m["x"] = np.ascontiguousarray(x[b])
        m["cT"] = np.ascontiguousarray(c[b].reshape(8, 128).T)
        maps.append(m)
    return maps


_NC_CACHE = {}


def kernel(x, c, ada_w, ada_b, norm_mix_w, norm_mlp_w, w_in, hg_lb_logits, hg_onorm_w,
           cv_w, cv_b, cv_ln_w, cv_ln_b, w_out, mlp_w1, mlp_w2, final_norm_w):
    inp = dict(x=x, c=c, ada_w=ada_w, ada_b=ada_b, norm_mix_w=norm_mix_w, norm_mlp_w=norm_mlp_w, w_in=w_in,
               hg_lb_logits=hg_lb_logits, hg_onorm_w=hg_onorm_w, cv_w=cv_w, cv_b=cv_b, cv_ln_w=cv_ln_w,
               cv_ln_b=cv_ln_b, w_out=w_out, mlp_w1=mlp_w1, mlp_w2=mlp_w2, final_norm_w=final_norm_w)
    B, S, _ = np.asarray(x).shape
    L = np.asarray(w_in).shape[0]
    topk = min(256, S // 4)
    key = (S, L, topk)
    if key not in _NC_CACHE:
        _NC_CACHE[key] = build_nc(S, L, topk)
    nc = _NC_CACHE[key]
    maps = make_in_maps(inp, L, B)
    res = run_bass_kernel_spmd(nc, maps, core_ids=list(range(B)))
    return np.stack([np.asarray(r["out"], np.float32) for r in res.results], axis=0)
```

```python
import numpy as np
from contextlib import ExitStack
import concourse.bass as bass
import concourse.mybir as mybir
from concourse.bass_utils import run_bass_kernel_spmd

F32 = mybir.dt.float32
BF16 = mybir.dt.bfloat16
U8 = mybir.dt.uint8
AF = mybir.ActivationFunctionType
ALU = mybir.AluOpType

D = 1024
DIN = 3624
DFF = 4096
EPS = 1e-6
NIT = 16
ENGS = ("tensor", "vector", "scalar", "gpsimd", "sync")
NDMA_SEMS = 24
import os as _os
_STOP = int(_os.environ.get('KSTOP', '99'))
_CUT = int(_os.environ.get('KCUT', '99'))
_SUB = int(_os.environ.get('KSUB', '99'))


class _Op:
    __slots__ = ("eng", "fn", "deps", "signals", "sem", "val", "is_dma")

    def __init__(self, eng, fn, is_dma=False):
        self.eng = eng
        self.fn = fn
        self.deps = []
        self.signals = False
        self.sem = None
        self.val = 0
        self.is_dma = is_dma


class _Slot:
    __slots__ = ("writer", "readers")

    def __init__(self):
        self.writer = None
        self.readers = []


class Sched:
    def __init__(self, nc, es):
        self.nc = nc
        self.q = {e: [] for e in ENGS}
        self.slots = {}
        self.phase_dmas = []
        self.esem = {e: es.enter_context(nc.semaphore("es_" + e)) for e in ENGS}
        self.dsem = {e: [es.enter_context(nc.semaphore("ds_%s_%d" % (e, i))) for i in range(NDMA_SEMS)]
                     for e in ("sync", "gpsimd")}
        self.cnt = {e: 0 for e in ENGS}
        self.dcnt = {e: [0] * NDMA_SEMS for e in self.dsem}
        self.drr = {e: 0 for e in self.dsem}
        self.dprev = {e: [None] * NDMA_SEMS for e in self.dsem}
        self.waited = {e: {} for e in ENGS}
        self.nops = 0

    def _slot(self, k):
        s = self.slots.get(k)
        if s is None:
            s = self.slots[k] = _Slot()
        return s

    def _add(self, op, reads, writes):
        deps = set()
        for k in reads:
            s = self._slot(k)
            if s.writer is not None:
                deps.add(s.writer)
            if k.startswith("ps"):
                for r in s.readers:
                    if r.eng != op.eng:
                        deps.add(r)
        for k in writes:
            s = self._slot(k)
            if s.writer is not None:
                deps.add(s.writer)
            for r in s.readers:
                deps.add(r)
        deps.discard(op)
        for d in deps:
            if d.eng == "tensor" and op.eng == "tensor":
                continue
            op.deps.append(d)
            d.signals = True
        for k in writes:
            s = self._slot(k)
            s.writer = op
            s.readers = []
        for k in reads:
            if k not in writes:
                self._slot(k).readers.append(op)
        self.q[op.eng].append(op)
        self.nops += 1
        return op

    cap = None

    fine = False

    def begin(self, fine=False):
        self.cap = [[]]
        self.fine = fine

    def unit(self):
        if self.cap is not None and self.cap[-1]:
            self.cap.append([])

    def end(self):
        u = [x for x in self.cap if x]
        self.cap = None
        return u

    def run_merged(self, streams):
        pos = [0] * len(streams)
        while True:
            best, bf = -1, 2.0
            for i, s in enumerate(streams):
                if pos[i] < len(s):
                    f = (pos[i] + 0.5) / len(s)
                    if f < bf:
                        best, bf = i, f
            if best < 0:
                break
            for item in streams[best][pos[best]]:
                if item[0] == "op":
                    self.op(*item[1:])
                else:
                    self.dma(item[1], item[2], item[3], item[4], item[5], **item[6])
            pos[best] += 1

    def op(self, eng, fn, reads=(), writes=()):
        if self.cap is not None:
            self.cap[-1].append(("op", eng, fn, tuple(reads), tuple(writes)))
            if self.fine and eng != "tensor":
                self.cap.append([])
            return None
        return self._add(_Op(eng, fn), list(reads), list(writes))

    def dma(self, eng, out, in_, reads=(), writes=(), **kw):
        if self.cap is not None:
            self.cap[-1].append(("dma", eng, out, in_, tuple(reads), tuple(writes), kw))
            return None
        fn = lambda e: e.dma_start(out=out, in_=in_, **kw)
        op = _Op(eng, fn, is_dma=True)
        op.signals = True
        self._add(op, list(reads), list(writes))
        self.phase_dmas.append(op)
        return op

    def flush(self):
        nc = self.nc
        fin = _Op("sync", None)
        fin.deps = list(self.phase_dmas)
        self.phase_dmas = []
        self.q["sync"].append(fin)
        for e in ENGS:
            for op in self.q[e]:
                if op.fn is None:
                    continue
                if op.is_dma:
                    j = self.drr[e]
                    self.drr[e] = (j + 1) % NDMA_SEMS
                    self.dcnt[e][j] += 16
                    op.sem = self.dsem[e][j]
                    op.val = self.dcnt[e][j]
                    prev = self.dprev[e][j]
                    if prev is not None:
                        op.deps.append(prev)
                    self.dprev[e][j] = op
                elif op.signals:
                    self.cnt[e] += 1
                    op.sem = self.esem[e]
                    op.val = self.cnt[e]
        q = self.q
        waited_all = self.waited

        def run(e):
            def body(eng):
                waited = waited_all[e]
                for op in q[e]:
                    for d in op.deps:
                        if d.sem is None:
                            continue
                        key = id(d.sem)
                        if waited.get(key, 0) < d.val:
                            eng.wait_ge(d.sem, d.val)
                            waited[key] = d.val
                    if op.fn is None:
                        continue
                    ins = op.fn(eng)
                    if op.signals:
                        ins.then_inc(op.sem, 16 if op.is_dma else 1)
            return body

        with nc.Block() as block:
            if q["tensor"]:
                block.tensor(run("tensor"))
            if q["vector"]:
                block.vector(run("vector"))
            if q["scalar"]:
                block.scalar(run("scalar"))
            if q["gpsimd"]:
                block.gpsimd(run("gpsimd"))
            block.sync(run("sync"))
        self.q = {e: [] for e in ENGS}


def build_nc(S, L, TOPK, dbg=False):
    NT = S // 128
    nc = bass.Bass("TRN2", target_bir_lowering=False)

    def din(name, shape, dt=F32):
        return nc.dram_tensor(name, list(shape), dt, kind="ExternalInput").ap()

    x_d = din("x", [S, D])
    cT_d = din("cT", [128, 8])
    adaw_d = din("ada_w", [L, D, 6 * D])
    adabT_d = din("ada_bT", [128, L * 48])
    nmixT_d = din("nmixT", [128, L * 8])
    nmlpT_d = din("nmlpT", [128, L * 8])
    fnw_d = din("fnw", [1, D])
    win_d = din("w_in", [L, D, DIN])
    wout_d = din("w_out", [L, D, D])
    w1_d = din("w1", [L, D, DFF])
    w2_d = din("w2", [L, DFF, D])
    lbT_d = din("lbT", [128, L * 4])
    onwT_d = din("onwT", [128, L * 4])
    cvw_d = din("cvw", [128, L * 62])
    cvb_d = din("cvb", [128, L * 2])
    lnw_d = din("lnw", [128, L * 2])
    lnb_d = din("lnb", [128, L * 2])
    out_d = nc.dram_tensor("out", [S, D], F32, kind="ExternalOutput").ap()
    hT_d = nc.dram_tensor("hT_scr", [NT, 128, 1024], BF16, kind="Internal").ap()
    cat_d = nc.dram_tensor("cat_scr", [NT, 128, 768], BF16, kind="Internal").ap()

    with ExitStack() as es:
        S_ = Sched(nc, es)

        def V(fn, r=(), w=()):
            return S_.op("vector", fn, r, w)

        def A(fn, r=(), w=()):
            return S_.op("scalar", fn, r, w)

        def G(fn, r=(), w=()):
            return S_.op("gpsimd", fn, r, w)

        def T(fn, r=(), w=()):
            return S_.op("tensor", fn, r, w)

        _fillregs = {}

        def fillreg(e, val):
            if val not in _fillregs:
                _fillregs[val] = e.to_reg(val)
            return _fillregs[val]

        def mm(out, lhsT, rhs, start, stop, r, w):
            return T(lambda e: e.matmul(out, lhsT, rhs, start=start, stop=stop, skip_group_check=True), r, w)

        def tr(out, in_, ident, r, w):
            return T(lambda e: e.transpose(out, in_, ident), r, w)

        def act(out, in_, func, r, w, **kw):
            return A(lambda e: e.activation(out, in_, func, **kw), r, w)

        def ld(out, in_, w, r=(), **kw):
            return S_.dma("sync", out, in_, reads=r, writes=w, **kw)

        def ldc(out, in_, w, r=()):
            return S_.dma("gpsimd", out, in_, reads=r, writes=w, max_dma_last_dim=2048)

        _uid = [0]

        def sb(stack, name, shape, dt):
            _uid[0] += 1
            return stack.enter_context(nc.sbuf_tensor("s%d_%s" % (_uid[0], name), list(shape), dt))

        pst = [es.enter_context(nc.psum_tensor("pst%d" % i, [128, 1024], F32)) for i in range(4)]

        def bank(i):
            return pst[i // 2][:, (i % 2) * 512:(i % 2 + 1) * 512]

        def bankb(i):
            return bank(i).bitcast(BF16)

        PS = lambda i: "ps%d" % i

        identF = sb(es, "identF", [128, 128], F32)
        identB = sb(es, "identB", [128, 128], BF16)
        onesM = sb(es, "onesM", [128, 128], F32)
        cT = sb(es, "cTs", [128, 8], F32)
        cact = sb(es, "cact", [128, 8], F32)
        modT = sb(es, "modT", [128, L * 48], F32)
        adabT = sb(es, "adabT", [128, L * 48], F32)
        nmixT = sb(es, "nmixT", [128, L * 8], F32)
        nmlpT = sb(es, "nmlpT", [128, L * 8], F32)
        G1T = sb(es, "G1T", [128, L * 8], F32)
        G2T = sb(es, "G2T", [128, L * 8], F32)
        lbT = sb(es, "lbT", [128, L * 4], F32)
        omlT = sb(es, "omlT", [128, L * 4], F32)
        lbtmp = sb(es, "lbtmp", [128, 8], F32)
        onwT = sb(es, "onwT", [128, L * 4], F32)
        cvw = sb(es, "cvw", [128, L * 62], F32)
        cvb = sb(es, "cvb", [128, L * 2], F32)
        lnw = sb(es, "lnw", [128, L * 2], F32)
        lnb = sb(es, "lnb", [128, L * 2], F32)
        tabA0 = sb(es, "tabA0", [128, NIT], F32)
        tabB0 = sb(es, "tabB0", [128, NIT], F32)
        thrneg = sb(es, "thrneg", [128, 1], F32)
        evt = sb(es, "evt", [128, 512], F32)
        negh = sb(es, "negh", [128, 128], F32)
        omlh = sb(es, "omlh", [128, L * 4], F32)
        lbp = sb(es, "lbp", [128, L * 4], F32)
        onwh = sb(es, "onwh", [128, L * 4], F32)
        lnwh = sb(es, "lnwh", [128, L * 2], F32)
        lnbh = sb(es, "lnbh", [128, L * 2], F32)

        def modcol(l, j, k):
            c = l * 48 + j * 8 + k
            return modT[:, c:c + 1]

        with ExitStack() as ps_:
            stage = [sb(ps_, "adast%d" % i, [128, 8, 512], F32) for i in range(2)]
            rowS = [sb(ps_, "rowS%d" % i, [1, 512], F32) for i in range(2)]
            one1f = sb(ps_, "one1f", [1, 1], F32)
            G(lambda e: e.memset(one1f[:, :], 1.0), w=["one1f"])
            G(lambda e: e.memset(identF[:, :], 1.0), w=["identF"])
            G(lambda e: e.affine_select(identF[:, :], identF[:, :], [[-1, 128]], ALU.is_equal, fillreg(e, 0.0),
                                        base=0, channel_multiplier=1), r=["identF"], w=["identF"])
            V(lambda e: e.tensor_copy(identB[:, :], identF[:, :]), r=["identF"], w=["identB"])
            G(lambda e: e.memset(onesM[:, :], 1.0 / 256.0), w=["onesM"])
            G(lambda e: e.memset(thrneg[:, :], -1e29), w=["thrneg"])
            for n in range(NIT):
                a_n = 2.0 ** -(n + 2) if n < NIT - 1 else 2.0 ** -(NIT)
                b_n = 2.0 ** -(n + 1)
                G(lambda e, n=n, a_n=a_n: e.memset(tabA0[:, n:n + 1], a_n), w=["tabA0"])
                G(lambda e, n=n, b_n=b_n: e.memset(tabB0[:, n:n + 1], b_n), w=["tabB0"])
            for (dst, src, nm) in ((cT, cT_d, "cT"), (adabT, adabT_d, "adabT"), (nmixT, nmixT_d, "nmixT"),
                                   (nmlpT, nmlpT_d, "nmlpT"), (lbT, lbT_d, "lbT"), (onwT, onwT_d, "onwT"),
                                   (cvw, cvw_d, "cvw"), (cvb, cvb_d, "cvb"), (lnw, lnw_d, "lnw"),
                                   (lnb, lnb_d, "lnb")):
                ld(dst[:, :], src[:, :], w=[nm])
            act(cact[:, :], cT[:, :], AF.Silu, r=["cT"], w=["cact"])
            lb3 = lbT[:, :].rearrange("p (l h) -> p l h", h=4)
            act(lbT[:, :], lbT[:, :], AF.Exp, r=["lbT"], w=["lbT"])
            V(lambda e: e.tensor_copy(lbtmp[:, 0:4], lb3[:, 0, :]), r=["lbT"], w=["lbtmp"])
            for l in range(1, L):
                V(lambda e, l=l: e.tensor_tensor(lbtmp[:, 0:4], lbtmp[:, 0:4], lb3[:, l, :], ALU.add),
                  r=["lbT", "lbtmp"], w=["lbtmp"])
            V(lambda e: e.reciprocal(lbtmp[:, 4:8], lbtmp[:, 0:4]), r=["lbtmp"], w=["lbtmp2"])
            for l in range(L):
                V(lambda e, l=l: e.tensor_tensor(lb3[:, l, :], lb3[:, l, :], lbtmp[:, 4:8], ALU.mult),
                  r=["lbT", "lbtmp2"], w=["lbT"])
            V(lambda e: e.memset(lb3[:, 0, :], 0.0), r=["lbT"], w=["lbT"])
            for l in range(2, L):
                V(lambda e, l=l: e.tensor_tensor(lb3[:, l, :], lb3[:, l, :], lb3[:, l - 1, :], ALU.add),
                  r=["lbT"], w=["lbT"])
            V(lambda e: e.tensor_scalar(omlT[:, :], lbT[:, :], -1.0, 1.0, ALU.mult, ALU.add),
              r=["lbT"], w=["omlT"])
            V(lambda e: e.tensor_scalar(omlh[:, :], omlT[:, :], 0.5, None, ALU.mult), r=["omlT"], w=["omlh"])
            V(lambda e: e.tensor_tensor(lbp[:, :], lbT[:, :], omlh[:, :], ALU.add), r=["lbT", "omlh"], w=["lbp"])
            V(lambda e: e.tensor_scalar(onwh[:, :], onwT[:, :], 0.5, None, ALU.mult), r=["onwT"], w=["onwh"])
            V(lambda e: e.tensor_scalar(lnwh[:, :], lnw[:, :], 0.5, None, ALU.mult), r=["lnw"], w=["lnwh"])
            V(lambda e: e.tensor_scalar(lnbh[:, :], lnb[:, :], 0.5, None, ALU.mult), r=["lnb"], w=["lnbh"])
            G(lambda e: e.memset(negh[:, :], -0.5), w=["negh"])
            piece = 0
            for l in range(L):
                for pc in range(12):
                    st = stage[piece % 2]
                    sn = "adast%d" % (piece % 2)
                    ld(st[:, :, :], adaw_d[l, :, pc * 512:(pc + 1) * 512].rearrange("(k p) n -> p k n", p=128),
                       w=[sn])
                    rb = 2 + (piece % 2)
                    for k in range(8):
                        mm(bank(rb)[0:1, :], cact[:, k:k + 1], st[:, k, :], k == 0, k == 7, r=[sn, "cact"], w=[PS(rb)])
                    rw = rowS[piece % 2]
                    rwn = "rowS%d" % (piece % 2)
                    act(rw[0:1, :], bank(rb)[0:1, :], AF.Copy, r=[PS(rb)], w=[rwn])
                    for f in range(4):
                        col = l * 48 + pc * 4 + f
                        mm(bank(0)[:, col:col + 1], rw[0:1, f * 128:(f + 1) * 128], one1f[0:1, 0:1],
                           piece == 0 and f == 0, True, r=[rwn, "one1f"], w=[PS(0)])
                    piece += 1
            V(lambda e: e.tensor_tensor(modT[:, :], bank(0)[:, 0:L * 48], adabT[:, :], ALU.add),
              r=[PS(0), "adabT"], w=["modT"])
            m4 = modT[:, :].rearrange("p (l j k) -> p l j k", j=6, k=8)
            V(lambda e: e.scalar_tensor_tensor(G1T[:, :].rearrange("p (l k) -> p l k", k=8), m4[:, :, 1, :], 1.0,
                                               nmixT[:, :].rearrange("p (l k) -> p l k", k=8), ALU.add, ALU.mult),
              r=["modT", "nmixT"], w=["G1T"])
            V(lambda e: e.scalar_tensor_tensor(G2T[:, :].rearrange("p (l k) -> p l k", k=8), m4[:, :, 4, :], 1.0,
                                               nmlpT[:, :].rearrange("p (l k) -> p l k", k=8), ALU.add, ALU.mult),
              r=["modT", "nmlpT"], w=["G2T"])
            S_.flush()
        if _STOP <= 0:
            return nc

        def build_bc(bc, bcname, l, j, dtmp, dname, pbanks):
            for k in range(8):
                V(lambda e, k=k: e.tensor_scalar(dtmp[:, :], identF[:, :], modcol(l, j, k), None, ALU.mult),
                  r=["identF", "modT", dname], w=[dname])
                bk = pbanks[k // 4]
                mm(bank(bk)[:, (k % 4) * 128:(k % 4 + 1) * 128], onesM[:, :], dtmp[:, :], True, True,
                   r=["onesM", dname], w=[PS(bk)])
            for hh in range(2):
                A(lambda e, hh=hh: e.activation(bc[:, hh * 512:(hh + 1) * 512], bank(pbanks[hh]), AF.Copy, scale=256.0),
                  r=[PS(pbanks[hh])], w=[bcname])

        def norm_tile(xt_ap, xtname, xn, ssq, hT_out, hTname, GT_, l, jsh, pb, ncols=128, coff=0):
            act(xn[:, :], xt_ap, AF.Square, r=[xtname], w=["xn", "ssq"], accum_out=ssq[:, 0:1])
            V(lambda e: e.tensor_scalar(ssq[:, 1:2], ssq[:, 0:1], 1.0 / D, EPS, ALU.mult, ALU.add),
              r=["ssq"], w=["ssq1"])
            act(ssq[:, 2:3], ssq[:, 1:2], AF.Ln, r=["ssq1"], w=["ssq2"])
            act(ssq[:, 3:4], ssq[:, 2:3], AF.Exp, r=["ssq2"], w=["ssq3"], scale=-0.5)
            V(lambda e: e.tensor_scalar(xn[:, :], xt_ap, ssq[:, 3:4], None, ALU.mult),
              r=[xtname, "ssq3", "xn"], w=["xn"])
            for half in range(2):
                bk = pb[half]
                for k in range(4 * half, 4 * half + 4):
                    tr(bank(bk)[:, (k % 4) * 128:(k % 4 + 1) * 128], xn[:, k * 128:(k + 1) * 128], identF[:, :],
                       r=["xn", "identF"], w=[PS(bk)])
                k0 = 4 * half
                g_b = GT_[:, l * 8 + k0:l * 8 + k0 + 4].rearrange("p (k o) -> p k o", o=1).broadcast_to([128, 4, 128])
                c0 = l * 48 + jsh * 8 + k0
                s_b = modT[:, c0:c0 + 4].rearrange("p (k o) -> p k o", o=1).broadcast_to([128, 4, 128])
                V(lambda e, bk=bk, g_b=g_b: e.tensor_tensor(evt[:, :].rearrange("p (k t) -> p k t", t=128),
                                                            bank(bk).rearrange("p (k t) -> p k t", t=128), g_b, ALU.mult),
                  r=[PS(bk), "G1T", "G2T"], w=["evt"])
                V(lambda e, k0=k0, s_b=s_b: e.tensor_tensor(hT_out[:, k0:k0 + 4, coff:coff + ncols],
                                                            evt[:, :].rearrange("p (k t) -> p k t", t=128), s_b, ALU.add),
                  r=["evt", "modT"], w=[hTname])

        for l in range(L):
            src_d = x_d if l == 0 else out_d
            pl = ExitStack()
            pl.__enter__()
            WB = 1128
            wB = sb(pl, "wB", [128, 8, WB], BF16)
            woS = sb(pl, "woS", [128, 8, D], BF16)
            with ExitStack() as p1:
                WA = 2560
                wA = sb(p1, "wA", [128, 8, WA], BF16)
                dg = sb(p1, "dg", [128, 2, 31, 128], BF16)
                xt = [sb(p1, "xt%d" % i, [128, D], F32) for i in range(2)]
                xn = sb(p1, "xn", [128, D], F32)
                ssq = sb(p1, "ssq", [128, 4], F32)
                hT = [sb(p1, "hT%d" % i, [128, 8, 128], BF16) for i in range(2)]
                tht = [sb(p1, "tht%d" % i, [128, 512], F32) for i in range(2)]
                qs = [sb(p1, "qs%d" % i, [128, 512], F32) for i in range(2)]
                sg = [sb(p1, "sg%d" % i, [128, 512], F32) for i in range(2)]
                gs = [sb(p1, "gs%d" % i, [128, 512], F32) for i in range(3)]
                vtok = [sb(p1, "vtok%d" % i, [128, 512], BF16) for i in range(3)]
                hcur = [sb(p1, "hcur%d" % i, [128, 2, 128], BF16) for i in range(2)]
                kT = sb(p1, "kT", [128, 512], F32)
                fT = sb(p1, "fT", [128, 512], F32)
                GT = sb(p1, "GT", [128, 512], F32)
                t1 = sb(p1, "t1", [128, 512], F32)
                E = sb(p1, "E", [128, 512], F32)
                E4 = [sb(p1, "E4%d" % i, [128, 512], F32) for i in range(2)]
                qtl = [sb(p1, "qtl%d" % i, [128, 512], BF16) for i in range(2)]
                ktl = [sb(p1, "ktl%d" % i, [128, 512], BF16) for i in range(2)]
                khT = sb(p1, "khT", [128, 512], BF16)
                qhA = [sb(p1, "qhA%d" % i, [128, 512], BF16) for i in range(2)]
                qhB = [sb(p1, "qhB%d" % i, [128, 512], BF16) for i in range(2)]
                khtok = [sb(p1, "khtok%d" % i, [128, 512], BF16) for i in range(2)]
                ATm = sb(p1, "ATm", [128, 512], BF16)
                Sst = sb(p1, "Sst", [128, 512], F32)
                Sbf0 = sb(p1, "Sbf0", [128, 512], BF16)
                Sbf1 = sb(p1, "Sbf1", [128, 512], BF16)
                oa = sb(p1, "oa", [128, 512], F32)
                oab = sb(p1, "oab", [128, 512], BF16)
                ss4 = sb(p1, "ss4", [128, 16], F32)
                scanm = sb(p1, "scanm", [128, 512], F32)
                cmask = sb(p1, "cmask", [128, 512], U8)
                cmf = sb(p1, "cmf", [128, 512], F32)
                hbuf = sb(p1, "hbuf", [128, 2, 160], BF16)
                cc = [sb(p1, "cc%d" % i, [128, 256], F32) for i in range(2)]
                csq = sb(p1, "csq", [128, 256], F32)
                stt_ = [sb(p1, "stt%d" % i, [128, 256], F32) for i in range(2)]
                yv = sb(p1, "yv", [128, 256], F32)
                thy = sb(p1, "thy", [128, 256], F32)
                rs_b = [sb(p1, "rs_b%d" % i, [128, 128], F32) for i in range(2)]
                cat6 = [sb(p1, "cat6_%d" % i, [128, 6, 128], BF16) for i in range(2)]

                wst1 = [sb(p1, "wst%d" % i, [128, D], F32) for i in range(2)]
                wi = 0
                for (gn, d0, s0) in (("v", 1024, 1024), ("g", 1536, 1536), ("q", 0, 0), ("f", 512, 512), ("cu", 2048, 3112)):
                    for k2 in range(4):
                        stg, stn = wst1[wi % 2], "wst%d" % (wi % 2)
                        wi += 1
                        ld(stg[:, :].rearrange("p (k n) -> p k n", n=512),
                           win_d[l, k2 * 256:(k2 + 1) * 256, s0:s0 + 512].rearrange("(k p) n -> p k n", p=128), w=[stn])
                        act(wA[:, 2 * k2:2 * k2 + 2, d0:d0 + 512], stg[:, :].rearrange("p (k n) -> p k n", n=512), AF.Copy,
                            r=[stn], w=["wA_" + gn])
                g1bc = sb(p1, "g1bc", [128, D], F32)
                stg2 = sb(p1, "stg2", [128, 8, 40], F32)
                dtmp1 = sb(p1, "dtmp", [128, 128], F32)
                def prefetch_a2():
                    ld(stg2[:, :, :], win_d[l, :, 3072:3112].rearrange("(k p) n -> p k n", p=128), w=["stg2"])
                    for k in range(8):
                        rows = slice(k * 128, (k + 1) * 128)
                        stg, stn = wst1[k % 2], "wst%d" % (k % 2)
                        ld(stg[:, :], win_d[l, rows, 2048:3072], w=[stn])
                        act(wB[:, k, 0:1024], stg[:, :], AF.Copy, r=[stn], w=["wB"])
                    act(wB[:, :, 1024:1032], stg2[:, :, 32:40], AF.Copy, r=["stg2"], w=["wB"])
                    for r3 in range(3):
                        act(wB[:, :, 1032 + 32 * r3:1064 + 32 * r3], stg2[:, :, 0:32], AF.Copy, r=["stg2"], w=["wB"])
                    build_bc(g1bc, "g1bc", l, 2, dtmp1, "dtmp", (0, 1))
                    for k in range(8):
                        ld(wst1[k % 2][:, :], wout_d[l, k * 128:(k + 1) * 128, :], w=["wst%d" % (k % 2)])
                        V(lambda e, k=k: e.tensor_tensor(woS[:, k, :], wst1[k % 2][:, :], g1bc[:, :], ALU.mult),
                          r=["wst%d" % (k % 2), "g1bc"], w=["woS"])
                for ct in range(2):
                    for j in range(31):
                        c = l * 62 + ct * 31 + j
                        V(lambda e, ct=ct, j=j, c=c: e.tensor_scalar(dg[:, ct, j, :], identF[:, :], cvw[:, c:c + 1],
                                                                     None, ALU.mult),
                          r=["identF", "cvw"], w=["dg"])
                G(lambda e: e.memset(scanm[:, :], 1.0), w=["scanm"])
                sm3 = scanm[:, :].rearrange("p (c j) -> p c j", j=64)
                G(lambda e: e.memset(sm3[:, :, 0:1], 0.0), r=["scanm"], w=["scanm"])
                G(lambda e: e.memset(cmf[:, :], 1.0), w=["cmf"])
                for h in range(4):
                    G(lambda e, h=h: e.affine_select(cmf[:, h * 128:(h + 1) * 128], cmf[:, h * 128:(h + 1) * 128],
                                                     [[1, 128]], ALU.is_ge, fillreg(e, 0.0), base=0, channel_multiplier=-1),
                      r=["cmf"], w=["cmf"])
                    G(lambda e, h=h: e.memset(cmf[0:64, h * 128 + 64:(h + 1) * 128], 0.0), r=["cmf"], w=["cmf"])
                V(lambda e: e.tensor_copy(cmask[:, :], cmf[:, :]), r=["cmf"], w=["cmask"])
                for (tl, nm) in ((ATm, "ATm"), (qhA[0], "qhA0"), (qhA[1], "qhA1"), (qhB[0], "qhB0"), (qhB[1], "qhB1"),
                                 (Sst, "Sst"), (Sbf0, "Sbf0"), (Sbf1, "Sbf1")):
                    G(lambda e, tl=tl: e.memset(tl[:, :], 0.0), w=[nm])
                G(lambda e: e.memset(hbuf[:, :, :], 0.0), w=["hbuf"])

                bc4 = lambda tl_: tl_[:, l * 4:(l + 1) * 4].rearrange("p (h o) -> p h o", o=1).broadcast_to([128, 4, 128])
                lb_b, oml_b = bc4(lbT), bc4(omlT)
                v3 = lambda t_: t_[:, :].rearrange("p (h t) -> p h t", t=128)
                c3 = lambda t_: t_[:, :].rearrange("p (c j) -> p c j", j=64)
                c4 = lambda t_: t_[:, :].rearrange("p (h c j) -> p h c j", c=2, j=64)
                QSC = 128.0 ** -0.5
                st1 = {"th": 0}

                def nxt_th():
                    i = st1["th"] % 2
                    st1["th"] += 1
                    return tht[i], "tht%d" % i

                def sigm(dst, src_ap, r, w):
                    act(dst, src_ap, AF.Exp, r=r, w=w, scale=-1.0)
                    act(dst, dst, AF.Ln, r=w, w=w, bias=1.0)
                    act(dst, dst, AF.Exp, r=w, w=w, scale=-1.0)

                def stageX(t):
                    b = t % 2
                    b3 = t % 3
                    xtn, hTn = "xt%d" % b, "hT%d" % b
                    if t + 1 < NT:
                        ld(xt[1 - b][:, :], src_d[(t + 1) * 128:(t + 2) * 128, :], w=["xt%d" % (1 - b)])
                    norm_tile(xt[b][:, :], xtn, xn, ssq, hT[b], hTn, G1T, l, 0, (0, 0))
                    ld(hT_d[t, :, :], hT[b][:, :, :].rearrange("p k n -> p (k n)"), r=[hTn], w=["hTd%d" % t])
                    S_.unit()
                    for k in range(8):
                        mm(bank(1), hT[b][:, k, :], wA[:, k, 1024:1536], k == 0, k == 7, r=[hTn, "wA_v"], w=[PS(1)])
                    act(vtok[b3][:, :], bank(1), AF.Copy, r=[PS(1)], w=["vtok%d" % b3])
                    S_.unit()
                    for k in range(8):
                        mm(bank(2), hT[b][:, k, :], wA[:, k, 1536:2048], k == 0, k == 7, r=[hTn, "wA_g"], w=[PS(2)])
                    th_, thn_ = nxt_th()
                    sigm(th_[:, :], bank(2), [PS(2)], [thn_])
                    V(lambda e, th_=th_: e.tensor_tensor(gs[b3][:, :], th_[:, :], bank(2), ALU.mult),
                      r=[thn_, PS(2)], w=["gs%d" % b3])
                    S_.unit()
                    for f in range(4):
                        for k in range(8):
                            mm(bank(1)[:, f * 128:(f + 1) * 128], wA[:, k, f * 128:(f + 1) * 128], hT[b][:, k, :],
                               k == 0, k == 7, r=[hTn, "wA_q"], w=[PS(1)])
                    th_, thn_ = nxt_th()
                    sigm(th_[:, :], bank(1), [PS(1)], [thn_])
                    V(lambda e, th_=th_: e.tensor_tensor(qs[b][:, :], th_[:, :], bank(1), ALU.mult),
                      r=[thn_, PS(1)], w=["qs%d" % b])
                    S_.unit()
                    for f in range(4):
                        for k in range(8):
                            mm(bank(2)[:, f * 128:(f + 1) * 128], wA[:, k, 512 + f * 128:512 + (f + 1) * 128], hT[b][:, k, :],
                               k == 0, k == 7, r=[hTn, "wA_f"], w=[PS(2)])
                    sigm(sg[b][:, :], bank(2), [PS(2)], ["sg%d" % b])
                    S_.unit()
                    for f in range(4):
                        for k in range(8):
                            mm(bank(1)[:, f * 128:(f + 1) * 128], wA[:, k, 2048 + f * 128:2048 + (f + 1) * 128], hT[b][:, k, :],
                               k == 0, k == 7, r=[hTn, "wA_cu"], w=[PS(1)])
                    th_, thn_ = nxt_th()
                    sigm(th_[:, 0:256], bank(1)[:, 256:512], [PS(1)], [thn_])
                    V(lambda e, th_=th_: e.tensor_tensor(hcur[b][:, :, :].rearrange("p c t -> p (c t)"), th_[:, 0:256],
                                                         bank(1)[:, 0:256], ALU.mult),
                      r=[thn_, PS(1)], w=["hcur%d" % b])

                def stageY1(t):
                    b = t % 2
                    b3 = t % 3
                    qsn, sgn_, gsn, vtn, hcn, c6n = "qs%d" % b, "sg%d" % b, "gs%d" % b3, "vtok%d" % b3, "hcur%d" % b, "cat6_%d" % b
                    qs_, sg_, gs_, vt_, c6 = qs[b], sg[b], gs[b3], vtok[b3], cat6[b]
                    qtl_, ktl_, khtok_, qhA_, qhB_, E4_ = qtl[b], ktl[b], khtok[b], qhA[b], qhB[b], E4[b]
                    cc_, stt2, rsb_ = cc[b], stt_[b], rs_b[b]
                    qtln, ktln, khtokn, qhAn, qhBn, E4n = "qtl%d" % b, "ktl%d" % b, "khtok%d" % b, "qhA%d" % b, "qhB%d" % b, "E4%d" % b
                    ccn, sttn, rsbn = "cc%d" % b, "stt%d" % b, "rs_b%d" % b
                    G(lambda e: e.tensor_copy(hbuf[:, :, 32:160], hcur[b][:, :, :]), r=[hcn, "hbuf"], w=["hbuf"])
                    for ct in range(2):
                        for j in range(31):
                            mm(bank(3)[:, ct * 128:(ct + 1) * 128], dg[:, ct, j, :],
                               hbuf[:, ct, 2 + j:2 + j + 128], j == 0, j == 30, r=["dg", "hbuf"], w=[PS(3)])
                        S_.unit()
                    for ct in range(2):
                        cs = slice(ct * 128, (ct + 1) * 128)
                        act(cc_[:, cs], bank(3)[:, cs], AF.Identity,
                            r=[PS(3), "cvb"], w=[ccn], bias=cvb[:, l * 2 + ct:l * 2 + ct + 1])
                    act(csq[:, :], cc_[:, :], AF.Square, r=[ccn], w=["csq"])
                    G(lambda e: e.tensor_copy(hbuf[:, :, 0:32], hbuf[:, :, 128:160]), r=["hbuf", PS(3)], w=["hbuf"])
                    S_.unit()
                    for (si, src_, sn) in ((0, cc_, ccn), (1, csq, "csq")):
                        for ct in range(2):
                            mm(bank(4)[:, si * 128:(si + 1) * 128], onesM[:, :], src_[:, ct * 128:(ct + 1) * 128],
                               ct == 0, ct == 1, r=["onesM", sn], w=[PS(4)])
                    act(stt2[:, :], bank(4)[:, 0:256], AF.Copy, r=[PS(4)], w=[sttn])
                    V(lambda e: e.tensor_tensor(rsb_[:, :], stt2[:, 0:128], stt2[:, 0:128], ALU.mult), r=[sttn], w=[rsbn])
                    V(lambda e: e.tensor_tensor(rsb_[:, :], stt2[:, 128:256], rsb_[:, :], ALU.subtract),
                      r=[sttn, rsbn], w=[rsbn])
                    V(lambda e: e.tensor_scalar(rsb_[:, :], rsb_[:, :], EPS, None, ALU.add), r=[rsbn], w=[rsbn])
                    S_.unit()
                    V(lambda e: e.tensor_tensor(v3(t1), v3(sg_), oml_b, ALU.mult), r=[sgn_, "omlT"], w=["t1"])
                    V(lambda e: e.tensor_tensor(v3(fT), v3(t1), lb_b, ALU.add), r=["t1", "lbT"], w=["fT"])
                    V(lambda e: e.tensor_tensor(v3(kT), oml_b, v3(t1), ALU.subtract), r=["t1", "omlT"], w=["kT"])
                    V(lambda e: e.tensor_scalar(fT[:, :], fT[:, :], 1e-30, None, ALU.max), r=["fT"], w=["fT"])
                    act(fT[:, :], fT[:, :], AF.Ln, r=["fT"], w=["fT"])
                    act(rsb_[:, :], rsb_[:, :], AF.Ln, r=[rsbn], w=[rsbn])
                    act(rsb_[:, :], rsb_[:, :], AF.Exp, r=[rsbn], w=[rsbn], scale=-0.5)
                    S_.unit()
                    V(lambda e: e.tensor_tensor_scan(GT[:, :], scanm[:, :], fT[:, :], 0.0, ALU.mult, ALU.add),
                      r=["scanm", "fT"], w=["GT"])
                    V(lambda e: e.tensor_tensor(c3(t1), c3(GT), c3(GT)[:, :, 32:33].broadcast_to([128, 8, 64]),
                                                ALU.subtract), r=["GT"], w=["t1"])
                    act(E[:, :], t1[:, :], AF.Exp, r=["t1"], w=["E"])
                    V(lambda e: e.scalar_tensor_tensor(qtl_[:, :], qs_[:, :], QSC, E[:, :], ALU.mult, ALU.mult),
                      r=[qsn, "E"], w=[qtln])
                    S_.unit()
                    act(E[:, :], t1[:, :], AF.Exp, r=["t1", qtln], w=["E"], scale=-1.0)
                    V(lambda e: e.tensor_tensor(ktl_[:, :], kT[:, :], E[:, :], ALU.mult), r=["kT", "E"], w=[ktln])
                    V(lambda e: e.tensor_tensor(c3(t1), c3(GT)[:, :, 63:64].broadcast_to([128, 8, 64]), c3(GT),
                                                ALU.subtract), r=["GT", "E"], w=["t1"])
                    S_.unit()
                    act(E[:, :], t1[:, :], AF.Exp, r=["t1", ktln], w=["E"])
                    V(lambda e: e.tensor_tensor(khT[:, :], kT[:, :], E[:, :], ALU.mult), r=["kT", "E"], w=["khT"])
                    act(E4_[:, :], GT[:, :], AF.Exp, r=["GT"], w=[E4n])
                    S_.unit()
                    V(lambda e: e.scalar_tensor_tensor(c4(qhA_)[:, :, 0, :], c4(qs_)[:, :, 0, :], QSC, c4(E4_)[:, :, 0, :],
                                                       ALU.mult, ALU.mult), r=[qsn, E4n], w=[qhAn])
                    V(lambda e: e.scalar_tensor_tensor(c4(qhB_)[:, :, 1, :], c4(qs_)[:, :, 1, :], QSC, c4(E4_)[:, :, 1, :],
                                                       ALU.mult, ALU.mult), r=[qsn, E4n], w=[qhBn])
                    for h in range(4):
                        tr(bankb(4)[:, h * 128:(h + 1) * 128], khT[:, h * 128:(h + 1) * 128], identB[:, :],
                           r=["khT", "identB"], w=[PS(4)])
                    act(khtok_[:, :], bankb(4)[:, 0:512], AF.Copy, r=[PS(4)], w=[khtokn])
                    S_.unit()

                def stageY2(t):
                    b = t % 2
                    b3 = t % 3
                    qsn, sgn_, gsn, vtn, hcn, c6n = "qs%d" % b, "sg%d" % b, "gs%d" % b3, "vtok%d" % b3, "hcur%d" % b, "cat6_%d" % b
                    qs_, sg_, gs_, vt_, c6 = qs[b], sg[b], gs[b3], vtok[b3], cat6[b]
                    qtl_, ktl_, khtok_, qhA_, qhB_, E4_ = qtl[b], ktl[b], khtok[b], qhA[b], qhB[b], E4[b]
                    cc_, stt2, rsb_ = cc[b], stt_[b], rs_b[b]
                    qtln, ktln, khtokn, qhAn, qhBn, E4n = "qtl%d" % b, "ktl%d" % b, "khtok%d" % b, "qhA%d" % b, "qhB%d" % b, "E4%d" % b
                    ccn, sttn, rsbn = "cc%d" % b, "stt%d" % b, "rs_b%d" % b
                    for h in range(4):
                        mm(bank(5)[:, h * 128:(h + 1) * 128], ktl_[:, h * 128:(h + 1) * 128],
                           qtl_[:, h * 128:(h + 1) * 128], True, True, r=[ktln, qtln], w=[PS(5)])
                    V(lambda e: e.copy_predicated(ATm[:, :], cmask[:, :], bank(5)), r=[PS(5), "cmask"], w=["ATm"])
                    S_.unit()
                    for h in range(4):
                        hs = slice(h * 128, (h + 1) * 128)
                        mm(bank(6)[:, hs], ATm[:, hs], vt_[:, hs], h == 0, False, r=["ATm", vtn], w=[PS(6)])
                        mm(bank(6)[:, hs], qhA_[:, hs], Sbf0[:, hs], False, False, r=[qhAn, "Sbf0"], w=[PS(6)])
                    for h in range(4):
                        hs = slice(h * 128, (h + 1) * 128)
                        mm(bank(7)[:, hs], khtok_[0:64, hs], vt_[0:64, hs], True, True, r=[khtokn, vtn], w=[PS(7)])
                    dec = lambda c: c4(E4_)[:, :, c, 63:64].broadcast_to([128, 4, 128])
                    V(lambda e: e.tensor_tensor(v3(Sst), v3(Sst), dec(0), ALU.mult), r=["Sst", E4n], w=["Sst"])
                    V(lambda e: e.tensor_tensor(Sst[:, :], Sst[:, :], bank(7), ALU.add), r=["Sst", PS(7)], w=["Sst"])
                    act(Sbf1[:, :], Sst[:, :], AF.Copy, r=["Sst"], w=["Sbf1"])
                    S_.unit()
                    for h in range(4):
                        hs = slice(h * 128, (h + 1) * 128)
                        mm(bank(6)[:, hs], qhB_[:, hs], Sbf1[:, hs], False, True, r=[qhBn, "Sbf1"], w=[PS(6)])
                    for h in range(4):
                        hs = slice(h * 128, (h + 1) * 128)
                        mm(bank(7)[:, hs], khtok_[64:128, hs], vt_[64:128, hs], True, True, r=[khtokn, vtn], w=[PS(7)])
                    V(lambda e: e.tensor_tensor(v3(Sst), v3(Sst), dec(1), ALU.mult), r=["Sst", E4n], w=["Sst"])
                    V(lambda e: e.tensor_tensor(Sst[:, :], Sst[:, :], bank(7), ALU.add), r=["Sst", PS(7)], w=["Sst"])
                    act(Sbf0[:, :], Sst[:, :], AF.Copy, r=["Sst"], w=["Sbf0"])
                    S_.unit()
                    for h in range(4):
                        act(oa[:, h * 128:(h + 1) * 128], bank(6)[:, h * 128:(h + 1) * 128], AF.Square,
                            r=[PS(6)], w=["oa", "ss4"], accum_out=ss4[:, h:h + 1])
                    V(lambda e: e.tensor_scalar(ss4[:, 4:8], ss4[:, 0:4], 1.0 / 128, EPS, ALU.mult, ALU.add),
                      r=["ss4"], w=["ss4b"])
                    act(ss4[:, 8:12], ss4[:, 4:8], AF.Ln, r=["ss4b"], w=["ss4c"])
                    act(ss4[:, 12:16], ss4[:, 8:12], AF.Exp, r=["ss4c"], w=["ss4d"], scale=-0.5)
                    V(lambda e: e.tensor_tensor(v3(oa), v3(bank(6)),
                                                ss4[:, 12:16].rearrange("p (h o) -> p h o", o=1).broadcast_to([128, 4, 128]),
                                                ALU.mult), r=[PS(6), "ss4d", "oa"], w=["oa"])
                    V(lambda e: e.tensor_tensor(oab[:, :], oa[:, :], gs_[:, :], ALU.mult), r=["oa", gsn], w=["oab"])
                    S_.unit()
                    for h in range(4):
                        tr(bankb(5)[:, 512 + h * 128:512 + (h + 1) * 128], oab[:, h * 128:(h + 1) * 128], identB[:, :],
                           r=["oab", "identB"], w=[PS(5)])
                    for h in range(4):
                        act(c6[:, h, :], bankb(5)[:, 512 + h * 128:512 + (h + 1) * 128], AF.Copy,
                            r=[PS(5), "onwT"], w=[c6n], scale=onwT[:, l * 4 + h:l * 4 + h + 1])
                    S_.unit()
                    y3 = yv[:, :].rearrange("p (c t) -> p c t", t=128)
                    V(lambda e: e.tensor_tensor(y3, cc_[:, :].rearrange("p (c t) -> p c t", t=128),
                                                stt2[:, 0:128].rearrange("p (o t) -> p o t", o=1).broadcast_to([128, 2, 128]),
                                                ALU.subtract), r=[ccn, sttn], w=["yv"])
                    V(lambda e: e.tensor_tensor(y3, y3,
                                                rsb_[:, :].rearrange("p (o t) -> p o t", o=1).broadcast_to([128, 2, 128]),
                                                ALU.mult), r=["yv", rsbn], w=["yv"])
                    for ct in range(2):
                        cs = slice(ct * 128, (ct + 1) * 128)
                        V(lambda e, cs=cs, ct=ct: e.tensor_scalar(yv[:, cs], yv[:, cs], lnw[:, l * 2 + ct:l * 2 + ct + 1],
                                                                  lnb[:, l * 2 + ct:l * 2 + ct + 1], ALU.mult, ALU.add),
                          r=["yv", "lnw", "lnb"], w=["yv"])
                    sigm(thy[:, :], yv[:, :], ["yv"], ["thy"])
                    V(lambda e: e.tensor_tensor(c6[:, 4:6, :].rearrange("p c t -> p (c t)"), thy[:, :], yv[:, :], ALU.mult),
                      r=["thy", "yv"], w=[c6n])
                    ld(cat_d[t, :, :], c6[:, :, :].rearrange("p k n -> p (k n)"), r=[c6n], w=["catd%d" % t])

                def cap1(fn, t):
                    S_.begin(fine=True)
                    fn(t)
                    return S_.end()

                ld(xt[0][:, :], src_d[0:128, :], w=["xt0"])
                for step in range(NT + 2):
                    streams = []
                    if step - 2 >= 0:
                        streams.append(cap1(stageY2, step - 2))
                    if 0 <= step - 1 < NT:
                        streams.append(cap1(stageY1, step - 1))
                    if step < NT:
                        streams.append(cap1(stageX, step))
                    S_.run_merged(streams)
                    if step == 1:
                        prefetch_a2()
                S_.flush()
            if _STOP <= 1:
                return nc

            with ExitStack() as p2:
                KTc = sb(p2, "KTc", [128, 2, S], BF16)
                Vaug = sb(p2, "Vaug", [128, NT, 4, 65], BF16)
                kidx = sb(p2, "kidx", [128, S], BF16)
                score = [sb(p2, "score%d" % i, [128, S], F32) for i in range(3)]
                junkb = sb(p2, "junkb", [128, S], BF16)
                junk8 = sb(p2, "junk8", [128, S], U8)
                cntd = sb(p2, "cntd", [128, NIT], F32)
                vc = sb(p2, "vc", [128, NIT], F32)
                thrc = sb(p2, "thrc", [128, 4], F32)
                ones1 = sb(p2, "ones1", [128, 1], BF16)
                xt = [sb(p2, "x2t%d" % i, [128, D], F32) for i in range(3)]
                hT = [sb(p2, "h2T%d" % i, [128, 8, 128], BF16) for i in range(3)]
                catT = [sb(p2, "catT%d" % i, [128, 8, 128], BF16) for i in range(3)]
                sqT = [sb(p2, "sqT%d" % i, [128, 2, 256], BF16) for i in range(3)]
                identB2 = sb(p2, "identB2", [128, 256], BF16)
                iqT = sb(p2, "iqT", [128, 3, 128], BF16)
                wab = sb(p2, "wab", [128, 8], F32)
                wsg = sb(p2, "wsg", [128, 8], F32)
                Rb = [sb(p2, "Rb%d" % i, [128, 512], F32) for i in range(3)]
                Mb = [sb(p2, "Mb%d" % i, [128, 512], BF16) for i in range(3)]
                PT = [sb(p2, "PT%d" % i, [128, 512], BF16) for i in range(3)]
                ob = sb(p2, "ob", [128, 256], BF16)
                bis = sb(p2, "bis", [128, 16], F32)
                tabA = sb(p2, "tabA", [128, NIT], F32)
                tabB = sb(p2, "tabB", [128, NIT], F32)
                cntc = sb(p2, "cntc", [128, NIT], F32)
                uc = sb(p2, "uc", [128, NIT], F32)
                mid = sb(p2, "mid", [128, NIT + 1], F32)
                top8 = sb(p2, "top8", [128, 8], F32)
                rs4 = sb(p2, "rs4", [128, 4], F32)

                G(lambda e: e.memset(Vaug[:, :, :, 64:65], 1.0), w=["Vaug"])
                G(lambda e: e.memset(ones1[:, :], 1.0), w=["ones1"])
                for i in range(3):
                    G(lambda e, i=i: e.memset(sqT[i][:, :, :], 0.0), w=["sqT%d" % i])
                for c in range(2):
                    V(lambda e, c=c: e.tensor_copy(identB2[:, c * 128:(c + 1) * 128], identB[:, :]), r=["identB"], w=["identB2"])

                IDXC = (32.0 ** -0.5) * (8.0 ** -0.5)
                state = {"rbi": 0, "mbi": 0, "mgi": 0}

                def stagePS(t):
                    b = t % 3
                    xtn, hTn, cTn, scn, sqn = "x2t%d" % b, "h2T%d" % b, "catT%d" % b, "score%d" % b, "sqT%d" % b
                    sc = score[b]
                    N = (t + 1) * 128
                    tok = slice(t * 128, (t + 1) * 128)
                    ld(xt[b][:, :], src_d[tok, :], w=[xtn])
                    ld(hT[b][:, :, :].rearrange("p k n -> p (k n)"), hT_d[t, :, :], r=["hTd%d" % t], w=[hTn])
                    ld(catT[b][:, 0:4, :].rearrange("p k n -> p (k n)"), cat_d[t, :, 0:512], r=["catd%d" % t], w=[cTn])
                    ld(catT[b][:, 6:8, :].rearrange("p k n -> p (k n)"), cat_d[t, :, 512:768], r=["catd%d" % t], w=[cTn])
                    S_.unit()
                    for k in range(8):
                        mm(bank(0)[:, 0:256], hT[b][:, k, :], wB[:, k, 512:768], k == 0, k == 7, r=[hTn, "wB"], w=[PS(0)])
                    for k in range(8):
                        mm(bank(0)[:, 256:264], hT[b][:, k, :], wB[:, k, 1024:1032], k == 0, k == 7,
                           r=[hTn, "wB"], w=[PS(0)])
                    act(Vaug[:, t, :, 0:64], bank(0)[:, 0:256].rearrange("p (h d) -> p h d", d=64), AF.Copy,
                        r=[PS(0)], w=["Vaug"])
                    act(wab[:, :], bank(0)[:, 256:264], AF.Abs, r=[PS(0)], w=["wab"], scale=IDXC)
                    act(wsg[:, :], bank(0)[:, 256:264], AF.Sign, r=[PS(0)], w=["wsg"])
                    S_.unit()
                    for f in range(4):
                        cb = (0, 128, 256, 384)[f]
                        for k in range(8):
                            mm(bank(1)[:, f * 128:(f + 1) * 128], wB[:, k, cb:cb + 128], hT[b][:, k, :],
                               k == 0, k == 7, r=[hTn, "wB"], w=[PS(1)])
                    for hl in range(2):
                        rows = slice(64 * hl, 64 * hl + 64)
                        act(sqT[b][rows, :, hl * 128:(hl + 1) * 128],
                            bank(1)[rows, 0:256].rearrange("p (c t) -> p c t", t=128), AF.Copy,
                            r=[PS(1)], w=[sqn], scale=0.125)
                    V(lambda e: e.tensor_copy(KTc[:, :, tok], bank(1)[:, 256:512].rearrange("p (c t) -> p c t", t=128)),
                      r=[PS(1)], w=["KTc"])
                    S_.unit()
                    for f, (cb, m) in enumerate(((768, 96), (864, 96), (960, 64), (1032, 96))):
                        for k in range(8):
                            mm(bank(2)[0:m, f * 128:(f + 1) * 128], wB[:, k, cb:cb + m], hT[b][:, k, :],
                               k == 0, k == 7, r=[hTn, "wB"], w=[PS(2)])
                    act(iqT[0:96, :, :], bank(2)[0:96, 0:384].rearrange("p (c t) -> p c t", t=128), AF.Copy,
                        r=[PS(2)], w=["iqT"])
                    V(lambda e: e.tensor_copy(kidx[0:96, tok], bank(2)[0:96, 384:512]), r=[PS(2)], w=["kidx"])
                    S_.unit()
                    NB = (N + 511) // 512
                    prev_acc = [None]
                    for kb in range(NB):
                        wN = min(512, N - kb * 512)
                        ks_ = slice(kb * 512, kb * 512 + wN)
                        for hh in range(8):
                            g_, r_ = hh // 3, hh % 3
                            pb = 3 + (hh % 2)
                            mm(bank(pb)[:, 0:wN], iqT[32 * r_:32 * r_ + 32, g_, :], kidx[32 * r_:32 * r_ + 32, ks_],
                               True, True, r=["iqT", "kidx"], w=[PS(pb)])
                            R_ = Rb[state["rbi"] % 3]
                            Rn = "Rb%d" % (state["rbi"] % 3)
                            state["rbi"] += 1
                            act(R_[:, 0:wN], bank(pb)[:, 0:wN], AF.Relu, r=[PS(pb), "wab"], w=[Rn],
                                scale=wab[:, hh:hh + 1])

                            def acc(R_=R_, Rn=Rn, ks_=ks_, wN=wN, hh=hh):
                                if hh == 0:
                                    V(lambda e: e.tensor_scalar(sc[:, ks_], R_[:, 0:wN], wsg[:, 0:1], None, ALU.mult),
                                      r=[Rn, "wsg"], w=[scn])
                                else:
                                    V(lambda e: e.scalar_tensor_tensor(sc[:, ks_], R_[:, 0:wN], wsg[:, hh:hh + 1], sc[:, ks_],
                                                                       ALU.mult, ALU.add),
                                      r=[Rn, "wsg", scn], w=[scn])
                            if prev_acc[0] is not None:
                                prev_acc[0]()
                            prev_acc[0] = acc
                            S_.unit()
                    prev_acc[0]()
                    G(lambda e: e.affine_select(sc[:, tok], sc[:, tok], [[-1, 128]], ALU.is_ge, fillreg(e, -1e30),
                                                base=0, channel_multiplier=1), r=[scn], w=[scn])

                def stageBI(t):
                    b = t % 3
                    scn = "score%d" % b
                    sc = score[b]
                    N = (t + 1) * 128
                    thn = "thr%d" % b
                    if t * 128 < TOPK:
                        V(lambda e: e.tensor_copy(thrc[:, b:b + 1], thrneg[:, 0:1]), r=["thrneg"], w=[thn])
                        return
                    Ka = max(128, min(N - 128, int(round(0.66 * (t + 1))) * 128))
                    V(lambda e: e.max(top8[:, :], sc[:, 0:N]), r=[scn], w=["top8"])
                    S_.unit()
                    V(lambda e: e.tensor_reduce(bis[:, 0:1], sc[:, 0:TOPK], mybir.AxisListType.X, ALU.min),
                      r=[scn], w=["bis0"])
                    V(lambda e: e.tensor_tensor(bis[:, 1:2], top8[:, 0:1], bis[:, 0:1], ALU.subtract),
                      r=["top8", "bis0"], w=["bis1"])
                    V(lambda e: e.tensor_scalar(tabA[:, :], tabA0[:, :], bis[:, 1:2], None, ALU.mult),
                      r=["bis1", "tabA0"], w=["tabA"])
                    V(lambda e: e.tensor_scalar(tabB[:, :], tabB0[:, :], bis[:, 1:2], None, ALU.mult),
                      r=["bis1", "tabB0"], w=["tabB"])
                    V(lambda e: e.scalar_tensor_tensor(mid[:, 0:1], bis[:, 1:2], 0.5, bis[:, 0:1], ALU.mult, ALU.add),
                      r=["bis0", "bis1"], w=["mid"])
                    S_.unit()
                    for n in range(NIT):
                        act(junkb[:, 0:Ka], sc[:, 0:Ka], AF.Sign, r=[scn, "mid"], w=["junkb", "cntA"],
                            scale=-1.0, bias=mid[:, n:n + 1], accum_out=cntc[:, n:n + 1])
                        V(lambda e, n=n: e.scalar_tensor_tensor(junk8[:, Ka:N], sc[:, Ka:N], mid[:, n:n + 1],
                                                                ones1[:, 0:1].broadcast_to([128, N - Ka]),
                                                                ALU.is_ge, ALU.mult, accum_out=cntd[:, n:n + 1]),
                          r=[scn, "mid", "ones1"], w=["junk8", "cntD"])
                        V(lambda e, n=n: e.scalar_tensor_tensor(vc[:, n:n + 1], cntd[:, n:n + 1], 2.0, cntc[:, n:n + 1],
                                                                ALU.mult, ALU.subtract),
                          r=["cntA", "cntD"], w=["vc"])
                        V(lambda e, n=n: e.scalar_tensor_tensor(uc[:, n:n + 1], vc[:, n:n + 1], float(2 * TOPK - Ka - 1),
                                                                tabB[:, n:n + 1], ALU.is_gt, ALU.mult),
                          r=["vc", "tabB"], w=["uc"])
                        V(lambda e, n=n: e.scalar_tensor_tensor(mid[:, n + 1:n + 2], mid[:, n:n + 1], tabA[:, n:n + 1],
                                                                uc[:, n:n + 1], ALU.subtract, ALU.add),
                          r=["mid", "tabA", "uc"], w=["mid"])
                        S_.unit()
                    V(lambda e: e.tensor_copy(thrc[:, b:b + 1], mid[:, NIT:NIT + 1]), r=["mid"], w=[thn])

                def stageAT(t):
                    b = t % 3
                    xtn, cTn, scn, sqn, thn = "x2t%d" % b, "catT%d" % b, "score%d" % b, "sqT%d" % b, "thr%d" % b
                    sc = score[b]
                    tok = slice(t * 128, (t + 1) * 128)
                    thr = thrc[:, b:b + 1]
                    prev_pv = [None]
                    NG = (t + 4) // 4
                    Nq = (t + 1) * 128
                    mg0 = state["mgi"]
                    state["mgi"] += NG

                    def genmask(g):
                        if g >= NG:
                            return
                        w_ = min(512, Nq - g * 512)
                        gi = mg0 + g
                        M_ = Mb[gi % 3]
                        V(lambda e: e.tensor_scalar(M_[:, 0:w_], sc[:, g * 512:g * 512 + w_], thr, -30000.0, ALU.is_lt, ALU.mult),
                          r=[scn, thn], w=["Mb%d" % (gi % 3)])
                    genmask(0)
                    genmask(1)
                    for kb in range(t + 1):
                        kcs = slice(kb * 128, (kb + 1) * 128)
                        mbi = state["mbi"]
                        state["mbi"] += 1
                        g = kb // 4
                        if kb % 4 == 0:
                            genmask(g + 2)
                        gi = mg0 + g
                        M_ = Mb[gi % 3][:, (kb % 4) * 128:(kb % 4 + 1) * 128]
                        Mn = "Mb%d" % (gi % 3)
                        P_ = PT[mbi % 3]
                        Pn = "PT%d" % (mbi % 3)
                        pb = 5 + (mbi % 2)
                        for c in range(2):
                            mm(bank(pb)[:, c * 256:(c + 1) * 256], KTc[:, c, kcs], sqT[b][:, c, :], True, False,
                               r=["KTc", sqn], w=[PS(pb)])
                            mm(bank(pb)[:, c * 256:(c + 1) * 256], M_, identB2[:, :], False, True,
                               r=[Mn, "identB2"], w=[PS(pb)])
                        act(P_[:, :], bank(pb), AF.Exp, r=[PS(pb)], w=[Pn])

                        def pv(kb=kb, P_=P_, Pn=Pn):
                            for h in range(4):
                                mm(bank(7)[:, h * 65:(h + 1) * 65], P_[:, h * 128:(h + 1) * 128], Vaug[:, kb, h, :],
                                   kb == 0 and h == 0, kb == t, r=[Pn, "Vaug"], w=[PS(7)])
                        if prev_pv[0] is not None:
                            prev_pv[0]()
                        prev_pv[0] = pv
                        S_.unit()
                    prev_pv[0]()
                    o3 = bank(7)[:, 0:260].rearrange("p (h d) -> p h d", d=65)
                    V(lambda e: e.reciprocal(rs4[:, :].rearrange("p (h o) -> p h o", o=1), o3[:, :, 64:65]), r=[PS(7)], w=["rs4"])
                    V(lambda e: e.tensor_tensor(ob[:, :].rearrange("p (h d) -> p h d", d=64), o3[:, :, 0:64],
                                                rs4[:, :].rearrange("p (h o) -> p h o", o=1).broadcast_to([128, 4, 64]),
                                                ALU.mult), r=[PS(7), "rs4"], w=["ob"])
                    for c in range(2):
                        tr(bankb(7)[:, c * 128:(c + 1) * 128], ob[:, c * 128:(c + 1) * 128], identB[:, :],
                           r=["ob", "identB"], w=[PS(7)])
                    act(catT[b][:, 4:6, :], bankb(7)[:, 0:256].rearrange("p (c t) -> p c t", t=128), AF.Copy,
                        r=[PS(7)], w=[cTn])
                    S_.unit()
                    for hf in range(2):
                        for c in range(8):
                            mm(bank(5 + hf), catT[b][:, c, :], woS[:, c, hf * 512:(hf + 1) * 512], c == 0, c == 7,
                               r=[cTn, "woS"], w=[PS(5 + hf)])
                        V(lambda e, hf=hf: e.tensor_tensor(xt[b][:, hf * 512:(hf + 1) * 512],
                                                           xt[b][:, hf * 512:(hf + 1) * 512], bank(5 + hf), ALU.add),
                          r=[xtn, PS(5 + hf)], w=[xtn])
                        S_.unit()
                    ld(out_d[tok, :], xt[b][:, :], r=[xtn], w=["outd%d" % t])

                def cap_(fn, t):
                    S_.begin()
                    fn(t)
                    return S_.end()

                for step in range(NT + 2):
                    streams = []
                    if step - 2 >= 0:
                        streams.append(cap_(stageAT, step - 2))
                    if 0 <= step - 1 < NT:
                        streams.append(cap_(stageBI, step - 1))
                    if step < NT:
                        streams.append(cap_(stagePS, step))
                    S_.run_merged(streams)
                S_.flush()
            if _STOP <= 2:
                return nc

            pl.__exit__(None, None, None)
            with ExitStack() as p3:
                w1S = sb(p3, "w1S", [128, 8, DFF], BF16)
                w2S = sb(p3, "w2S", [128, 32, D], BF16)
                g2bc = sb(p3, "g2bc", [128, D], F32)
                dtmp = sb(p3, "dtmp3", [128, 128], F32)
                xtN = [sb(p3, "x3n%d" % i, [128, D], F32) for i in range(2)]
                xr = [sb(p3, "x3r%d" % i, [128, D], F32) for i in range(2)]
                xn = sb(p3, "xn3", [128, D], F32)
                ssq = sb(p3, "ssq3", [128, 4], F32)
                ssf = sb(p3, "ssf3", [128, 4], F32)
                hTb = sb(p3, "hTb", [128, 8, 512], BF16)
                h1raw = sb(p3, "h1raw", [128, 8192], F32)
                h1T = h1raw[:, :].bitcast(BF16).rearrange("p (f t) -> p f t", t=512)
                wst = [h1raw[:, 0:1024], h1raw[:, 1024:2048]]
                wst_b = [h1raw[:, 2048:3072], h1raw[:, 3072:4096]]
                rl = [sb(p3, "rl%d" % i, [128, 512], F32) for i in range(2)]
                last = (l == L - 1)
                NBLK = S // 512
                stB = {"ri": 0, "ni": 0, "xi": 0}

                def stageN(blk):
                    for i in range(4):
                        t = blk * 4 + i
                        j = stB["ni"] % 2
                        stB["ni"] += 1
                        ld(xtN[j][:, :], out_d[t * 128:(t + 1) * 128, :], r=["outd%d" % t], w=["x3n%d" % j])
                        norm_tile(xtN[j][:, :], "x3n%d" % j, xn, ssq, hTb, "hTb", G2T, l, 3, (0, 1),
                                  ncols=128, coff=i * 128)
                        S_.unit()

                w1i = 0
                for cb in range(4):
                    for k in range(8):
                        stg, stn = wst_b[w1i % 2], "w1st%d" % (w1i % 2)
                        w1i += 1
                        ld(stg, w1_d[l, k * 128:(k + 1) * 128, cb * 1024:(cb + 1) * 1024], w=[stn])
                        act(w1S[:, k, cb * 1024:(cb + 1) * 1024], stg, AF.Copy, r=[stn], w=["w1S_%d" % cb])
                stageN(0)
                build_bc(g2bc, "g2bc", l, 5, dtmp, "dtmp3", (0, 1))
                for k in range(32):
                    ld(wst[k % 2], w2_d[l, k * 128:(k + 1) * 128, :], w=["w2st%d" % (k % 2)])
                    eng = V if k % 2 == 0 else G
                    eng(lambda e, k=k: e.tensor_tensor(w2S[:, k, :], wst[k % 2], g2bc[:, :], ALU.mult),
                        r=["w2st%d" % (k % 2), "g2bc"], w=["w2S_%d" % k])
                if last:
                    ld(g2bc[:, :], fnw_d[0:1, :].partition_broadcast(128), r=["w2S_%d" % k for k in range(32)], w=["g2bc"])
                def stageM1(blk):
                    for f in range(32):
                        pb = 2 + (f % 2)
                        for k in range(8):
                            mm(bank(pb), w1S[:, k, f * 128:(f + 1) * 128], hTb[:, k, :], k == 0, k == 7,
                               r=["w1S_%d" % (f // 8), "hTb"], w=[PS(pb)])
                        r_ = rl[stB["ri"] % 2]
                        rn = "rl%d" % (stB["ri"] % 2)
                        stB["ri"] += 1
                        act(r_[:, :], bank(pb), AF.Relu, r=[PS(pb)], w=[rn])
                        G(lambda e, r_=r_, f=f: e.tensor_tensor(h1T[:, f, :], r_[:, :], r_[:, :], ALU.mult),
                          r=[rn], w=["h1T", "w2st0", "w2st1", "w1st0", "w1st1"])

                def stageM2(blk):
                    for i in range(4):
                        t = blk * 4 + i
                        j = stB["xi"] % 2
                        stB["xi"] += 1
                        xb, xbn = xr[j], "x3r%d" % j
                        ld(xb[:, :], out_d[t * 128:(t + 1) * 128, :], r=["outd%d" % t], w=[xbn])
                        for hf in range(2):
                            pb = 4 + 2 * (i % 2) + hf
                            for f in range(32):
                                mm(bank(pb), h1T[:, f, i * 128:(i + 1) * 128], w2S[:, f, hf * 512:(hf + 1) * 512],
                                   f == 0, f == 31, r=["h1T", "w2S_%d" % f], w=[PS(pb)])
                                if f % 8 == 7:
                                    S_.unit()
                            V(lambda e, xb=xb, hf=hf, pb=pb: e.tensor_tensor(xb[:, hf * 512:(hf + 1) * 512],
                                                                             xb[:, hf * 512:(hf + 1) * 512], bank(pb), ALU.add),
                              r=[xbn, PS(pb)], w=[xbn])
                        if last:
                            act(xn[:, :], xb[:, :], AF.Square, r=[xbn], w=["xn", "ssf"], accum_out=ssf[:, 0:1])
                            V(lambda e: e.tensor_scalar(ssf[:, 1:2], ssf[:, 0:1], 1.0 / D, EPS, ALU.mult, ALU.add),
                              r=["ssf"], w=["ssf1"])
                            act(ssf[:, 2:3], ssf[:, 1:2], AF.Ln, r=["ssf1"], w=["ssf2"])
                            act(ssf[:, 3:4], ssf[:, 2:3], AF.Exp, r=["ssf2"], w=["ssf3"], scale=-0.5)
                            V(lambda e, xb=xb: e.scalar_tensor_tensor(xb[:, :], xb[:, :], ssf[:, 3:4], g2bc[:, :],
                                                                      ALU.mult, ALU.mult),
                              r=[xbn, "ssf3", "g2bc"], w=[xbn])
                        ld(out_d[t * 128:(t + 1) * 128, :], xb[:, :], r=[xbn], w=["outd%d" % t])
                        S_.unit()

                def capB(fn, blk):
                    S_.begin()
                    fn(blk)
                    return S_.end()

                for blk in range(NBLK):
                    stageM1(blk)
                    streams = [capB(stageM2, blk)]
                    if blk + 1 < NBLK:
                        streams.append(capB(stageN, blk + 1))
                    S_.run_merged(streams)
                S_.flush()
    return nc


def _colsT(v, L, n):
    v = np.asarray(v, np.float32).reshape(L, n, 128)
    return np.ascontiguousarray(v.transpose(2, 0, 1).reshape(128, L * n))


def make_in_maps(inp, L, nb):
    f = lambda a: np.ascontiguousarray(np.asarray(a, np.float32))
    shared = {
        "ada_w": f(inp["ada_w"]),
        "ada_bT": _colsT(inp["ada_b"], L, 48),
        "nmixT": _colsT(inp["norm_mix_w"], L, 8),
        "nmlpT": _colsT(inp["norm_mlp_w"], L, 8),
        "fnw": f(inp["final_norm_w"]).reshape(1, D),
        "w_in": f(inp["w_in"]), "w_out": f(inp["w_out"]), "w1": f(inp["mlp_w1"]), "w2": f(inp["mlp_w2"]),
        "lbT": _colsT(inp["hg_lb_logits"], L, 4),
        "onwT": _colsT(inp["hg_onorm_w"], L, 4),
        "cvw": np.ascontiguousarray(np.asarray(inp["cv_w"], np.float32).reshape(L, 31, 2, 128)
                                    .transpose(3, 0, 2, 1).reshape(128, L * 62)),
        "cvb": _colsT(inp["cv_b"], L, 2),
        "lnw": _colsT(inp["cv_ln_w"], L, 2),
        "lnb": _colsT(inp["cv_ln_b"], L, 2),
    }
    maps = []
    x = np.asarray(inp["x"], np.float32)
    c = np.asarray(inp["c"], np.float32)
    for b in range(nb):
        m = dict(shared)
        m["x"] = np.ascontiguousarray(x[b])
        m["cT"] = np.ascontiguousarray(c[b].reshape(8, 128).T)
        maps.append(m)
    return maps


_NC_CACHE = {}


def kernel(x, c, ada_w, ada_b, norm_mix_w, norm_mlp_w, w_in, hg_lb_logits, hg_onorm_w,
           cv_w, cv_b, cv_ln_w, cv_ln_b, w_out, mlp_w1, mlp_w2, final_norm_w):
    inp = dict(x=x, c=c, ada_w=ada_w, ada_b=ada_b, norm_mix_w=norm_mix_w, norm_mlp_w=norm_mlp_w, w_in=w_in,
               hg_lb_logits=hg_lb_logits, hg_onorm_w=hg_onorm_w, cv_w=cv_w, cv_b=cv_b, cv_ln_w=cv_ln_w,
               cv_ln_b=cv_ln_b, w_out=w_out, mlp_w1=mlp_w1, mlp_w2=mlp_w2, final_norm_w=final_norm_w)
    B, S, _ = np.asarray(x).shape
    L = np.asarray(w_in).shape[0]
    topk = min(256, S // 4)
    key = (S, L, topk)
    if key not in _NC_CACHE:
        _NC_CACHE[key] = build_nc(S, L, topk)
    nc = _NC_CACHE[key]
    maps = make_in_maps(inp, L, B)
    res = run_bass_kernel_spmd(nc, maps, core_ids=list(range(B)))
    return np.stack([np.asarray(r["out"], np.float32) for r in res.results], axis=0)
```

```python
import numpy as np
from contextlib import ExitStack
import concourse.bass as bass
import concourse.mybir as mybir
from concourse.bass_utils import run_bass_kernel_spmd

F32 = mybir.dt.float32
BF16 = mybir.dt.bfloat16
U8 = mybir.dt.uint8
AF = mybir.ActivationFunctionType
ALU = mybir.AluOpType

D = 1024
DIN = 3624
DFF = 4096
EPS = 1e-6
NIT = 16
ENGS = ("tensor", "vector", "scalar", "gpsimd", "sync")
NDMA_SEMS = 24
import os as _os
_STOP = int(_os.environ.get('KSTOP', '99'))
_CUT = int(_os.environ.get('KCUT', '99'))
_SUB = int(_os.environ.get('KSUB', '99'))


class _Op:
    __slots__ = ("eng", "fn", "deps", "signals", "sem", "val", "is_dma")

    def __init__(self, eng, fn, is_dma=False):
        self.eng = eng
        self.fn = fn
        self.deps = []
        self.signals = False
        self.sem = None
        self.val = 0
        self.is_dma = is_dma


class _Slot:
    __slots__ = ("writer", "readers")

    def __init__(self):
        self.writer = None
        self.readers = []


class Sched:
    def __init__(self, nc, es):
        self.nc = nc
        self.q = {e: [] for e in ENGS}
        self.slots = {}
        self.phase_dmas = []
        self.esem = {e: es.enter_context(nc.semaphore("es_" + e)) for e in ENGS}
        self.dsem = {e: [es.enter_context(nc.semaphore("ds_%s_%d" % (e, i))) for i in range(NDMA_SEMS)]
                     for e in ("sync", "gpsimd")}
        self.cnt = {e: 0 for e in ENGS}
        self.dcnt = {e: [0] * NDMA_SEMS for e in self.dsem}
        self.drr = {e: 0 for e in self.dsem}
        self.dprev = {e: [None] * NDMA_SEMS for e in self.dsem}
        self.waited = {e: {} for e in ENGS}
        self.nops = 0

    def _slot(self, k):
        s = self.slots.get(k)
        if s is None:
            s = self.slots[k] = _Slot()
        return s

    def _add(self, op, reads, writes):
        deps = set()
        for k in reads:
            s = self._slot(k)
            if s.writer is not None:
                deps.add(s.writer)
            if k.startswith("ps"):
                for r in s.readers:
                    if r.eng != op.eng:
                        deps.add(r)
        for k in writes:
            s = self._slot(k)
            if s.writer is not None:
                deps.add(s.writer)
            for r in s.readers:
                deps.add(r)
        deps.discard(op)
        for d in deps:
            if d.eng == "tensor" and op.eng == "tensor":
                continue
            op.deps.append(d)
            d.signals = True
        for k in writes:
            s = self._slot(k)
            s.writer = op
            s.readers = []
        for k in reads:
            if k not in writes:
                self._slot(k).readers.append(op)
        self.q[op.eng].append(op)
        self.nops += 1
        return op

    cap = None

    fine = False

    def begin(self, fine=False):
        self.cap = [[]]
        self.fine = fine

    def unit(self):
        if self.cap is not None and self.cap[-1]:
            self.cap.append([])

    def end(self):
        u = [x for x in self.cap if x]
        self.cap = None
        return u

    def run_merged(self, streams):
        pos = [0] * len(streams)
        while True:
            best, bf = -1, 2.0
            for i, s in enumerate(streams):
                if pos[i] < len(s):
                    f = (pos[i] + 0.5) / len(s)
                    if f < bf:
                        best, bf = i, f
            if best < 0:
                break
            for item in streams[best][pos[best]]:
                if item[0] == "op":
                    self.op(*item[1:])
                else:
                    self.dma(item[1], item[2], item[3], item[4], item[5], **item[6])
            pos[best] += 1

    def op(self, eng, fn, reads=(), writes=()):
        if self.cap is not None:
            self.cap[-1].append(("op", eng, fn, tuple(reads), tuple(writes)))
            if self.fine and eng != "tensor":
                self.cap.append([])
            return None
        return self._add(_Op(eng, fn), list(reads), list(writes))

    def dma(self, eng, out, in_, reads=(), writes=(), **kw):
        if self.cap is not None:
            self.cap[-1].append(("dma", eng, out, in_, tuple(reads), tuple(writes), kw))
            return None
        fn = lambda e: e.dma_start(out=out, in_=in_, **kw)
        op = _Op(eng, fn, is_dma=True)
        op.signals = True
        self._add(op, list(reads), list(writes))
        self.phase_dmas.append(op)
        return op

    def flush(self):
        nc = self.nc
        fin = _Op("sync", None)
        fin.deps = list(self.phase_dmas)
        self.phase_dmas = []
        self.q["sync"].append(fin)
        for e in ENGS:
            for op in self.q[e]:
                if op.fn is None:
                    continue
                if op.is_dma:
                    j = self.drr[e]
                    self.drr[e] = (j + 1) % NDMA_SEMS
                    self.dcnt[e][j] += 16
                    op.sem = self.dsem[e][j]
                    op.val = self.dcnt[e][j]
                    prev = self.dprev[e][j]
                    if prev is not None:
                        op.deps.append(prev)
                    self.dprev[e][j] = op
                elif op.signals:
                    self.cnt[e] += 1
                    op.sem = self.esem[e]
                    op.val = self.cnt[e]
        q = self.q
        waited_all = self.waited

        def run(e):
            def body(eng):
                waited = waited_all[e]
                for op in q[e]:
                    for d in op.deps:
                        if d.sem is None:
                            continue
                        key = id(d.sem)
                        if waited.get(key, 0) < d.val:
                            eng.wait_ge(d.sem, d.val)
                            waited[key] = d.val
                    if op.fn is None:
                        continue
                    ins = op.fn(eng)
                    if op.signals:
                        ins.then_inc(op.sem, 16 if op.is_dma else 1)
            return body

        with nc.Block() as block:
            if q["tensor"]:
                block.tensor(run("tensor"))
            if q["vector"]:
                block.vector(run("vector"))
            if q["scalar"]:
                block.scalar(run("scalar"))
            if q["gpsimd"]:
                block.gpsimd(run("gpsimd"))
            block.sync(run("sync"))
        self.q = {e: [] for e in ENGS}


def build_nc(S, L, TOPK, dbg=False):
    NT = S // 128
    nc = bass.Bass("TRN2", target_bir_lowering=False)

    def din(name, shape, dt=F32):
        return nc.dram_tensor(name, list(shape), dt, kind="ExternalInput").ap()

    x_d = din("x", [S, D])
    cT_d = din("cT", [128, 8])
    adaw_d = din("ada_w", [L, D, 6 * D])
    adabT_d = din("ada_bT", [128, L * 48])
    nmixT_d = din("nmixT", [128, L * 8])
    nmlpT_d = din("nmlpT", [128, L * 8])
    fnw_d = din("fnw", [1, D])
    win_d = din("w_in", [L, D, DIN])
    wout_d = din("w_out", [L, D, D])
    w1_d = din("w1", [L, D, DFF])
    w2_d = din("w2", [L, DFF, D])
    lbT_d = din("lbT", [128, L * 4])
    onwT_d = din("onwT", [128, L * 4])
    cvw_d = din("cvw", [128, L * 62])
    cvb_d = din("cvb", [128, L * 2])
    lnw_d = din("lnw", [128, L * 2])
    lnb_d = din("lnb", [128, L * 2])
    out_d = nc.dram_tensor("out", [S, D], F32, kind="ExternalOutput").ap()
    hT_d = nc.dram_tensor("hT_scr", [NT, 128, 1024], BF16, kind="Internal").ap()
    cat_d = nc.dram_tensor("cat_scr", [NT, 128, 768], BF16, kind="Internal").ap()

    with ExitStack() as es:
        S_ = Sched(nc, es)

        def V(fn, r=(), w=()):
            return S_.op("vector", fn, r, w)

        def A(fn, r=(), w=()):
            return S_.op("scalar", fn, r, w)

        def G(fn, r=(), w=()):
            return S_.op("gpsimd", fn, r, w)

        def T(fn, r=(), w=()):
            return S_.op("tensor", fn, r, w)

        _fillregs = {}

        def fillreg(e, val):
            if val not in _fillregs:
                _fillregs[val] = e.to_reg(val)
            return _fillregs[val]

        def mm(out, lhsT, rhs, start, stop, r, w):
            return T(lambda e: e.matmul(out, lhsT, rhs, start=start, stop=stop, skip_group_check=True), r, w)

        def tr(out, in_, ident, r, w):
            return T(lambda e: e.transpose(out, in_, ident), r, w)

        def act(out, in_, func, r, w, **kw):
            return A(lambda e: e.activation(out, in_, func, **kw), r, w)

        def ld(out, in_, w, r=(), **kw):
            return S_.dma("sync", out, in_, reads=r, writes=w, **kw)

        def ldc(out, in_, w, r=()):
            return S_.dma("gpsimd", out, in_, reads=r, writes=w, max_dma_last_dim=2048)

        _uid = [0]

        def sb(stack, name, shape, dt):
            _uid[0] += 1
            return stack.enter_context(nc.sbuf_tensor("s%d_%s" % (_uid[0], name), list(shape), dt))

        pst = [es.enter_context(nc.psum_tensor("pst%d" % i, [128, 1024], F32)) for i in range(4)]

        def bank(i):
            return pst[i // 2][:, (i % 2) * 512:(i % 2 + 1) * 512]

        def bankb(i):
            return bank(i).bitcast(BF16)

        PS = lambda i: "ps%d" % i

        identF = sb(es, "identF", [128, 128], F32)
        identB = sb(es, "identB", [128, 128], BF16)
        onesM = sb(es, "onesM", [128, 128], F32)
        cT = sb(es, "cTs", [128, 8], F32)
        cact = sb(es, "cact", [128, 8], F32)
        modT = sb(es, "modT", [128, L * 48], F32)
        adabT = sb(es, "adabT", [128, L * 48], F32)
        nmixT = sb(es, "nmixT", [128, L * 8], F32)
        nmlpT = sb(es, "nmlpT", [128, L * 8], F32)
        G1T = sb(es, "G1T", [128, L * 8], F32)
        G2T = sb(es, "G2T", [128, L * 8], F32)
        lbT = sb(es, "lbT", [128, L * 4], F32)
        omlT = sb(es, "omlT", [128, L * 4], F32)
        lbtmp = sb(es, "lbtmp", [128, 8], F32)
        onwT = sb(es, "onwT", [128, L * 4], F32)
        cvw = sb(es, "cvw", [128, L * 62], F32)
        cvb = sb(es, "cvb", [128, L * 2], F32)
        lnw = sb(es, "lnw", [128, L * 2], F32)
        lnb = sb(es, "lnb", [128, L * 2], F32)
        tabA0 = sb(es, "tabA0", [128, NIT], F32)
        tabB0 = sb(es, "tabB0", [128, NIT], F32)
        thrneg = sb(es, "thrneg", [128, 1], F32)
        evt = sb(es, "evt", [128, 512], F32)
        negh = sb(es, "negh", [128, 128], F32)
        omlh = sb(es, "omlh", [128, L * 4], F32)
        lbp = sb(es, "lbp", [128, L * 4], F32)
        onwh = sb(es, "onwh", [128, L * 4], F32)
        lnwh = sb(es, "lnwh", [128, L * 2], F32)
        lnbh = sb(es, "lnbh", [128, L * 2], F32)

        def modcol(l, j, k):
            c = l * 48 + j * 8 + k
            return modT[:, c:c + 1]

        with ExitStack() as ps_:
            stage = [sb(ps_, "adast%d" % i, [128, 8, 512], F32) for i in range(2)]
            rowS = [sb(ps_, "rowS%d" % i, [1, 512], F32) for i in range(2)]
            one1f = sb(ps_, "one1f", [1, 1], F32)
            G(lambda e: e.memset(one1f[:, :], 1.0), w=["one1f"])
            G(lambda e: e.memset(identF[:, :], 1.0), w=["identF"])
            G(lambda e: e.affine_select(identF[:, :], identF[:, :], [[-1, 128]], ALU.is_equal, fillreg(e, 0.0),
                                        base=0, channel_multiplier=1), r=["identF"], w=["identF"])
            V(lambda e: e.tensor_copy(identB[:, :], identF[:, :]), r=["identF"], w=["identB"])
            G(lambda e: e.memset(onesM[:, :], 1.0 / 256.0), w=["onesM"])
            G(lambda e: e.memset(thrneg[:, :], -1e29), w=["thrneg"])
            for n in range(NIT):
                a_n = 2.0 ** -(n + 2) if n < NIT - 1 else 2.0 ** -(NIT)
                b_n = 2.0 ** -(n + 1)
                G(lambda e, n=n, a_n=a_n: e.memset(tabA0[:, n:n + 1], a_n), w=["tabA0"])
                G(lambda e, n=n, b_n=b_n: e.memset(tabB0[:, n:n + 1], b_n), w=["tabB0"])
            for (dst, src, nm) in ((cT, cT_d, "cT"), (adabT, adabT_d, "adabT"), (nmixT, nmixT_d, "nmixT"),
                                   (nmlpT, nmlpT_d, "nmlpT"), (lbT, lbT_d, "lbT"), (onwT, onwT_d, "onwT"),
                                   (cvw, cvw_d, "cvw"), (cvb, cvb_d, "cvb"), (lnw, lnw_d, "lnw"),
                                   (lnb, lnb_d, "lnb")):
                ld(dst[:, :], src[:, :], w=[nm])
            act(cact[:, :], cT[:, :], AF.Silu, r=["cT"], w=["cact"])
            lb3 = lbT[:, :].rearrange("p (l h) -> p l h", h=4)
            act(lbT[:, :], lbT[:, :], AF.Exp, r=["lbT"], w=["lbT"])
            V(lambda e: e.tensor_copy(lbtmp[:, 0:4], lb3[:, 0, :]), r=["lbT"], w=["lbtmp"])
            for l in range(1, L):
                V(lambda e, l=l: e.tensor_tensor(lbtmp[:, 0:4], lbtmp[:, 0:4], lb3[:, l, :], ALU.add),
                  r=["lbT", "lbtmp"], w=["lbtmp"])
            V(lambda e: e.reciprocal(lbtmp[:, 4:8], lbtmp[:, 0:4]), r=["lbtmp"], w=["lbtmp2"])
            for l in range(L):
                V(lambda e, l=l: e.tensor_tensor(lb3[:, l, :], lb3[:, l, :], lbtmp[:, 4:8], ALU.mult),
                  r=["lbT", "lbtmp2"], w=["lbT"])
            V(lambda e: e.memset(lb3[:, 0, :], 0.0), r=["lbT"], w=["lbT"])
            for l in range(2, L):
                V(lambda e, l=l: e.tensor_tensor(lb3[:, l, :], lb3[:, l, :], lb3[:, l - 1, :], ALU.add),
                  r=["lbT"], w=["lbT"])
            V(lambda e: e.tensor_scalar(omlT[:, :], lbT[:, :], -1.0, 1.0, ALU.mult, ALU.add),
              r=["lbT"], w=["omlT"])
            V(lambda e: e.tensor_scalar(omlh[:, :], omlT[:, :], 0.5, None, ALU.mult), r=["omlT"], w=["omlh"])
            V(lambda e: e.tensor_tensor(lbp[:, :], lbT[:, :], omlh[:, :], ALU.add), r=["lbT", "omlh"], w=["lbp"])
            V(lambda e: e.tensor_scalar(onwh[:, :], onwT[:, :], 0.5, None, ALU.mult), r=["onwT"], w=["onwh"])
            V(lambda e: e.tensor_scalar(lnwh[:, :], lnw[:, :], 0.5, None, ALU.mult), r=["lnw"], w=["lnwh"])
            V(lambda e: e.tensor_scalar(lnbh[:, :], lnb[:, :], 0.5, None, ALU.mult), r=["lnb"], w=["lnbh"])
            G(lambda e: e.memset(negh[:, :], -0.5), w=["negh"])
            piece = 0
            for l in range(L):
                for pc in range(12):
                    st = stage[piece % 2]
                    sn = "adast%d" % (piece % 2)
                    ld(st[:, :, :], adaw_d[l, :, pc * 512:(pc + 1) * 512].rearrange("(k p) n -> p k n", p=128),
                       w=[sn])
                    rb = 2 + (piece % 2)
                    for k in range(8):
                        mm(bank(rb)[0:1, :], cact[:, k:k + 1], st[:, k, :], k == 0, k == 7, r=[sn, "cact"], w=[PS(rb)])
                    rw = rowS[piece % 2]
                    rwn = "rowS%d" % (piece % 2)
                    act(rw[0:1, :], bank(rb)[0:1, :], AF.Copy, r=[PS(rb)], w=[rwn])
                    for f in range(4):
                        col = l * 48 + pc * 4 + f
                        mm(bank(0)[:, col:col + 1], rw[0:1, f * 128:(f + 1) * 128], one1f[0:1, 0:1],
                           piece == 0 and f == 0, True, r=[rwn, "one1f"], w=[PS(0)])
                    piece += 1
            V(lambda e: e.tensor_tensor(modT[:, :], bank(0)[:, 0:L * 48], adabT[:, :], ALU.add),
              r=[PS(0), "adabT"], w=["modT"])
            m4 = modT[:, :].rearrange("p (l j k) -> p l j k", j=6, k=8)
            V(lambda e: e.scalar_tensor_tensor(G1T[:, :].rearrange("p (l k) -> p l k", k=8), m4[:, :, 1, :], 1.0,
                                               nmixT[:, :].rearrange("p (l k) -> p l k", k=8), ALU.add, ALU.mult),
              r=["modT", "nmixT"], w=["G1T"])
            V(lambda e: e.scalar_tensor_tensor(G2T[:, :].rearrange("p (l k) -> p l k", k=8), m4[:, :, 4, :], 1.0,
                                               nmlpT[:, :].rearrange("p (l k) -> p l k", k=8), ALU.add, ALU.mult),
              r=["modT", "nmlpT"], w=["G2T"])
            S_.flush()
        if _STOP <= 0:
            return nc

        def build_bc(bc, bcname, l, j, dtmp, dname, pbanks):
            for k in range(8):
                V(lambda e, k=k: e.tensor_scalar(dtmp[:, :], identF[:, :], modcol(l, j, k), None, ALU.mult),
                  r=["identF", "modT", dname], w=[dname])
                bk = pbanks[k // 4]
                mm(bank(bk)[:, (k % 4) * 128:(k % 4 + 1) * 128], onesM[:, :], dtmp[:, :], True, True,
                   r=["onesM", dname], w=[PS(bk)])
            for hh in range(2):
                A(lambda e, hh=hh: e.activation(bc[:, hh * 512:(hh + 1) * 512], bank(pbanks[hh]), AF.Copy, scale=256.0),
                  r=[PS(pbanks[hh])], w=[bcname])

        def norm_tile(xt_ap, xtname, xn, ssq, hT_out, hTname, GT_, l, jsh, pb, ncols=128, coff=0):
            act(xn[:, :], xt_ap, AF.Square, r=[xtname], w=["xn", "ssq"], accum_out=ssq[:, 0:1])
            V(lambda e: e.tensor_scalar(ssq[:, 1:2], ssq[:, 0:1], 1.0 / D, EPS, ALU.mult, ALU.add),
              r=["ssq"], w=["ssq1"])
            act(ssq[:, 2:3], ssq[:, 1:2], AF.Ln, r=["ssq1"], w=["ssq2"])
            act(ssq[:, 3:4], ssq[:, 2:3], AF.Exp, r=["ssq2"], w=["ssq3"], scale=-0.5)
            V(lambda e: e.tensor_scalar(xn[:, :], xt_ap, ssq[:, 3:4], None, ALU.mult),
              r=[xtname, "ssq3", "xn"], w=["xn"])
            for half in range(2):
                bk = pb[half]
                for k in range(4 * half, 4 * half + 4):
                    tr(bank(bk)[:, (k % 4) * 128:(k % 4 + 1) * 128], xn[:, k * 128:(k + 1) * 128], identF[:, :],
                       r=["xn", "identF"], w=[PS(bk)])
                k0 = 4 * half
                g_b = GT_[:, l * 8 + k0:l * 8 + k0 + 4].rearrange("p (k o) -> p k o", o=1).broadcast_to([128, 4, 128])
                c0 = l * 48 + jsh * 8 + k0
                s_b = modT[:, c0:c0 + 4].rearrange("p (k o) -> p k o", o=1).broadcast_to([128, 4, 128])
                V(lambda e, bk=bk, g_b=g_b: e.tensor_tensor(evt[:, :].rearrange("p (k t) -> p k t", t=128),
                                                            bank(bk).rearrange("p (k t) -> p k t", t=128), g_b, ALU.mult),
                  r=[PS(bk), "G1T", "G2T"], w=["evt"])
                V(lambda e, k0=k0, s_b=s_b: e.tensor_tensor(hT_out[:, k0:k0 + 4, coff:coff + ncols],
                                                            evt[:, :].rearrange("p (k t) -> p k t", t=128), s_b, ALU.add),
                  r=["evt", "modT"], w=[hTname])

        for l in range(L):
            src_d = x_d if l == 0 else out_d
            pl = ExitStack()
            pl.__enter__()
            WB = 1128
            wB = sb(pl, "wB", [128, 8, WB], BF16)
            woS = sb(pl, "woS", [128, 8, D], BF16)
            with ExitStack() as p1:
                WA = 2560
                wA = sb(p1, "wA", [128, 8, WA], BF16)
                dg = sb(p1, "dg", [128, 2, 31, 128], BF16)
                xt = [sb(p1, "xt%d" % i, [128, D], F32) for i in range(2)]
                xn = sb(p1, "xn", [128, D], F32)
                ssq = sb(p1, "ssq", [128, 4], F32)
                hT = [sb(p1, "hT%d" % i, [128, 8, 128], BF16) for i in range(2)]
                tht = [sb(p1, "tht%d" % i, [128, 512], F32) for i in range(2)]
                qs = [sb(p1, "qs%d" % i, [128, 512], F32) for i in range(2)]
                sg = [sb(p1, "sg%d" % i, [128, 512], F32) for i in range(2)]
                gs = [sb(p1, "gs%d" % i, [128, 512], F32) for i in range(3)]
                vtok = [sb(p1, "vtok%d" % i, [128, 512], BF16) for i in range(3)]
                hcur = [sb(p1, "hcur%d" % i, [128, 2, 128], BF16) for i in range(2)]
                kT = sb(p1, "kT", [128, 512], F32)
                fT = sb(p1, "fT", [128, 512], F32)
                GT = sb(p1, "GT", [128, 512], F32)
                t1 = sb(p1, "t1", [128, 512], F32)
                E = sb(p1, "E", [128, 512], F32)
                E4 = [sb(p1, "E4%d" % i, [128, 512], F32) for i in range(2)]
                qtl = [sb(p1, "qtl%d" % i, [128, 512], BF16) for i in range(2)]
                ktl = [sb(p1, "ktl%d" % i, [128, 512], BF16) for i in range(2)]
                khT = sb(p1, "khT", [128, 512], BF16)
                qhA = [sb(p1, "qhA%d" % i, [128, 512], BF16) for i in range(2)]
                qhB = [sb(p1, "qhB%d" % i, [128, 512], BF16) for i in range(2)]
                khtok = [sb(p1, "khtok%d" % i, [128, 512], BF16) for i in range(2)]
                ATm = sb(p1, "ATm", [128, 512], BF16)
                Sst = sb(p1, "Sst", [128, 512], F32)
                Sbf0 = sb(p1, "Sbf0", [128, 512], BF16)
                Sbf1 = sb(p1, "Sbf1", [128, 512], BF16)
                oa = sb(p1, "oa", [128, 512], F32)
                oab = sb(p1, "oab", [128, 512], BF16)
                ss4 = sb(p1, "ss4", [128, 16], F32)
                scanm = sb(p1, "scanm", [128, 512], F32)
                cmask = sb(p1, "cmask", [128, 512], U8)
                cmf = sb(p1, "cmf", [128, 512], F32)
                hbuf = sb(p1, "hbuf", [128, 2, 160], BF16)
                cc = [sb(p1, "cc%d" % i, [128, 256], F32) for i in range(2)]
                csq = sb(p1, "csq", [128, 256], F32)
                stt_ = [sb(p1, "stt%d" % i, [128, 256], F32) for i in range(2)]
                yv = sb(p1, "yv", [128, 256], F32)
                thy = sb(p1, "thy", [128, 256], F32)
                rs_b = [sb(p1, "rs_b%d" % i, [128, 128], F32) for i in range(2)]
                cat6 = [sb(p1, "cat6_%d" % i, [128, 6, 128], BF16) for i in range(2)]

                wst1 = [sb(p1, "wst%d" % i, [128, D], F32) for i in range(2)]
                wi = 0
                for (gn, d0, s0) in (("v", 1024, 1024), ("g", 1536, 1536), ("q", 0, 0), ("f", 512, 512), ("cu", 2048, 3112)):
                    for k2 in range(4):
                        stg, stn = wst1[wi % 2], "wst%d" % (wi % 2)
                        wi += 1
                        ld(stg[:, :].rearrange("p (k n) -> p k n", n=512),
                           win_d[l, k2 * 256:(k2 + 1) * 256, s0:s0 + 512].rearrange("(k p) n -> p k n", p=128), w=[stn])
                        act(wA[:, 2 * k2:2 * k2 + 2, d0:d0 + 512], stg[:, :].rearrange("p (k n) -> p k n", n=512), AF.Copy,
                            r=[stn], w=["wA_" + gn])
                g1bc = sb(p1, "g1bc", [128, D], F32)
                stg2 = sb(p1, "stg2", [128, 8, 40], F32)
                dtmp1 = sb(p1, "dtmp", [128, 128], F32)
                def prefetch_a2():
                    ld(stg2[:, :, :], win_d[l, :, 3072:3112].rearrange("(k p) n -> p k n", p=128), w=["stg2"])
                    for k in range(8):
                        rows = slice(k * 128, (k + 1) * 128)
                        stg, stn = wst1[k % 2], "wst%d" % (k % 2)
                        ld(stg[:, :], win_d[l, rows, 2048:3072], w=[stn])
                        act(wB[:, k, 0:1024], stg[:, :], AF.Copy, r=[stn], w=["wB"])
                    act(wB[:, :, 1024:1032], stg2[:, :, 32:40], AF.Copy, r=["stg2"], w=["wB"])
                    for r3 in range(3):
                        act(wB[:, :, 1032 + 32 * r3:1064 + 32 * r3], stg2[:, :, 0:32], AF.Copy, r=["stg2"], w=["wB"])
                    build_bc(g1bc, "g1bc", l, 2, dtmp1, "dtmp", (0, 1))
                    for k in range(8):
                        ld(wst1[k % 2][:, :], wout_d[l, k * 128:(k + 1) * 128, :], w=["wst%d" % (k % 2)])
                        V(lambda e, k=k: e.tensor_tensor(woS[:, k, :], wst1[k % 2][:, :], g1bc[:, :], ALU.mult),
                          r=["wst%d" % (k % 2), "g1bc"], w=["woS"])
                for ct in range(2):
                    for j in range(31):
                        c = l * 62 + ct * 31 + j
                        V(lambda e, ct=ct, j=j, c=c: e.tensor_scalar(dg[:, ct, j, :], identF[:, :], cvw[:, c:c + 1],
                                                                     None, ALU.mult),
                          r=["identF", "cvw"], w=["dg"])
                G(lambda e: e.memset(scanm[:, :], 1.0), w=["scanm"])
                sm3 = scanm[:, :].rearrange("p (c j) -> p c j", j=64)
                G(lambda e: e.memset(sm3[:, :, 0:1], 0.0), r=["scanm"], w=["scanm"])
                G(lambda e: e.memset(cmf[:, :], 1.0), w=["cmf"])
                for h in range(4):
                    G(lambda e, h=h: e.affine_select(cmf[:, h * 128:(h + 1) * 128], cmf[:, h * 128:(h + 1) * 128],
                                                     [[1, 128]], ALU.is_ge, fillreg(e, 0.0), base=0, channel_multiplier=-1),
                      r=["cmf"], w=["cmf"])
                    G(lambda e, h=h: e.memset(cmf[0:64, h * 128 + 64:(h + 1) * 128], 0.0), r=["cmf"], w=["cmf"])
                V(lambda e: e.tensor_copy(cmask[:, :], cmf[:, :]), r=["cmf"], w=["cmask"])
                for (tl, nm) in ((ATm, "ATm"), (qhA[0], "qhA0"), (qhA[1], "qhA1"), (qhB[0], "qhB0"), (qhB[1], "qhB1"),
                                 (Sst, "Sst"), (Sbf0, "Sbf0"), (Sbf1, "Sbf1")):
                    G(lambda e, tl=tl: e.memset(tl[:, :], 0.0), w=[nm])
                G(lambda e: e.memset(hbuf[:, :, :], 0.0), w=["hbuf"])

                bc4 = lambda tl_: tl_[:, l * 4:(l + 1) * 4].rearrange("p (h o) -> p h o", o=1).broadcast_to([128, 4, 128])
                lb_b, oml_b = bc4(lbT), bc4(omlT)
                v3 = lambda t_: t_[:, :].rearrange("p (h t) -> p h t", t=128)
                c3 = lambda t_: t_[:, :].rearrange("p (c j) -> p c j", j=64)
                c4 = lambda t_: t_[:, :].rearrange("p (h c j) -> p h c j", c=2, j=64)
                QSC = 128.0 ** -0.5
                st1 = {"th": 0}

                def nxt_th():
                    i = st1["th"] % 2
                    st1["th"] += 1
                    return tht[i], "tht%d" % i

                def sigm(dst, src_ap, r, w):
                    act(dst, src_ap, AF.Exp, r=r, w=w, scale=-1.0)
                    act(dst, dst, AF.Ln, r=w, w=w, bias=1.0)
                    act(dst, dst, AF.Exp, r=w, w=w, scale=-1.0)

                def stageX(t):
                    b = t % 2
                    b3 = t % 3
                    xtn, hTn = "xt%d" % b, "hT%d" % b
                    if t + 1 < NT:
                        ld(xt[1 - b][:, :], src_d[(t + 1) * 128:(t + 2) * 128, :], w=["xt%d" % (1 - b)])
                    norm_tile(xt[b][:, :], xtn, xn, ssq, hT[b], hTn, G1T, l, 0, (0, 0))
                    ld(hT_d[t, :, :], hT[b][:, :, :].rearrange("p k n -> p (k n)"), r=[hTn], w=["hTd%d" % t])
                    S_.unit()
                    for k in range(8):
                        mm(bank(1), hT[b][:, k, :], wA[:, k, 1024:1536], k == 0, k == 7, r=[hTn, "wA_v"], w=[PS(1)])
                    act(vtok[b3][:, :], bank(1), AF.Copy, r=[PS(1)], w=["vtok%d" % b3])
                    S_.unit()
                    for k in range(8):
                        mm(bank(2), hT[b][:, k, :], wA[:, k, 1536:2048], k == 0, k == 7, r=[hTn, "wA_g"], w=[PS(2)])
                    th_, thn_ = nxt_th()
                    sigm(th_[:, :], bank(2), [PS(2)], [thn_])
                    V(lambda e, th_=th_: e.tensor_tensor(gs[b3][:, :], th_[:, :], bank(2), ALU.mult),
                      r=[thn_, PS(2)], w=["gs%d" % b3])
                    S_.unit()
                    for f in range(4):
                        for k in range(8):
                            mm(bank(1)[:, f * 128:(f + 1) * 128], wA[:, k, f * 128:(f + 1) * 128], hT[b][:, k, :],
                               k == 0, k == 7, r=[hTn, "wA_q"], w=[PS(1)])
                    th_, thn_ = nxt_th()
                    sigm(th_[:, :], bank(1), [PS(1)], [thn_])
                    V(lambda e, th_=th_: e.tensor_tensor(qs[b][:, :], th_[:, :], bank(1), ALU.mult),
                      r=[thn_, PS(1)], w=["qs%d" % b])
                    S_.unit()
                    for f in range(4):
                        for k in range(8):
                            mm(bank(2)[:, f * 128:(f + 1) * 128], wA[:, k, 512 + f * 128:512 + (f + 1) * 128], hT[b][:, k, :],
                               k == 0, k == 7, r=[hTn, "wA_f"], w=[PS(2)])
                    sigm(sg[b][:, :], bank(2), [PS(2)], ["sg%d" % b])
                    S_.unit()
                    for f in range(4):
                        for k in range(8):
                            mm(bank(1)[:, f * 128:(f + 1) * 128], wA[:, k, 2048 + f * 128:2048 + (f + 1) * 128], hT[b][:, k, :],
                               k == 0, k == 7, r=[hTn, "wA_cu"], w=[PS(1)])
                    th_, thn_ = nxt_th()
                    sigm(th_[:, 0:256], bank(1)[:, 256:512], [PS(1)], [thn_])
                    V(lambda e, th_=th_: e.tensor_tensor(hcur[b][:, :, :].rearrange("p c t -> p (c t)"), th_[:, 0:256],
                                                         bank(1)[:, 0:256], ALU.mult),
                      r=[thn_, PS(1)], w=["hcur%d" % b])

                def stageY1(t):
                    b = t % 2
                    b3 = t % 3
                    qsn, sgn_, gsn, vtn, hcn, c6n = "qs%d" % b, "sg%d" % b, "gs%d" % b3, "vtok%d" % b3, "hcur%d" % b, "cat6_%d" % b
                    qs_, sg_, gs_, vt_, c6 = qs[b], sg[b], gs[b3], vtok[b3], cat6[b]
                    qtl_, ktl_, khtok_, qhA_, qhB_, E4_ = qtl[b], ktl[b], khtok[b], qhA[b], qhB[b], E4[b]
                    cc_, stt2, rsb_ = cc[b], stt_[b], rs_b[b]
                    qtln, ktln, khtokn, qhAn, qhBn, E4n = "qtl%d" % b, "ktl%d" % b, "khtok%d" % b, "qhA%d" % b, "qhB%d" % b, "E4%d" % b
                    ccn, sttn, rsbn = "cc%d" % b, "stt%d" % b, "rs_b%d" % b
                    G(lambda e: e.tensor_copy(hbuf[:, :, 32:160], hcur[b][:, :, :]), r=[hcn, "hbuf"], w=["hbuf"])
                    for ct in range(2):
                        for j in range(31):
                            mm(bank(3)[:, ct * 128:(ct + 1) * 128], dg[:, ct, j, :],
                               hbuf[:, ct, 2 + j:2 + j + 128], j == 0, j == 30, r=["dg", "hbuf"], w=[PS(3)])
                        S_.unit()
                    for ct in range(2):
                        cs = slice(ct * 128, (ct + 1) * 128)
                        act(cc_[:, cs], bank(3)[:, cs], AF.Identity,
                            r=[PS(3), "cvb"], w=[ccn], bias=cvb[:, l * 2 + ct:l * 2 + ct + 1])
                    act(csq[:, :], cc_[:, :], AF.Square, r=[ccn], w=["csq"])
                    G(lambda e: e.tensor_copy(hbuf[:, :, 0:32], hbuf[:, :, 128:160]), r=["hbuf", PS(3)], w=["hbuf"])
                    S_.unit()
                    for (si, src_, sn) in ((0, cc_, ccn), (1, csq, "csq")):
                        for ct in range(2):
                            mm(bank(4)[:, si * 128:(si + 1) * 128], onesM[:, :], src_[:, ct * 128:(ct + 1) * 128],
                               ct == 0, ct == 1, r=["onesM", sn], w=[PS(4)])
                    act(stt2[:, :], bank(4)[:, 0:256], AF.Copy, r=[PS(4)], w=[sttn])
                    V(lambda e: e.tensor_tensor(rsb_[:, :], stt2[:, 0:128], stt2[:, 0:128], ALU.mult), r=[sttn], w=[rsbn])
                    V(lambda e: e.tensor_tensor(rsb_[:, :], stt2[:, 128:256], rsb_[:, :], ALU.subtract),
                      r=[sttn, rsbn], w=[rsbn])
                    V(lambda e: e.tensor_scalar(rsb_[:, :], rsb_[:, :], EPS, None, ALU.add), r=[rsbn], w=[rsbn])
                    S_.unit()
                    V(lambda e: e.tensor_tensor(v3(t1), v3(sg_), oml_b, ALU.mult), r=[sgn_, "omlT"], w=["t1"])
                    V(lambda e: e.tensor_tensor(v3(fT), v3(t1), lb_b, ALU.add), r=["t1", "lbT"], w=["fT"])
                    V(lambda e: e.tensor_tensor(v3(kT), oml_b, v3(t1), ALU.subtract), r=["t1", "omlT"], w=["kT"])
                    V(lambda e: e.tensor_scalar(fT[:, :], fT[:, :], 1e-30, None, ALU.max), r=["fT"], w=["fT"])
                    act(fT[:, :], fT[:, :], AF.Ln, r=["fT"], w=["fT"])
                    act(rsb_[:, :], rsb_[:, :], AF.Ln, r=[rsbn], w=[rsbn])
                    act(rsb_[:, :], rsb_[:, :], AF.Exp, r=[rsbn], w=[rsbn], scale=-0.5)
                    S_.unit()
                    V(lambda e: e.tensor_tensor_scan(GT[:, :], scanm[:, :], fT[:, :], 0.0, ALU.mult, ALU.add),
                      r=["scanm", "fT"], w=["GT"])
                    V(lambda e: e.tensor_tensor(c3(t1), c3(GT), c3(GT)[:, :, 32:33].broadcast_to([128, 8, 64]),
                                                ALU.subtract), r=["GT"], w=["t1"])
                    act(E[:, :], t1[:, :], AF.Exp, r=["t1"], w=["E"])
                    V(lambda e: e.scalar_tensor_tensor(qtl_[:, :], qs_[:, :], QSC, E[:, :], ALU.mult, ALU.mult),
                      r=[qsn, "E"], w=[qtln])
                    S_.unit()
                    act(E[:, :], t1[:, :], AF.Exp, r=["t1", qtln], w=["E"], scale=-1.0)
                    V(lambda e: e.tensor_tensor(ktl_[:, :], kT[:, :], E[:, :], ALU.mult), r=["kT", "E"], w=[ktln])
                    V(lambda e: e.tensor_tensor(c3(t1), c3(GT)[:, :, 63:64].broadcast_to([128, 8, 64]), c3(GT),
                                                ALU.subtract), r=["GT", "E"], w=["t1"])
                    S_.unit()
                    act(E[:, :], t1[:, :], AF.Exp, r=["t1", ktln], w=["E"])
                    V(lambda e: e.tensor_tensor(khT[:, :], kT[:, :], E[:, :], ALU.mult), r=["kT", "E"], w=["khT"])
                    act(E4_[:, :], GT[:, :], AF.Exp, r=["GT"], w=[E4n])
                    S_.unit()
                    V(lambda e: e.scalar_tensor_tensor(c4(qhA_)[:, :, 0, :], c4(qs_)[:, :, 0, :], QSC, c4(E4_)[:, :, 0, :],
                                                       ALU.mult, ALU.mult), r=[qsn, E4n], w=[qhAn])
                    V(lambda e: e.scalar_tensor_tensor(c4(qhB_)[:, :, 1, :], c4(qs_)[:, :, 1, :], QSC, c4(E4_)[:, :, 1, :],
                                                       ALU.mult, ALU.mult), r=[qsn, E4n], w=[qhBn])
                    for h in range(4):
                        tr(bankb(4)[:, h * 128:(h + 1) * 128], khT[:, h * 128:(h + 1) * 128], identB[:, :],
                           r=["khT", "identB"], w=[PS(4)])
                    act(khtok_[:, :], bankb(4)[:, 0:512], AF.Copy, r=[PS(4)], w=[khtokn])
                    S_.unit()

                def stageY2(t):
                    b = t % 2
                    b3 = t % 3
                    qsn, sgn_, gsn, vtn, hcn, c6n = "qs%d" % b, "sg%d" % b, "gs%d" % b3, "vtok%d" % b3, "hcur%d" % b, "cat6_%d" % b
                    qs_, sg_, gs_, vt_, c6 = qs[b], sg[b], gs[b3], vtok[b3], cat6[b]
                    qtl_, ktl_, khtok_, qhA_, qhB_, E4_ = qtl[b], ktl[b], khtok[b], qhA[b], qhB[b], E4[b]
                    cc_, stt2, rsb_ = cc[b], stt_[b], rs_b[b]
                    qtln, ktln, khtokn, qhAn, qhBn, E4n = "qtl%d" % b, "ktl%d" % b, "khtok%d" % b, "qhA%d" % b, "qhB%d" % b, "E4%d" % b
                    ccn, sttn, rsbn = "cc%d" % b, "stt%d" % b, "rs_b%d" % b
                    for h in range(4):
                        mm(bank(5)[:, h * 128:(h + 1) * 128], ktl_[:, h * 128:(h + 1) * 128],
                           qtl_[:, h * 128:(h + 1) * 128], True, True, r=[ktln, qtln], w=[PS(5)])
                    V(lambda e: e.copy_predicated(ATm[:, :], cmask[:, :], bank(5)), r=[PS(5), "cmask"], w=["ATm"])
                    S_.unit()
                    for h in range(4):
                        hs = slice(h * 128, (h + 1) * 128)
                        mm(bank(6)[:, hs], ATm[:, hs], vt_[:, hs], h == 0, False, r=["ATm", vtn], w=[PS(6)])
                        mm(bank(6)[:, hs], qhA_[:, hs], Sbf0[:, hs], False, False, r=[qhAn, "Sbf0"], w=[PS(6)])
                    for h in range(4):
                        hs = slice(h * 128, (h + 1) * 128)
                        mm(bank(7)[:, hs], khtok_[0:64, hs], vt_[0:64, hs], True, True, r=[khtokn, vtn], w=[PS(7)])
                    dec = lambda c: c4(E4_)[:, :, c, 63:64].broadcast_to([128, 4, 128])
                    V(lambda e: e.tensor_tensor(v3(Sst), v3(Sst), dec(0), ALU.mult), r=["Sst", E4n], w=["Sst"])
                    V(lambda e: e.tensor_tensor(Sst[:, :], Sst[:, :], bank(7), ALU.add), r=["Sst", PS(7)], w=["Sst"])
                    act(Sbf1[:, :], Sst[:, :], AF.Copy, r=["Sst"], w=["Sbf1"])
                    S_.unit()
                    for h in range(4):
                        hs = slice(h * 128, (h + 1) * 128)
                        mm(bank(6)[:, hs], qhB_[:, hs], Sbf1[:, hs], False, True, r=[qhBn, "Sbf1"], w=[PS(6)])
                    for h in range(4):
                        hs = slice(h * 128, (h + 1) * 128)
                        mm(bank(7)[:, hs], khtok_[64:128, hs], vt_[64:128, hs], True, True, r=[khtokn, vtn], w=[PS(7)])
                    V(lambda e: e.tensor_tensor(v3(Sst), v3(Sst), dec(1), ALU.mult), r=["Sst", E4n], w=["Sst"])
                    V(lambda e: e.tensor_tensor(Sst[:, :], Sst[:, :], bank(7), ALU.add), r=["Sst", PS(7)], w=["Sst"])
                    act(Sbf0[:, :], Sst[:, :], AF.Copy, r=["Sst"], w=["Sbf0"])
                    S_.unit()
                    for h in range(4):
                        act(oa[:, h * 128:(h + 1) * 128], bank(6)[:, h * 128:(h + 1) * 128], AF.Square,
                            r=[PS(6)], w=["oa", "ss4"], accum_out=ss4[:, h:h + 1])
                    V(lambda e: e.tensor_scalar(ss4[:, 4:8], ss4[:, 0:4], 1.0 / 128, EPS, ALU.mult, ALU.add),
                      r=["ss4"], w=["ss4b"])
                    act(ss4[:, 8:12], ss4[:, 4:8], AF.Ln, r=["ss4b"], w=["ss4c"])
                    act(ss4[:, 12:16], ss4[:, 8:12], AF.Exp, r=["ss4c"], w=["ss4d"], scale=-0.5)
                    V(lambda e: e.tensor_tensor(v3(oa), v3(bank(6)),
                                                ss4[:, 12:16].rearrange("p (h o) -> p h o", o=1).broadcast_to([128, 4, 128]),
                                                ALU.mult), r=[PS(6), "ss4d", "oa"], w=["oa"])
                    V(lambda e: e.tensor_tensor(oab[:, :], oa[:, :], gs_[:, :], ALU.mult), r=["oa", gsn], w=["oab"])
                    S_.unit()
                    for h in range(4):
                        tr(bankb(5)[:, 512 + h * 128:512 + (h + 1) * 128], oab[:, h * 128:(h + 1) * 128], identB[:, :],
                           r=["oab", "identB"], w=[PS(5)])
                    for h in range(4):
                        act(c6[:, h, :], bankb(5)[:, 512 + h * 128:512 + (h + 1) * 128], AF.Copy,
                            r=[PS(5), "onwT"], w=[c6n], scale=onwT[:, l * 4 + h:l * 4 + h + 1])
                    S_.unit()
                    y3 = yv[:, :].rearrange("p (c t) -> p c t", t=128)
                    V(lambda e: e.tensor_tensor(y3, cc_[:, :].rearrange("p (c t) -> p c t", t=128),
                                                stt2[:, 0:128].rearrange("p (o t) -> p o t", o=1).broadcast_to([128, 2, 128]),
                                                ALU.subtract), r=[ccn, sttn], w=["yv"])
                    V(lambda e: e.tensor_tensor(y3, y3,
                                                rsb_[:, :].rearrange("p (o t) -> p o t", o=1).broadcast_to([128, 2, 128]),
                                                ALU.mult), r=["yv", rsbn], w=["yv"])
                    for ct in range(2):
                        cs = slice(ct * 128, (ct + 1) * 128)
                        V(lambda e, cs=cs, ct=ct: e.tensor_scalar(yv[:, cs], yv[:, cs], lnw[:, l * 2 + ct:l * 2 + ct + 1],
                                                                  lnb[:, l * 2 + ct:l * 2 + ct + 1], ALU.mult, ALU.add),
                          r=["yv", "lnw", "lnb"], w=["yv"])
                    sigm(thy[:, :], yv[:, :], ["yv"], ["thy"])
                    V(lambda e: e.tensor_tensor(c6[:, 4:6, :].rearrange("p c t -> p (c t)"), thy[:, :], yv[:, :], ALU.mult),
                      r=["thy", "yv"], w=[c6n])
                    ld(cat_d[t, :, :], c6[:, :, :].rearrange("p k n -> p (k n)"), r=[c6n], w=["catd%d" % t])

                def cap1(fn, t):
                    S_.begin(fine=True)
                    fn(t)
                    return S_.end()

                ld(xt[0][:, :], src_d[0:128, :], w=["xt0"])
                for step in range(NT + 2):
                    streams = []
                    if step - 2 >= 0:
                        streams.append(cap1(stageY2, step - 2))
                    if 0 <= step - 1 < NT:
                        streams.append(cap1(stageY1, step - 1))
                    if step < NT:
                        streams.append(cap1(stageX, step))
                    S_.run_merged(streams)
                    if step == 1:
                        prefetch_a2()
                S_.flush()
            if _STOP <= 1:
                return nc

            with ExitStack() as p2:
                KTc = sb(p2, "KTc", [128, 2, S], BF16)
                Vaug = sb(p2, "Vaug", [128, NT, 4, 65], BF16)
                kidx = sb(p2, "kidx", [128, S], BF16)
                score = [sb(p2, "score%d" % i, [128, S], F32) for i in range(3)]
                junkb = sb(p2, "junkb", [128, S], BF16)
                junk8 = sb(p2, "junk8", [128, S], U8)
                cntd = sb(p2, "cntd", [128, NIT], F32)
                vc = sb(p2, "vc", [128, NIT], F32)
                thrc = sb(p2, "thrc", [128, 4], F32)
                ones1 = sb(p2, "ones1", [128, 1], BF16)
                xt = [sb(p2, "x2t%d" % i, [128, D], F32) for i in range(3)]
                hT = [sb(p2, "h2T%d" % i, [128, 8, 128], BF16) for i in range(3)]
                catT = [sb(p2, "catT%d" % i, [128, 8, 128], BF16) for i in range(3)]
                sqT = [sb(p2, "sqT%d" % i, [128, 2, 256], BF16) for i in range(3)]
                identB2 = sb(p2, "identB2", [128, 256], BF16)
                iqT = sb(p2, "iqT", [128, 3, 128], BF16)
                wab = sb(p2, "wab", [128, 8], F32)
                wsg = sb(p2, "wsg", [128, 8], F32)
                Rb = [sb(p2, "Rb%d" % i, [128, 512], F32) for i in range(3)]
                Mb = [sb(p2, "Mb%d" % i, [128, 512], BF16) for i in range(3)]
                PT = [sb(p2, "PT%d" % i, [128, 512], BF16) for i in range(3)]
                ob = sb(p2, "ob", [128, 256], BF16)
                bis = sb(p2, "bis", [128, 16], F32)
                tabA = sb(p2, "tabA", [128, NIT], F32)
                tabB = sb(p2, "tabB", [128, NIT], F32)
                cntc = sb(p2, "cntc", [128, NIT], F32)
                uc = sb(p2, "uc", [128, NIT], F32)
                mid = sb(p2, "mid", [128, NIT + 1], F32)
                top8 = sb(p2, "top8", [128, 8], F32)
                rs4 = sb(p2, "rs4", [128, 4], F32)

                G(lambda e: e.memset(Vaug[:, :, :, 64:65], 1.0), w=["Vaug"])
                G(lambda e: e.memset(ones1[:, :], 1.0), w=["ones1"])
                for i in range(3):
                    G(lambda e, i=i: e.memset(sqT[i][:, :, :], 0.0), w=["sqT%d" % i])
                for c in range(2):
                    V(lambda e, c=c: e.tensor_copy(identB2[:, c * 128:(c + 1) * 128], identB[:, :]), r=["identB"], w=["identB2"])

                IDXC = (32.0 ** -0.5) * (8.0 ** -0.5)
                state = {"rbi": 0, "mbi": 0, "mgi": 0}

                def stagePS(t):
                    b = t % 3
                    xtn, hTn, cTn, scn, sqn = "x2t%d" % b, "h2T%d" % b, "catT%d" % b, "score%d" % b, "sqT%d" % b
                    sc = score[b]
                    N = (t + 1) * 128
                    tok = slice(t * 128, (t + 1) * 128)
                    ld(xt[b][:, :], src_d[tok, :], w=[xtn])
                    ld(hT[b][:, :, :].rearrange("p k n -> p (k n)"), hT_d[t, :, :], r=["hTd%d" % t], w=[hTn])
                    ld(catT[b][:, 0:4, :].rearrange("p k n -> p (k n)"), cat_d[t, :, 0:512], r=["catd%d" % t], w=[cTn])
                    ld(catT[b][:, 6:8, :].rearrange("p k n -> p (k n)"), cat_d[t, :, 512:768], r=["catd%d" % t], w=[cTn])
                    S_.unit()
                    for k in range(8):
                        mm(bank(0)[:, 0:256], hT[b][:, k, :], wB[:, k, 512:768], k == 0, k == 7, r=[hTn, "wB"], w=[PS(0)])
                    for k in range(8):
                        mm(bank(0)[:, 256:264], hT[b][:, k, :], wB[:, k, 1024:1032], k == 0, k == 7,
                           r=[hTn, "wB"], w=[PS(0)])
                    act(Vaug[:, t, :, 0:64], bank(0)[:, 0:256].rearrange("p (h d) -> p h d", d=64), AF.Copy,
                        r=[PS(0)], w=["Vaug"])
                    act(wab[:, :], bank(0)[:, 256:264], AF.Abs, r=[PS(0)], w=["wab"], scale=IDXC)
                    act(wsg[:, :], bank(0)[:, 256:264], AF.Sign, r=[PS(0)], w=["wsg"])
                    S_.unit()
                    for f in range(4):
                        cb = (0, 128, 256, 384)[f]
                        for k in range(8):
                            mm(bank(1)[:, f * 128:(f + 1) * 128], wB[:, k, cb:cb + 128], hT[b][:, k, :],
                               k == 0, k == 7, r=[hTn, "wB"], w=[PS(1)])
                    for hl in range(2):
                        rows = slice(64 * hl, 64 * hl + 64)
                        act(sqT[b][rows, :, hl * 128:(hl + 1) * 128],
                            bank(1)[rows, 0:256].rearrange("p (c t) -> p c t", t=128), AF.Copy,
                            r=[PS(1)], w=[sqn], scale=0.125)
                    V(lambda e: e.tensor_copy(KTc[:, :, tok], bank(1)[:, 256:512].rearrange("p (c t) -> p c t", t=128)),
                      r=[PS(1)], w=["KTc"])
                    S_.unit()
                    for f, (cb, m) in enumerate(((768, 96), (864, 96), (960, 64), (1032, 96))):
                        for k in range(8):
                            mm(bank(2)[0:m, f * 128:(f + 1) * 128], wB[:, k, cb:cb + m], hT[b][:, k, :],
                               k == 0, k == 7, r=[hTn, "wB"], w=[PS(2)])
                    act(iqT[0:96, :, :], bank(2)[0:96, 0:384].rearrange("p (c t) -> p c t", t=128), AF.Copy,
                        r=[PS(2)], w=["iqT"])
                    V(lambda e: e.tensor_copy(kidx[0:96, tok], bank(2)[0:96, 384:512]), r=[PS(2)], w=["kidx"])
                    S_.unit()
                    NB = (N + 511) // 512
                    prev_acc = [None]
                    for kb in range(NB):
                        wN = min(512, N - kb * 512)
                        ks_ = slice(kb * 512, kb * 512 + wN)
                        for hh in range(8):
                            g_, r_ = hh // 3, hh % 3
                            pb = 3 + (hh % 2)
                            mm(bank(pb)[:, 0:wN], iqT[32 * r_:32 * r_ + 32, g_, :], kidx[32 * r_:32 * r_ + 32, ks_],
                               True, True, r=["iqT", "kidx"], w=[PS(pb)])
                            R_ = Rb[state["rbi"] % 3]
                            Rn = "Rb%d" % (state["rbi"] % 3)
                            state["rbi"] += 1
                            act(R_[:, 0:wN], bank(pb)[:, 0:wN], AF.Relu, r=[PS(pb), "wab"], w=[Rn],
                                scale=wab[:, hh:hh + 1])

                            def acc(R_=R_, Rn=Rn, ks_=ks_, wN=wN, hh=hh):
                                if hh == 0:
                                    V(lambda e: e.tensor_scalar(sc[:, ks_], R_[:, 0:wN], wsg[:, 0:1], None, ALU.mult),
                                      r=[Rn, "wsg"], w=[scn])
                                else:
                                    V(lambda e: e.scalar_tensor_tensor(sc[:, ks_], R_[:, 0:wN], wsg[:, hh:hh + 1], sc[:, ks_],
                                                                       ALU.mult, ALU.add),
                                      r=[Rn, "wsg", scn], w=[scn])
                            if prev_acc[0] is not None:
                                prev_acc[0]()
                            prev_acc[0] = acc
                            S_.unit()
                    prev_acc[0]()
                    G(lambda e: e.affine_select(sc[:, tok], sc[:, tok], [[-1, 128]], ALU.is_ge, fillreg(e, -1e30),
                                                base=0, channel_multiplier=1), r=[scn], w=[scn])

                def stageBI(t):
                    b = t % 3
                    scn = "score%d" % b
                    sc = score[b]
                    N = (t + 1) * 128
                    thn = "thr%d" % b
                    if t * 128 < TOPK:
                        V(lambda e: e.tensor_copy(thrc[:, b:b + 1], thrneg[:, 0:1]), r=["thrneg"], w=[thn])
                        return
                    Ka = max(128, min(N - 128, int(round(0.66 * (t + 1))) * 128))
                    V(lambda e: e.max(top8[:, :], sc[:, 0:N]), r=[scn], w=["top8"])
                    S_.unit()
                    V(lambda e: e.tensor_reduce(bis[:, 0:1], sc[:, 0:TOPK], mybir.AxisListType.X, ALU.min),
                      r=[scn], w=["bis0"])
                    V(lambda e: e.tensor_tensor(bis[:, 1:2], top8[:, 0:1], bis[:, 0:1], ALU.subtract),
                      r=["top8", "bis0"], w=["bis1"])
                    V(lambda e: e.tensor_scalar(tabA[:, :], tabA0[:, :], bis[:, 1:2], None, ALU.mult),
                      r=["bis1", "tabA0"], w=["tabA"])
                    V(lambda e: e.tensor_scalar(tabB[:, :], tabB0[:, :], bis[:, 1:2], None, ALU.mult),
                      r=["bis1", "tabB0"], w=["tabB"])
                    V(lambda e: e.scalar_tensor_tensor(mid[:, 0:1], bis[:, 1:2], 0.5, bis[:, 0:1], ALU.mult, ALU.add),
                      r=["bis0", "bis1"], w=["mid"])
                    S_.unit()
                    for n in range(NIT):
                        act(junkb[:, 0:Ka], sc[:, 0:Ka], AF.Sign, r=[scn, "mid"], w=["junkb", "cntA"],
                            scale=-1.0, bias=mid[:, n:n + 1], accum_out=cntc[:, n:n + 1])
                        V(lambda e, n=n: e.scalar_tensor_tensor(junk8[:, Ka:N], sc[:, Ka:N], mid[:, n:n + 1],
                                                                ones1[:, 0:1].broadcast_to([128, N - Ka]),
                                                                ALU.is_ge, ALU.mult, accum_out=cntd[:, n:n + 1]),
                          r=[scn, "mid", "ones1"], w=["junk8", "cntD"])
                        V(lambda e, n=n: e.scalar_tensor_tensor(vc[:, n:n + 1], cntd[:, n:n + 1], 2.0, cntc[:, n:n + 1],
                                                                ALU.mult, ALU.subtract),
                          r=["cntA", "cntD"], w=["vc"])
                        V(lambda e, n=n: e.scalar_tensor_tensor(uc[:, n:n + 1], vc[:, n:n + 1], float(2 * TOPK - Ka - 1),
                                                                tabB[:, n:n + 1], ALU.is_gt, ALU.mult),
                          r=["vc", "tabB"], w=["uc"])
                        V(lambda e, n=n: e.scalar_tensor_tensor(mid[:, n + 1:n + 2], mid[:, n:n + 1], tabA[:, n:n + 1],
                                                                uc[:, n:n + 1], ALU.subtract, ALU.add),
                          r=["mid", "tabA", "uc"], w=["mid"])
                        S_.unit()
                    V(lambda e: e.tensor_copy(thrc[:, b:b + 1], mid[:, NIT:NIT + 1]), r=["mid"], w=[thn])

                def stageAT(t):
                    b = t % 3
                    xtn, cTn, scn, sqn, thn = "x2t%d" % b, "catT%d" % b, "score%d" % b, "sqT%d" % b, "thr%d" % b
                    sc = score[b]
                    tok = slice(t * 128, (t + 1) * 128)
                    thr = thrc[:, b:b + 1]
                    prev_pv = [None]
                    NG = (t + 4) // 4
                    Nq = (t + 1) * 128
                    mg0 = state["mgi"]
                    state["mgi"] += NG

                    def genmask(g):
                        if g >= NG:
                            return
                        w_ = min(512, Nq - g * 512)
                        gi = mg0 + g
                        M_ = Mb[gi % 3]
                        V(lambda e: e.tensor_scalar(M_[:, 0:w_], sc[:, g * 512:g * 512 + w_], thr, -30000.0, ALU.is_lt, ALU.mult),
                          r=[scn, thn], w=["Mb%d" % (gi % 3)])
                    genmask(0)
                    genmask(1)
                    for kb in range(t + 1):
                        kcs = slice(kb * 128, (kb + 1) * 128)
                        mbi = state["mbi"]
                        state["mbi"] += 1
                        g = kb // 4
                        if kb % 4 == 0:
                            genmask(g + 2)
                        gi = mg0 + g
                        M_ = Mb[gi % 3][:, (kb % 4) * 128:(kb % 4 + 1) * 128]
                        Mn = "Mb%d" % (gi % 3)
                        P_ = PT[mbi % 3]
                        Pn = "PT%d" % (mbi % 3)
                        pb = 5 + (mbi % 2)
                        for c in range(2):
                            mm(bank(pb)[:, c * 256:(c + 1) * 256], KTc[:, c, kcs], sqT[b][:, c, :], True, False,
                               r=["KTc", sqn], w=[PS(pb)])
                            mm(bank(pb)[:, c * 256:(c + 1) * 256], M_, identB2[:, :], False, True,
                               r=[Mn, "identB2"], w=[PS(pb)])
                        act(P_[:, :], bank(pb), AF.Exp, r=[PS(pb)], w=[Pn])

                        def pv(kb=kb, P_=P_, Pn=Pn):
                            for h in range(4):
                                mm(bank(7)[:, h * 65:(h + 1) * 65], P_[:, h * 128:(h + 1) * 128], Vaug[:, kb, h, :],
                                   kb == 0 and h == 0, kb == t, r=[Pn, "Vaug"], w=[PS(7)])
                        if prev_pv[0] is not None:
                            prev_pv[0]()
                        prev_pv[0] = pv
                        S_.unit()
                    prev_pv[0]()
                    o3 = bank(7)[:, 0:260].rearrange("p (h d) -> p h d", d=65)
                    V(lambda e: e.reciprocal(rs4[:, :].rearrange("p (h o) -> p h o", o=1), o3[:, :, 64:65]), r=[PS(7)], w=["rs4"])
                    V(lambda e: e.tensor_tensor(ob[:, :].rearrange("p (h d) -> p h d", d=64), o3[:, :, 0:64],
                                                rs4[:, :].rearrange("p (h o) -> p h o", o=1).broadcast_to([128, 4, 64]),
                                                ALU.mult), r=[PS(7), "rs4"], w=["ob"])
                    for c in range(2):
                        tr(bankb(7)[:, c * 128:(c + 1) * 128], ob[:, c * 128:(c + 1) * 128], identB[:, :],
                           r=["ob", "identB"], w=[PS(7)])
                    act(catT[b][:, 4:6, :], bankb(7)[:, 0:256].rearrange("p (c t) -> p c t", t=128), AF.Copy,
                        r=[PS(7)], w=[cTn])
                    S_.unit()
                    for hf in range(2):
                        for c in range(8):
                            mm(bank(5 + hf), catT[b][:, c, :], woS[:, c, hf * 512:(hf + 1) * 512], c == 0, c == 7,
                               r=[cTn, "woS"], w=[PS(5 + hf)])
                        V(lambda e, hf=hf: e.tensor_tensor(xt[b][:, hf * 512:(hf + 1) * 512],
                                                           xt[b][:, hf * 512:(hf + 1) * 512], bank(5 + hf), ALU.add),
                          r=[xtn, PS(5 + hf)], w=[xtn])
                        S_.unit()
                    ld(out_d[tok, :], xt[b][:, :], r=[xtn], w=["outd%d" % t])

                def cap_(fn, t):
                    S_.begin(fine=True)
                    fn(t)
                    return S_.end()

                for step in range(NT + 2):
                    streams = []
                    if step - 2 >= 0:
                        streams.append(cap_(stageAT, step - 2))
                    if 0 <= step - 1 < NT:
                        streams.append(cap_(stageBI, step - 1))
                    if step < NT:
                        streams.append(cap_(stagePS, step))
                    S_.run_merged(streams)
                S_.flush()
            if _STOP <= 2:
                return nc

            pl.__exit__(None, None, None)
            with ExitStack() as p3:
                w1S = sb(p3, "w1S", [128, 8, DFF], BF16)
                w2S = sb(p3, "w2S", [128, 32, D], BF16)
                g2bc = sb(p3, "g2bc", [128, D], F32)
                dtmp = sb(p3, "dtmp3", [128, 128], F32)
                xtN = [sb(p3, "x3n%d" % i, [128, D], F32) for i in range(2)]
                xr = [sb(p3, "x3r%d" % i, [128, D], F32) for i in range(2)]
                xn = sb(p3, "xn3", [128, D], F32)
                ssq = sb(p3, "ssq3", [128, 4], F32)
                ssf = sb(p3, "ssf3", [128, 4], F32)
                hTb = sb(p3, "hTb", [128, 8, 512], BF16)
                h1raw = sb(p3, "h1raw", [128, 8192], F32)
                h1T = h1raw[:, :].bitcast(BF16).rearrange("p (f t) -> p f t", t=512)
                wst = [h1raw[:, 0:1024], h1raw[:, 1024:2048]]
                wst_b = [h1raw[:, 2048:3072], h1raw[:, 3072:4096]]
                rl = [sb(p3, "rl%d" % i, [128, 512], F32) for i in range(2)]
                last = (l == L - 1)
                NBLK = S // 512
                stB = {"ri": 0, "ni": 0, "xi": 0}

                def stageN(blk):
                    for i in range(4):
                        t = blk * 4 + i
                        j = stB["ni"] % 2
                        stB["ni"] += 1
                        ld(xtN[j][:, :], out_d[t * 128:(t + 1) * 128, :], r=["outd%d" % t], w=["x3n%d" % j])
                        norm_tile(xtN[j][:, :], "x3n%d" % j, xn, ssq, hTb, "hTb", G2T, l, 3, (0, 1),
                                  ncols=128, coff=i * 128)
                        S_.unit()

                w1i = 0
                for cb in range(4):
                    for k in range(8):
                        stg, stn = wst_b[w1i % 2], "w1st%d" % (w1i % 2)
                        w1i += 1
                        ld(stg, w1_d[l, k * 128:(k + 1) * 128, cb * 1024:(cb + 1) * 1024], w=[stn])
                        act(w1S[:, k, cb * 1024:(cb + 1) * 1024], stg, AF.Copy, r=[stn], w=["w1S_%d" % cb])
                stageN(0)
                build_bc(g2bc, "g2bc", l, 5, dtmp, "dtmp3", (0, 1))
                for k in range(32):
                    ld(wst[k % 2], w2_d[l, k * 128:(k + 1) * 128, :], w=["w2st%d" % (k % 2)])
                    eng = V if k % 2 == 0 else G
                    eng(lambda e, k=k: e.tensor_tensor(w2S[:, k, :], wst[k % 2], g2bc[:, :], ALU.mult),
                        r=["w2st%d" % (k % 2), "g2bc"], w=["w2S_%d" % k])
                if last:
                    ld(g2bc[:, :], fnw_d[0:1, :].partition_broadcast(128), r=["w2S_%d" % k for k in range(32)], w=["g2bc"])
                def stageM1(blk):
                    for f in range(32):
                        pb = 2 + (f % 2)
                        for k in range(8):
                            mm(bank(pb), w1S[:, k, f * 128:(f + 1) * 128], hTb[:, k, :], k == 0, k == 7,
                               r=["w1S_%d" % (f // 8), "hTb"], w=[PS(pb)])
                        r_ = rl[stB["ri"] % 2]
                        rn = "rl%d" % (stB["ri"] % 2)
                        stB["ri"] += 1
                        act(r_[:, :], bank(pb), AF.Relu, r=[PS(pb)], w=[rn])
                        G(lambda e, r_=r_, f=f: e.tensor_tensor(h1T[:, f, :], r_[:, :], r_[:, :], ALU.mult),
                          r=[rn], w=["h1T", "w2st0", "w2st1", "w1st0", "w1st1"])

                def stageM2(blk):
                    for i in range(4):
                        t = blk * 4 + i
                        j = stB["xi"] % 2
                        stB["xi"] += 1
                        xb, xbn = xr[j], "x3r%d" % j
                        ld(xb[:, :], out_d[t * 128:(t + 1) * 128, :], r=["outd%d" % t], w=[xbn])
                        for hf in range(2):
                            pb = 4 + 2 * (i % 2) + hf
                            for f in range(32):
                                mm(bank(pb), h1T[:, f, i * 128:(i + 1) * 128], w2S[:, f, hf * 512:(hf + 1) * 512],
                                   f == 0, f == 31, r=["h1T", "w2S_%d" % f], w=[PS(pb)])
                                if f % 8 == 7:
                                    S_.unit()
                            V(lambda e, xb=xb, hf=hf, pb=pb: e.tensor_tensor(xb[:, hf * 512:(hf + 1) * 512],
                                                                             xb[:, hf * 512:(hf + 1) * 512], bank(pb), ALU.add),
                              r=[xbn, PS(pb)], w=[xbn])
                        if last:
                            act(xn[:, :], xb[:, :], AF.Square, r=[xbn], w=["xn", "ssf"], accum_out=ssf[:, 0:1])
                            V(lambda e: e.tensor_scalar(ssf[:, 1:2], ssf[:, 0:1], 1.0 / D, EPS, ALU.mult, ALU.add),
                              r=["ssf"], w=["ssf1"])
                            act(ssf[:, 2:3], ssf[:, 1:2], AF.Ln, r=["ssf1"], w=["ssf2"])
                            act(ssf[:, 3:4], ssf[:, 2:3], AF.Exp, r=["ssf2"], w=["ssf3"], scale=-0.5)
                            V(lambda e, xb=xb: e.scalar_tensor_tensor(xb[:, :], xb[:, :], ssf[:, 3:4], g2bc[:, :],
                                                                      ALU.mult, ALU.mult),
                              r=[xbn, "ssf3", "g2bc"], w=[xbn])
                        ld(out_d[t * 128:(t + 1) * 128, :], xb[:, :], r=[xbn], w=["outd%d" % t])
                        S_.unit()

                def capB(fn, blk):
                    S_.begin()
                    fn(blk)
                    return S_.end()

                for blk in range(NBLK):
                    stageM1(blk)
                    streams = [capB(stageM2, blk)]
                    if blk + 1 < NBLK:
                        streams.append(capB(stageN, blk + 1))
                    S_.run_merged(streams)
                S_.flush()
    return nc


def _colsT(v, L, n):
    v = np.asarray(v, np.float32).reshape(L, n, 128)
    return np.ascontiguousarray(v.transpose(2, 0, 1).reshape(128, L * n))


def make_in_maps(inp, L, nb):
    f = lambda a: np.ascontiguousarray(np.asarray(a, np.float32))
    shared = {
        "ada_w": f(inp["ada_w"]),
        "ada_bT": _colsT(inp["ada_b"], L, 48),
        "nmixT": _colsT(inp["norm_mix_w"], L, 8),
        "nmlpT": _colsT(inp["norm_mlp_w"], L, 8),
        "fnw": f(inp["final_norm_w"]).reshape(1, D),
        "w_in": f(inp["w_in"]), "w_out": f(inp["w_out"]), "w1": f(inp["mlp_w1"]), "w2": f(inp["mlp_w2"]),
        "lbT": _colsT(inp["hg_lb_logits"], L, 4),
        "onwT": _colsT(inp["hg_onorm_w"], L, 4),
        "cvw": np.ascontiguousarray(np.asarray(inp["cv_w"], np.float32).reshape(L, 31, 2, 128)
                                    .transpose(3, 0, 2, 1).reshape(128, L * 62)),
        "cvb": _colsT(inp["cv_b"], L, 2),
        "lnw": _colsT(inp["cv_ln_w"], L, 2),
        "lnb": _colsT(inp["cv_ln_b"], L, 2),
    }
    maps = []
    x = np.asarray(inp["x"], np.float32)
    c = np.asarray(inp["c"], np.float32)
    for b in range(nb):
        m = dict(shared)
        m["x"] = np.ascontiguousarray(x[b])
        m["cT"] = np.ascontiguousarray(c[b].reshape(8, 128).T)
        maps.append(m)
    return maps


_NC_CACHE = {}


def kernel(x, c, ada_w, ada_b, norm_mix_w, norm_mlp_w, w_in, hg_lb_logits, hg_onorm_w,
           cv_w, cv_b, cv_ln_w, cv_ln_b, w_out, mlp_w1, mlp_w2, final_norm_w):
    inp = dict(x=x, c=c, ada_w=ada_w, ada_b=ada_b, norm_mix_w=norm_mix_w, norm_mlp_w=norm_mlp_w, w_in=w_in,
               hg_lb_logits=hg_lb_logits, hg_onorm_w=hg_onorm_w, cv_w=cv_w, cv_b=cv_b, cv_ln_w=cv_ln_w,
               cv_ln_b=cv_ln_b, w_out=w_out, mlp_w1=mlp_w1, mlp_w2=mlp_w2, final_norm_w=final_norm_w)
    B, S, _ = np.asarray(x).shape
    L = np.asarray(w_in).shape[0]
    topk = min(256, S // 4)
    key = (S, L, topk)
    if key not in _NC_CACHE:
        _NC_CACHE[key] = build_nc(S, L, topk)
    nc = _NC_CACHE[key]
    maps = make_in_maps(inp, L, B)
    res = run_bass_kernel_spmd(nc, maps, core_ids=list(range(B)))
    return np.stack([np.asarray(r["out"], np.float32) for r in res.results], axis=0)
```

```python
import numpy as np
from contextlib import ExitStack
import concourse.bass as bass
import concourse.mybir as mybir
from concourse.bass_utils import run_bass_kernel_spmd

F32 = mybir.dt.float32
BF16 = mybir.dt.bfloat16
U8 = mybir.dt.uint8
AF = mybir.ActivationFunctionType
ALU = mybir.AluOpType

D = 1024
DIN = 3624
DFF = 4096
EPS = 1e-6
NIT = 16
ENGS = ("tensor", "vector", "scalar", "gpsimd", "sync")
NDMA_SEMS = 24
import os as _os
_STOP = int(_os.environ.get('KSTOP', '99'))
_CUT = int(_os.environ.get('KCUT', '99'))
_SUB = int(_os.environ.get('KSUB', '99'))


class _Op:
    __slots__ = ("eng", "fn", "deps", "signals", "sem", "val", "is_dma")

    def __init__(self, eng, fn, is_dma=False):
        self.eng = eng
        self.fn = fn
        self.deps = []
        self.signals = False
        self.sem = None
        self.val = 0
        self.is_dma = is_dma


class _Slot:
    __slots__ = ("writer", "readers")

    def __init__(self):
        self.writer = None
        self.readers = []


class Sched:
    def __init__(self, nc, es):
        self.nc = nc
        self.q = {e: [] for e in ENGS}
        self.slots = {}
        self.phase_dmas = []
        self.esem = {e: es.enter_context(nc.semaphore("es_" + e)) for e in ENGS}
        self.dsem = {e: [es.enter_context(nc.semaphore("ds_%s_%d" % (e, i))) for i in range(NDMA_SEMS)]
                     for e in ("sync", "gpsimd")}
        self.cnt = {e: 0 for e in ENGS}
        self.dcnt = {e: [0] * NDMA_SEMS for e in self.dsem}
        self.drr = {e: 0 for e in self.dsem}
        self.dprev = {e: [None] * NDMA_SEMS for e in self.dsem}
        self.waited = {e: {} for e in ENGS}
        self.nops = 0

    def _slot(self, k):
        s = self.slots.get(k)
        if s is None:
            s = self.slots[k] = _Slot()
        return s

    def _add(self, op, reads, writes):
        deps = set()
        for k in reads:
            s = self._slot(k)
            if s.writer is not None:
                deps.add(s.writer)
            if k.startswith("ps"):
                for r in s.readers:
                    if r.eng != op.eng:
                        deps.add(r)
        for k in writes:
            s = self._slot(k)
            if s.writer is not None:
                deps.add(s.writer)
            for r in s.readers:
                deps.add(r)
        deps.discard(op)
        for d in deps:
            if d.eng == "tensor" and op.eng == "tensor":
                continue
            op.deps.append(d)
            d.signals = True
        for k in writes:
            s = self._slot(k)
            s.writer = op
            s.readers = []
        for k in reads:
            if k not in writes:
                self._slot(k).readers.append(op)
        self.q[op.eng].append(op)
        self.nops += 1
        return op

    cap = None

    fine = False

    def begin(self, fine=False):
        self.cap = [[]]
        self.fine = fine

    def unit(self):
        if self.cap is not None and self.cap[-1]:
            self.cap.append([])

    def end(self):
        u = [x for x in self.cap if x]
        self.cap = None
        return u

    def run_merged(self, streams):
        pos = [0] * len(streams)
        while True:
            best, bf = -1, 2.0
            for i, s in enumerate(streams):
                if pos[i] < len(s):
                    f = (pos[i] + 0.5) / len(s)
                    if f < bf:
                        best, bf = i, f
            if best < 0:
                break
            for item in streams[best][pos[best]]:
                if item[0] == "op":
                    self.op(*item[1:])
                else:
                    self.dma(item[1], item[2], item[3], item[4], item[5], **item[6])
            pos[best] += 1

    def op(self, eng, fn, reads=(), writes=()):
        if self.cap is not None:
            self.cap[-1].append(("op", eng, fn, tuple(reads), tuple(writes)))
            if self.fine and eng != "tensor":
                self.cap.append([])
            return None
        return self._add(_Op(eng, fn), list(reads), list(writes))

    def dma(self, eng, out, in_, reads=(), writes=(), **kw):
        if self.cap is not None:
            self.cap[-1].append(("dma", eng, out, in_, tuple(reads), tuple(writes), kw))
            return None
        fn = lambda e: e.dma_start(out=out, in_=in_, **kw)
        op = _Op(eng, fn, is_dma=True)
        op.signals = True
        self._add(op, list(reads), list(writes))
        self.phase_dmas.append(op)
        return op

    def flush(self):
        nc = self.nc
        fin = _Op("sync", None)
        fin.deps = list(self.phase_dmas)
        self.phase_dmas = []
        self.q["sync"].append(fin)
        for e in ENGS:
            for op in self.q[e]:
                if op.fn is None:
                    continue
                if op.is_dma:
                    j = self.drr[e]
                    self.drr[e] = (j + 1) % NDMA_SEMS
                    self.dcnt[e][j] += 16
                    op.sem = self.dsem[e][j]
                    op.val = self.dcnt[e][j]
                    prev = self.dprev[e][j]
                    if prev is not None:
                        op.deps.append(prev)
                    self.dprev[e][j] = op
                elif op.signals:
                    self.cnt[e] += 1
                    op.sem = self.esem[e]
                    op.val = self.cnt[e]
        q = self.q
        waited_all = self.waited

        def run(e):
            def body(eng):
                waited = waited_all[e]
                for op in q[e]:
                    for d in op.deps:
                        if d.sem is None:
                            continue
                        key = id(d.sem)
                        if waited.get(key, 0) < d.val:
                            eng.wait_ge(d.sem, d.val)
                            waited[key] = d.val
                    if op.fn is None:
                        continue
                    ins = op.fn(eng)
                    if op.signals:
                        ins.then_inc(op.sem, 16 if op.is_dma else 1)
            return body

        with nc.Block() as block:
            if q["tensor"]:
                block.tensor(run("tensor"))
            if q["vector"]:
                block.vector(run("vector"))
            if q["scalar"]:
                block.scalar(run("scalar"))
            if q["gpsimd"]:
                block.gpsimd(run("gpsimd"))
            block.sync(run("sync"))
        self.q = {e: [] for e in ENGS}


def build_nc(S, L, TOPK, dbg=False):
    NT = S // 128
    nc = bass.Bass("TRN2", target_bir_lowering=False)

    def din(name, shape, dt=F32):
        return nc.dram_tensor(name, list(shape), dt, kind="ExternalInput").ap()

    x_d = din("x", [S, D])
    cT_d = din("cT", [128, 8])
    adaw_d = din("ada_w", [L, D, 6 * D])
    adabT_d = din("ada_bT", [128, L * 48])
    nmixT_d = din("nmixT", [128, L * 8])
    nmlpT_d = din("nmlpT", [128, L * 8])
    fnw_d = din("fnw", [1, D])
    win_d = din("w_in", [L, D, DIN])
    wout_d = din("w_out", [L, D, D])
    w1_d = din("w1", [L, D, DFF])
    w2_d = din("w2", [L, DFF, D])
    lbT_d = din("lbT", [128, L * 4])
    onwT_d = din("onwT", [128, L * 4])
    cvw_d = din("cvw", [128, L * 62])
    cvb_d = din("cvb", [128, L * 2])
    lnw_d = din("lnw", [128, L * 2])
    lnb_d = din("lnb", [128, L * 2])
    out_d = nc.dram_tensor("out", [S, D], F32, kind="ExternalOutput").ap()
    hT_d = nc.dram_tensor("hT_scr", [NT, 128, 1024], BF16, kind="Internal").ap()
    cat_d = nc.dram_tensor("cat_scr", [NT, 128, 768], BF16, kind="Internal").ap()

    with ExitStack() as es:
        S_ = Sched(nc, es)

        def V(fn, r=(), w=()):
            return S_.op("vector", fn, r, w)

        def A(fn, r=(), w=()):
            return S_.op("scalar", fn, r, w)

        def G(fn, r=(), w=()):
            return S_.op("gpsimd", fn, r, w)

        def T(fn, r=(), w=()):
            return S_.op("tensor", fn, r, w)

        _fillregs = {}

        def fillreg(e, val):
            if val not in _fillregs:
                _fillregs[val] = e.to_reg(val)
            return _fillregs[val]

        def mm(out, lhsT, rhs, start, stop, r, w):
            return T(lambda e: e.matmul(out, lhsT, rhs, start=start, stop=stop, skip_group_check=True), r, w)

        def tr(out, in_, ident, r, w):
            return T(lambda e: e.transpose(out, in_, ident), r, w)

        def act(out, in_, func, r, w, **kw):
            return A(lambda e: e.activation(out, in_, func, **kw), r, w)

        def ld(out, in_, w, r=(), **kw):
            return S_.dma("sync", out, in_, reads=r, writes=w, **kw)

        def ldc(out, in_, w, r=()):
            return S_.dma("gpsimd", out, in_, reads=r, writes=w, max_dma_last_dim=2048)

        _uid = [0]

        def sb(stack, name, shape, dt):
            _uid[0] += 1
            return stack.enter_context(nc.sbuf_tensor("s%d_%s" % (_uid[0], name), list(shape), dt))

        pst = [es.enter_context(nc.psum_tensor("pst%d" % i, [128, 1024], F32)) for i in range(4)]

        def bank(i):
            return pst[i // 2][:, (i % 2) * 512:(i % 2 + 1) * 512]

        def bankb(i):
            return bank(i).bitcast(BF16)

        PS = lambda i: "ps%d" % i

        identF = sb(es, "identF", [128, 128], F32)
        identB = sb(es, "identB", [128, 128], BF16)
        onesM = sb(es, "onesM", [128, 128], F32)
        cT = sb(es, "cTs", [128, 8], F32)
        cact = sb(es, "cact", [128, 8], F32)
        modT = sb(es, "modT", [128, L * 48], F32)
        adabT = sb(es, "adabT", [128, L * 48], F32)
        nmixT = sb(es, "nmixT", [128, L * 8], F32)
        nmlpT = sb(es, "nmlpT", [128, L * 8], F32)
        G1T = sb(es, "G1T", [128, L * 8], F32)
        G2T = sb(es, "G2T", [128, L * 8], F32)
        lbT = sb(es, "lbT", [128, L * 4], F32)
        omlT = sb(es, "omlT", [128, L * 4], F32)
        lbtmp = sb(es, "lbtmp", [128, 8], F32)
        onwT = sb(es, "onwT", [128, L * 4], F32)
        cvw = sb(es, "cvw", [128, L * 62], F32)
        cvb = sb(es, "cvb", [128, L * 2], F32)
        lnw = sb(es, "lnw", [128, L * 2], F32)
        lnb = sb(es, "lnb", [128, L * 2], F32)
        tabA0 = sb(es, "tabA0", [128, NIT], F32)
        tabB0 = sb(es, "tabB0", [128, NIT], F32)
        thrneg = sb(es, "thrneg", [128, 1], F32)
        evt = sb(es, "evt", [128, 512], F32)
        negh = sb(es, "negh", [128, 128], F32)
        omlh = sb(es, "omlh", [128, L * 4], F32)
        lbp = sb(es, "lbp", [128, L * 4], F32)
        onwh = sb(es, "onwh", [128, L * 4], F32)
        lnwh = sb(es, "lnwh", [128, L * 2], F32)
        lnbh = sb(es, "lnbh", [128, L * 2], F32)

        def modcol(l, j, k):
            c = l * 48 + j * 8 + k
            return modT[:, c:c + 1]

        with ExitStack() as ps_:
            stage = [sb(ps_, "adast%d" % i, [128, 8, 512], F32) for i in range(2)]
            rowS = [sb(ps_, "rowS%d" % i, [1, 512], F32) for i in range(2)]
            one1f = sb(ps_, "one1f", [1, 1], F32)
            G(lambda e: e.memset(one1f[:, :], 1.0), w=["one1f"])
            G(lambda e: e.memset(identF[:, :], 1.0), w=["identF"])
            G(lambda e: e.affine_select(identF[:, :], identF[:, :], [[-1, 128]], ALU.is_equal, fillreg(e, 0.0),
                                        base=0, channel_multiplier=1), r=["identF"], w=["identF"])
            V(lambda e: e.tensor_copy(identB[:, :], identF[:, :]), r=["identF"], w=["identB"])
            G(lambda e: e.memset(onesM[:, :], 1.0 / 256.0), w=["onesM"])
            G(lambda e: e.memset(thrneg[:, :], -1e29), w=["thrneg"])
            for n in range(NIT):
                a_n = 2.0 ** -(n + 2) if n < NIT - 1 else 2.0 ** -(NIT)
                b_n = 2.0 ** -(n + 1)
                G(lambda e, n=n, a_n=a_n: e.memset(tabA0[:, n:n + 1], a_n), w=["tabA0"])
                G(lambda e, n=n, b_n=b_n: e.memset(tabB0[:, n:n + 1], b_n), w=["tabB0"])
            for (dst, src, nm) in ((cT, cT_d, "cT"), (adabT, adabT_d, "adabT"), (nmixT, nmixT_d, "nmixT"),
                                   (nmlpT, nmlpT_d, "nmlpT"), (lbT, lbT_d, "lbT"), (onwT, onwT_d, "onwT"),
                                   (cvw, cvw_d, "cvw"), (cvb, cvb_d, "cvb"), (lnw, lnw_d, "lnw"),
                                   (lnb, lnb_d, "lnb")):
                ld(dst[:, :], src[:, :], w=[nm])
            act(cact[:, :], cT[:, :], AF.Silu, r=["cT"], w=["cact"])
            lb3 = lbT[:, :].rearrange("p (l h) -> p l h", h=4)
            act(lbT[:, :], lbT[:, :], AF.Exp, r=["lbT"], w=["lbT"])
            V(lambda e: e.tensor_copy(lbtmp[:, 0:4], lb3[:, 0, :]), r=["lbT"], w=["lbtmp"])
            for l in range(1, L):
                V(lambda e, l=l: e.tensor_tensor(lbtmp[:, 0:4], lbtmp[:, 0:4], lb3[:, l, :], ALU.add),
                  r=["lbT", "lbtmp"], w=["lbtmp"])
            V(lambda e: e.reciprocal(lbtmp[:, 4:8], lbtmp[:, 0:4]), r=["lbtmp"], w=["lbtmp2"])
            for l in range(L):
                V(lambda e, l=l: e.tensor_tensor(lb3[:, l, :], lb3[:, l, :], lbtmp[:, 4:8], ALU.mult),
                  r=["lbT", "lbtmp2"], w=["lbT"])
            V(lambda e: e.memset(lb3[:, 0, :], 0.0), r=["lbT"], w=["lbT"])
            for l in range(2, L):
                V(lambda e, l=l: e.tensor_tensor(lb3[:, l, :], lb3[:, l, :], lb3[:, l - 1, :], ALU.add),
                  r=["lbT"], w=["lbT"])
            V(lambda e: e.tensor_scalar(omlT[:, :], lbT[:, :], -1.0, 1.0, ALU.mult, ALU.add),
              r=["lbT"], w=["omlT"])
            V(lambda e: e.tensor_scalar(omlh[:, :], omlT[:, :], 0.5, None, ALU.mult), r=["omlT"], w=["omlh"])
            V(lambda e: e.tensor_tensor(lbp[:, :], lbT[:, :], omlh[:, :], ALU.add), r=["lbT", "omlh"], w=["lbp"])
            V(lambda e: e.tensor_scalar(onwh[:, :], onwT[:, :], 0.5, None, ALU.mult), r=["onwT"], w=["onwh"])
            V(lambda e: e.tensor_scalar(lnwh[:, :], lnw[:, :], 0.5, None, ALU.mult), r=["lnw"], w=["lnwh"])
            V(lambda e: e.tensor_scalar(lnbh[:, :], lnb[:, :], 0.5, None, ALU.mult), r=["lnb"], w=["lnbh"])
            G(lambda e: e.memset(negh[:, :], -0.5), w=["negh"])
            piece = 0
            for l in range(L):
                for pc in range(12):
                    st = stage[piece % 2]
                    sn = "adast%d" % (piece % 2)
                    ld(st[:, :, :], adaw_d[l, :, pc * 512:(pc + 1) * 512].rearrange("(k p) n -> p k n", p=128),
                       w=[sn])
                    rb = 2 + (piece % 2)
                    for k in range(8):
                        mm(bank(rb)[0:1, :], cact[:, k:k + 1], st[:, k, :], k == 0, k == 7, r=[sn, "cact"], w=[PS(rb)])
                    rw = rowS[piece % 2]
                    rwn = "rowS%d" % (piece % 2)
                    act(rw[0:1, :], bank(rb)[0:1, :], AF.Copy, r=[PS(rb)], w=[rwn])
                    for f in range(4):
                        col = l * 48 + pc * 4 + f
                        mm(bank(0)[:, col:col + 1], rw[0:1, f * 128:(f + 1) * 128], one1f[0:1, 0:1],
                           piece == 0 and f == 0, True, r=[rwn, "one1f"], w=[PS(0)])
                    piece += 1
            V(lambda e: e.tensor_tensor(modT[:, :], bank(0)[:, 0:L * 48], adabT[:, :], ALU.add),
              r=[PS(0), "adabT"], w=["modT"])
            m4 = modT[:, :].rearrange("p (l j k) -> p l j k", j=6, k=8)
            V(lambda e: e.scalar_tensor_tensor(G1T[:, :].rearrange("p (l k) -> p l k", k=8), m4[:, :, 1, :], 1.0,
                                               nmixT[:, :].rearrange("p (l k) -> p l k", k=8), ALU.add, ALU.mult),
              r=["modT", "nmixT"], w=["G1T"])
            V(lambda e: e.scalar_tensor_tensor(G2T[:, :].rearrange("p (l k) -> p l k", k=8), m4[:, :, 4, :], 1.0,
                                               nmlpT[:, :].rearrange("p (l k) -> p l k", k=8), ALU.add, ALU.mult),
              r=["modT", "nmlpT"], w=["G2T"])
            S_.flush()
        if _STOP <= 0:
            return nc

        def build_bc(bc, bcname, l, j, dtmp, dname, pbanks):
            for k in range(8):
                V(lambda e, k=k: e.tensor_scalar(dtmp[:, :], identF[:, :], modcol(l, j, k), None, ALU.mult),
                  r=["identF", "modT", dname], w=[dname])
                bk = pbanks[k // 4]
                mm(bank(bk)[:, (k % 4) * 128:(k % 4 + 1) * 128], onesM[:, :], dtmp[:, :], True, True,
                   r=["onesM", dname], w=[PS(bk)])
            for hh in range(2):
                A(lambda e, hh=hh: e.activation(bc[:, hh * 512:(hh + 1) * 512], bank(pbanks[hh]), AF.Copy, scale=256.0),
                  r=[PS(pbanks[hh])], w=[bcname])

        def norm_tile(xt_ap, xtname, xn, ssq, hT_out, hTname, GT_, l, jsh, pb, ncols=128, coff=0):
            act(xn[:, :], xt_ap, AF.Square, r=[xtname], w=["xn", "ssq"], accum_out=ssq[:, 0:1])
            V(lambda e: e.tensor_scalar(ssq[:, 1:2], ssq[:, 0:1], 1.0 / D, EPS, ALU.mult, ALU.add),
              r=["ssq"], w=["ssq1"])
            act(ssq[:, 2:3], ssq[:, 1:2], AF.Ln, r=["ssq1"], w=["ssq2"])
            act(ssq[:, 3:4], ssq[:, 2:3], AF.Exp, r=["ssq2"], w=["ssq3"], scale=-0.5)
            V(lambda e: e.tensor_scalar(xn[:, :], xt_ap, ssq[:, 3:4], None, ALU.mult),
              r=[xtname, "ssq3", "xn"], w=["xn"])
            for half in range(2):
                bk = pb[half]
                for k in range(4 * half, 4 * half + 4):
                    tr(bank(bk)[:, (k % 4) * 128:(k % 4 + 1) * 128], xn[:, k * 128:(k + 1) * 128], identF[:, :],
                       r=["xn", "identF"], w=[PS(bk)])
                k0 = 4 * half
                g_b = GT_[:, l * 8 + k0:l * 8 + k0 + 4].rearrange("p (k o) -> p k o", o=1).broadcast_to([128, 4, 128])
                c0 = l * 48 + jsh * 8 + k0
                s_b = modT[:, c0:c0 + 4].rearrange("p (k o) -> p k o", o=1).broadcast_to([128, 4, 128])
                V(lambda e, bk=bk, g_b=g_b: e.tensor_tensor(evt[:, :].rearrange("p (k t) -> p k t", t=128),
                                                            bank(bk).rearrange("p (k t) -> p k t", t=128), g_b, ALU.mult),
                  r=[PS(bk), "G1T", "G2T"], w=["evt"])
                V(lambda e, k0=k0, s_b=s_b: e.tensor_tensor(hT_out[:, k0:k0 + 4, coff:coff + ncols],
                                                            evt[:, :].rearrange("p (k t) -> p k t", t=128), s_b, ALU.add),
                  r=["evt", "modT"], w=[hTname])

        for l in range(L):
            src_d = x_d if l == 0 else out_d
            pl = ExitStack()
            pl.__enter__()
            WB = 1128
            wB = sb(pl, "wB", [128, 8, WB], BF16)
            woS = sb(pl, "woS", [128, 8, D], BF16)
            with ExitStack() as p1:
                WA = 2560
                wA = sb(p1, "wA", [128, 8, WA], BF16)
                dg = sb(p1, "dg", [128, 2, 31, 128], BF16)
                xt = [sb(p1, "xt%d" % i, [128, D], F32) for i in range(2)]
                xn = sb(p1, "xn", [128, D], F32)
                ssq = sb(p1, "ssq", [128, 4], F32)
                hT = [sb(p1, "hT%d" % i, [128, 8, 128], BF16) for i in range(2)]
                tht = [sb(p1, "tht%d" % i, [128, 512], F32) for i in range(2)]
                qs = [sb(p1, "qs%d" % i, [128, 512], F32) for i in range(2)]
                sg = [sb(p1, "sg%d" % i, [128, 512], F32) for i in range(2)]
                gs = [sb(p1, "gs%d" % i, [128, 512], F32) for i in range(3)]
                vtok = [sb(p1, "vtok%d" % i, [128, 512], BF16) for i in range(3)]
                hcur = [sb(p1, "hcur%d" % i, [128, 2, 128], BF16) for i in range(2)]
                kT = sb(p1, "kT", [128, 512], F32)
                fT = sb(p1, "fT", [128, 512], F32)
                GT = sb(p1, "GT", [128, 512], F32)
                t1 = sb(p1, "t1", [128, 512], F32)
                E = sb(p1, "E", [128, 512], F32)
                E4 = [sb(p1, "E4%d" % i, [128, 512], F32) for i in range(2)]
                qtl = [sb(p1, "qtl%d" % i, [128, 512], BF16) for i in range(2)]
                ktl = [sb(p1, "ktl%d" % i, [128, 512], BF16) for i in range(2)]
                khT = sb(p1, "khT", [128, 512], BF16)
                qhA = [sb(p1, "qhA%d" % i, [128, 512], BF16) for i in range(2)]
                qhB = [sb(p1, "qhB%d" % i, [128, 512], BF16) for i in range(2)]
                khtok = [sb(p1, "khtok%d" % i, [128, 512], BF16) for i in range(2)]
                ATm = sb(p1, "ATm", [128, 512], BF16)
                Sst = sb(p1, "Sst", [128, 512], F32)
                Sbf0 = sb(p1, "Sbf0", [128, 512], BF16)
                Sbf1 = sb(p1, "Sbf1", [128, 512], BF16)
                oa = sb(p1, "oa", [128, 512], F32)
                oab = sb(p1, "oab", [128, 512], BF16)
                ss4 = sb(p1, "ss4", [128, 16], F32)
                scanm = sb(p1, "scanm", [128, 512], F32)
                cmask = sb(p1, "cmask", [128, 512], U8)
                cmf = sb(p1, "cmf", [128, 512], F32)
                hbuf = sb(p1, "hbuf", [128, 2, 160], BF16)
                cc = [sb(p1, "cc%d" % i, [128, 256], F32) for i in range(2)]
                csq = sb(p1, "csq", [128, 256], F32)
                stt_ = [sb(p1, "stt%d" % i, [128, 256], F32) for i in range(2)]
                yv = sb(p1, "yv", [128, 256], F32)
                thy = sb(p1, "thy", [128, 256], F32)
                rs_b = [sb(p1, "rs_b%d" % i, [128, 128], F32) for i in range(2)]
                cat6 = [sb(p1, "cat6_%d" % i, [128, 6, 128], BF16) for i in range(2)]

                wst1 = [sb(p1, "wst%d" % i, [128, D], F32) for i in range(2)]
                wi = 0
                for (gn, d0, s0) in (("v", 1024, 1024), ("g", 1536, 1536), ("q", 0, 0), ("f", 512, 512), ("cu", 2048, 3112)):
                    for k2 in range(4):
                        stg, stn = wst1[wi % 2], "wst%d" % (wi % 2)
                        wi += 1
                        ld(stg[:, :].rearrange("p (k n) -> p k n", n=512),
                           win_d[l, k2 * 256:(k2 + 1) * 256, s0:s0 + 512].rearrange("(k p) n -> p k n", p=128), w=[stn])
                        act(wA[:, 2 * k2:2 * k2 + 2, d0:d0 + 512], stg[:, :].rearrange("p (k n) -> p k n", n=512), AF.Copy,
                            r=[stn], w=["wA_" + gn])
                g1bc = sb(p1, "g1bc", [128, D], F32)
                stg2 = sb(p1, "stg2", [128, 8, 40], F32)
                dtmp1 = sb(p1, "dtmp", [128, 128], F32)
                def prefetch_a2():
                    ld(stg2[:, :, :], win_d[l, :, 3072:3112].rearrange("(k p) n -> p k n", p=128), w=["stg2"])
                    for k in range(8):
                        rows = slice(k * 128, (k + 1) * 128)
                        stg, stn = wst1[k % 2], "wst%d" % (k % 2)
                        ld(stg[:, :], win_d[l, rows, 2048:3072], w=[stn])
                        act(wB[:, k, 0:1024], stg[:, :], AF.Copy, r=[stn], w=["wB"])
                    act(wB[:, :, 1024:1032], stg2[:, :, 32:40], AF.Copy, r=["stg2"], w=["wB"])
                    for r3 in range(3):
                        act(wB[:, :, 1032 + 32 * r3:1064 + 32 * r3], stg2[:, :, 0:32], AF.Copy, r=["stg2"], w=["wB"])
                    build_bc(g1bc, "g1bc", l, 2, dtmp1, "dtmp", (0, 1))
                    for k in range(8):
                        ld(wst1[k % 2][:, :], wout_d[l, k * 128:(k + 1) * 128, :], w=["wst%d" % (k % 2)])
                        V(lambda e, k=k: e.tensor_tensor(woS[:, k, :], wst1[k % 2][:, :], g1bc[:, :], ALU.mult),
                          r=["wst%d" % (k % 2), "g1bc"], w=["woS"])
                for ct in range(2):
                    for j in range(31):
                        c = l * 62 + ct * 31 + j
                        V(lambda e, ct=ct, j=j, c=c: e.tensor_scalar(dg[:, ct, j, :], identF[:, :], cvw[:, c:c + 1],
                                                                     None, ALU.mult),
                          r=["identF", "cvw"], w=["dg"])
                G(lambda e: e.memset(scanm[:, :], 1.0), w=["scanm"])
                sm3 = scanm[:, :].rearrange("p (c j) -> p c j", j=64)
                G(lambda e: e.memset(sm3[:, :, 0:1], 0.0), r=["scanm"], w=["scanm"])
                G(lambda e: e.memset(cmf[:, :], 1.0), w=["cmf"])
                for h in range(4):
                    G(lambda e, h=h: e.affine_select(cmf[:, h * 128:(h + 1) * 128], cmf[:, h * 128:(h + 1) * 128],
                                                     [[1, 128]], ALU.is_ge, fillreg(e, 0.0), base=0, channel_multiplier=-1),
                      r=["cmf"], w=["cmf"])
                    G(lambda e, h=h: e.memset(cmf[0:64, h * 128 + 64:(h + 1) * 128], 0.0), r=["cmf"], w=["cmf"])
                V(lambda e: e.tensor_copy(cmask[:, :], cmf[:, :]), r=["cmf"], w=["cmask"])
                for (tl, nm) in ((ATm, "ATm"), (qhA[0], "qhA0"), (qhA[1], "qhA1"), (qhB[0], "qhB0"), (qhB[1], "qhB1"),
                                 (Sst, "Sst"), (Sbf0, "Sbf0"), (Sbf1, "Sbf1")):
                    G(lambda e, tl=tl: e.memset(tl[:, :], 0.0), w=[nm])
                G(lambda e: e.memset(hbuf[:, :, :], 0.0), w=["hbuf"])

                bc4 = lambda tl_: tl_[:, l * 4:(l + 1) * 4].rearrange("p (h o) -> p h o", o=1).broadcast_to([128, 4, 128])
                lb_b, oml_b = bc4(lbT), bc4(omlT)
                v3 = lambda t_: t_[:, :].rearrange("p (h t) -> p h t", t=128)
                c3 = lambda t_: t_[:, :].rearrange("p (c j) -> p c j", j=64)
                c4 = lambda t_: t_[:, :].rearrange("p (h c j) -> p h c j", c=2, j=64)
                QSC = 128.0 ** -0.5
                st1 = {"th": 0}

                def nxt_th():
                    i = st1["th"] % 2
                    st1["th"] += 1
                    return tht[i], "tht%d" % i

                def sigm(dst, src_ap, r, w):
                    act(dst, src_ap, AF.Exp, r=r, w=w, scale=-1.0)
                    act(dst, dst, AF.Ln, r=w, w=w, bias=1.0)
                    act(dst, dst, AF.Exp, r=w, w=w, scale=-1.0)

                def stageX(t):
                    b = t % 2
                    b3 = t % 3
                    xtn, hTn = "xt%d" % b, "hT%d" % b
                    if t + 1 < NT:
                        ld(xt[1 - b][:, :], src_d[(t + 1) * 128:(t + 2) * 128, :], w=["xt%d" % (1 - b)])
                    norm_tile(xt[b][:, :], xtn, xn, ssq, hT[b], hTn, G1T, l, 0, (0, 0))
                    ld(hT_d[t, :, :], hT[b][:, :, :].rearrange("p k n -> p (k n)"), r=[hTn], w=["hTd%d" % t])
                    S_.unit()
                    for k in range(8):
                        mm(bank(1), hT[b][:, k, :], wA[:, k, 1024:1536], k == 0, k == 7, r=[hTn, "wA_v"], w=[PS(1)])
                    act(vtok[b3][:, :], bank(1), AF.Copy, r=[PS(1)], w=["vtok%d" % b3])
                    S_.unit()
                    for k in range(8):
                        mm(bank(2), hT[b][:, k, :], wA[:, k, 1536:2048], k == 0, k == 7, r=[hTn, "wA_g"], w=[PS(2)])
                    th_, thn_ = nxt_th()
                    sigm(th_[:, :], bank(2), [PS(2)], [thn_])
                    V(lambda e, th_=th_: e.tensor_tensor(gs[b3][:, :], th_[:, :], bank(2), ALU.mult),
                      r=[thn_, PS(2)], w=["gs%d" % b3])
                    S_.unit()
                    for f in range(4):
                        for k in range(8):
                            mm(bank(1)[:, f * 128:(f + 1) * 128], wA[:, k, f * 128:(f + 1) * 128], hT[b][:, k, :],
                               k == 0, k == 7, r=[hTn, "wA_q"], w=[PS(1)])
                    th_, thn_ = nxt_th()
                    sigm(th_[:, :], bank(1), [PS(1)], [thn_])
                    V(lambda e, th_=th_: e.tensor_tensor(qs[b][:, :], th_[:, :], bank(1), ALU.mult),
                      r=[thn_, PS(1)], w=["qs%d" % b])
                    S_.unit()
                    for f in range(4):
                        for k in range(8):
                            mm(bank(2)[:, f * 128:(f + 1) * 128], wA[:, k, 512 + f * 128:512 + (f + 1) * 128], hT[b][:, k, :],
                               k == 0, k == 7, r=[hTn, "wA_f"], w=[PS(2)])
                    sigm(sg[b][:, :], bank(2), [PS(2)], ["sg%d" % b])
                    S_.unit()
                    for f in range(4):
                        for k in range(8):
                            mm(bank(1)[:, f * 128:(f + 1) * 128], wA[:, k, 2048 + f * 128:2048 + (f + 1) * 128], hT[b][:, k, :],
                               k == 0, k == 7, r=[hTn, "wA_cu"], w=[PS(1)])
                    th_, thn_ = nxt_th()
                    sigm(th_[:, 0:256], bank(1)[:, 256:512], [PS(1)], [thn_])
                    V(lambda e, th_=th_: e.tensor_tensor(hcur[b][:, :, :].rearrange("p c t -> p (c t)"), th_[:, 0:256],
                                                         bank(1)[:, 0:256], ALU.mult),
                      r=[thn_, PS(1)], w=["hcur%d" % b])

                def stageY1(t):
                    b = t % 2
                    b3 = t % 3
                    qsn, sgn_, gsn, vtn, hcn, c6n = "qs%d" % b, "sg%d" % b, "gs%d" % b3, "vtok%d" % b3, "hcur%d" % b, "cat6_%d" % b
                    qs_, sg_, gs_, vt_, c6 = qs[b], sg[b], gs[b3], vtok[b3], cat6[b]
                    qtl_, ktl_, khtok_, qhA_, qhB_, E4_ = qtl[b], ktl[b], khtok[b], qhA[b], qhB[b], E4[b]
                    cc_, stt2, rsb_ = cc[b], stt_[b], rs_b[b]
                    qtln, ktln, khtokn, qhAn, qhBn, E4n = "qtl%d" % b, "ktl%d" % b, "khtok%d" % b, "qhA%d" % b, "qhB%d" % b, "E4%d" % b
                    ccn, sttn, rsbn = "cc%d" % b, "stt%d" % b, "rs_b%d" % b
                    G(lambda e: e.tensor_copy(hbuf[:, :, 32:160], hcur[b][:, :, :]), r=[hcn, "hbuf"], w=["hbuf"])
                    for ct in range(2):
                        for j in range(31):
                            mm(bank(3)[:, ct * 128:(ct + 1) * 128], dg[:, ct, j, :],
                               hbuf[:, ct, 2 + j:2 + j + 128], j == 0, j == 30, r=["dg", "hbuf"], w=[PS(3)])
                        S_.unit()
                    for ct in range(2):
                        cs = slice(ct * 128, (ct + 1) * 128)
                        act(cc_[:, cs], bank(3)[:, cs], AF.Identity,
                            r=[PS(3), "cvb"], w=[ccn], bias=cvb[:, l * 2 + ct:l * 2 + ct + 1])
                    act(csq[:, :], cc_[:, :], AF.Square, r=[ccn], w=["csq"])
                    G(lambda e: e.tensor_copy(hbuf[:, :, 0:32], hbuf[:, :, 128:160]), r=["hbuf", PS(3)], w=["hbuf"])
                    S_.unit()
                    for (si, src_, sn) in ((0, cc_, ccn), (1, csq, "csq")):
                        for ct in range(2):
                            mm(bank(4)[:, si * 128:(si + 1) * 128], onesM[:, :], src_[:, ct * 128:(ct + 1) * 128],
                               ct == 0, ct == 1, r=["onesM", sn], w=[PS(4)])
                    act(stt2[:, :], bank(4)[:, 0:256], AF.Copy, r=[PS(4)], w=[sttn])
                    V(lambda e: e.tensor_tensor(rsb_[:, :], stt2[:, 0:128], stt2[:, 0:128], ALU.mult), r=[sttn], w=[rsbn])
                    V(lambda e: e.tensor_tensor(rsb_[:, :], stt2[:, 128:256], rsb_[:, :], ALU.subtract),
                      r=[sttn, rsbn], w=[rsbn])
                    V(lambda e: e.tensor_scalar(rsb_[:, :], rsb_[:, :], EPS, None, ALU.add), r=[rsbn], w=[rsbn])
                    S_.unit()
                    V(lambda e: e.tensor_tensor(v3(t1), v3(sg_), oml_b, ALU.mult), r=[sgn_, "omlT"], w=["t1"])
                    V(lambda e: e.tensor_tensor(v3(fT), v3(t1), lb_b, ALU.add), r=["t1", "lbT"], w=["fT"])
                    V(lambda e: e.tensor_tensor(v3(kT), oml_b, v3(t1), ALU.subtract), r=["t1", "omlT"], w=["kT"])
                    V(lambda e: e.tensor_scalar(fT[:, :], fT[:, :], 1e-30, None, ALU.max), r=["fT"], w=["fT"])
                    act(fT[:, :], fT[:, :], AF.Ln, r=["fT"], w=["fT"])
                    act(rsb_[:, :], rsb_[:, :], AF.Ln, r=[rsbn], w=[rsbn])
                    act(rsb_[:, :], rsb_[:, :], AF.Exp, r=[rsbn], w=[rsbn], scale=-0.5)
                    S_.unit()
                    V(lambda e: e.tensor_tensor_scan(GT[:, :], scanm[:, :], fT[:, :], 0.0, ALU.mult, ALU.add),
                      r=["scanm", "fT"], w=["GT"])
                    V(lambda e: e.tensor_tensor(c3(t1), c3(GT), c3(GT)[:, :, 32:33].broadcast_to([128, 8, 64]),
                                                ALU.subtract), r=["GT"], w=["t1"])
                    act(E[:, :], t1[:, :], AF.Exp, r=["t1"], w=["E"])
                    V(lambda e: e.scalar_tensor_tensor(qtl_[:, :], qs_[:, :], QSC, E[:, :], ALU.mult, ALU.mult),
                      r=[qsn, "E"], w=[qtln])
                    S_.unit()
                    act(E[:, :], t1[:, :], AF.Exp, r=["t1", qtln], w=["E"], scale=-1.0)
                    V(lambda e: e.tensor_tensor(ktl_[:, :], kT[:, :], E[:, :], ALU.mult), r=["kT", "E"], w=[ktln])
                    V(lambda e: e.tensor_tensor(c3(t1), c3(GT)[:, :, 63:64].broadcast_to([128, 8, 64]), c3(GT),
                                                ALU.subtract), r=["GT", "E"], w=["t1"])
                    S_.unit()
                    act(E[:, :], t1[:, :], AF.Exp, r=["t1", ktln], w=["E"])
                    V(lambda e: e.tensor_tensor(khT[:, :], kT[:, :], E[:, :], ALU.mult), r=["kT", "E"], w=["khT"])
                    act(E4_[:, :], GT[:, :], AF.Exp, r=["GT"], w=[E4n])
                    S_.unit()
                    V(lambda e: e.scalar_tensor_tensor(c4(qhA_)[:, :, 0, :], c4(qs_)[:, :, 0, :], QSC, c4(E4_)[:, :, 0, :],
                                                       ALU.mult, ALU.mult), r=[qsn, E4n], w=[qhAn])
                    V(lambda e: e.scalar_tensor_tensor(c4(qhB_)[:, :, 1, :], c4(qs_)[:, :, 1, :], QSC, c4(E4_)[:, :, 1, :],
                                                       ALU.mult, ALU.mult), r=[qsn, E4n], w=[qhBn])
                    for h in range(4):
                        tr(bankb(4)[:, h * 128:(h + 1) * 128], khT[:, h * 128:(h + 1) * 128], identB[:, :],
                           r=["khT", "identB"], w=[PS(4)])
                    act(khtok_[:, :], bankb(4)[:, 0:512], AF.Copy, r=[PS(4)], w=[khtokn])
                    S_.unit()

                def stageY2(t):
                    b = t % 2
                    b3 = t % 3
                    qsn, sgn_, gsn, vtn, hcn, c6n = "qs%d" % b, "sg%d" % b, "gs%d" % b3, "vtok%d" % b3, "hcur%d" % b, "cat6_%d" % b
                    qs_, sg_, gs_, vt_, c6 = qs[b], sg[b], gs[b3], vtok[b3], cat6[b]
                    qtl_, ktl_, khtok_, qhA_, qhB_, E4_ = qtl[b], ktl[b], khtok[b], qhA[b], qhB[b], E4[b]
                    cc_, stt2, rsb_ = cc[b], stt_[b], rs_b[b]
                    qtln, ktln, khtokn, qhAn, qhBn, E4n = "qtl%d" % b, "ktl%d" % b, "khtok%d" % b, "qhA%d" % b, "qhB%d" % b, "E4%d" % b
                    ccn, sttn, rsbn = "cc%d" % b, "stt%d" % b, "rs_b%d" % b
                    for h in range(4):
                        mm(bank(5)[:, h * 128:(h + 1) * 128], ktl_[:, h * 128:(h + 1) * 128],
                           qtl_[:, h * 128:(h + 1) * 128], True, True, r=[ktln, qtln], w=[PS(5)])
                    V(lambda e: e.copy_predicated(ATm[:, :], cmask[:, :], bank(5)), r=[PS(5), "cmask"], w=["ATm"])
                    S_.unit()
                    for h in range(4):
                        hs = slice(h * 128, (h + 1) * 128)
                        mm(bank(6)[:, hs], ATm[:, hs], vt_[:, hs], h == 0, False, r=["ATm", vtn], w=[PS(6)])
                        mm(bank(6)[:, hs], qhA_[:, hs], Sbf0[:, hs], False, False, r=[qhAn, "Sbf0"], w=[PS(6)])
                    for h in range(4):
                        hs = slice(h * 128, (h + 1) * 128)
                        mm(bank(7)[:, hs], khtok_[0:64, hs], vt_[0:64, hs], True, True, r=[khtokn, vtn], w=[PS(7)])
                    dec = lambda c: c4(E4_)[:, :, c, 63:64].broadcast_to([128, 4, 128])
                    V(lambda e: e.tensor_tensor(v3(Sst), v3(Sst), dec(0), ALU.mult), r=["Sst", E4n], w=["Sst"])
                    V(lambda e: e.tensor_tensor(Sst[:, :], Sst[:, :], bank(7), ALU.add), r=["Sst", PS(7)], w=["Sst"])
                    act(Sbf1[:, :], Sst[:, :], AF.Copy, r=["Sst"], w=["Sbf1"])
                    S_.unit()
                    for h in range(4):
                        hs = slice(h * 128, (h + 1) * 128)
                        mm(bank(6)[:, hs], qhB_[:, hs], Sbf1[:, hs], False, True, r=[qhBn, "Sbf1"], w=[PS(6)])
                    for h in range(4):
                        hs = slice(h * 128, (h + 1) * 128)
                        mm(bank(7)[:, hs], khtok_[64:128, hs], vt_[64:128, hs], True, True, r=[khtokn, vtn], w=[PS(7)])
                    V(lambda e: e.tensor_tensor(v3(Sst), v3(Sst), dec(1), ALU.mult), r=["Sst", E4n], w=["Sst"])
                    V(lambda e: e.tensor_tensor(Sst[:, :], Sst[:, :], bank(7), ALU.add), r=["Sst", PS(7)], w=["Sst"])
                    act(Sbf0[:, :], Sst[:, :], AF.Copy, r=["Sst"], w=["Sbf0"])
                    S_.unit()
                    for h in range(4):
                        act(oa[:, h * 128:(h + 1) * 128], bank(6)[:, h * 128:(h + 1) * 128], AF.Square,
                            r=[PS(6)], w=["oa", "ss4"], accum_out=ss4[:, h:h + 1])
                    V(lambda e: e.tensor_scalar(ss4[:, 4:8], ss4[:, 0:4], 1.0 / 128, EPS, ALU.mult, ALU.add),
                      r=["ss4"], w=["ss4b"])
                    act(ss4[:, 8:12], ss4[:, 4:8], AF.Ln, r=["ss4b"], w=["ss4c"])
                    act(ss4[:, 12:16], ss4[:, 8:12], AF.Exp, r=["ss4c"], w=["ss4d"], scale=-0.5)
                    V(lambda e: e.tensor_tensor(v3(oa), v3(bank(6)),
                                                ss4[:, 12:16].rearrange("p (h o) -> p h o", o=1).broadcast_to([128, 4, 128]),
                                                ALU.mult), r=[PS(6), "ss4d", "oa"], w=["oa"])
                    V(lambda e: e.tensor_tensor(oab[:, :], oa[:, :], gs_[:, :], ALU.mult), r=["oa", gsn], w=["oab"])
                    S_.unit()
                    for h in range(4):
                        tr(bankb(5)[:, 512 + h * 128:512 + (h + 1) * 128], oab[:, h * 128:(h + 1) * 128], identB[:, :],
                           r=["oab", "identB"], w=[PS(5)])
                    for h in range(4):
                        act(c6[:, h, :], bankb(5)[:, 512 + h * 128:512 + (h + 1) * 128], AF.Copy,
                            r=[PS(5), "onwT"], w=[c6n], scale=onwT[:, l * 4 + h:l * 4 + h + 1])
                    S_.unit()
                    y3 = yv[:, :].rearrange("p (c t) -> p c t", t=128)
                    V(lambda e: e.tensor_tensor(y3, cc_[:, :].rearrange("p (c t) -> p c t", t=128),
                                                stt2[:, 0:128].rearrange("p (o t) -> p o t", o=1).broadcast_to([128, 2, 128]),
                                                ALU.subtract), r=[ccn, sttn], w=["yv"])
                    V(lambda e: e.tensor_tensor(y3, y3,
                                                rsb_[:, :].rearrange("p (o t) -> p o t", o=1).broadcast_to([128, 2, 128]),
                                                ALU.mult), r=["yv", rsbn], w=["yv"])
                    for ct in range(2):
                        cs = slice(ct * 128, (ct + 1) * 128)
                        V(lambda e, cs=cs, ct=ct: e.tensor_scalar(yv[:, cs], yv[:, cs], lnw[:, l * 2 + ct:l * 2 + ct + 1],
                                                                  lnb[:, l * 2 + ct:l * 2 + ct + 1], ALU.mult, ALU.add),
                          r=["yv", "lnw", "lnb"], w=["yv"])
                    sigm(thy[:, :], yv[:, :], ["yv"], ["thy"])
                    V(lambda e: e.tensor_tensor(c6[:, 4:6, :].rearrange("p c t -> p (c t)"), thy[:, :], yv[:, :], ALU.mult),
                      r=["thy", "yv"], w=[c6n])
                    ld(cat_d[t, :, :], c6[:, :, :].rearrange("p k n -> p (k n)"), r=[c6n], w=["catd%d" % t])

                def cap1(fn, t):
                    S_.begin(fine=True)
                    fn(t)
                    return S_.end()

                ld(xt[0][:, :], src_d[0:128, :], w=["xt0"])
                for step in range(NT + 2):
                    streams = []
                    if step - 2 >= 0:
                        streams.append(cap1(stageY2, step - 2))
                    if 0 <= step - 1 < NT:
                        streams.append(cap1(stageY1, step - 1))
                    if step < NT:
                        streams.append(cap1(stageX, step))
                    S_.run_merged(streams)
                    if step == 1:
                        prefetch_a2()
                S_.flush()
            if _STOP <= 1:
                return nc

            with ExitStack() as p2:
                KTc = sb(p2, "KTc", [128, 2, S], BF16)
                Vaug = sb(p2, "Vaug", [128, NT, 4, 65], BF16)
                kidx = sb(p2, "kidx", [128, S], BF16)
                score = [sb(p2, "score%d" % i, [128, S], F32) for i in range(3)]
                junkb = sb(p2, "junkb", [128, S], BF16)
                junk8 = sb(p2, "junk8", [128, S], U8)
                cntd = sb(p2, "cntd", [128, NIT], F32)
                vc = sb(p2, "vc", [128, NIT], F32)
                thrc = sb(p2, "thrc", [128, 4], F32)
                ones1 = sb(p2, "ones1", [128, 1], BF16)
                xt = [sb(p2, "x2t%d" % i, [128, D], F32) for i in range(3)]
                hT = [sb(p2, "h2T%d" % i, [128, 8, 128], BF16) for i in range(3)]
                catT = [sb(p2, "catT%d" % i, [128, 8, 128], BF16) for i in range(3)]
                sqT = [sb(p2, "sqT%d" % i, [128, 2, 256], BF16) for i in range(3)]
                identB2 = sb(p2, "identB2", [128, 256], BF16)
                iqT = sb(p2, "iqT", [128, 3, 128], BF16)
                wab = sb(p2, "wab", [128, 8], F32)
                wsg = sb(p2, "wsg", [128, 8], F32)
                Rb = [sb(p2, "Rb%d" % i, [128, 512], F32) for i in range(3)]
                Mb = [sb(p2, "Mb%d" % i, [128, 512], BF16) for i in range(3)]
                PT = [sb(p2, "PT%d" % i, [128, 512], BF16) for i in range(3)]
                ob = sb(p2, "ob", [128, 256], BF16)
                bis = sb(p2, "bis", [128, 16], F32)
                tabA = sb(p2, "tabA", [128, NIT], F32)
                tabB = sb(p2, "tabB", [128, NIT], F32)
                cntc = sb(p2, "cntc", [128, NIT], F32)
                uc = sb(p2, "uc", [128, NIT], F32)
                mid = sb(p2, "mid", [128, NIT + 1], F32)
                top8 = sb(p2, "top8", [128, 8], F32)
                rs4 = sb(p2, "rs4", [128, 4], F32)

                G(lambda e: e.memset(Vaug[:, :, :, 64:65], 1.0), w=["Vaug"])
                G(lambda e: e.memset(ones1[:, :], 1.0), w=["ones1"])
                for i in range(3):
                    G(lambda e, i=i: e.memset(sqT[i][:, :, :], 0.0), w=["sqT%d" % i])
                for c in range(2):
                    V(lambda e, c=c: e.tensor_copy(identB2[:, c * 128:(c + 1) * 128], identB[:, :]), r=["identB"], w=["identB2"])

                IDXC = (32.0 ** -0.5) * (8.0 ** -0.5)
                state = {"rbi": 0, "mbi": 0, "mgi": 0}

                def stagePS(t):
                    b = t % 3
                    xtn, hTn, cTn, scn, sqn = "x2t%d" % b, "h2T%d" % b, "catT%d" % b, "score%d" % b, "sqT%d" % b
                    sc = score[b]
                    N = (t + 1) * 128
                    tok = slice(t * 128, (t + 1) * 128)
                    ld(xt[b][:, :], src_d[tok, :], w=[xtn])
                    ld(hT[b][:, :, :].rearrange("p k n -> p (k n)"), hT_d[t, :, :], r=["hTd%d" % t], w=[hTn])
                    ld(catT[b][:, 0:4, :].rearrange("p k n -> p (k n)"), cat_d[t, :, 0:512], r=["catd%d" % t], w=[cTn])
                    ld(catT[b][:, 6:8, :].rearrange("p k n -> p (k n)"), cat_d[t, :, 512:768], r=["catd%d" % t], w=[cTn])
                    S_.unit()
                    for k in range(8):
                        mm(bank(0)[:, 0:256], hT[b][:, k, :], wB[:, k, 512:768], k == 0, k == 7, r=[hTn, "wB"], w=[PS(0)])
                    for k in range(8):
                        mm(bank(0)[:, 256:264], hT[b][:, k, :], wB[:, k, 1024:1032], k == 0, k == 7,
                           r=[hTn, "wB"], w=[PS(0)])
                    act(Vaug[:, t, :, 0:64], bank(0)[:, 0:256].rearrange("p (h d) -> p h d", d=64), AF.Copy,
                        r=[PS(0)], w=["Vaug"])
                    act(wab[:, :], bank(0)[:, 256:264], AF.Abs, r=[PS(0)], w=["wab"], scale=IDXC)
                    act(wsg[:, :], bank(0)[:, 256:264], AF.Sign, r=[PS(0)], w=["wsg"])
                    S_.unit()
                    for f in range(4):
                        cb = (0, 128, 256, 384)[f]
                        for k in range(8):
                            mm(bank(1)[:, f * 128:(f + 1) * 128], wB[:, k, cb:cb + 128], hT[b][:, k, :],
                               k == 0, k == 7, r=[hTn, "wB"], w=[PS(1)])
                    for hl in range(2):
                        rows = slice(64 * hl, 64 * hl + 64)
                        act(sqT[b][rows, :, hl * 128:(hl + 1) * 128],
                            bank(1)[rows, 0:256].rearrange("p (c t) -> p c t", t=128), AF.Copy,
                            r=[PS(1)], w=[sqn], scale=0.125)
                    V(lambda e: e.tensor_copy(KTc[:, :, tok], bank(1)[:, 256:512].rearrange("p (c t) -> p c t", t=128)),
                      r=[PS(1)], w=["KTc"])
                    S_.unit()
                    for f, (cb, m) in enumerate(((768, 96), (864, 96), (960, 64), (1032, 96))):
                        for k in range(8):
                            mm(bank(2)[0:m, f * 128:(f + 1) * 128], wB[:, k, cb:cb + m], hT[b][:, k, :],
                               k == 0, k == 7, r=[hTn, "wB"], w=[PS(2)])
                    act(iqT[0:96, :, :], bank(2)[0:96, 0:384].rearrange("p (c t) -> p c t", t=128), AF.Copy,
                        r=[PS(2)], w=["iqT"])
                    V(lambda e: e.tensor_copy(kidx[0:96, tok], bank(2)[0:96, 384:512]), r=[PS(2)], w=["kidx"])
                    S_.unit()
                    NB = (N + 511) // 512
                    prev_acc = [None]
                    for kb in range(NB):
                        wN = min(512, N - kb * 512)
                        ks_ = slice(kb * 512, kb * 512 + wN)
                        for hh in range(8):
                            g_, r_ = hh // 3, hh % 3
                            pb = 3 + (hh % 2)
                            mm(bank(pb)[:, 0:wN], iqT[32 * r_:32 * r_ + 32, g_, :], kidx[32 * r_:32 * r_ + 32, ks_],
                               True, True, r=["iqT", "kidx"], w=[PS(pb)])
                            R_ = Rb[state["rbi"] % 3]
                            Rn = "Rb%d" % (state["rbi"] % 3)
                            state["rbi"] += 1
                            act(R_[:, 0:wN], bank(pb)[:, 0:wN], AF.Relu, r=[PS(pb), "wab"], w=[Rn],
                                scale=wab[:, hh:hh + 1])

                            def acc(R_=R_, Rn=Rn, ks_=ks_, wN=wN, hh=hh):
                                if hh == 0:
                                    V(lambda e: e.tensor_scalar(sc[:, ks_], R_[:, 0:wN], wsg[:, 0:1], None, ALU.mult),
                                      r=[Rn, "wsg"], w=[scn])
                                else:
                                    V(lambda e: e.scalar_tensor_tensor(sc[:, ks_], R_[:, 0:wN], wsg[:, hh:hh + 1], sc[:, ks_],
                                                                       ALU.mult, ALU.add),
                                      r=[Rn, "wsg", scn], w=[scn])
                            if prev_acc[0] is not None:
                                prev_acc[0]()
                            prev_acc[0] = acc
                            S_.unit()
                    prev_acc[0]()
                    G(lambda e: e.affine_select(sc[:, tok], sc[:, tok], [[-1, 128]], ALU.is_ge, fillreg(e, -1e30),
                                                base=0, channel_multiplier=1), r=[scn], w=[scn])

                def stageBI(t):
                    b = t % 3
                    scn = "score%d" % b
                    sc = score[b]
                    N = (t + 1) * 128
                    thn = "thr%d" % b
                    if t * 128 < TOPK:
                        V(lambda e: e.tensor_copy(thrc[:, b:b + 1], thrneg[:, 0:1]), r=["thrneg"], w=[thn])
                        return
                    Ka = max(128, min(N - 128, int(round(0.66 * (t + 1))) * 128))
                    V(lambda e: e.max(top8[:, :], sc[:, 0:N]), r=[scn], w=["top8"])
                    S_.unit()
                    V(lambda e: e.tensor_reduce(bis[:, 0:1], sc[:, 0:TOPK], mybir.AxisListType.X, ALU.min),
                      r=[scn], w=["bis0"])
                    V(lambda e: e.tensor_tensor(bis[:, 1:2], top8[:, 0:1], bis[:, 0:1], ALU.subtract),
                      r=["top8", "bis0"], w=["bis1"])
                    V(lambda e: e.tensor_scalar(tabA[:, :], tabA0[:, :], bis[:, 1:2], None, ALU.mult),
                      r=["bis1", "tabA0"], w=["tabA"])
                    V(lambda e: e.tensor_scalar(tabB[:, :], tabB0[:, :], bis[:, 1:2], None, ALU.mult),
                      r=["bis1", "tabB0"], w=["tabB"])
                    V(lambda e: e.scalar_tensor_tensor(mid[:, 0:1], bis[:, 1:2], 0.5, bis[:, 0:1], ALU.mult, ALU.add),
                      r=["bis0", "bis1"], w=["mid"])
                    S_.unit()
                    for n in range(NIT):
                        act(junkb[:, 0:Ka], sc[:, 0:Ka], AF.Sign, r=[scn, "mid"], w=["junkb", "cntA"],
                            scale=-1.0, bias=mid[:, n:n + 1], accum_out=cntc[:, n:n + 1])
                        V(lambda e, n=n: e.scalar_tensor_tensor(junk8[:, Ka:N], sc[:, Ka:N], mid[:, n:n + 1],
                                                                ones1[:, 0:1].broadcast_to([128, N - Ka]),
                                                                ALU.is_ge, ALU.mult, accum_out=cntd[:, n:n + 1]),
                          r=[scn, "mid", "ones1"], w=["junk8", "cntD"])
                        V(lambda e, n=n: e.scalar_tensor_tensor(vc[:, n:n + 1], cntd[:, n:n + 1], 2.0, cntc[:, n:n + 1],
                                                                ALU.mult, ALU.subtract),
                          r=["cntA", "cntD"], w=["vc"])
                        V(lambda e, n=n: e.scalar_tensor_tensor(uc[:, n:n + 1], vc[:, n:n + 1], float(2 * TOPK - Ka - 1),
                                                                tabB[:, n:n + 1], ALU.is_gt, ALU.mult),
                          r=["vc", "tabB"], w=["uc"])
                        V(lambda e, n=n: e.scalar_tensor_tensor(mid[:, n + 1:n + 2], mid[:, n:n + 1], tabA[:, n:n + 1],
                                                                uc[:, n:n + 1], ALU.subtract, ALU.add),
                          r=["mid", "tabA", "uc"], w=["mid"])
                        S_.unit()
                    V(lambda e: e.tensor_copy(thrc[:, b:b + 1], mid[:, NIT:NIT + 1]), r=["mid"], w=[thn])

                def stageAT(t):
                    b = t % 3
                    xtn, cTn, scn, sqn, thn = "x2t%d" % b, "catT%d" % b, "score%d" % b, "sqT%d" % b, "thr%d" % b
                    sc = score[b]
                    tok = slice(t * 128, (t + 1) * 128)
                    thr = thrc[:, b:b + 1]
                    prev_pv = [None]
                    NG = (t + 4) // 4
                    Nq = (t + 1) * 128
                    mg0 = state["mgi"]
                    state["mgi"] += NG

                    def genmask(g):
                        if g >= NG:
                            return
                        w_ = min(512, Nq - g * 512)
                        gi = mg0 + g
                        M_ = Mb[gi % 3]
                        V(lambda e: e.tensor_scalar(M_[:, 0:w_], sc[:, g * 512:g * 512 + w_], thr, -30000.0, ALU.is_lt, ALU.mult),
                          r=[scn, thn], w=["Mb%d" % (gi % 3)])
                    genmask(0)
                    genmask(1)
                    for kb in range(t + 1):
                        kcs = slice(kb * 128, (kb + 1) * 128)
                        mbi = state["mbi"]
                        state["mbi"] += 1
                        g = kb // 4
                        if kb % 4 == 0:
                            genmask(g + 2)
                        gi = mg0 + g
                        M_ = Mb[gi % 3][:, (kb % 4) * 128:(kb % 4 + 1) * 128]
                        Mn = "Mb%d" % (gi % 3)
                        P_ = PT[mbi % 3]
                        Pn = "PT%d" % (mbi % 3)
                        pb = 5 + (mbi % 2)
                        for c in range(2):
                            mm(bank(pb)[:, c * 256:(c + 1) * 256], KTc[:, c, kcs], sqT[b][:, c, :], True, False,
                               r=["KTc", sqn], w=[PS(pb)])
                            mm(bank(pb)[:, c * 256:(c + 1) * 256], M_, identB2[:, :], False, True,
                               r=[Mn, "identB2"], w=[PS(pb)])
                        act(P_[:, :], bank(pb), AF.Exp, r=[PS(pb)], w=[Pn])

                        def pv(kb=kb, P_=P_, Pn=Pn):
                            for h in range(4):
                                mm(bank(7)[:, h * 65:(h + 1) * 65], P_[:, h * 128:(h + 1) * 128], Vaug[:, kb, h, :],
                                   kb == 0 and h == 0, kb == t, r=[Pn, "Vaug"], w=[PS(7)])
                        if prev_pv[0] is not None:
                            prev_pv[0]()
                        prev_pv[0] = pv
                        S_.unit()
                    prev_pv[0]()
                    o3 = bank(7)[:, 0:260].rearrange("p (h d) -> p h d", d=65)
                    V(lambda e: e.reciprocal(rs4[:, :].rearrange("p (h o) -> p h o", o=1), o3[:, :, 64:65]), r=[PS(7)], w=["rs4"])
                    V(lambda e: e.tensor_tensor(ob[:, :].rearrange("p (h d) -> p h d", d=64), o3[:, :, 0:64],
                                                rs4[:, :].rearrange("p (h o) -> p h o", o=1).broadcast_to([128, 4, 64]),
                                                ALU.mult), r=[PS(7), "rs4"], w=["ob"])
                    for c in range(2):
                        tr(bankb(7)[:, c * 128:(c + 1) * 128], ob[:, c * 128:(c + 1) * 128], identB[:, :],
                           r=["ob", "identB"], w=[PS(7)])
                    act(catT[b][:, 4:6, :], bankb(7)[:, 0:256].rearrange("p (c t) -> p c t", t=128), AF.Copy,
                        r=[PS(7)], w=[cTn])
                    S_.unit()
                    for hf in range(2):
                        for c in range(8):
                            mm(bank(5 + hf), catT[b][:, c, :], woS[:, c, hf * 512:(hf + 1) * 512], c == 0, c == 7,
                               r=[cTn, "woS"], w=[PS(5 + hf)])
                        V(lambda e, hf=hf: e.tensor_tensor(xt[b][:, hf * 512:(hf + 1) * 512],
                                                           xt[b][:, hf * 512:(hf + 1) * 512], bank(5 + hf), ALU.add),
                          r=[xtn, PS(5 + hf)], w=[xtn])
                        S_.unit()
                    ld(out_d[tok, :], xt[b][:, :], r=[xtn], w=["outd%d" % t])

                def cap_(fn, t):
                    S_.begin(fine=True)
                    fn(t)
                    return S_.end()

                for step in range(NT + 2):
                    streams = []
                    if step - 2 >= 0:
                        streams.append(cap_(stageAT, step - 2))
                    if 0 <= step - 1 < NT:
                        streams.append(cap_(stageBI, step - 1))
                    if step < NT:
                        streams.append(cap_(stagePS, step))
                    S_.run_merged(streams)
                S_.flush()
            if _STOP <= 2:
                return nc

            pl.__exit__(None, None, None)
            with ExitStack() as p3:
                w1S = sb(p3, "w1S", [128, 8, DFF], BF16)
                w2S = sb(p3, "w2S", [128, 32, D], BF16)
                g2bc = sb(p3, "g2bc", [128, D], F32)
                dtmp = sb(p3, "dtmp3", [128, 128], F32)
                xtN = [sb(p3, "x3n%d" % i, [128, D], F32) for i in range(2)]
                xr = [sb(p3, "x3r%d" % i, [128, D], F32) for i in range(2)]
                xn = sb(p3, "xn3", [128, D], F32)
                ssq = sb(p3, "ssq3", [128, 4], F32)
                ssf = sb(p3, "ssf3", [128, 4], F32)
                hTb = sb(p3, "hTb", [128, 8, 512], BF16)
                h1raw = sb(p3, "h1raw", [128, 8192], F32)
                h1T = h1raw[:, :].bitcast(BF16).rearrange("p (f t) -> p f t", t=512)
                wst = [h1raw[:, 0:1024], h1raw[:, 1024:2048]]
                wst_b = [h1raw[:, 2048:3072], h1raw[:, 3072:4096]]
                rl = [sb(p3, "rl%d" % i, [128, 512], F32) for i in range(2)]
                last = (l == L - 1)
                NBLK = S // 512
                stB = {"ri": 0, "ni": 0, "xi": 0}

                def stageN(blk):
                    for i in range(4):
                        t = blk * 4 + i
                        j = stB["ni"] % 2
                        stB["ni"] += 1
                        ld(xtN[j][:, :], out_d[t * 128:(t + 1) * 128, :], r=["outd%d" % t], w=["x3n%d" % j])
                        norm_tile(xtN[j][:, :], "x3n%d" % j, xn, ssq, hTb, "hTb", G2T, l, 3, (0, 1),
                                  ncols=128, coff=i * 128)
                        S_.unit()

                w1i = 0
                for cb in range(4):
                    for k in range(8):
                        stg, stn = wst_b[w1i % 2], "w1st%d" % (w1i % 2)
                        w1i += 1
                        ld(stg, w1_d[l, k * 128:(k + 1) * 128, cb * 1024:(cb + 1) * 1024], w=[stn])
                        act(w1S[:, k, cb * 1024:(cb + 1) * 1024], stg, AF.Copy, r=[stn], w=["w1S_%d" % cb])
                stageN(0)
                build_bc(g2bc, "g2bc", l, 5, dtmp, "dtmp3", (0, 1))
                for k in range(32):
                    ld(wst[k % 2], w2_d[l, k * 128:(k + 1) * 128, :], w=["w2st%d" % (k % 2)])
                    eng = V if k % 2 == 0 else G
                    eng(lambda e, k=k: e.tensor_tensor(w2S[:, k, :], wst[k % 2], g2bc[:, :], ALU.mult),
                        r=["w2st%d" % (k % 2), "g2bc"], w=["w2S_%d" % k])
                if last:
                    ld(g2bc[:, :], fnw_d[0:1, :].partition_broadcast(128), r=["w2S_%d" % k for k in range(32)], w=["g2bc"])
                def stageM1(blk):
                    for f in range(32):
                        pb = 2 + (f % 2)
                        for k in range(8):
                            mm(bank(pb), w1S[:, k, f * 128:(f + 1) * 128], hTb[:, k, :], k == 0, k == 7,
                               r=["w1S_%d" % (f // 8), "hTb"], w=[PS(pb)])
                        r_ = rl[stB["ri"] % 2]
                        rn = "rl%d" % (stB["ri"] % 2)
                        stB["ri"] += 1
                        act(r_[:, :], bank(pb), AF.Relu, r=[PS(pb)], w=[rn])
                        G(lambda e, r_=r_, f=f: e.tensor_tensor(h1T[:, f, :], r_[:, :], r_[:, :], ALU.mult),
                          r=[rn], w=["h1T", "w2st0", "w2st1", "w1st0", "w1st1"])

                def stageM2(blk):
                    for i in range(4):
                        t = blk * 4 + i
                        j = stB["xi"] % 2
                        stB["xi"] += 1
                        xb, xbn = xr[j], "x3r%d" % j
                        ld(xb[:, :], out_d[t * 128:(t + 1) * 128, :], r=["outd%d" % t], w=[xbn])
                        for hf in range(2):
                            pb = 4 + 2 * (i % 2) + hf
                            for f in range(32):
                                mm(bank(pb), h1T[:, f, i * 128:(i + 1) * 128], w2S[:, f, hf * 512:(hf + 1) * 512],
                                   f == 0, f == 31, r=["h1T", "w2S_%d" % f], w=[PS(pb)])
                                if f % 8 == 7:
                                    S_.unit()
                            V(lambda e, xb=xb, hf=hf, pb=pb: e.tensor_tensor(xb[:, hf * 512:(hf + 1) * 512],
                                                                             xb[:, hf * 512:(hf + 1) * 512], bank(pb), ALU.add),
                              r=[xbn, PS(pb)], w=[xbn])
                        if last:
                            act(rl[0][:, :].bitcast(BF16), xb[:, :], AF.Square, r=[xbn], w=["rl0", "ssf"], accum_out=ssf[:, 0:1])
                            V(lambda e: e.tensor_scalar(ssf[:, 1:2], ssf[:, 0:1], 1.0 / D, EPS, ALU.mult, ALU.add),
                              r=["ssf"], w=["ssf1"])
                            act(ssf[:, 2:3], ssf[:, 1:2], AF.Ln, r=["ssf1"], w=["ssf2"])
                            act(ssf[:, 3:4], ssf[:, 2:3], AF.Exp, r=["ssf2"], w=["ssf3"], scale=-0.5)
                            V(lambda e, xb=xb: e.scalar_tensor_tensor(xb[:, :], xb[:, :], ssf[:, 3:4], g2bc[:, :],
                                                                      ALU.mult, ALU.mult),
                              r=[xbn, "ssf3", "g2bc"], w=[xbn])
                        ld(out_d[t * 128:(t + 1) * 128, :], xb[:, :], r=[xbn], w=["outd%d" % t])
                        S_.unit()

                def capB(fn, blk):
                    S_.begin(fine=True)
                    fn(blk)
                    return S_.end()

                for blk in range(NBLK):
                    stageM1(blk)
                    streams = [capB(stageM2, blk)]
                    if blk + 1 < NBLK:
                        streams.append(capB(stageN, blk + 1))
                    S_.run_merged(streams)
                S_.flush()
    return nc


def _colsT(v, L, n):
    v = np.asarray(v, np.float32).reshape(L, n, 128)
    return np.ascontiguousarray(v.transpose(2, 0, 1).reshape(128, L * n))


def make_in_maps(inp, L, nb):
    f = lambda a: np.ascontiguousarray(np.asarray(a, np.float32))
    shared = {
        "ada_w": f(inp["ada_w"]),
        "ada_bT": _colsT(inp["ada_b"], L, 48),
        "nmixT": _colsT(inp["norm_mix_w"], L, 8),
        "nmlpT": _colsT(inp["norm_mlp_w"], L, 8),
        "fnw": f(inp["final_norm_w"]).reshape(1, D),
        "w_in": f(inp["w_in"]), "w_out": f(inp["w_out"]), "w1": f(inp["mlp_w1"]), "w2": f(inp["mlp_w2"]),
        "lbT": _colsT(inp["hg_lb_logits"], L, 4),
        "onwT": _colsT(inp["hg_onorm_w"], L, 4),
        "cvw": np.ascontiguousarray(np.asarray(inp["cv_w"], np.float32).reshape(L, 31, 2, 128)
                                    .transpose(3, 0, 2, 1).reshape(128, L * 62)),
        "cvb": _colsT(inp["cv_b"], L, 2),
        "lnw": _colsT(inp["cv_ln_w"], L, 2),
        "lnb": _colsT(inp["cv_ln_b"], L, 2),
    }
    maps = []
    x = np.asarray(inp["x"], np.float32)
    c = np.asarray(inp["c"], np.float32)
    for b in range(nb):
        m = dict(shared)
        m["x"] = np.ascontiguousarray(x[b])
        m["cT"] = np.ascontiguousarray(c[b].reshape(8, 128).T)
        maps.append(m)
    return maps


_NC_CACHE = {}


def kernel(x, c, ada_w, ada_b, norm_mix_w, norm_mlp_w, w_in, hg_lb_logits, hg_onorm_w,
           cv_w, cv_b, cv_ln_w, cv_ln_b, w_out, mlp_w1, mlp_w2, final_norm_w):
    inp = dict(x=x, c=c, ada_w=ada_w, ada_b=ada_b, norm_mix_w=norm_mix_w, norm_mlp_w=norm_mlp_w, w_in=w_in,
               hg_lb_logits=hg_lb_logits, hg_onorm_w=hg_onorm_w, cv_w=cv_w, cv_b=cv_b, cv_ln_w=cv_ln_w,
               cv_ln_b=cv_ln_b, w_out=w_out, mlp_w1=mlp_w1, mlp_w2=mlp_w2, final_norm_w=final_norm_w)
    B, S, _ = np.asarray(x).shape
    L = np.asarray(w_in).shape[0]
    topk = min(256, S // 4)
    key = (S, L, topk)
    if key not in _NC_CACHE:
        _NC_CACHE[key] = build_nc(S, L, topk)
    nc = _NC_CACHE[key]
    maps = make_in_maps(inp, L, B)
    res = run_bass_kernel_spmd(nc, maps, core_ids=list(range(B)))
    return np.stack([np.asarray(r["out"], np.float32) for r in res.results], axis=0)
```

```python
import numpy as np
from contextlib import ExitStack
import concourse.bass as bass
import concourse.mybir as mybir
from concourse.bass_utils import run_bass_kernel_spmd

F32 = mybir.dt.float32
BF16 = mybir.dt.bfloat16
U8 = mybir.dt.uint8
AF = mybir.ActivationFunctionType
ALU = mybir.AluOpType

D = 1024
DIN = 3624
DFF = 4096
EPS = 1e-6
NIT = 16
ENGS = ("tensor", "vector", "scalar", "gpsimd", "sync")
NDMA_SEMS = 24
import os as _os
_STOP = int(_os.environ.get('KSTOP', '99'))
_CUT = int(_os.environ.get('KCUT', '99'))
_SUB = int(_os.environ.get('KSUB', '99'))


class _Op:
    __slots__ = ("eng", "fn", "deps", "signals", "sem", "val", "is_dma")

    def __init__(self, eng, fn, is_dma=False):
        self.eng = eng
        self.fn = fn
        self.deps = []
        self.signals = False
        self.sem = None
        self.val = 0
        self.is_dma = is_dma


class _Slot:
    __slots__ = ("writer", "readers")

    def __init__(self):
        self.writer = None
        self.readers = []


class Sched:
    def __init__(self, nc, es):
        self.nc = nc
        self.q = {e: [] for e in ENGS}
        self.slots = {}
        self.phase_dmas = []
        self.esem = {e: es.enter_context(nc.semaphore("es_" + e)) for e in ENGS}
        self.dsem = {e: [es.enter_context(nc.semaphore("ds_%s_%d" % (e, i))) for i in range(NDMA_SEMS)]
                     for e in ("sync", "gpsimd")}
        self.cnt = {e: 0 for e in ENGS}
        self.dcnt = {e: [0] * NDMA_SEMS for e in self.dsem}
        self.drr = {e: 0 for e in self.dsem}
        self.dprev = {e: [None] * NDMA_SEMS for e in self.dsem}
        self.waited = {e: {} for e in ENGS}
        self.nops = 0

    def _slot(self, k):
        s = self.slots.get(k)
        if s is None:
            s = self.slots[k] = _Slot()
        return s

    def _add(self, op, reads, writes):
        deps = set()
        for k in reads:
            s = self._slot(k)
            if s.writer is not None:
                deps.add(s.writer)
            if k.startswith("ps"):
                for r in s.readers:
                    if r.eng != op.eng:
                        deps.add(r)
        for k in writes:
            s = self._slot(k)
            if s.writer is not None:
                deps.add(s.writer)
            for r in s.readers:
                deps.add(r)
        deps.discard(op)
        for d in deps:
            if d.eng == "tensor" and op.eng == "tensor":
                continue
            op.deps.append(d)
            d.signals = True
        for k in writes:
            s = self._slot(k)
            s.writer = op
            s.readers = []
        for k in reads:
            if k not in writes:
                self._slot(k).readers.append(op)
        self.q[op.eng].append(op)
        self.nops += 1
        return op

    cap = None

    fine = False

    def begin(self, fine=False):
        self.cap = [[]]
        self.fine = fine

    def unit(self):
        if self.cap is not None and self.cap[-1]:
            self.cap.append([])

    def end(self):
        u = [x for x in self.cap if x]
        self.cap = None
        return u

    def run_merged(self, streams):
        pos = [0] * len(streams)
        while True:
            best, bf = -1, 2.0
            for i, s in enumerate(streams):
                if pos[i] < len(s):
                    f = (pos[i] + 0.5) / len(s)
                    if f < bf:
                        best, bf = i, f
            if best < 0:
                break
            for item in streams[best][pos[best]]:
                if item[0] == "op":
                    self.op(*item[1:])
                else:
                    self.dma(item[1], item[2], item[3], item[4], item[5], **item[6])
            pos[best] += 1

    def op(self, eng, fn, reads=(), writes=()):
        if self.cap is not None:
            self.cap[-1].append(("op", eng, fn, tuple(reads), tuple(writes)))
            if self.fine and eng != "tensor":
                self.cap.append([])
            return None
        return self._add(_Op(eng, fn), list(reads), list(writes))

    def dma(self, eng, out, in_, reads=(), writes=(), **kw):
        if self.cap is not None:
            self.cap[-1].append(("dma", eng, out, in_, tuple(reads), tuple(writes), kw))
            return None
        fn = lambda e: e.dma_start(out=out, in_=in_, **kw)
        op = _Op(eng, fn, is_dma=True)
        op.signals = True
        self._add(op, list(reads), list(writes))
        self.phase_dmas.append(op)
        return op

    def flush(self):
        nc = self.nc
        fin = _Op("sync", None)
        fin.deps = list(self.phase_dmas)
        self.phase_dmas = []
        self.q["sync"].append(fin)
        for e in ENGS:
            for op in self.q[e]:
                if op.fn is None:
                    continue
                if op.is_dma:
                    j = self.drr[e]
                    self.drr[e] = (j + 1) % NDMA_SEMS
                    self.dcnt[e][j] += 16
                    op.sem = self.dsem[e][j]
                    op.val = self.dcnt[e][j]
                    prev = self.dprev[e][j]
                    if prev is not None:
                        op.deps.append(prev)
                    self.dprev[e][j] = op
                elif op.signals:
                    self.cnt[e] += 1
                    op.sem = self.esem[e]
                    op.val = self.cnt[e]
        q = self.q
        waited_all = self.waited

        def run(e):
            def body(eng):
                waited = waited_all[e]
                for op in q[e]:
                    for d in op.deps:
                        if d.sem is None:
                            continue
                        key = id(d.sem)
                        if waited.get(key, 0) < d.val:
                            eng.wait_ge(d.sem, d.val)
                            waited[key] = d.val
                    if op.fn is None:
                        continue
                    ins = op.fn(eng)
                    if op.signals:
                        ins.then_inc(op.sem, 16 if op.is_dma else 1)
            return body

        with nc.Block() as block:
            if q["tensor"]:
                block.tensor(run("tensor"))
            if q["vector"]:
                block.vector(run("vector"))
            if q["scalar"]:
                block.scalar(run("scalar"))
            if q["gpsimd"]:
                block.gpsimd(run("gpsimd"))
            block.sync(run("sync"))
        self.q = {e: [] for e in ENGS}


def build_nc(S, L, TOPK, dbg=False):
    NT = S // 128
    nc = bass.Bass("TRN2", target_bir_lowering=False)

    def din(name, shape, dt=F32):
        return nc.dram_tensor(name, list(shape), dt, kind="ExternalInput").ap()

    x_d = din("x", [S, D])
    cT_d = din("cT", [128, 8])
    adaw_d = din("ada_w", [L, D, 6 * D])
    adabT_d = din("ada_bT", [128, L * 48])
    nmixT_d = din("nmixT", [128, L * 8])
    nmlpT_d = din("nmlpT", [128, L * 8])
    fnw_d = din("fnw", [1, D])
    win_d = din("w_in", [L, D, DIN])
    wout_d = din("w_out", [L, D, D])
    w1_d = din("w1", [L, D, DFF])
    w2_d = din("w2", [L, DFF, D])
    lbT_d = din("lbT", [128, L * 4])
    onwT_d = din("onwT", [128, L * 4])
    cvw_d = din("cvw", [128, L * 62])
    cvb_d = din("cvb", [128, L * 2])
    lnw_d = din("lnw", [128, L * 2])
    lnb_d = din("lnb", [128, L * 2])
    out_d = nc.dram_tensor("out", [S, D], F32, kind="ExternalOutput").ap()
    hT_d = nc.dram_tensor("hT_scr", [NT, 128, 1024], BF16, kind="Internal").ap()
    cat_d = nc.dram_tensor("cat_scr", [NT, 128, 768], BF16, kind="Internal").ap()

    with ExitStack() as es:
        S_ = Sched(nc, es)

        def V(fn, r=(), w=()):
            return S_.op("vector", fn, r, w)

        def A(fn, r=(), w=()):
            return S_.op("scalar", fn, r, w)

        def G(fn, r=(), w=()):
            return S_.op("gpsimd", fn, r, w)

        def T(fn, r=(), w=()):
            return S_.op("tensor", fn, r, w)

        _fillregs = {}

        def fillreg(e, val):
            if val not in _fillregs:
                _fillregs[val] = e.to_reg(val)
            return _fillregs[val]

        def mm(out, lhsT, rhs, start, stop, r, w):
            return T(lambda e: e.matmul(out, lhsT, rhs, start=start, stop=stop, skip_group_check=True), r, w)

        def tr(out, in_, ident, r, w):
            return T(lambda e: e.transpose(out, in_, ident), r, w)

        def act(out, in_, func, r, w, **kw):
            return A(lambda e: e.activation(out, in_, func, **kw), r, w)

        def ld(out, in_, w, r=(), **kw):
            return S_.dma("sync", out, in_, reads=r, writes=w, **kw)

        def ldc(out, in_, w, r=()):
            return S_.dma("gpsimd", out, in_, reads=r, writes=w, max_dma_last_dim=2048)

        _uid = [0]

        def sb(stack, name, shape, dt):
            _uid[0] += 1
            return stack.enter_context(nc.sbuf_tensor("s%d_%s" % (_uid[0], name), list(shape), dt))

        pst = [es.enter_context(nc.psum_tensor("pst%d" % i, [128, 1024], F32)) for i in range(4)]

        def bank(i):
            return pst[i // 2][:, (i % 2) * 512:(i % 2 + 1) * 512]

        def bankb(i):
            return bank(i).bitcast(BF16)

        PS = lambda i: "ps%d" % i

        identF = sb(es, "identF", [128, 128], F32)
        identB = sb(es, "identB", [128, 128], BF16)
        onesM = sb(es, "onesM", [128, 128], F32)
        cT = sb(es, "cTs", [128, 8], F32)
        cact = sb(es, "cact", [128, 8], F32)
        modT = sb(es, "modT", [128, L * 48], F32)
        adabT = sb(es, "adabT", [128, L * 48], F32)
        nmixT = sb(es, "nmixT", [128, L * 8], F32)
        nmlpT = sb(es, "nmlpT", [128, L * 8], F32)
        G1T = sb(es, "G1T", [128, L * 8], F32)
        G2T = sb(es, "G2T", [128, L * 8], F32)
        lbT = sb(es, "lbT", [128, L * 4], F32)
        omlT = sb(es, "omlT", [128, L * 4], F32)
        lbtmp = sb(es, "lbtmp", [128, 8], F32)
        onwT = sb(es, "onwT", [128, L * 4], F32)
        cvw = sb(es, "cvw", [128, L * 62], F32)
        cvb = sb(es, "cvb", [128, L * 2], F32)
        lnw = sb(es, "lnw", [128, L * 2], F32)
        lnb = sb(es, "lnb", [128, L * 2], F32)
        tabA0 = sb(es, "tabA0", [128, NIT], F32)
        tabB0 = sb(es, "tabB0", [128, NIT], F32)
        thrneg = sb(es, "thrneg", [128, 1], F32)
        evt = sb(es, "evt", [128, 512], F32)
        negh = sb(es, "negh", [128, 128], F32)
        omlh = sb(es, "omlh", [128, L * 4], F32)
        lbp = sb(es, "lbp", [128, L * 4], F32)
        onwh = sb(es, "onwh", [128, L * 4], F32)
        lnwh = sb(es, "lnwh", [128, L * 2], F32)
        lnbh = sb(es, "lnbh", [128, L * 2], F32)

        def modcol(l, j, k):
            c = l * 48 + j * 8 + k
            return modT[:, c:c + 1]

        with ExitStack() as ps_:
            stage = [sb(ps_, "adast%d" % i, [128, 8, 512], F32) for i in range(2)]
            rowS = [sb(ps_, "rowS%d" % i, [1, 512], F32) for i in range(2)]
            one1f = sb(ps_, "one1f", [1, 1], F32)
            G(lambda e: e.memset(one1f[:, :], 1.0), w=["one1f"])
            G(lambda e: e.memset(identF[:, :], 1.0), w=["identF"])
            G(lambda e: e.affine_select(identF[:, :], identF[:, :], [[-1, 128]], ALU.is_equal, fillreg(e, 0.0),
                                        base=0, channel_multiplier=1), r=["identF"], w=["identF"])
            V(lambda e: e.tensor_copy(identB[:, :], identF[:, :]), r=["identF"], w=["identB"])
            G(lambda e: e.memset(onesM[:, :], 1.0 / 256.0), w=["onesM"])
            G(lambda e: e.memset(thrneg[:, :], -1e29), w=["thrneg"])
            for n in range(NIT):
                a_n = 2.0 ** -(n + 2) if n < NIT - 1 else 2.0 ** -(NIT)
                b_n = 2.0 ** -(n + 1)
                G(lambda e, n=n, a_n=a_n: e.memset(tabA0[:, n:n + 1], a_n), w=["tabA0"])
                G(lambda e, n=n, b_n=b_n: e.memset(tabB0[:, n:n + 1], b_n), w=["tabB0"])
            for (dst, src, nm) in ((cT, cT_d, "cT"), (adabT, adabT_d, "adabT"), (nmixT, nmixT_d, "nmixT"),
                                   (nmlpT, nmlpT_d, "nmlpT"), (lbT, lbT_d, "lbT"), (onwT, onwT_d, "onwT"),
                                   (cvw, cvw_d, "cvw"), (cvb, cvb_d, "cvb"), (lnw, lnw_d, "lnw"),
                                   (lnb, lnb_d, "lnb")):
                ld(dst[:, :], src[:, :], w=[nm])
            act(cact[:, :], cT[:, :], AF.Silu, r=["cT"], w=["cact"])
            lb3 = lbT[:, :].rearrange("p (l h) -> p l h", h=4)
            act(lbT[:, :], lbT[:, :], AF.Exp, r=["lbT"], w=["lbT"])
            V(lambda e: e.tensor_copy(lbtmp[:, 0:4], lb3[:, 0, :]), r=["lbT"], w=["lbtmp"])
            for l in range(1, L):
                V(lambda e, l=l: e.tensor_tensor(lbtmp[:, 0:4], lbtmp[:, 0:4], lb3[:, l, :], ALU.add),
                  r=["lbT", "lbtmp"], w=["lbtmp"])
            V(lambda e: e.reciprocal(lbtmp[:, 4:8], lbtmp[:, 0:4]), r=["lbtmp"], w=["lbtmp2"])
            for l in range(L):
                V(lambda e, l=l: e.tensor_tensor(lb3[:, l, :], lb3[:, l, :], lbtmp[:, 4:8], ALU.mult),
                  r=["lbT", "lbtmp2"], w=["lbT"])
            V(lambda e: e.memset(lb3[:, 0, :], 0.0), r=["lbT"], w=["lbT"])
            for l in range(2, L):
                V(lambda e, l=l: e.tensor_tensor(lb3[:, l, :], lb3[:, l, :], lb3[:, l - 1, :], ALU.add),
                  r=["lbT"], w=["lbT"])
            V(lambda e: e.tensor_scalar(omlT[:, :], lbT[:, :], -1.0, 1.0, ALU.mult, ALU.add),
              r=["lbT"], w=["omlT"])
            V(lambda e: e.tensor_scalar(omlh[:, :], omlT[:, :], 0.5, None, ALU.mult), r=["omlT"], w=["omlh"])
            V(lambda e: e.tensor_tensor(lbp[:, :], lbT[:, :], omlh[:, :], ALU.add), r=["lbT", "omlh"], w=["lbp"])
            V(lambda e: e.tensor_scalar(onwh[:, :], onwT[:, :], 0.5, None, ALU.mult), r=["onwT"], w=["onwh"])
            V(lambda e: e.tensor_scalar(lnwh[:, :], lnw[:, :], 0.5, None, ALU.mult), r=["lnw"], w=["lnwh"])
            V(lambda e: e.tensor_scalar(lnbh[:, :], lnb[:, :], 0.5, None, ALU.mult), r=["lnb"], w=["lnbh"])
            G(lambda e: e.memset(negh[:, :], -0.5), w=["negh"])
            piece = 0
            for l in range(L):
                for pc in range(12):
                    st = stage[piece % 2]
                    sn = "adast%d" % (piece % 2)
                    ld(st[:, :, :], adaw_d[l, :, pc * 512:(pc + 1) * 512].rearrange("(k p) n -> p k n", p=128),
                       w=[sn])
                    rb = 2 + (piece % 2)
                    for k in range(8):
                        mm(bank(rb)[0:1, :], cact[:, k:k + 1], st[:, k, :], k == 0, k == 7, r=[sn, "cact"], w=[PS(rb)])
                    rw = rowS[piece % 2]
                    rwn = "rowS%d" % (piece % 2)
                    act(rw[0:1, :], bank(rb)[0:1, :], AF.Copy, r=[PS(rb)], w=[rwn])
                    for f in range(4):
                        col = l * 48 + pc * 4 + f
                        mm(bank(0)[:, col:col + 1], rw[0:1, f * 128:(f + 1) * 128], one1f[0:1, 0:1],
                           piece == 0 and f == 0, True, r=[rwn, "one1f"], w=[PS(0)])
                    piece += 1
            V(lambda e: e.tensor_tensor(modT[:, :], bank(0)[:, 0:L * 48], adabT[:, :], ALU.add),
              r=[PS(0), "adabT"], w=["modT"])
            m4 = modT[:, :].rearrange("p (l j k) -> p l j k", j=6, k=8)
            V(lambda e: e.scalar_tensor_tensor(G1T[:, :].rearrange("p (l k) -> p l k", k=8), m4[:, :, 1, :], 1.0,
                                               nmixT[:, :].rearrange("p (l k) -> p l k", k=8), ALU.add, ALU.mult),
              r=["modT", "nmixT"], w=["G1T"])
            V(lambda e: e.scalar_tensor_tensor(G2T[:, :].rearrange("p (l k) -> p l k", k=8), m4[:, :, 4, :], 1.0,
                                               nmlpT[:, :].rearrange("p (l k) -> p l k", k=8), ALU.add, ALU.mult),
              r=["modT", "nmlpT"], w=["G2T"])
            S_.flush()
        if _STOP <= 0:
            return nc

        def build_bc(bc, bcname, l, j, dtmp, dname, pbanks):
            for k in range(8):
                V(lambda e, k=k: e.tensor_scalar(dtmp[:, :], identF[:, :], modcol(l, j, k), None, ALU.mult),
                  r=["identF", "modT", dname], w=[dname])
                bk = pbanks[k // 4]
                mm(bank(bk)[:, (k % 4) * 128:(k % 4 + 1) * 128], onesM[:, :], dtmp[:, :], True, True,
                   r=["onesM", dname], w=[PS(bk)])
            for hh in range(2):
                A(lambda e, hh=hh: e.activation(bc[:, hh * 512:(hh + 1) * 512], bank(pbanks[hh]), AF.Copy, scale=256.0),
                  r=[PS(pbanks[hh])], w=[bcname])

        def norm_tile(xt_ap, xtname, xn, ssq, hT_out, hTname, GT_, l, jsh, pb, ncols=128, coff=0):
            act(xn[:, :], xt_ap, AF.Square, r=[xtname], w=["xn", "ssq"], accum_out=ssq[:, 0:1])
            V(lambda e: e.tensor_scalar(ssq[:, 1:2], ssq[:, 0:1], 1.0 / D, EPS, ALU.mult, ALU.add),
              r=["ssq"], w=["ssq1"])
            act(ssq[:, 2:3], ssq[:, 1:2], AF.Ln, r=["ssq1"], w=["ssq2"])
            act(ssq[:, 3:4], ssq[:, 2:3], AF.Exp, r=["ssq2"], w=["ssq3"], scale=-0.5)
            V(lambda e: e.tensor_scalar(xn[:, :], xt_ap, ssq[:, 3:4], None, ALU.mult),
              r=[xtname, "ssq3", "xn"], w=["xn"])
            for half in range(2):
                bk = pb[half]
                for k in range(4 * half, 4 * half + 4):
                    tr(bank(bk)[:, (k % 4) * 128:(k % 4 + 1) * 128], xn[:, k * 128:(k + 1) * 128], identF[:, :],
                       r=["xn", "identF"], w=[PS(bk)])
                k0 = 4 * half
                g_b = GT_[:, l * 8 + k0:l * 8 + k0 + 4].rearrange("p (k o) -> p k o", o=1).broadcast_to([128, 4, 128])
                c0 = l * 48 + jsh * 8 + k0
                s_b = modT[:, c0:c0 + 4].rearrange("p (k o) -> p k o", o=1).broadcast_to([128, 4, 128])
                V(lambda e, bk=bk, g_b=g_b: e.tensor_tensor(evt[:, :].rearrange("p (k t) -> p k t", t=128),
                                                            bank(bk).rearrange("p (k t) -> p k t", t=128), g_b, ALU.mult),
                  r=[PS(bk), "G1T", "G2T"], w=["evt"])
                V(lambda e, k0=k0, s_b=s_b: e.tensor_tensor(hT_out[:, k0:k0 + 4, coff:coff + ncols],
                                                            evt[:, :].rearrange("p (k t) -> p k t", t=128), s_b, ALU.add),
                  r=["evt", "modT"], w=[hTname])

        for l in range(L):
            src_d = x_d if l == 0 else out_d
            pl = ExitStack()
            pl.__enter__()
            WB = 1128
            wB = sb(pl, "wB", [128, 8, WB], BF16)
            woS = sb(pl, "woS", [128, 8, D], BF16)
            with ExitStack() as p1:
                WA = 2560
                wA = sb(p1, "wA", [128, 8, WA], BF16)
                dg = sb(p1, "dg", [128, 2, 31, 128], BF16)
                xt = [sb(p1, "xt%d" % i, [128, D], F32) for i in range(2)]
                xn = sb(p1, "xn", [128, D], F32)
                ssq = sb(p1, "ssq", [128, 4], F32)
                hT = [sb(p1, "hT%d" % i, [128, 8, 128], BF16) for i in range(2)]
                tht = [sb(p1, "tht%d" % i, [128, 512], F32) for i in range(2)]
                qs = [sb(p1, "qs%d" % i, [128, 512], F32) for i in range(2)]
                sg = [sb(p1, "sg%d" % i, [128, 512], F32) for i in range(2)]
                gs = [sb(p1, "gs%d" % i, [128, 512], F32) for i in range(3)]
                vtok = [sb(p1, "vtok%d" % i, [128, 512], BF16) for i in range(3)]
                hcur = [sb(p1, "hcur%d" % i, [128, 2, 128], BF16) for i in range(2)]
                kT = sb(p1, "kT", [128, 512], F32)
                fT = sb(p1, "fT", [128, 512], F32)
                GT = sb(p1, "GT", [128, 512], F32)
                t1 = sb(p1, "t1", [128, 512], F32)
                E = sb(p1, "E", [128, 512], F32)
                E4 = [sb(p1, "E4%d" % i, [128, 512], F32) for i in range(2)]
                qtl = [sb(p1, "qtl%d" % i, [128, 512], BF16) for i in range(2)]
                ktl = [sb(p1, "ktl%d" % i, [128, 512], BF16) for i in range(2)]
                khT = sb(p1, "khT", [128, 512], BF16)
                qhA = [sb(p1, "qhA%d" % i, [128, 512], BF16) for i in range(2)]
                qhB = [sb(p1, "qhB%d" % i, [128, 512], BF16) for i in range(2)]
                khtok = [sb(p1, "khtok%d" % i, [128, 512], BF16) for i in range(2)]
                ATm = sb(p1, "ATm", [128, 512], BF16)
                Sst = sb(p1, "Sst", [128, 512], F32)
                Sbf0 = sb(p1, "Sbf0", [128, 512], BF16)
                Sbf1 = sb(p1, "Sbf1", [128, 512], BF16)
                oa = sb(p1, "oa", [128, 512], F32)
                oab = sb(p1, "oab", [128, 512], BF16)
                ss4 = sb(p1, "ss4", [128, 16], F32)
                scanm = sb(p1, "scanm", [128, 512], F32)
                cmask = sb(p1, "cmask", [128, 512], U8)
                cmf = sb(p1, "cmf", [128, 512], F32)
                hbuf = sb(p1, "hbuf", [128, 2, 160], BF16)
                cc = [sb(p1, "cc%d" % i, [128, 256], F32) for i in range(2)]
                csq = sb(p1, "csq", [128, 256], F32)
                stt_ = [sb(p1, "stt%d" % i, [128, 256], F32) for i in range(2)]
                yv = sb(p1, "yv", [128, 256], F32)
                thy = sb(p1, "thy", [128, 256], F32)
                rs_b = [sb(p1, "rs_b%d" % i, [128, 128], F32) for i in range(2)]
                cat6 = [sb(p1, "cat6_%d" % i, [128, 6, 128], BF16) for i in range(2)]

                wst1 = [sb(p1, "wst%d" % i, [128, D], F32) for i in range(2)]
                wi = 0
                for (gn, d0, s0) in (("v", 1024, 1024), ("g", 1536, 1536), ("q", 0, 0), ("f", 512, 512), ("cu", 2048, 3112)):
                    for k2 in range(4):
                        stg, stn = wst1[wi % 2], "wst%d" % (wi % 2)
                        wi += 1
                        ld(stg[:, :].rearrange("p (k n) -> p k n", n=512),
                           win_d[l, k2 * 256:(k2 + 1) * 256, s0:s0 + 512].rearrange("(k p) n -> p k n", p=128), w=[stn])
                        act(wA[:, 2 * k2:2 * k2 + 2, d0:d0 + 512], stg[:, :].rearrange("p (k n) -> p k n", n=512), AF.Copy,
                            r=[stn], w=["wA_" + gn])
                g1bc = sb(p1, "g1bc", [128, D], F32)
                stg2 = sb(p1, "stg2", [128, 8, 40], F32)
                dtmp1 = sb(p1, "dtmp", [128, 128], F32)
                def prefetch_a2():
                    ld(stg2[:, :, :], win_d[l, :, 3072:3112].rearrange("(k p) n -> p k n", p=128), w=["stg2"])
                    for k in range(8):
                        rows = slice(k * 128, (k + 1) * 128)
                        stg, stn = wst1[k % 2], "wst%d" % (k % 2)
                        ld(stg[:, :], win_d[l, rows, 2048:3072], w=[stn])
                        act(wB[:, k, 0:1024], stg[:, :], AF.Copy, r=[stn], w=["wB"])
                    act(wB[:, :, 1024:1032], stg2[:, :, 32:40], AF.Copy, r=["stg2"], w=["wB"])
                    for r3 in range(3):
                        act(wB[:, :, 1032 + 32 * r3:1064 + 32 * r3], stg2[:, :, 0:32], AF.Copy, r=["stg2"], w=["wB"])
                    build_bc(g1bc, "g1bc", l, 2, dtmp1, "dtmp", (0, 1))
                    for k in range(8):
                        ld(wst1[k % 2][:, :], wout_d[l, k * 128:(k + 1) * 128, :], w=["wst%d" % (k % 2)])
                        V(lambda e, k=k: e.tensor_tensor(woS[:, k, :], wst1[k % 2][:, :], g1bc[:, :], ALU.mult),
                          r=["wst%d" % (k % 2), "g1bc"], w=["woS"])
                for ct in range(2):
                    for j in range(31):
                        c = l * 62 + ct * 31 + j
                        V(lambda e, ct=ct, j=j, c=c: e.tensor_scalar(dg[:, ct, j, :], identF[:, :], cvw[:, c:c + 1],
                                                                     None, ALU.mult),
                          r=["identF", "cvw"], w=["dg"])
                G(lambda e: e.memset(scanm[:, :], 1.0), w=["scanm"])
                sm3 = scanm[:, :].rearrange("p (c j) -> p c j", j=64)
                G(lambda e: e.memset(sm3[:, :, 0:1], 0.0), r=["scanm"], w=["scanm"])
                G(lambda e: e.memset(cmf[:, :], 1.0), w=["cmf"])
                for h in range(4):
                    G(lambda e, h=h: e.affine_select(cmf[:, h * 128:(h + 1) * 128], cmf[:, h * 128:(h + 1) * 128],
                                                     [[1, 128]], ALU.is_ge, fillreg(e, 0.0), base=0, channel_multiplier=-1),
                      r=["cmf"], w=["cmf"])
                    G(lambda e, h=h: e.memset(cmf[0:64, h * 128 + 64:(h + 1) * 128], 0.0), r=["cmf"], w=["cmf"])
                V(lambda e: e.tensor_copy(cmask[:, :], cmf[:, :]), r=["cmf"], w=["cmask"])
                for (tl, nm) in ((ATm, "ATm"), (qhA[0], "qhA0"), (qhA[1], "qhA1"), (qhB[0], "qhB0"), (qhB[1], "qhB1"),
                                 (Sst, "Sst"), (Sbf0, "Sbf0"), (Sbf1, "Sbf1")):
                    G(lambda e, tl=tl: e.memset(tl[:, :], 0.0), w=[nm])
                G(lambda e: e.memset(hbuf[:, :, :], 0.0), w=["hbuf"])

                bc4 = lambda tl_: tl_[:, l * 4:(l + 1) * 4].rearrange("p (h o) -> p h o", o=1).broadcast_to([128, 4, 128])
                lb_b, oml_b = bc4(lbT), bc4(omlT)
                v3 = lambda t_: t_[:, :].rearrange("p (h t) -> p h t", t=128)
                c3 = lambda t_: t_[:, :].rearrange("p (c j) -> p c j", j=64)
                c4 = lambda t_: t_[:, :].rearrange("p (h c j) -> p h c j", c=2, j=64)
                QSC = 128.0 ** -0.5
                st1 = {"th": 0}

                def nxt_th():
                    i = st1["th"] % 2
                    st1["th"] += 1
                    return tht[i], "tht%d" % i

                def sigm(dst, src_ap, r, w):
                    act(dst, src_ap, AF.Exp, r=r, w=w, scale=-1.0)
                    act(dst, dst, AF.Ln, r=w, w=w, bias=1.0)
                    act(dst, dst, AF.Exp, r=w, w=w, scale=-1.0)

                def stageX(t):
                    b = t % 2
                    b3 = t % 3
                    xtn, hTn = "xt%d" % b, "hT%d" % b
                    if t + 1 < NT:
                        ld(xt[1 - b][:, :], src_d[(t + 1) * 128:(t + 2) * 128, :], w=["xt%d" % (1 - b)])
                    norm_tile(xt[b][:, :], xtn, xn, ssq, hT[b], hTn, G1T, l, 0, (0, 0))
                    ld(hT_d[t, :, :], hT[b][:, :, :].rearrange("p k n -> p (k n)"), r=[hTn], w=["hTd%d" % t])
                    S_.unit()
                    for k in range(8):
                        mm(bank(1), hT[b][:, k, :], wA[:, k, 1024:1536], k == 0, k == 7, r=[hTn, "wA_v"], w=[PS(1)])
                    act(vtok[b3][:, :], bank(1), AF.Copy, r=[PS(1)], w=["vtok%d" % b3])
                    S_.unit()
                    for k in range(8):
                        mm(bank(2), hT[b][:, k, :], wA[:, k, 1536:2048], k == 0, k == 7, r=[hTn, "wA_g"], w=[PS(2)])
                    th_, thn_ = nxt_th()
                    sigm(th_[:, :], bank(2), [PS(2)], [thn_])
                    V(lambda e, th_=th_: e.tensor_tensor(gs[b3][:, :], th_[:, :], bank(2), ALU.mult),
                      r=[thn_, PS(2)], w=["gs%d" % b3])
                    S_.unit()
                    for f in range(4):
                        for k in range(8):
                            mm(bank(1)[:, f * 128:(f + 1) * 128], wA[:, k, f * 128:(f + 1) * 128], hT[b][:, k, :],
                               k == 0, k == 7, r=[hTn, "wA_q"], w=[PS(1)])
                    th_, thn_ = nxt_th()
                    sigm(th_[:, :], bank(1), [PS(1)], [thn_])
                    V(lambda e, th_=th_: e.tensor_tensor(qs[b][:, :], th_[:, :], bank(1), ALU.mult),
                      r=[thn_, PS(1)], w=["qs%d" % b])
                    S_.unit()
                    for f in range(4):
                        for k in range(8):
                            mm(bank(2)[:, f * 128:(f + 1) * 128], wA[:, k, 512 + f * 128:512 + (f + 1) * 128], hT[b][:, k, :],
                               k == 0, k == 7, r=[hTn, "wA_f"], w=[PS(2)])
                    sigm(sg[b][:, :], bank(2), [PS(2)], ["sg%d" % b])
                    S_.unit()
                    for f in range(4):
                        for k in range(8):
                            mm(bank(1)[:, f * 128:(f + 1) * 128], wA[:, k, 2048 + f * 128:2048 + (f + 1) * 128], hT[b][:, k, :],
                               k == 0, k == 7, r=[hTn, "wA_cu"], w=[PS(1)])
                    th_, thn_ = nxt_th()
                    sigm(th_[:, 0:256], bank(1)[:, 256:512], [PS(1)], [thn_])
                    V(lambda e, th_=th_: e.tensor_tensor(hcur[b][:, :, :].rearrange("p c t -> p (c t)"), th_[:, 0:256],
                                                         bank(1)[:, 0:256], ALU.mult),
                      r=[thn_, PS(1)], w=["hcur%d" % b])

                def stageY1(t):
                    b = t % 2
                    b3 = t % 3
                    qsn, sgn_, gsn, vtn, hcn, c6n = "qs%d" % b, "sg%d" % b, "gs%d" % b3, "vtok%d" % b3, "hcur%d" % b, "cat6_%d" % b
                    qs_, sg_, gs_, vt_, c6 = qs[b], sg[b], gs[b3], vtok[b3], cat6[b]
                    qtl_, ktl_, khtok_, qhA_, qhB_, E4_ = qtl[b], ktl[b], khtok[b], qhA[b], qhB[b], E4[b]
                    cc_, stt2, rsb_ = cc[b], stt_[b], rs_b[b]
                    qtln, ktln, khtokn, qhAn, qhBn, E4n = "qtl%d" % b, "ktl%d" % b, "khtok%d" % b, "qhA%d" % b, "qhB%d" % b, "E4%d" % b
                    ccn, sttn, rsbn = "cc%d" % b, "stt%d" % b, "rs_b%d" % b
                    G(lambda e: e.tensor_copy(hbuf[:, :, 32:160], hcur[b][:, :, :]), r=[hcn, "hbuf"], w=["hbuf"])
                    for ct in range(2):
                        for j in range(31):
                            mm(bank(3)[:, ct * 128:(ct + 1) * 128], dg[:, ct, j, :],
                               hbuf[:, ct, 2 + j:2 + j + 128], j == 0, j == 30, r=["dg", "hbuf"], w=[PS(3)])
                        S_.unit()
                    for ct in range(2):
                        cs = slice(ct * 128, (ct + 1) * 128)
                        act(cc_[:, cs], bank(3)[:, cs], AF.Identity,
                            r=[PS(3), "cvb"], w=[ccn], bias=cvb[:, l * 2 + ct:l * 2 + ct + 1])
                    act(csq[:, :], cc_[:, :], AF.Square, r=[ccn], w=["csq"])
                    G(lambda e: e.tensor_copy(hbuf[:, :, 0:32], hbuf[:, :, 128:160]), r=["hbuf", PS(3)], w=["hbuf"])
                    S_.unit()
                    for (si, src_, sn) in ((0, cc_, ccn), (1, csq, "csq")):
                        for ct in range(2):
                            mm(bank(4)[:, si * 128:(si + 1) * 128], onesM[:, :], src_[:, ct * 128:(ct + 1) * 128],
                               ct == 0, ct == 1, r=["onesM", sn], w=[PS(4)])
                    act(stt2[:, :], bank(4)[:, 0:256], AF.Copy, r=[PS(4)], w=[sttn])
                    V(lambda e: e.tensor_tensor(rsb_[:, :], stt2[:, 0:128], stt2[:, 0:128], ALU.mult), r=[sttn], w=[rsbn])
                    V(lambda e: e.tensor_tensor(rsb_[:, :], stt2[:, 128:256], rsb_[:, :], ALU.subtract),
                      r=[sttn, rsbn], w=[rsbn])
                    V(lambda e: e.tensor_scalar(rsb_[:, :], rsb_[:, :], EPS, None, ALU.add), r=[rsbn], w=[rsbn])
                    S_.unit()
                    V(lambda e: e.tensor_tensor(v3(t1), v3(sg_), oml_b, ALU.mult), r=[sgn_, "omlT"], w=["t1"])
                    V(lambda e: e.tensor_tensor(v3(fT), v3(t1), lb_b, ALU.add), r=["t1", "lbT"], w=["fT"])
                    V(lambda e: e.tensor_tensor(v3(kT), oml_b, v3(t1), ALU.subtract), r=["t1", "omlT"], w=["kT"])
                    V(lambda e: e.tensor_scalar(fT[:, :], fT[:, :], 1e-30, None, ALU.max), r=["fT"], w=["fT"])
                    act(fT[:, :], fT[:, :], AF.Ln, r=["fT"], w=["fT"])
                    act(rsb_[:, :], rsb_[:, :], AF.Ln, r=[rsbn], w=[rsbn])
                    act(rsb_[:, :], rsb_[:, :], AF.Exp, r=[rsbn], w=[rsbn], scale=-0.5)
                    S_.unit()
                    V(lambda e: e.tensor_tensor_scan(GT[:, :], scanm[:, :], fT[:, :], 0.0, ALU.mult, ALU.add),
                      r=["scanm", "fT"], w=["GT"])
                    V(lambda e: e.tensor_tensor(c3(t1), c3(GT), c3(GT)[:, :, 32:33].broadcast_to([128, 8, 64]),
                                                ALU.subtract), r=["GT"], w=["t1"])
                    act(E[:, :], t1[:, :], AF.Exp, r=["t1"], w=["E"])
                    V(lambda e: e.scalar_tensor_tensor(qtl_[:, :], qs_[:, :], QSC, E[:, :], ALU.mult, ALU.mult),
                      r=[qsn, "E"], w=[qtln])
                    S_.unit()
                    act(E[:, :], t1[:, :], AF.Exp, r=["t1", qtln], w=["E"], scale=-1.0)
                    V(lambda e: e.tensor_tensor(ktl_[:, :], kT[:, :], E[:, :], ALU.mult), r=["kT", "E"], w=[ktln])
                    V(lambda e: e.tensor_tensor(c3(t1), c3(GT)[:, :, 63:64].broadcast_to([128, 8, 64]), c3(GT),
                                                ALU.subtract), r=["GT", "E"], w=["t1"])
                    S_.unit()
                    act(E[:, :], t1[:, :], AF.Exp, r=["t1", ktln], w=["E"])
                    V(lambda e: e.tensor_tensor(khT[:, :], kT[:, :], E[:, :], ALU.mult), r=["kT", "E"], w=["khT"])
                    act(E4_[:, :], GT[:, :], AF.Exp, r=["GT"], w=[E4n])
                    S_.unit()
                    V(lambda e: e.scalar_tensor_tensor(c4(qhA_)[:, :, 0, :], c4(qs_)[:, :, 0, :], QSC, c4(E4_)[:, :, 0, :],
                                                       ALU.mult, ALU.mult), r=[qsn, E4n], w=[qhAn])
                    V(lambda e: e.scalar_tensor_tensor(c4(qhB_)[:, :, 1, :], c4(qs_)[:, :, 1, :], QSC, c4(E4_)[:, :, 1, :],
                                                       ALU.mult, ALU.mult), r=[qsn, E4n], w=[qhBn])
                    for h in range(4):
                        tr(bankb(4)[:, h * 128:(h + 1) * 128], khT[:, h * 128:(h + 1) * 128], identB[:, :],
                           r=["khT", "identB"], w=[PS(4)])
                    act(khtok_[:, :], bankb(4)[:, 0:512], AF.Copy, r=[PS(4)], w=[khtokn])
                    S_.unit()

                def stageY2(t):
                    b = t % 2
                    b3 = t % 3
                    qsn, sgn_, gsn, vtn, hcn, c6n = "qs%d" % b, "sg%d" % b, "gs%d" % b3, "vtok%d" % b3, "hcur%d" % b, "cat6_%d" % b
                    qs_, sg_, gs_, vt_, c6 = qs[b], sg[b], gs[b3], vtok[b3], cat6[b]
                    qtl_, ktl_, khtok_, qhA_, qhB_, E4_ = qtl[b], ktl[b], khtok[b], qhA[b], qhB[b], E4[b]
                    cc_, stt2, rsb_ = cc[b], stt_[b], rs_b[b]
                    qtln, ktln, khtokn, qhAn, qhBn, E4n = "qtl%d" % b, "ktl%d" % b, "khtok%d" % b, "qhA%d" % b, "qhB%d" % b, "E4%d" % b
                    ccn, sttn, rsbn = "cc%d" % b, "stt%d" % b, "rs_b%d" % b
                    for h in range(4):
                        mm(bank(5)[:, h * 128:(h + 1) * 128], ktl_[:, h * 128:(h + 1) * 128],
                           qtl_[:, h * 128:(h + 1) * 128], True, True, r=[ktln, qtln], w=[PS(5)])
                    V(lambda e: e.copy_predicated(ATm[:, :], cmask[:, :], bank(5)), r=[PS(5), "cmask"], w=["ATm"])
                    S_.unit()
                    for h in range(4):
                        hs = slice(h * 128, (h + 1) * 128)
                        mm(bank(6)[:, hs], ATm[:, hs], vt_[:, hs], h == 0, False, r=["ATm", vtn], w=[PS(6)])
                        mm(bank(6)[:, hs], qhA_[:, hs], Sbf0[:, hs], False, False, r=[qhAn, "Sbf0"], w=[PS(6)])
                    for h in range(4):
                        hs = slice(h * 128, (h + 1) * 128)
                        mm(bank(7)[:, hs], khtok_[0:64, hs], vt_[0:64, hs], True, True, r=[khtokn, vtn], w=[PS(7)])
                    dec = lambda c: c4(E4_)[:, :, c, 63:64].broadcast_to([128, 4, 128])
                    V(lambda e: e.tensor_tensor(v3(Sst), v3(Sst), dec(0), ALU.mult), r=["Sst", E4n], w=["Sst"])
                    V(lambda e: e.tensor_tensor(Sst[:, :], Sst[:, :], bank(7), ALU.add), r=["Sst", PS(7)], w=["Sst"])
                    act(Sbf1[:, :], Sst[:, :], AF.Copy, r=["Sst"], w=["Sbf1"])
                    S_.unit()
                    for h in range(4):
                        hs = slice(h * 128, (h + 1) * 128)
                        mm(bank(6)[:, hs], qhB_[:, hs], Sbf1[:, hs], False, True, r=[qhBn, "Sbf1"], w=[PS(6)])
                    for h in range(4):
                        hs = slice(h * 128, (h + 1) * 128)
                        mm(bank(7)[:, hs], khtok_[64:128, hs], vt_[64:128, hs], True, True, r=[khtokn, vtn], w=[PS(7)])
                    V(lambda e: e.tensor_tensor(v3(Sst), v3(Sst), dec(1), ALU.mult), r=["Sst", E4n], w=["Sst"])
                    V(lambda e: e.tensor_tensor(Sst[:, :], Sst[:, :], bank(7), ALU.add), r=["Sst", PS(7)], w=["Sst"])
                    act(Sbf0[:, :], Sst[:, :], AF.Copy, r=["Sst"], w=["Sbf0"])
                    S_.unit()
                    for h in range(4):
                        act(oa[:, h * 128:(h + 1) * 128], bank(6)[:, h * 128:(h + 1) * 128], AF.Square,
                            r=[PS(6)], w=["oa", "ss4"], accum_out=ss4[:, h:h + 1])
                    V(lambda e: e.tensor_scalar(ss4[:, 4:8], ss4[:, 0:4], 1.0 / 128, EPS, ALU.mult, ALU.add),
                      r=["ss4"], w=["ss4b"])
                    act(ss4[:, 8:12], ss4[:, 4:8], AF.Ln, r=["ss4b"], w=["ss4c"])
                    act(ss4[:, 12:16], ss4[:, 8:12], AF.Exp, r=["ss4c"], w=["ss4d"], scale=-0.5)
                    V(lambda e: e.tensor_tensor(v3(oa), v3(bank(6)),
                                                ss4[:, 12:16].rearrange("p (h o) -> p h o", o=1).broadcast_to([128, 4, 128]),
                                                ALU.mult), r=[PS(6), "ss4d", "oa"], w=["oa"])
                    V(lambda e: e.tensor_tensor(oab[:, :], oa[:, :], gs_[:, :], ALU.mult), r=["oa", gsn], w=["oab"])
                    S_.unit()
                    for h in range(4):
                        tr(bankb(5)[:, 512 + h * 128:512 + (h + 1) * 128], oab[:, h * 128:(h + 1) * 128], identB[:, :],
                           r=["oab", "identB"], w=[PS(5)])
                    for h in range(4):
                        act(c6[:, h, :], bankb(5)[:, 512 + h * 128:512 + (h + 1) * 128], AF.Copy,
                            r=[PS(5), "onwT"], w=[c6n], scale=onwT[:, l * 4 + h:l * 4 + h + 1])
                    S_.unit()
                    y3 = yv[:, :].rearrange("p (c t) -> p c t", t=128)
                    V(lambda e: e.tensor_tensor(y3, cc_[:, :].rearrange("p (c t) -> p c t", t=128),
                                                stt2[:, 0:128].rearrange("p (o t) -> p o t", o=1).broadcast_to([128, 2, 128]),
                                                ALU.subtract), r=[ccn, sttn], w=["yv"])
                    V(lambda e: e.tensor_tensor(y3, y3,
                                                rsb_[:, :].rearrange("p (o t) -> p o t", o=1).broadcast_to([128, 2, 128]),
                                                ALU.mult), r=["yv", rsbn], w=["yv"])
                    for ct in range(2):
                        cs = slice(ct * 128, (ct + 1) * 128)
                        V(lambda e, cs=cs, ct=ct: e.tensor_scalar(yv[:, cs], yv[:, cs], lnw[:, l * 2 + ct:l * 2 + ct + 1],
                                                                  lnb[:, l * 2 + ct:l * 2 + ct + 1], ALU.mult, ALU.add),
                          r=["yv", "lnw", "lnb"], w=["yv"])
                    sigm(thy[:, :], yv[:, :], ["yv"], ["thy"])
                    V(lambda e: e.tensor_tensor(c6[:, 4:6, :].rearrange("p c t -> p (c t)"), thy[:, :], yv[:, :], ALU.mult),
                      r=["thy", "yv"], w=[c6n])
                    ld(cat_d[t, :, :], c6[:, :, :].rearrange("p k n -> p (k n)"), r=[c6n], w=["catd%d" % t])

                def cap1(fn, t):
                    S_.begin(fine=True)
                    fn(t)
                    return S_.end()

                ld(xt[0][:, :], src_d[0:128, :], w=["xt0"])
                for step in range(NT + 2):
                    streams = []
                    if step - 2 >= 0:
                        streams.append(cap1(stageY2, step - 2))
                    if 0 <= step - 1 < NT:
                        streams.append(cap1(stageY1, step - 1))
                    if step < NT:
                        streams.append(cap1(stageX, step))
                    S_.run_merged(streams)
                    if step == 1:
                        prefetch_a2()
                S_.flush()
            if _STOP <= 1:
                return nc

            with ExitStack() as p2:
                KTc = sb(p2, "KTc", [128, 2, S], BF16)
                Vaug = sb(p2, "Vaug", [128, NT, 4, 65], BF16)
                kidx = sb(p2, "kidx", [128, S], BF16)
                score = [sb(p2, "score%d" % i, [128, S], F32) for i in range(3)]
                junkb = sb(p2, "junkb", [128, S], BF16)
                junk8 = sb(p2, "junk8", [128, S], U8)
                cntd = sb(p2, "cntd", [128, NIT], F32)
                vc = sb(p2, "vc", [128, NIT], F32)
                thrc = sb(p2, "thrc", [128, 4], F32)
                ones1 = sb(p2, "ones1", [128, 1], BF16)
                xt = [sb(p2, "x2t%d" % i, [128, D], F32) for i in range(3)]
                hT = [sb(p2, "h2T%d" % i, [128, 8, 128], BF16) for i in range(3)]
                catT = [sb(p2, "catT%d" % i, [128, 8, 128], BF16) for i in range(3)]
                sqT = [sb(p2, "sqT%d" % i, [128, 2, 256], BF16) for i in range(3)]
                identB2 = sb(p2, "identB2", [128, 256], BF16)
                iqT = sb(p2, "iqT", [128, 3, 128], BF16)
                wab = sb(p2, "wab", [128, 8], F32)
                wsg = sb(p2, "wsg", [128, 8], F32)
                Rb = [sb(p2, "Rb%d" % i, [128, 512], F32) for i in range(3)]
                Mb = [sb(p2, "Mb%d" % i, [128, 512], BF16) for i in range(3)]
                PT = [sb(p2, "PT%d" % i, [128, 512], BF16) for i in range(3)]
                ob = sb(p2, "ob", [128, 256], BF16)
                bis = sb(p2, "bis", [128, 16], F32)
                tabA = sb(p2, "tabA", [128, NIT], F32)
                tabB = sb(p2, "tabB", [128, NIT], F32)
                cntc = sb(p2, "cntc", [128, NIT], F32)
                uc = sb(p2, "uc", [128, NIT], F32)
                mid = sb(p2, "mid", [128, NIT + 1], F32)
                top8 = sb(p2, "top8", [128, 8], F32)
                rs4 = sb(p2, "rs4", [128, 4], F32)

                G(lambda e: e.memset(Vaug[:, :, :, 64:65], 1.0), w=["Vaug"])
                G(lambda e: e.memset(ones1[:, :], 1.0), w=["ones1"])
                for i in range(3):
                    G(lambda e, i=i: e.memset(sqT[i][:, :, :], 0.0), w=["sqT%d" % i])
                for c in range(2):
                    V(lambda e, c=c: e.tensor_copy(identB2[:, c * 128:(c + 1) * 128], identB[:, :]), r=["identB"], w=["identB2"])

                IDXC = (32.0 ** -0.5) * (8.0 ** -0.5)
                state = {"rbi": 0, "mbi": 0, "mgi": 0}

                def stagePS(t):
                    b = t % 3
                    xtn, hTn, cTn, scn, sqn = "x2t%d" % b, "h2T%d" % b, "catT%d" % b, "score%d" % b, "sqT%d" % b
                    sc = score[b]
                    N = (t + 1) * 128
                    tok = slice(t * 128, (t + 1) * 128)
                    ld(xt[b][:, :], src_d[tok, :], w=[xtn])
                    ld(hT[b][:, :, :].rearrange("p k n -> p (k n)"), hT_d[t, :, :], r=["hTd%d" % t], w=[hTn])
                    ld(catT[b][:, 0:4, :].rearrange("p k n -> p (k n)"), cat_d[t, :, 0:512], r=["catd%d" % t], w=[cTn])
                    ld(catT[b][:, 6:8, :].rearrange("p k n -> p (k n)"), cat_d[t, :, 512:768], r=["catd%d" % t], w=[cTn])
                    S_.unit()
                    for k in range(8):
                        mm(bank(0)[:, 0:256], hT[b][:, k, :], wB[:, k, 512:768], k == 0, k == 7, r=[hTn, "wB"], w=[PS(0)])
                    for k in range(8):
                        mm(bank(0)[:, 256:264], hT[b][:, k, :], wB[:, k, 1024:1032], k == 0, k == 7,
                           r=[hTn, "wB"], w=[PS(0)])
                    V(lambda e: e.tensor_copy(Vaug[:, t, :, 0:64], bank(0)[:, 0:256].rearrange("p (h d) -> p h d", d=64)),
                      r=[PS(0)], w=["Vaug"])
                    act(wab[:, :], bank(0)[:, 256:264], AF.Abs, r=[PS(0)], w=["wab"], scale=IDXC)
                    act(wsg[:, :], bank(0)[:, 256:264], AF.Sign, r=[PS(0)], w=["wsg"])
                    S_.unit()
                    for f in range(4):
                        cb = (0, 128, 256, 384)[f]
                        for k in range(8):
                            mm(bank(1)[:, f * 128:(f + 1) * 128], wB[:, k, cb:cb + 128], hT[b][:, k, :],
                               k == 0, k == 7, r=[hTn, "wB"], w=[PS(1)])
                    for hl in range(2):
                        rows = slice(64 * hl, 64 * hl + 64)
                        act(sqT[b][rows, :, hl * 128:(hl + 1) * 128],
                            bank(1)[rows, 0:256].rearrange("p (c t) -> p c t", t=128), AF.Copy,
                            r=[PS(1)], w=[sqn], scale=0.125)
                    V(lambda e: e.tensor_copy(KTc[:, :, tok], bank(1)[:, 256:512].rearrange("p (c t) -> p c t", t=128)),
                      r=[PS(1)], w=["KTc"])
                    S_.unit()
                    for f, (cb, m) in enumerate(((768, 96), (864, 96), (960, 64), (1032, 96))):
                        for k in range(8):
                            mm(bank(2)[0:m, f * 128:(f + 1) * 128], wB[:, k, cb:cb + m], hT[b][:, k, :],
                               k == 0, k == 7, r=[hTn, "wB"], w=[PS(2)])
                    V(lambda e: e.tensor_copy(iqT[0:96, :, :], bank(2)[0:96, 0:384].rearrange("p (c t) -> p c t", t=128)),
                      r=[PS(2)], w=["iqT"])
                    V(lambda e: e.tensor_copy(kidx[0:96, tok], bank(2)[0:96, 384:512]), r=[PS(2)], w=["kidx"])
                    S_.unit()
                    NB = (N + 511) // 512
                    prev_acc = [None]
                    for kb in range(NB):
                        wN = min(512, N - kb * 512)
                        ks_ = slice(kb * 512, kb * 512 + wN)
                        for hh in range(8):
                            g_, r_ = hh // 3, hh % 3
                            pb = 3 + (hh % 2)
                            mm(bank(pb)[:, 0:wN], iqT[32 * r_:32 * r_ + 32, g_, :], kidx[32 * r_:32 * r_ + 32, ks_],
                               True, True, r=["iqT", "kidx"], w=[PS(pb)])
                            R_ = Rb[state["rbi"] % 3]
                            Rn = "Rb%d" % (state["rbi"] % 3)
                            state["rbi"] += 1
                            act(R_[:, 0:wN], bank(pb)[:, 0:wN], AF.Relu, r=[PS(pb), "wab"], w=[Rn],
                                scale=wab[:, hh:hh + 1])

                            def acc(R_=R_, Rn=Rn, ks_=ks_, wN=wN, hh=hh):
                                if hh == 0:
                                    V(lambda e: e.tensor_scalar(sc[:, ks_], R_[:, 0:wN], wsg[:, 0:1], None, ALU.mult),
                                      r=[Rn, "wsg"], w=[scn])
                                else:
                                    V(lambda e: e.scalar_tensor_tensor(sc[:, ks_], R_[:, 0:wN], wsg[:, hh:hh + 1], sc[:, ks_],
                                                                       ALU.mult, ALU.add),
                                      r=[Rn, "wsg", scn], w=[scn])
                            if prev_acc[0] is not None:
                                prev_acc[0]()
                            prev_acc[0] = acc
                            S_.unit()
                    prev_acc[0]()
                    G(lambda e: e.affine_select(sc[:, tok], sc[:, tok], [[-1, 128]], ALU.is_ge, fillreg(e, -1e30),
                                                base=0, channel_multiplier=1), r=[scn], w=[scn])

                def stageBI(t):
                    b = t % 3
                    scn = "score%d" % b
                    sc = score[b]
                    N = (t + 1) * 128
                    thn = "thr%d" % b
                    if t * 128 < TOPK:
                        V(lambda e: e.tensor_copy(thrc[:, b:b + 1], thrneg[:, 0:1]), r=["thrneg"], w=[thn])
                        return
                    Ka = max(128, min(N - 128, int(round(0.66 * (t + 1))) * 128))
                    V(lambda e: e.max(top8[:, :], sc[:, 0:N]), r=[scn], w=["top8"])
                    S_.unit()
                    V(lambda e: e.tensor_reduce(bis[:, 0:1], sc[:, 0:TOPK], mybir.AxisListType.X, ALU.min),
                      r=[scn], w=["bis0"])
                    V(lambda e: e.tensor_tensor(bis[:, 1:2], top8[:, 0:1], bis[:, 0:1], ALU.subtract),
                      r=["top8", "bis0"], w=["bis1"])
                    V(lambda e: e.tensor_scalar(tabA[:, :], tabA0[:, :], bis[:, 1:2], None, ALU.mult),
                      r=["bis1", "tabA0"], w=["tabA"])
                    V(lambda e: e.tensor_scalar(tabB[:, :], tabB0[:, :], bis[:, 1:2], None, ALU.mult),
                      r=["bis1", "tabB0"], w=["tabB"])
                    V(lambda e: e.scalar_tensor_tensor(mid[:, 0:1], bis[:, 1:2], 0.5, bis[:, 0:1], ALU.mult, ALU.add),
                      r=["bis0", "bis1"], w=["mid"])
                    S_.unit()
                    for n in range(NIT):
                        act(junkb[:, 0:Ka], sc[:, 0:Ka], AF.Sign, r=[scn, "mid"], w=["junkb", "cntA"],
                            scale=-1.0, bias=mid[:, n:n + 1], accum_out=cntc[:, n:n + 1])
                        V(lambda e, n=n: e.scalar_tensor_tensor(junk8[:, Ka:N], sc[:, Ka:N], mid[:, n:n + 1],
                                                                ones1[:, 0:1].broadcast_to([128, N - Ka]),
                                                                ALU.is_ge, ALU.mult, accum_out=cntd[:, n:n + 1]),
                          r=[scn, "mid", "ones1"], w=["junk8", "cntD"])
                        V(lambda e, n=n: e.scalar_tensor_tensor(vc[:, n:n + 1], cntd[:, n:n + 1], 2.0, cntc[:, n:n + 1],
                                                                ALU.mult, ALU.subtract),
                          r=["cntA", "cntD"], w=["vc"])
                        V(lambda e, n=n: e.scalar_tensor_tensor(uc[:, n:n + 1], vc[:, n:n + 1], float(2 * TOPK - Ka - 1),
                                                                tabB[:, n:n + 1], ALU.is_gt, ALU.mult),
                          r=["vc", "tabB"], w=["uc"])
                        V(lambda e, n=n: e.scalar_tensor_tensor(mid[:, n + 1:n + 2], mid[:, n:n + 1], tabA[:, n:n + 1],
                                                                uc[:, n:n + 1], ALU.subtract, ALU.add),
                          r=["mid", "tabA", "uc"], w=["mid"])
                        S_.unit()
                    V(lambda e: e.tensor_copy(thrc[:, b:b + 1], mid[:, NIT:NIT + 1]), r=["mid"], w=[thn])

                def stageAT(t):
                    b = t % 3
                    xtn, cTn, scn, sqn, thn = "x2t%d" % b, "catT%d" % b, "score%d" % b, "sqT%d" % b, "thr%d" % b
                    sc = score[b]
                    tok = slice(t * 128, (t + 1) * 128)
                    thr = thrc[:, b:b + 1]
                    prev_pv = [None]
                    NG = (t + 4) // 4
                    Nq = (t + 1) * 128
                    mg0 = state["mgi"]
                    state["mgi"] += NG

                    def genmask(g):
                        if g >= NG:
                            return
                        w_ = min(512, Nq - g * 512)
                        gi = mg0 + g
                        M_ = Mb[gi % 3]
                        V(lambda e: e.tensor_scalar(M_[:, 0:w_], sc[:, g * 512:g * 512 + w_], thr, -30000.0, ALU.is_lt, ALU.mult),
                          r=[scn, thn], w=["Mb%d" % (gi % 3)])
                    genmask(0)
                    genmask(1)
                    for kb in range(t + 1):
                        kcs = slice(kb * 128, (kb + 1) * 128)
                        mbi = state["mbi"]
                        state["mbi"] += 1
                        g = kb // 4
                        if kb % 4 == 0:
                            genmask(g + 2)
                        gi = mg0 + g
                        M_ = Mb[gi % 3][:, (kb % 4) * 128:(kb % 4 + 1) * 128]
                        Mn = "Mb%d" % (gi % 3)
                        P_ = PT[mbi % 3]
                        Pn = "PT%d" % (mbi % 3)
                        pb = 5 + (mbi % 2)
                        for c in range(2):
                            mm(bank(pb)[:, c * 256:(c + 1) * 256], KTc[:, c, kcs], sqT[b][:, c, :], True, False,
                               r=["KTc", sqn], w=[PS(pb)])
                            mm(bank(pb)[:, c * 256:(c + 1) * 256], M_, identB2[:, :], False, True,
                               r=[Mn, "identB2"], w=[PS(pb)])
                        act(P_[:, :], bank(pb), AF.Exp, r=[PS(pb)], w=[Pn])

                        def pv(kb=kb, P_=P_, Pn=Pn):
                            for h in range(4):
                                mm(bank(7)[:, h * 65:(h + 1) * 65], P_[:, h * 128:(h + 1) * 128], Vaug[:, kb, h, :],
                                   kb == 0 and h == 0, kb == t, r=[Pn, "Vaug"], w=[PS(7)])
                        if prev_pv[0] is not None:
                            prev_pv[0]()
                        prev_pv[0] = pv
                        S_.unit()
                    prev_pv[0]()
                    o3 = bank(7)[:, 0:260].rearrange("p (h d) -> p h d", d=65)
                    V(lambda e: e.reciprocal(rs4[:, :].rearrange("p (h o) -> p h o", o=1), o3[:, :, 64:65]), r=[PS(7)], w=["rs4"])
                    V(lambda e: e.tensor_tensor(ob[:, :].rearrange("p (h d) -> p h d", d=64), o3[:, :, 0:64],
                                                rs4[:, :].rearrange("p (h o) -> p h o", o=1).broadcast_to([128, 4, 64]),
                                                ALU.mult), r=[PS(7), "rs4"], w=["ob"])
                    for c in range(2):
                        tr(bankb(7)[:, c * 128:(c + 1) * 128], ob[:, c * 128:(c + 1) * 128], identB[:, :],
                           r=["ob", "identB"], w=[PS(7)])
                    act(catT[b][:, 4:6, :], bankb(7)[:, 0:256].rearrange("p (c t) -> p c t", t=128), AF.Copy,
                        r=[PS(7)], w=[cTn])
                    S_.unit()
                    for hf in range(2):
                        for c in range(8):
                            mm(bank(5 + hf), catT[b][:, c, :], woS[:, c, hf * 512:(hf + 1) * 512], c == 0, c == 7,
                               r=[cTn, "woS"], w=[PS(5 + hf)])
                        V(lambda e, hf=hf: e.tensor_tensor(xt[b][:, hf * 512:(hf + 1) * 512],
                                                           xt[b][:, hf * 512:(hf + 1) * 512], bank(5 + hf), ALU.add),
                          r=[xtn, PS(5 + hf)], w=[xtn])
                        S_.unit()
                    ld(out_d[tok, :], xt[b][:, :], r=[xtn], w=["outd%d" % t])

                def cap_(fn, t):
                    S_.begin(fine=True)
                    fn(t)
                    return S_.end()

                for step in range(NT + 2):
                    streams = []
                    if step - 2 >= 0:
                        streams.append(cap_(stageAT, step - 2))
                    if 0 <= step - 1 < NT:
                        streams.append(cap_(stageBI, step - 1))
                    if step < NT:
                        streams.append(cap_(stagePS, step))
                    S_.run_merged(streams)
                S_.flush()
            if _STOP <= 2:
                return nc

            pl.__exit__(None, None, None)
            with ExitStack() as p3:
                w1S = sb(p3, "w1S", [128, 8, DFF], BF16)
                w2S = sb(p3, "w2S", [128, 32, D], BF16)
                g2bc = sb(p3, "g2bc", [128, D], F32)
                dtmp = sb(p3, "dtmp3", [128, 128], F32)
                xtN = [sb(p3, "x3n%d" % i, [128, D], F32) for i in range(2)]
                xr = [sb(p3, "x3r%d" % i, [128, D], F32) for i in range(2)]
                xn = sb(p3, "xn3", [128, D], F32)
                ssq = sb(p3, "ssq3", [128, 4], F32)
                ssf = sb(p3, "ssf3", [128, 4], F32)
                hTb = sb(p3, "hTb", [128, 8, 512], BF16)
                h1raw = sb(p3, "h1raw", [128, 8192], F32)
                h1T = h1raw[:, :].bitcast(BF16).rearrange("p (f t) -> p f t", t=512)
                wst = [h1raw[:, 0:1024], h1raw[:, 1024:2048]]
                wst_b = [h1raw[:, 2048:3072], h1raw[:, 3072:4096]]
                rl = [sb(p3, "rl%d" % i, [128, 512], F32) for i in range(2)]
                last = (l == L - 1)
                NBLK = S // 512
                stB = {"ri": 0, "ni": 0, "xi": 0}

                def stageN(blk):
                    for i in range(4):
                        t = blk * 4 + i
                        j = stB["ni"] % 2
                        stB["ni"] += 1
                        ld(xtN[j][:, :], out_d[t * 128:(t + 1) * 128, :], r=["outd%d" % t], w=["x3n%d" % j])
                        norm_tile(xtN[j][:, :], "x3n%d" % j, xn, ssq, hTb, "hTb", G2T, l, 3, (0, 1),
                                  ncols=128, coff=i * 128)
                        S_.unit()

                w1i = 0
                for cb in range(4):
                    for k in range(8):
                        stg, stn = wst_b[w1i % 2], "w1st%d" % (w1i % 2)
                        w1i += 1
                        ld(stg, w1_d[l, k * 128:(k + 1) * 128, cb * 1024:(cb + 1) * 1024], w=[stn])
                        act(w1S[:, k, cb * 1024:(cb + 1) * 1024], stg, AF.Copy, r=[stn], w=["w1S_%d" % cb])
                stageN(0)
                build_bc(g2bc, "g2bc", l, 5, dtmp, "dtmp3", (0, 1))
                for k in range(32):
                    ld(wst[k % 2], w2_d[l, k * 128:(k + 1) * 128, :], w=["w2st%d" % (k % 2)])
                    eng = V if k % 2 == 0 else G
                    eng(lambda e, k=k: e.tensor_tensor(w2S[:, k, :], wst[k % 2], g2bc[:, :], ALU.mult),
                        r=["w2st%d" % (k % 2), "g2bc"], w=["w2S_%d" % k])
                if last:
                    ld(g2bc[:, :], fnw_d[0:1, :].partition_broadcast(128), r=["w2S_%d" % k for k in range(32)], w=["g2bc"])
                def stageM1(blk):
                    for f in range(32):
                        pb = 2 + (f % 2)
                        for k in range(8):
                            mm(bank(pb), w1S[:, k, f * 128:(f + 1) * 128], hTb[:, k, :], k == 0, k == 7,
                               r=["w1S_%d" % (f // 8), "hTb"], w=[PS(pb)])
                        r_ = rl[stB["ri"] % 2]
                        rn = "rl%d" % (stB["ri"] % 2)
                        stB["ri"] += 1
                        act(r_[:, :], bank(pb), AF.Relu, r=[PS(pb)], w=[rn])
                        G(lambda e, r_=r_, f=f: e.tensor_tensor(h1T[:, f, :], r_[:, :], r_[:, :], ALU.mult),
                          r=[rn], w=["h1T", "w2st0", "w2st1", "w1st0", "w1st1"])

                def stageM2(blk):
                    for i in range(4):
                        t = blk * 4 + i
                        j = stB["xi"] % 2
                        stB["xi"] += 1
                        xb, xbn = xr[j], "x3r%d" % j
                        ld(xb[:, :], out_d[t * 128:(t + 1) * 128, :], r=["outd%d" % t], w=[xbn])
                        for hf in range(2):
                            pb = 4 + 2 * (i % 2) + hf
                            for f in range(32):
                                mm(bank(pb), h1T[:, f, i * 128:(i + 1) * 128], w2S[:, f, hf * 512:(hf + 1) * 512],
                                   f == 0, f == 31, r=["h1T", "w2S_%d" % f], w=[PS(pb)])
                                if f % 8 == 7:
                                    S_.unit()
                            V(lambda e, xb=xb, hf=hf, pb=pb: e.tensor_tensor(xb[:, hf * 512:(hf + 1) * 512],
                                                                             xb[:, hf * 512:(hf + 1) * 512], bank(pb), ALU.add),
                              r=[xbn, PS(pb)], w=[xbn])
                        if last:
                            act(rl[0][:, :].bitcast(BF16), xb[:, :], AF.Square, r=[xbn], w=["rl0", "ssf"], accum_out=ssf[:, 0:1])
                            V(lambda e: e.tensor_scalar(ssf[:, 1:2], ssf[:, 0:1], 1.0 / D, EPS, ALU.mult, ALU.add),
                              r=["ssf"], w=["ssf1"])
                            act(ssf[:, 2:3], ssf[:, 1:2], AF.Ln, r=["ssf1"], w=["ssf2"])
                            act(ssf[:, 3:4], ssf[:, 2:3], AF.Exp, r=["ssf2"], w=["ssf3"], scale=-0.5)
                            V(lambda e, xb=xb: e.scalar_tensor_tensor(xb[:, :], xb[:, :], ssf[:, 3:4], g2bc[:, :],
                                                                      ALU.mult, ALU.mult),
                              r=[xbn, "ssf3", "g2bc"], w=[xbn])
                        ld(out_d[t * 128:(t + 1) * 128, :], xb[:, :], r=[xbn], w=["outd%d" % t])
                        S_.unit()

                def capB(fn, blk):
                    S_.begin(fine=True)
                    fn(blk)
                    return S_.end()

                for blk in range(NBLK):
                    stageM1(blk)
                    streams = [capB(stageM2, blk)]
                    if blk + 1 < NBLK:
                        streams.append(capB(stageN, blk + 1))
                    S_.run_merged(streams)
                S_.flush()
    return nc


def _colsT(v, L, n):
    v = np.asarray(v, np.float32).reshape(L, n, 128)
    return np.ascontiguousarray(v.transpose(2, 0, 1).reshape(128, L * n))


def make_in_maps(inp, L, nb):
    f = lambda a: np.ascontiguousarray(np.asarray(a, np.float32))
    shared = {
        "ada_w": f(inp["ada_w"]),
        "ada_bT": _colsT(inp["ada_b"], L, 48),
        "nmixT": _colsT(inp["norm_mix_w"], L, 8),
        "nmlpT": _colsT(inp["norm_mlp_w"], L, 8),
        "fnw": f(inp["final_norm_w"]).reshape(1, D),
        "w_in": f(inp["w_in"]), "w_out": f(inp["w_out"]), "w1": f(inp["mlp_w1"]), "w2": f(inp["mlp_w2"]),
        "lbT": _colsT(inp["hg_lb_logits"], L, 4),
        "onwT": _colsT(inp["hg_onorm_w"], L, 4),
        "cvw": np.ascontiguousarray(np.asarray(inp["cv_w"], np.float32).reshape(L, 31, 2, 128)
                                    .transpose(3, 0, 2, 1).reshape(128, L * 62)),
        "cvb": _colsT(inp["cv_b"], L, 2),
        "lnw": _colsT(inp["cv_ln_w"], L, 2),
        "lnb": _colsT(inp["cv_ln_b"], L, 2),
    }
    maps = []
    x = np.asarray(inp["x"], np.float32)
    c = np.asarray(inp["c"], np.float32)
    for b in range(nb):
        m = dict(shared)
        m["x"] = np.ascontiguousarray(x[b])
        m["cT"] = np.ascontiguousarray(c[b].reshape(8, 128).T)
        maps.append(m)
    return maps


_NC_CACHE = {}


def kernel(x, c, ada_w, ada_b, norm_mix_w, norm_mlp_w, w_in, hg_lb_logits, hg_onorm_w,
           cv_w, cv_b, cv_ln_w, cv_ln_b, w_out, mlp_w1, mlp_w2, final_norm_w):
    inp = dict(x=x, c=c, ada_w=ada_w, ada_b=ada_b, norm_mix_w=norm_mix_w, norm_mlp_w=norm_mlp_w, w_in=w_in,
               hg_lb_logits=hg_lb_logits, hg_onorm_w=hg_onorm_w, cv_w=cv_w, cv_b=cv_b, cv_ln_w=cv_ln_w,
               cv_ln_b=cv_ln_b, w_out=w_out, mlp_w1=mlp_w1, mlp_w2=mlp_w2, final_norm_w=final_norm_w)
    B, S, _ = np.asarray(x).shape
    L = np.asarray(w_in).shape[0]
    topk = min(256, S // 4)
    key = (S, L, topk)
    if key not in _NC_CACHE:
        _NC_CACHE[key] = build_nc(S, L, topk)
    nc = _NC_CACHE[key]
    maps = make_in_maps(inp, L, B)
    res = run_bass_kernel_spmd(nc, maps, core_ids=list(range(B)))
    return np.stack([np.asarray(r["out"], np.float32) for r in res.results], axis=0)
```
